# Optimizing a Trainium2 kernel written in Bass

```python
import math
import jax, jax.numpy as jnp
from jax import lax
import numpy as np

D_MODEL = 1024
BATCH = 4
SEQ = 4096
DEPTH = 1
DEC_BATCH = 128
DEC_SEQ = 8
PAST_LEN = 2048
PAGE_SIZE = 128

HEAD_DIM = 64
N_ATTN_HEADS = 8
N_KV_HEADS = 2
GROUP = N_ATTN_HEADS // N_KV_HEADS
N_RNN_HEADS = 8
RNN_DK = 64
RNN_DV = 64
ATTN_WIDTH = N_ATTN_HEADS * HEAD_DIM
RNN_WIDTH = N_RNN_HEADS * RNN_DV
MIX_WIDTH = ATTN_WIDTH + RNN_WIDTH
KV_WIDTH = N_KV_HEADS * HEAD_DIM
N_BRANCH = 3
CMP_BLOCK = 64
SEL_BLOCK = 64
SEL_TOPK = 16
WINDOW = 512
CMP_HIDDEN = 128
D_FF = 4 * D_MODEL
NSA_Q_BLOCK = 64
WIN_Q_BLOCK = 128
HGRN_CHUNK = 64
IN_WIDTH = ATTN_WIDTH + 6 * KV_WIDTH + N_ATTN_HEADS * N_BRANCH + 2 * N_RNN_HEADS * RNN_DK + 2 * RNN_WIDTH
SCALE = HEAD_DIM ** -0.5
EPS = 1e-6
NEG = -1e30
F32 = jnp.float32

kernel_name = 'nsa_hgrn2_hybrid_step'


def rms_norm(x, gain):
    xf = x.astype(F32)
    y = xf * lax.rsqrt(jnp.mean(xf * xf, axis=-1, keepdims=True) + EPS)
    return (y * gain.astype(F32)).astype(x.dtype)


def alibi_slopes():
    h = jnp.arange(1, N_ATTN_HEADS + 1, dtype=F32)
    return jnp.exp2(-8.0 * h / N_ATTN_HEADS).reshape(N_KV_HEADS, GROUP)


def split_points():
    sizes = (ATTN_WIDTH, 2 * KV_WIDTH, 2 * KV_WIDTH, 2 * KV_WIDTH, N_ATTN_HEADS * N_BRANCH,
             N_RNN_HEADS * RNN_DK, N_RNN_HEADS * RNN_DK, RNN_WIDTH, RNN_WIDTH)
    return [int(v) for v in np.cumsum(sizes)[:-1]]


def mix_projections(x, ln, w_in, q_gain, k_gain, lb):
    b, t, _ = x.shape
    z = rms_norm(x, ln) @ w_in
    q, kv_c, kv_s, kv_w, gates, rq, rf, ri, rg = jnp.split(z, split_points(), axis=-1)
    q = rms_norm(q.reshape(b, t, N_ATTN_HEADS, HEAD_DIM), q_gain)
    kv_c = kv_c.reshape(b, t, 2, N_KV_HEADS, HEAD_DIM)
    kv_s = kv_s.reshape(b, t, 2, N_KV_HEADS, HEAD_DIM)
    kv_w = kv_w.reshape(b, t, 2, N_KV_HEADS, HEAD_DIM)
    k_cmp, v_cmp = kv_c[:, :, 0], kv_c[:, :, 1]
    k_sel, v_sel = rms_norm(kv_s[:, :, 0], k_gain[1]), kv_s[:, :, 1]
    k_win, v_win = rms_norm(kv_w[:, :, 0], k_gain[2]), kv_w[:, :, 1]
    gates = jax.nn.sigmoid(gates.astype(F32)).reshape(b, t, N_ATTN_HEADS, N_BRANCH)
    rq = jax.nn.silu(rq.astype(F32)).reshape(b, t, N_RNN_HEADS, RNN_DK)
    f = (lb + (1.0 - lb) * jax.nn.sigmoid(rf.astype(F32))).reshape(b, t, N_RNN_HEADS, RNN_DK)
    rk = 1.0 - f
    g_log = jnp.log(f)
    rv = ri.astype(F32).reshape(b, t, N_RNN_HEADS, RNN_DV)
    return q, k_cmp, v_cmp, k_sel, v_sel, k_win, v_win, gates, rq, rk, rv, g_log, rg


def compress(rows, pe, w1, w2):
    b, l = rows.shape[:2]
    nc = l // CMP_BLOCK
    blk = rows[:, :nc * CMP_BLOCK].reshape(b, nc, CMP_BLOCK, N_KV_HEADS, HEAD_DIM) + pe[None, None, :, None, :]
    blk = blk.transpose(0, 1, 3, 2, 4).reshape(b, nc, N_KV_HEADS, CMP_BLOCK * HEAD_DIM)
    return jax.nn.silu(blk @ w1) @ w2


def nsa_memory(k_cmp, v_cmp, k_sel, v_sel, k_gain_cmp, pe, w1, w2):
    b, l = k_cmp.shape[:2]
    kc = rms_norm(compress(k_cmp, pe[0], w1[0], w2[0]), k_gain_cmp)
    vc = compress(v_cmp, pe[1], w1[1], w2[1])
    ns = -(-l // SEL_BLOCK)
    padlen = ns * SEL_BLOCK - l

    def blocks(x):
        return jnp.pad(x, ((0, 0), (0, padlen), (0, 0), (0, 0))).reshape(b, ns, SEL_BLOCK, N_KV_HEADS, HEAD_DIM)

    return kc, vc, blocks(k_sel), blocks(v_sel)


def cmp_sel_attend(q, q_pos, kc, vc, ksb, vsb, slopes):
    b, tq = q.shape[:2]
    nc, ns = kc.shape[1], ksb.shape[1]
    qg = q.reshape(b, tq, N_KV_HEADS, GROUP, HEAD_DIM).astype(F32)
    end = jnp.arange(nc, dtype=jnp.int32) * CMP_BLOCK + (CMP_BLOCK - 1)
    dist = (q_pos[:, None] - end[None, :]).astype(F32)
    valid = dist >= 0
    s = jnp.einsum('btgrd,bngd->bgrtn', qg, kc.astype(F32)) * SCALE - slopes[None, :, :, None, None] * dist
    s = jnp.where(valid, s, NEG)
    p = jax.nn.softmax(s, axis=-1) * valid
    o_cmp = jnp.einsum('bgrtn,bngd->btgrd', p, vc.astype(F32))
    imp = jnp.pad(p.sum(axis=2), ((0, 0), (0, 0), (0, 0), (0, ns - nc)))
    blk = jnp.arange(ns, dtype=jnp.int32)
    cur = q_pos // SEL_BLOCK
    force = (blk[None, :] == 0) | (blk[None, :] == cur[:, None])
    allowed = blk[None, :] <= cur[:, None]
    score = jnp.where(force, GROUP + 1.0, imp)
    score = jnp.where(allowed, score, -1.0)
    top_s, idx = lax.top_k(score, min(SEL_TOPK, ns))
    blk_ok = top_s >= 0
    gather = jax.vmap(jax.vmap(lambda bl, ix: bl[ix]))
    ks = gather(ksb.transpose(0, 3, 1, 2, 4), idx)
    vs = gather(vsb.transpose(0, 3, 1, 2, 4), idx)
    tok = idx[..., None] * SEL_BLOCK + jnp.arange(SEL_BLOCK, dtype=jnp.int32)
    dist_s = (q_pos[None, None, :, None, None] - tok).astype(F32)
    ok = blk_ok[..., None] & (dist_s >= 0)
    ss = jnp.einsum('btgrd,bgtksd->bgrtks', qg, ks.astype(F32)) * SCALE - slopes[None, :, :, None, None, None] * dist_s[:, :, None]
    ss = jnp.where(ok[:, :, None], ss, NEG)
    ps = jax.nn.softmax(ss.reshape(ss.shape[:4] + (-1,)), axis=-1).reshape(ss.shape)
    o_sel = jnp.einsum('bgrtks,bgtksd->btgrd', ps, vs.astype(F32))
    return (o_cmp.reshape(b, tq, N_ATTN_HEADS, HEAD_DIM), o_sel.reshape(b, tq, N_ATTN_HEADS, HEAD_DIM))


def nsa_prompt_cmp_sel(q, mem, slopes):
    b, t = q.shape[:2]
    nb = t // NSA_Q_BLOCK
    qb = q.reshape(b, nb, NSA_Q_BLOCK, N_ATTN_HEADS, HEAD_DIM).transpose(1, 0, 2, 3, 4)
    starts = jnp.arange(nb, dtype=jnp.int32) * NSA_Q_BLOCK
    kc, vc, ksb, vsb = mem

    def one(args):
        qblk, st = args
        return cmp_sel_attend(qblk, st + jnp.arange(NSA_Q_BLOCK, dtype=jnp.int32), kc, vc, ksb, vsb, slopes)

    o_c, o_s = lax.map(one, (qb, starts))
    back = lambda o: o.transpose(1, 0, 2, 3, 4).reshape(b, t, N_ATTN_HEADS, HEAD_DIM)
    return back(o_c), back(o_s)


def window_banded(q, k, v, slopes):
    b, t = q.shape[:2]
    qbl = WIN_Q_BLOCK
    nb, nprev = t // qbl, WINDOW // qbl
    pad = nprev * qbl

    def band(x):
        xp = jnp.pad(x, ((0, 0), (pad, 0), (0, 0), (0, 0))).reshape(b, nb + nprev, qbl, N_KV_HEADS, HEAD_DIM)
        return jnp.concatenate([xp[:, j:j + nb] for j in range(nprev + 1)], axis=2)

    kb, vb = band(k), band(v)
    qg = q.reshape(b, nb, qbl, N_KV_HEADS, GROUP, HEAD_DIM).astype(F32)
    start = jnp.arange(nb, dtype=jnp.int32)[:, None] * qbl
    q_pos = start + jnp.arange(qbl, dtype=jnp.int32)
    k_pos = start - pad + jnp.arange((nprev + 1) * qbl, dtype=jnp.int32)
    dist = (q_pos[:, :, None] - k_pos[:, None, :]).astype(F32)
    ok = (dist >= 0) & (dist < WINDOW) & (k_pos[:, None, :] >= 0)
    s = jnp.einsum('bnqgrd,bnkgd->bngrqk', qg, kb.astype(F32)) * SCALE - slopes[None, None, :, :, None, None] * dist[None, :, None, None]
    s = jnp.where(ok[None, :, None, None], s, NEG)
    p = jax.nn.softmax(s, axis=-1)
    o = jnp.einsum('bngrqk,bnkgd->bnqgrd', p, vb.astype(F32))
    return o.reshape(b, t, N_ATTN_HEADS, HEAD_DIM)


def window_dense(q, q_pos, k, v, k_pos, slopes):
    b, tq = q.shape[:2]
    qg = q.reshape(b, tq, N_KV_HEADS, GROUP, HEAD_DIM).astype(F32)
    dist = (q_pos[:, None] - k_pos[None, :]).astype(F32)
    ok = (dist >= 0) & (dist < WINDOW)
    s = jnp.einsum('btgrd,bkgd->bgrtk', qg, k.astype(F32)) * SCALE - slopes[None, :, :, None, None] * dist
    s = jnp.where(ok, s, NEG)
    p = jax.nn.softmax(s, axis=-1)
    o = jnp.einsum('bgrtk,bkgd->btgrd', p, v.astype(F32))
    return o.reshape(b, tq, N_ATTN_HEADS, HEAD_DIM)


def hgrn2_chunked(q, k, v, g_log, s0):
    b, t, nh = q.shape[:3]
    c = math.gcd(t, HGRN_CHUNK)
    n = t // c

    def chunks(x):
        return x.astype(F32).reshape(b, n, c, nh, x.shape[-1]).transpose(1, 0, 3, 2, 4)

    tri = jnp.tril(jnp.ones((c, c), dtype=bool))

    def step(S, inp):
        qc, kc, vc, gc = inp
        cum = jnp.cumsum(gc, axis=2)
        o_inter = jnp.einsum('bhtd,bhde->bhte', qc * jnp.exp(cum), S)
        decay = jnp.exp(jnp.where(tri[:, :, None], cum[:, :, :, None, :] - cum[:, :, None, :, :], -jnp.inf))
        a = jnp.einsum('bhtd,bhsd,bhtsd->bhts', qc, kc, decay)
        o = o_inter + jnp.einsum('bhts,bhse->bhte', a, vc)
        last = cum[:, :, -1:]
        S = jnp.exp(last[:, :, 0])[..., None] * S + jnp.einsum('bhsd,bhse->bhde', kc * jnp.exp(last - cum), vc)
        return S, o

    S, o = lax.scan(step, s0.astype(F32), (chunks(q), chunks(k), chunks(v), chunks(g_log)))
    return o.transpose(1, 0, 3, 2, 4).reshape(b, t, nh, v.shape[-1]), S


def finish(x, o_cmp, o_sel, o_win, gates, o_rnn, rg, attn_gain, rnn_gain, w_out, ln_mlp, w_up, w_down):
    b, t, _ = x.shape
    o_a = gates[..., 0:1] * o_cmp + gates[..., 1:2] * o_sel + gates[..., 2:3] * o_win
    o_a = rms_norm(o_a, attn_gain.reshape(N_ATTN_HEADS, HEAD_DIM)).reshape(b, t, ATTN_WIDTH)
    o_r = rms_norm(o_rnn, rnn_gain.reshape(N_RNN_HEADS, RNN_DV)).reshape(b, t, RNN_WIDTH) * jax.nn.silu(rg.astype(F32))
    h = x + jnp.concatenate([o_a, o_r], axis=-1).astype(x.dtype) @ w_out
    u = jax.nn.relu(rms_norm(h, ln_mlp) @ w_up)
    return h + (u * u) @ w_down


def setup_inputs(seed: int = 0) -> dict:
    key = jax.random.key(seed)
    k = jax.random.split(key, 20)
    n_pages = PAST_LEN // PAGE_SIZE
    n_used = DEC_BATCH * n_pages
    n_pool = n_used + (n_used + 3) // 4
    w_buf = min(WINDOW, PAST_LEN)

    def normal(kk, shape, scale):
        return jax.random.normal(kk, shape, F32) * scale

    def gain(kk, shape):
        return 1.0 + normal(kk, shape, 0.05)

    page_table = jax.random.permutation(k[5], n_pool)[:n_used].reshape(DEC_BATCH, n_pages).astype(jnp.int32)
    return {
        'x_prompt': normal(k[0], (BATCH, SEQ, D_MODEL), 1.0),
        'x_sample': normal(k[1], (DEC_BATCH, DEC_SEQ, D_MODEL), 1.0),
        'cache_kv': normal(k[2], (DEPTH, n_pool, PAGE_SIZE, 4, N_KV_HEADS, HEAD_DIM), 1.0),
        'cache_win': normal(k[3], (DEPTH, DEC_BATCH, w_buf, 2, N_KV_HEADS, HEAD_DIM), 1.0),
        'state_rnn': normal(k[4], (DEPTH, DEC_BATCH, N_RNN_HEADS, RNN_DK, RNN_DV), 0.3),
        'page_table': page_table,
        'ln_mix': gain(k[6], (DEPTH, D_MODEL)),
        'w_in': normal(k[7], (DEPTH, D_MODEL, IN_WIDTH), D_MODEL ** -0.5),
        'q_norm': gain(k[8], (DEPTH, HEAD_DIM)),
        'k_norm': gain(k[9], (DEPTH, N_BRANCH, HEAD_DIM)),
        'cmp_pe': normal(k[10], (DEPTH, 2, CMP_BLOCK, HEAD_DIM), 0.1),
        'cmp_w1': normal(k[11], (DEPTH, 2, CMP_BLOCK * HEAD_DIM, CMP_HIDDEN), (CMP_BLOCK * HEAD_DIM) ** -0.5),
        'cmp_w2': normal(k[12], (DEPTH, 2, CMP_HIDDEN, HEAD_DIM), CMP_HIDDEN ** -0.5),
        'attn_out_norm': gain(k[13], (DEPTH, ATTN_WIDTH)),
        'rnn_lb_logits': normal(k[14], (DEPTH + 1, N_RNN_HEADS * RNN_DK), 0.5),
        'rnn_out_norm': gain(k[15], (DEPTH, RNN_WIDTH)),
        'w_out': normal(k[16], (DEPTH, MIX_WIDTH, D_MODEL), MIX_WIDTH ** -0.5),
        'ln_mlp': gain(k[17], (DEPTH, D_MODEL)),
        'w_up': normal(k[18], (DEPTH, D_MODEL, D_FF), D_MODEL ** -0.5),
        'w_down': normal(k[19], (DEPTH, D_FF, D_MODEL), D_FF ** -0.5),
    }


def reference(x_prompt, x_sample, cache_kv, cache_win, state_rnn, page_table, ln_mix, w_in, q_norm, k_norm,
              cmp_pe, cmp_w1, cmp_w2, attn_out_norm, rnn_lb_logits, rnn_out_norm, w_out, ln_mlp, w_up, w_down):
    slopes = alibi_slopes()
    lb_all = jnp.cumsum(jax.nn.softmax(rnn_lb_logits.astype(F32), axis=0), axis=0)
    dec_b, dec_t = x_sample.shape[:2]
    past_len = page_table.shape[1] * PAGE_SIZE
    w_buf = cache_win.shape[2]
    xp, xs = x_prompt, x_sample
    kv_p, kv_s, win_p, win_s, rnn_p, rnn_s = [], [], [], [], [], []
    for l in range(DEPTH):
        lb = lb_all[l]
        b, t = xp.shape[:2]
        q, kc, vc, ksl, vsl, kw, vw, gates, rq, rk, rv, glog, rg = mix_projections(xp, ln_mix[l], w_in[l], q_norm[l], k_norm[l], lb)
        mem = nsa_memory(kc, vc, ksl, vsl, k_norm[l, 0], cmp_pe[l], cmp_w1[l], cmp_w2[l])
        o_c, o_s = nsa_prompt_cmp_sel(q, mem, slopes)
        o_w = window_banded(q, kw, vw, slopes)
        o_r, s_p = hgrn2_chunked(rq, rk, rv, glog, jnp.zeros((b, N_RNN_HEADS, RNN_DK, RNN_DV), F32))
        kv_p.append(jnp.stack([kc, vc, ksl, vsl], axis=2))
        win_p.append(jnp.stack([kw, vw], axis=2)[:, t - min(WINDOW, t):])
        rnn_p.append(s_p.astype(state_rnn.dtype))
        xp = finish(xp, o_c, o_s, o_w, gates, o_r, rg, attn_out_norm[l], rnn_out_norm[l], w_out[l], ln_mlp[l], w_up[l], w_down[l])
        q, kc, vc, ksl, vsl, kw, vw, gates, rq, rk, rv, glog, rg = mix_projections(xs, ln_mix[l], w_in[l], q_norm[l], k_norm[l], lb)
        new_kv = jnp.stack([kc, vc, ksl, vsl], axis=2)
        past = cache_kv[l][page_table].reshape(dec_b, past_len, 4, N_KV_HEADS, HEAD_DIM)
        all_kv = jnp.concatenate([past, new_kv.astype(past.dtype)], axis=1)
        mem = nsa_memory(all_kv[:, :, 0], all_kv[:, :, 1], all_kv[:, :, 2], all_kv[:, :, 3],
                         k_norm[l, 0], cmp_pe[l], cmp_w1[l], cmp_w2[l])
        q_pos = past_len + jnp.arange(dec_t, dtype=jnp.int32)
        o_c, o_s = cmp_sel_attend(q, q_pos, mem[0], mem[1], mem[2], mem[3], slopes)
        win_all = jnp.concatenate([cache_win[l], jnp.stack([kw, vw], axis=2).astype(cache_win.dtype)], axis=1)
        k_pos = past_len - w_buf + jnp.arange(w_buf + dec_t, dtype=jnp.int32)
        o_w = window_dense(q, q_pos, win_all[:, :, 0], win_all[:, :, 1], k_pos, slopes)
        o_r, s_s = hgrn2_chunked(rq, rk, rv, glog, state_rnn[l])
        kv_s.append(new_kv)
        win_s.append(win_all[:, dec_t:])
        rnn_s.append(s_s.astype(state_rnn.dtype))
        xs = finish(xs, o_c, o_s, o_w, gates, o_r, rg, attn_out_norm[l], rnn_out_norm[l], w_out[l], ln_mlp[l], w_up[l], w_down[l])
    return (xp, xs, jnp.stack(kv_p), jnp.stack(kv_s), jnp.stack(win_p), jnp.stack(win_s), jnp.stack(rnn_p), jnp.stack(rnn_s))
```

```python
import contextlib
import numpy as np
import concourse.bass as bass
import concourse.mybir as mybir
from concourse.bass_utils import run_bass_kernel_spmd

F32 = mybir.dt.float32
BF16 = mybir.dt.bfloat16
AF = mybir.ActivationFunctionType
ALU = mybir.AluOpType
AX = mybir.AxisListType

NT = 33
D = 1024
INW = 3352
C_Q, C_KV, C_G, C_RQ, C_RF, C_RI, C_RG = 0, 512, 1280, 1304, 1816, 2328, 2840
EPS = 1e-6

ENGS = ("pe", "act", "dve", "pool", "sp")
DMA_POOL = 8


class Op:
    __slots__ = ("eng", "fn", "deps", "dma", "idx", "needed", "token")

    def __init__(self, eng, fn, deps, dma, idx):
        self.eng, self.fn, self.deps, self.dma, self.idx = eng, fn, deps, dma, idx
        self.needed = False
        self.token = None


class Sched:
    def __init__(self, nc):
        self.nc = nc
        self.ops = []
        self.last_w = {}
        self.readers = {}
        self.final = []

    def add(self, eng, fn, r=(), w=(), dma=False, final=False, rg=0):
        idx = len(self.ops)
        deps = {}
        if eng == "pe":
            prev = getattr(self, "prev_pe", None)
            if prev is not None and prev[1] != rg:
                deps[prev[0]] = "force"
            self.prev_pe = (idx, rg)
        for k in r:
            lw = self.last_w.get(k)
            if lw is not None and deps.get(lw) != "force":
                deps[lw] = True
        for k in w:
            lw = self.last_w.get(k)
            if lw is not None and deps.get(lw) != "force":
                deps[lw] = True
            for rd in self.readers.get(k, ()):
                if rd not in deps:
                    deps[rd] = False
        for k in r:
            self.readers.setdefault(k, []).append(idx)
        for k in w:
            self.last_w[k] = idx
            self.readers[k] = []
        self.ops.append(Op(eng, fn, deps, dma, idx))
        if final:
            self.final.append(idx)
        return idx

    def emit(self):
        nc = self.nc
        ops = self.ops
        for op in ops:
            nd = {}
            for d, strong in op.deps.items():
                p = ops[d]
                if p.eng == op.eng and not p.dma:
                    if op.eng == "pe" and strong != "force":
                        continue
                nd[d] = strong
            best = {}
            keep = {}
            for d, strong in nd.items():
                p = ops[d]
                if p.dma:
                    keep[d] = strong
                elif p.eng not in best or d > best[p.eng]:
                    best[p.eng] = d
            for d in best.values():
                keep[d] = True
            op.deps = keep
            for d in keep:
                ops[d].needed = True
        for f in self.final:
            ops[f].needed = True
        cnt = {e: 0 for e in ENGS}
        dcnt = {e: 0 for e in ENGS}
        with contextlib.ExitStack() as es:
            NEP = 8
            EPOCH = 1500
            csem = {e: [es.enter_context(nc.semaphore("c_%s%d" % (e, i))) for i in range(NEP if e in ("pe", "dve", "act") else 1)]
                    for e in ENGS}
            dsem = {e: [es.enter_context(nc.semaphore("d_%s%d" % (e, i))) for i in range(DMA_POOL)]
                    for e in ("sp", "pool", "act")}
            for op in ops:
                if op.dma:
                    j = dcnt[op.eng]
                    dcnt[op.eng] += 1
                    op.token = (dsem[op.eng][j % DMA_POOL], 16 * (j // DMA_POOL + 1))
                elif op.needed:
                    ep = cnt[op.eng] // EPOCH
                    assert ep < len(csem[op.eng]), (op.eng, cnt[op.eng])
                    op.token = (csem[op.eng][ep], cnt[op.eng] % EPOCH + 1)
                    cnt[op.eng] += 1
            per = {e: [op for op in ops if op.eng == e] for e in ENGS}
            final_tokens = [ops[f].token for f in self.final]

            def run(engname, eng):
                known = {}
                for op in per[engname]:
                    for d in op.deps:
                        sem, val = ops[d].token
                        if known.get(id(sem), 0) >= val:
                            continue
                        eng.wait_ge(sem, val)
                        known[id(sem)] = val
                    if op.dma:
                        sem, val = op.token
                        if val > 16 and known.get(id(sem), 0) < val - 16:
                            eng.wait_ge(sem, val - 16)
                            known[id(sem)] = val - 16
                        op.fn(eng).then_inc(sem, 16)
                    else:
                        ins = op.fn(eng)
                        if op.needed:
                            ins.then_inc(op.token[0], 1)
                if engname == "sp":
                    for sem, val in final_tokens:
                        if known.get(id(sem), 0) >= val:
                            continue
                        eng.wait_ge(sem, val)
                        known[id(sem)] = val

            with nc.Block() as block:
                @block.tensor
                def _(e):
                    run("pe", e)

                @block.scalar
                def _(e):
                    run("act", e)

                @block.vector
                def _(e):
                    run("dve", e)

                @block.gpsimd
                def _(e):
                    run("pool", e)

                @block.sync
                def _(e):
                    run("sp", e)


import os
BIG = 30000.0
STAGE = int(os.environ.get("KSTAGE", "9"))
SUB = int(os.environ.get("KSUB", "9"))
SKIP = os.environ.get("KSKIP", "")
NOWN = 17
SCALE = 0.125
DEBUG = False


def build_nc(debug=False):
    nc = bass.Bass("TRN2", target_bir_lowering=False)

    def din(name, shape, dt=F32):
        return nc.dram_tensor(name, list(shape), dt, kind="ExternalInput").ap()

    def dout(name, shape, dt=F32):
        return nc.dram_tensor(name, list(shape), dt, kind="ExternalOutput").ap()

    xv = din("xv", [NT * 128, D])
    xsm = din("xsm", [128, D])
    w_in = din("w_in", [D, INW])
    w_out = din("w_out", [D, D])
    w_up = din("w_up", [D, 4 * D])
    w_down = din("w_down", [4 * D, D])
    lnmcol = din("lnmcol", [128, 8])
    lncol = din("lncol", [128, 8])
    gvec_d = din("gvecd", [128, 1280])
    lbl = din("lbl", [128, 2, 512])
    ident = din("ident", [128, 128])
    tmat = din("tmat", [128, 4, 128])
    rowmask = din("rowmask", [128, 16])
    cst_d = din("cstd", [128, 747])
    masks_d = din("masks4", [128, 4, 512])
    eexp_d = din("eexp", [66, NT * 128])
    kpos_d = din("kposrows", [4, NT * 128])
    qpos_d = din("qposrows", [NOWN, 4, 1024])
    pet_d = din("peTd", [128, 2, 32])
    w1_d = din("cmp_w1", [2, 4096, 128])
    w2_d = din("cmp_w2", [128, 2, 64])
    cwin = din("cwin", [16, 512, 256])
    cache_d = din("cache", [327680, 512])
    ptrep_d = din("ptrep", [128, 256], mybir.dt.int32)
    csts_d = din("cstsd", [128, 74])
    iota_d = din("iotad", [128, 1])
    sq_d = din("sqd", [8, 64])
    sm_d = din("smd", [128, 192])
    srnn = din("srnn", [16, 8, 64, 64])

    yp = dout("yp", [NOWN * 128, D])
    ys = dout("ys", [128, D])
    kvp = dout("kvp", [NT * 128, 512])
    kvs = dout("kvs", [128, 512])
    winp = dout("winp", [5 * 128, 256])
    wins = dout("wins", [16, 512, 256])
    rnnp = dout("rnnp", [8, 64, 64])
    rnns = dout("rnns", [16, 8, 64, 64])
    if debug:
        dbg = dout("dbg", [(NOWN + 1) * 128, 5, 520])

    S = Sched(nc)
    with contextlib.ExitStack() as es:
        def sb(name, shape, dt=F32):
            return es.enter_context(nc.sbuf_tensor(name, list(shape), dt))

        def ps(name, shape, dt=F32):
            return es.enter_context(nc.psum_tensor(name, list(shape), dt))

        def A(eng, r, w, fn, **kw):
            S.add(eng, fn, r=r, w=w, **kw)

        BIGW = sb("BIGW", [128, 65536], BF16)
        W = BIGW[:, 0:26816].rearrange("p (k n) -> p k n", k=8)
        ARENA = BIGW[:, 26816:47808]
        WU = BIGW[:, 0:32768].rearrange("p (k n) -> p k n", k=8)
        WD = BIGW[:, 32768:65536].rearrange("p (f n) -> p f n", f=32)
        _bo = [47808]

        def bw(n, parts=128):
            a = _bo[0]
            _bo[0] += n
            assert _bo[0] <= 65536
            return BIGW[0:parts, a:a + n]
        XcT = ARENA[:, 0:8448].rearrange("p (a n) -> p a n", a=4)
        W1b = ARENA[:, 8448:16640].rearrange("p (a j h) -> p a j h", a=2, j=32)
        KTs = ARENA[:, 0:8448].rearrange("p (g n) -> p g n", g=2)
        VVs = ARENA[:, 8448:12738].rearrange("p (t g d) -> p t g d", t=NT, g=2)
        WO = ARENA[:, 12800:20992].rearrange("p (k n) -> p k n", k=8)
        AKEYS = ["XcT", "W1b", "KTs", "VVs", "WO"]
        KTw = sb("KTw", [128, 2, 768], BF16)
        VVw = sb("VVw", [128, 6, 2, 65], BF16)
        Eexp = bw(NT * 128, 128)
        MSK = bw(2048).rearrange("p (m n) -> p m n", m=4)
        cst = sb("cst", [128, 747])
        gvec = sb("gvec", [128, 1280])
        qgs = sb("qgs", [128, 64])
        lnc = sb("lnc", [128, 8])
        lbb = sb("lbb", [128, 512])
        omlb = sb("omlb", [128, 512])
        idb = sb("idb", [128, 128], BF16)
        tm = sb("tm", [128, 4, 128])
        rmask = sb("rmask", [128, 16])
        ones = sb("ones", [128, 1])
        peT = sb("peT", [128, 2, 32])
        W2b = sb("W2b", [128, 2, 64], BF16)
        kcT = sb("kcT", [64, 2, 66], BF16)
        vcb = sb("vcb", [66, 2, 64], BF16)
        fdummy = sb("fdummy", [128, 1])
        acore = cst[:, 0:528].rearrange("p (h j) -> p h j", h=8)
        fix2 = cst[:, 528:530]
        rowvalid = cst[:, 530:547]
        keepc = cst[:, 547:613]
        addc = cst[:, 613:679]
        mlo5 = cst[:, 679:680]
        mhi = cst[:, 680:681]
        c1col = cst[:, 681:682]
        tri2 = cst[:, 682:746]
        xs = [sb("xs%d" % i, [128, D]) for i in range(2)]
        xb = sb("xb", [128, D], BF16)
        xT = [bw(1024).rearrange("p (k t) -> p k t", k=8) for i in range(2)]
        ss = sb("ss", [128, 1])
        rstd = sb("rstd", [128, 1])
        Z = [sb("Z0", [128, 768])] * 2
        zb = sb("zb", [128, 4, 64], BF16)
        sq4 = sb("sq4", [128, 2, 2, 64])
        ms4 = sb("ms4", [128, 2, 2])
        T = [sb("T%d" % i, [128, 512]) for i in range(6)]
        QS = sb("QS", [128, 512])
        RGS = sb("RGS", [128, 512])
        KD = bw(512)
        VB = bw(512)
        QD = bw(512)
        KDI = bw(512)
        qkT = bw(1024).rearrange("p (a t) -> p a t", a=8)
        AT = bw(512).rearrange("p (h t) -> p h t", h=8)
        Sb = [bw(256).rearrange("p (a d) -> p a d", a=4) for i in range(2)]
        ET = sb("ET", [128, 2, 4])
        St = sb("St", [128, 4, 64])
        ZQ = sb("ZQ", [128, 8, 64])
        QN = bw(512).rearrange("p (h d) -> p h d", h=8)
        st8 = sb("st8", [128, 8])
        QTa = sb("QTa", [128, 8, 128], BF16)
        gsig = sb("gsig", [128, 8, 3])
        LG = sb("LG", [128, 4, 66])
        EX = sb("EX", [128, 4, 66])
        PB = bw(264).rearrange("p (h j) -> p h j", h=4)
        st4 = sb("st4", [128, 4])
        st4b = sb("st4b", [128, 4])
        SC = sb("SC", [128, 66])
        WK = sb("WK", [128, 66])
        SEL = sb("SEL", [128, 66])
        m8 = sb("m8", [128, 8])
        thr = sb("thr", [128, 1])
        MBF = sb("MBF", [128, 66], BF16)
        RHSM_ = [bw(512, 128).rearrange("p (h t) -> p h t", h=4) for i in range(2)]
        PT4 = bw(512, 66).rearrange("p (h t) -> p h t", h=4)
        PTb = [sb("PTb%d" % i, [128, 512], BF16) for i in range(3)]
        OSEL = sb("OSEL", [128, 8, 65])
        OWIN = sb("OWIN", [128, 8, 65])
        OCMP = sb("OCMP", [128, 8, 64])
        OR32 = sb("OR32", [128, 8, 64])
        g8 = sb("g8", [128, 8, 3])
        OC = bw(1024)
        OCT = bw(1024).rearrange("p (k t) -> p k t", k=8)
        uT = sb("uT", [128, 16, 128], BF16)
        lnmc = sb("lnmc", [128, 8])
        H = sb("H", [128, D])
        stgq = H
        TMP = T[5][:].rearrange("p (h d) -> p h d", h=8)
        ACC = T[4][:].rearrange("p (h d) -> p h d", h=8)
        junk = OCT
        hs = bw(264).rearrange("p (h j) -> p h j", h=4)
        kcr = sb("kcr", [66, 2, 64])
        kcbf = sb("kcbf", [66, 2, 64], BF16)
        S0 = ARENA[:, 0:8192].bitcast(F32).rearrange("p (a d) -> p a d", a=64)
        etots = sb("etots", [128, 4, 16])
        kdm = PTb

        K0 = ps("K0", [128, 1024], BF16)
        K = [None] + [ps("K%d" % i, [128, 512]) for i in range(1, 8)]

        def ld(dst, src, key, eng="sp"):
            S.add(eng, lambda e: e.dma_start(out=dst, in_=src), w=[key], dma=True)

        ld(lnc[:], lncol, "lnc")
        ld(gvec[:], gvec_d, "gvec")
        ld(cst[:], cst_d, "cst")
        ld(tm[:], tmat, "tm")
        ld(rmask[:], rowmask, "rmask")
        ld(peT[:], pet_d, "peT")
        ld(T[0][:], lbl[:, 0, :], "T0")
        ld(T[1][:], lbl[:, 1, :], "T1")
        ld(T[2][:, 0:128], ident, "T2")
        ld(T[3][:, 0:128], w2_d.rearrange("p a d -> p (a d)"), "T3")
        A("dve", ["T2"], ["idb"], lambda e: e.tensor_copy(out=idb[:], in_=T[2][:, 0:128]))
        A("dve", ["T3"], ["W2b"], lambda e: e.tensor_copy(out=W2b[:].rearrange("p a d -> p (a d)"), in_=T[3][:, 0:128]))
        A("dve", [], ["ones"], lambda e: e.memset(ones[:], 1.0))
        A("dve", [], ["St"], lambda e: e.memset(St[:], 0.0))
        A("dve", ["gvec"], ["qgs"], lambda e: e.tensor_scalar_mul(out=qgs[:], in0=gvec[:, 192:256], scalar1=SCALE))
        A("dve", ["T0", "T1"], ["lbb"], lambda e: e.tensor_sub(out=lbb[:], in0=T[1][:], in1=T[0][:]))
        A("act", ["lbb"], ["lbb"], lambda e: e.activation(out=lbb[:], in_=lbb[:], func=AF.Exp))
        A("dve", ["lbb"], ["lbb"], lambda e: e.tensor_scalar_add(out=lbb[:], in0=lbb[:], scalar1=1.0))
        A("dve", ["lbb"], ["lbb"], lambda e: e.reciprocal(out=lbb[:], in_=lbb[:]))
        A("dve", ["lbb"], ["omlb"], lambda e: e.tensor_scalar(out=omlb[:], in0=lbb[:], scalar1=-1.0, scalar2=1.0,
                                                              op0=ALU.mult, op1=ALU.add))

        for k in range(8):
            for hf in range(4):
                c0, c1 = hf * 838, (hf + 1) * 838
                A("sp", [], ["H"], lambda e, k=k, c0=c0, c1=c1: e.dma_start(
                    out=stgq[:, 0:838], in_=w_in[k * 128:(k + 1) * 128, c0:c1]), dma=True)
                if hf % 2 == 0:
                    A("act", ["H", "lnc"], ["W%d" % k], lambda e, k=k, c0=c0, c1=c1: e.activation(
                        out=W[:, k, c0:c1], in_=stgq[:, 0:838], func=AF.Copy, scale=lnc[:, k:k + 1]))
                else:
                    A("dve", ["H", "lnc"], ["W%d" % k], lambda e, k=k, c0=c0, c1=c1: e.tensor_scalar(
                        out=W[:, k, c0:c1], in0=stgq[:, 0:838], scalar1=lnc[:, k:k + 1], scalar2=None, op0=ALU.mult))
        WK_ = ["W%d" % k for k in range(8)]

        def prep_x(xsrc, p):
            X, XT = xs[p], xT[p]
            kx, kxt = "xs%d" % p, "xT%d" % p
            A("sp", [], [kx], lambda e: e.dma_start(out=X[:], in_=xsrc), dma=True)
            A("act", [kx], ["OCT", "ss"], lambda e: e.activation(out=OCT[:].rearrange("p k t -> p (k t)"), in_=X[:], func=AF.Square, accum_out=ss[:]))
            A("act", ["ss"], ["rstd"], lambda e: e.activation(out=rstd[:], in_=ss[:], func=AF.Ln, scale=1.0 / D, bias=EPS))
            A("act", ["rstd"], ["rstd"], lambda e: e.activation(out=rstd[:], in_=rstd[:], func=AF.Exp, scale=-0.5))
            A("dve", [kx, "rstd"], ["xb"], lambda e: e.tensor_scalar(out=xb[:], in0=X[:], scalar1=rstd[:, 0:1],
                                                                     scalar2=None, op0=ALU.mult))
            for k in range(8):
                A("pe", ["xb", "idb"], ["K0"], lambda e, k=k: e.transpose(
                    out=K0[:, k * 128:(k + 1) * 128], in_=xb[:, k * 128:(k + 1) * 128], identity=idb[:]))
            A("dve", ["K0"], [kxt], lambda e: e.tensor_copy(out=XT[:].rearrange("p k t -> p (k t)"), in_=K0[:]))
            return X, XT, kx, kxt

        def mm(XT, kxt, bank, n, c0):
            for k in range(8):
                A("pe", [kxt] + WK_, ["K%d" % bank], lambda e, k=k: e.matmul(
                    K[bank][:, 0:n], lhsT=XT[:, k, :], rhs=W[:, k, c0:c0 + n], start=(k == 0), stop=(k == 7)))

        A("dve", [], AKEYS + ["fdummy"], lambda e: e.memset(fdummy[:], 0.0))
        for typ in range(2):
            for jq in range(4):
                A("sp", [], ["H"], lambda e, typ=typ, jq=jq: e.dma_start(
                    out=stgq[:].rearrange("p (j h) -> p j h", j=8),
                    in_=w1_d[typ, jq * 1024:(jq + 1) * 1024, :].rearrange("(j p) h -> p j h", p=128)), dma=True)
                A("dve", ["H"], ["W1b"], lambda e, typ=typ, jq=jq: e.tensor_copy(
                    out=W1b[:, typ, jq * 8:(jq + 1) * 8, :], in_=stgq[:].rearrange("p (j h) -> p j h", j=8)))
        def tile_1a(v):
            p = v % 2
            X, XT, kx, kxt = prep_x(xv[v * 128:(v + 1) * 128, :], p)
            K1v = K[1][:, 0:256].rearrange("p (a t) -> p a t", a=4)
            for tg in range(4):
                c0 = C_KV + tg * 64
                for par in range(2):
                    for k in range(8):
                        A("pe", [kxt] + WK_, ["K1"], lambda e, tg=tg, par=par, k=k, c0=c0: e.matmul(
                            K1v[64 * par:64 * par + 64, tg, :], lhsT=W[:, k, c0:c0 + 64], rhs=XT[:, k, par:128:2],
                            start=(k == 0), stop=(k == 7)))
            for a in range(2):
                A("dve", ["K1", "peT"], ["XcT"], lambda e, v=v, K1v=K1v, a=a: e.tensor_tensor(
                    out=XcT[:, 2 * a:2 * a + 2, v * 64:(v + 1) * 64].rearrange("p g (b j) -> p g b j", b=2),
                    in0=K1v[:, 2 * a:2 * a + 2, :].rearrange("p g (b j) -> p g b j", b=2),
                    in1=peT[:, a, :].unsqueeze(1).unsqueeze(1).to_broadcast([128, 2, 2, 32]), op=ALU.add))
        for v_ in range(NT if STAGE >= 1 else 0):
            tile_1a(v_)
        K3v = K[3][:, 0:256].rearrange("p (a d) -> p a d", a=4)
        for tg in range(4):
            typ = tg // 2
            for j in range(32):
                A("pe", ["XcT", "W1b"], ["K2"], lambda e, tg=tg, typ=typ, j=j: e.matmul(
                    K[2][:, 0:66], lhsT=W1b[:, typ, j, :], rhs=XcT[:, tg, j:2112:32], start=(j == 0), stop=(j == 31)))
            A("act", ["K2"], ["T4"], lambda e: e.activation(out=T[4][:, 0:66], in_=K[2][:, 0:66], func=AF.Exp, scale=-1.0))
            A("dve", ["T4"], ["T4"], lambda e: e.tensor_scalar_add(out=T[4][:, 0:66], in0=T[4][:, 0:66], scalar1=1.0))
            A("dve", ["T4"], ["T4"], lambda e: e.reciprocal(out=T[4][:, 0:66], in_=T[4][:, 0:66]))
            A("dve", ["T4", "K2"], ["hs"], lambda e, tg=tg: e.tensor_tensor(out=hs[:, tg, :], in0=K[2][:, 0:66],
                                                                           in1=T[4][:, 0:66], op=ALU.mult))
            A("pe", ["hs", "W2b"], ["K3"], lambda e, tg=tg, typ=typ: e.matmul(
                K3v[0:66, tg, :], lhsT=hs[:, tg, :], rhs=W2b[:, typ, :], start=True, stop=True))
        A("act", ["K3"], ["kcr"], lambda e: e.activation(out=kcr[:], in_=K3v[0:66, 0:2, :], func=AF.Copy))
        A("act", ["K3"], ["vcb"], lambda e: e.activation(out=vcb[:], in_=K3v[0:66, 2:4, :], func=AF.Copy))
        A("dve", ["kcr"], ["T5"], lambda e: e.tensor_tensor(out=T[5][0:66, 0:128].rearrange("p (g d) -> p g d", g=2),
                                                            in0=kcr[:], in1=kcr[:], op=ALU.mult))
        A("dve", ["T5"], ["st4"], lambda e: e.tensor_reduce(out=st4[0:66, 0:2],
                                                            in_=T[5][0:66, 0:128].rearrange("p (g d) -> p g d", g=2),
                                                            axis=AX.X, op=ALU.add))
        A("act", ["st4"], ["st4"], lambda e: e.activation(out=st4[0:66, 0:2], in_=st4[0:66, 0:2], func=AF.Ln,
                                                          scale=1.0 / 64, bias=EPS))
        A("act", ["st4"], ["st4"], lambda e: e.activation(out=st4[0:66, 0:2], in_=st4[0:66, 0:2], func=AF.Exp, scale=-0.5))
        A("dve", ["kcr", "st4"], ["kcr"], lambda e: e.tensor_tensor(
            out=kcr[:], in0=kcr[:], in1=st4[0:66, 0:2].unsqueeze(2).to_broadcast([66, 2, 64]), op=ALU.mult))
        A("dve", ["kcr", "gvec"], ["kcbf"], lambda e: e.tensor_tensor(
            out=kcbf[:], in0=kcr[:], in1=gvec[0:66, 0:64].unsqueeze(1).to_broadcast([66, 2, 64]), op=ALU.mult))
        for g in range(2):
            A("pe", ["kcbf", "idb"], ["K0"], lambda e, g=g: e.transpose(
                out=K0[0:64, g * 128:g * 128 + 66], in_=kcbf[:, g, :], identity=idb[0:66, 0:66]))
        A("dve", ["K0"], ["kcT"], lambda e: e.tensor_copy(
            out=kcT[:], in_=K0[0:64, 0:256].rearrange("p (g n) -> p g n", g=2)[:, :, 0:66]))
        A("dve", [], AKEYS + ["fdummy"], lambda e: e.memset(fdummy[:], 0.0))

        A("dve", [], ["KTs"], lambda e: e.memset(KTs[64:128, :, :], 0.0))
        for c0 in range(0, NT * 128, 1024):
            n = min(1024, NT * 128 - c0)
            A("sp", [], ["H"], lambda e, c0=c0, n=n: e.dma_start(out=stgq[0:66, 0:n], in_=eexp_d[:, c0:c0 + n]), dma=True)
            A("dve", [], ["Eexp"], lambda e, c0=c0, n=n: e.memset(Eexp[64:128, c0:c0 + n], 0.0))
            A("dve", ["H"], ["Eexp"], lambda e, c0=c0, n=n: e.tensor_copy(out=Eexp[0:66, c0:c0 + n], in_=stgq[0:66, 0:n]))
            A("sp", [], ["H"], lambda e, c0=c0, n=n: e.dma_start(out=stgq[64:68, 0:n], in_=kpos_d[:, c0:c0 + n]), dma=True)
            for g in range(2):
                A("dve", ["H"], ["KTs"], lambda e, c0=c0, n=n, g=g: e.tensor_copy(
                    out=KTs[64:68, g, c0:c0 + n], in_=stgq[64:68, 0:n]))
        for m in range(4):
            A("sp", [], ["H"], lambda e, m=m: e.dma_start(out=stgq[:, 0:512], in_=masks_d[:, m, :]), dma=True)
            A("dve", ["H"], ["MSK"], lambda e, m=m: e.tensor_copy(out=MSK[:, m, :], in_=stgq[:, 0:512]))
        for k in range(8):
            A("sp", [], ["H"], lambda e, k=k: e.dma_start(out=stgq[:], in_=w_out[k * 128:(k + 1) * 128, :]), dma=True)
            A("dve", ["H"], ["WO"], lambda e, k=k: e.tensor_copy(out=WO[:, k, :], in_=stgq[:]))
        A("dve", [], ["KTw"], lambda e: e.memset(KTw[64:128, :, :], 0.0))
        A("dve", [], ["QTa"], lambda e: e.memset(QTa[64:128, :, :], 0.0))
        for g_ in range(2):
            A("dve", [], ["RHSM%d" % g_], lambda e, g_=g_: e.memset(RHSM_[g_][64:128, :, :], 0.0))
        A("dve", [], ["VVs"], lambda e: e.memset(VVs[:, :, :, 64:65], 1.0))
        A("dve", [], ["VVw"], lambda e: e.memset(VVw[:, :, :, 64:65], 1.0))

        def rnn_gates(bf, bi, sample):
            kf, ki = "K%d" % bf, "K%d" % bi
            e1, uu, kk, G = T[0], T[1], T[2], T[3]
            A("act", [kf], ["T0"], lambda e: e.activation(out=e1[:], in_=K[bf][:], func=AF.Exp, scale=-1.0))
            A("act", [ki], ["VB"], lambda e: e.activation(out=VB[:], in_=K[bi][:], func=AF.Copy))
            A("dve", ["T0"], ["T0"], lambda e: e.tensor_scalar_add(out=e1[:], in0=e1[:], scalar1=1.0))
            A("dve", ["T0"], ["T0"], lambda e: e.reciprocal(out=e1[:], in_=e1[:]))
            A("dve", ["T0", "omlb"], ["T1"], lambda e: e.tensor_tensor(out=uu[:], in0=e1[:], in1=omlb[:], op=ALU.mult))
            A("dve", ["T1", "lbb"], ["T0"], lambda e: e.tensor_tensor(out=e1[:], in0=uu[:], in1=lbb[:], op=ALU.add))
            A("dve", ["T1", "omlb"], ["T2"], lambda e: e.tensor_tensor(out=kk[:], in0=omlb[:], in1=uu[:], op=ALU.subtract))
            A("act", ["T0"], ["T3"], lambda e: e.activation(out=G[:], in_=e1[:], func=AF.Ln))
            mi = 3 if sample else 1
            A("pe", ["tm", "T3"], ["K3"], lambda e: e.matmul(K[3][:], lhsT=tm[:, mi, :], rhs=G[:], start=True, stop=True))
            A("act", ["K3"], ["T4"], lambda e: e.activation(out=T[4][:], in_=K[3][:], func=AF.Exp))
            A("dve", ["T2", "T4"], ["KD"], lambda e: e.tensor_tensor(out=KD[:], in0=kk[:], in1=T[4][:], op=ALU.mult))

        pst = K[4][:, 0:256].rearrange("p (a d) -> p a d", a=4)
        ptot = K[4][:, 256:264].rearrange("p (c a) -> p c a", c=2)

        def state_tot():
            G = T[3]
            for c in range(2):
                for h in range(8):
                    hp, ee = h // 2, h % 2
                    A("pe", ["T3", "ones"], ["K4"], lambda e, c=c, h=h, hp=hp, ee=ee: e.matmul(
                        ptot[64 * ee:64 * ee + 64, c, hp:hp + 1], lhsT=G[c * 64:(c + 1) * 64, h * 64:(h + 1) * 64],
                        rhs=ones[c * 64:(c + 1) * 64, 0:1], start=True, stop=True), rg=64 * c)
            A("act", ["K4"], ["ET"], lambda e: e.activation(out=ET[:], in_=ptot, func=AF.Exp))

        def state_update(c):
            for h in range(8):
                hp, ee = h // 2, h % 2
                A("pe", ["KD", "VB"], ["K4"], lambda e, c=c, h=h, hp=hp, ee=ee: e.matmul(
                    pst[64 * ee:64 * ee + 64, hp, :], lhsT=KD[c * 64:(c + 1) * 64, h * 64:(h + 1) * 64],
                    rhs=VB[c * 64:(c + 1) * 64, h * 64:(h + 1) * 64], start=True, stop=True), rg=64 * c)
            A("dve", ["St", "ET"], ["St"], lambda e, c=c: e.tensor_tensor(
                out=St[:], in0=St[:], in1=ET[:, c, :].unsqueeze(2).to_broadcast([128, 4, 64]), op=ALU.mult))
            A("dve", ["St", "K4"], ["St"], lambda e: e.tensor_tensor(out=St[:], in0=St[:], in1=pst, op=ALU.add))

        def k_norms(Zt, kz):
            Z5 = Zt[:].rearrange("p (a b g d) -> p a b g d", a=3, b=2, g=2)
            KS = Z5[:, 1:3, 0, :, :]
            A("dve", [kz], ["sq4"], lambda e: e.tensor_tensor(out=sq4[:], in0=KS, in1=KS, op=ALU.mult))
            A("dve", ["sq4"], ["ms4"], lambda e: e.tensor_reduce(out=ms4[:], in_=sq4[:], axis=AX.X, op=ALU.add))
            A("act", ["ms4"], ["ms4"], lambda e: e.activation(out=ms4[:], in_=ms4[:], func=AF.Ln, scale=1.0 / 64, bias=EPS))
            A("act", ["ms4"], ["ms4"], lambda e: e.activation(out=ms4[:], in_=ms4[:], func=AF.Exp, scale=-0.5))
            A("dve", [kz, "ms4"], [kz], lambda e: e.tensor_tensor(
                out=KS, in0=KS, in1=ms4[:].unsqueeze(3).to_broadcast([128, 2, 2, 64]), op=ALU.mult))
            A("dve", [kz, "gvec"], [kz], lambda e: e.tensor_tensor(
                out=KS, in0=KS, in1=gvec[:, 64:192].rearrange("p (a d) -> p a d", a=2).unsqueeze(2).to_broadcast(
                    [128, 2, 2, 64]), op=ALU.mult))
            return Z5

        def silu_from(bank, dst, kdst):
            kb = "K%d" % bank
            A("act", [kb], ["T5"], lambda e: e.activation(out=T[5][:], in_=K[bank][:], func=AF.Exp, scale=-1.0))
            A("dve", ["T5"], ["T5"], lambda e: e.tensor_scalar_add(out=T[5][:], in0=T[5][:], scalar1=1.0))
            A("dve", ["T5"], ["T5"], lambda e: e.reciprocal(out=T[5][:], in_=T[5][:]))
            A("dve", ["T5", kb], [kdst], lambda e: e.tensor_tensor(out=dst[:], in0=K[bank][:], in1=T[5][:], op=ALU.mult))

        def head_rms(src3, ksrc, n):
            A("dve", [ksrc], ["T5"], lambda e: e.tensor_tensor(out=TMP[:, 0:n, :], in0=src3, in1=src3, op=ALU.mult))
            A("dve", ["T5"], ["st8"], lambda e: e.tensor_reduce(out=st8[:, 0:n], in_=TMP[:, 0:n, :], axis=AX.X, op=ALU.add))
            A("act", ["st8"], ["st8"], lambda e: e.activation(out=st8[:, 0:n], in_=st8[:, 0:n], func=AF.Ln,
                                                              scale=1.0 / 64, bias=EPS))
            A("act", ["st8"], ["st8"], lambda e: e.activation(out=st8[:, 0:n], in_=st8[:, 0:n], func=AF.Exp, scale=-0.5))

        def finish(X, kx, ydst, ykey, dbi):
            A("dve", ["OSEL"], ["st8"], lambda e: e.tensor_scalar_max(out=st8[:], in0=OSEL[:, :, 64], scalar1=1e-30))
            A("dve", ["st8"], ["st8"], lambda e: e.reciprocal(out=st8[:], in_=st8[:]))
            A("dve", ["st8", "gsig"], ["g8"], lambda e: e.tensor_tensor(out=g8[:, :, 1], in0=gsig[:, :, 1], in1=st8[:],
                                                                         op=ALU.mult))
            A("dve", ["OWIN"], ["st8"], lambda e: e.tensor_scalar_max(out=st8[:], in0=OWIN[:, :, 64], scalar1=1e-30))
            A("dve", ["st8"], ["st8"], lambda e: e.reciprocal(out=st8[:], in_=st8[:]))
            A("dve", ["st8", "gsig"], ["g8"], lambda e: e.tensor_tensor(out=g8[:, :, 2], in0=gsig[:, :, 2], in1=st8[:],
                                                                         op=ALU.mult))
            A("dve", ["OCMP", "gsig"], ["T4"], lambda e: e.tensor_tensor(
                out=ACC[:], in0=OCMP[:], in1=gsig[:, :, 0:1].to_broadcast([128, 8, 64]), op=ALU.mult))
            A("dve", ["OSEL", "g8"], ["T5"], lambda e: e.tensor_tensor(
                out=TMP[:], in0=OSEL[:, :, 0:64], in1=g8[:, :, 1:2].to_broadcast([128, 8, 64]), op=ALU.mult))
            A("dve", ["T4", "T5"], ["T4"], lambda e: e.tensor_tensor(out=ACC[:], in0=ACC[:], in1=TMP[:], op=ALU.add))
            A("dve", ["OWIN", "g8"], ["T5"], lambda e: e.tensor_tensor(
                out=TMP[:], in0=OWIN[:, :, 0:64], in1=g8[:, :, 2:3].to_broadcast([128, 8, 64]), op=ALU.mult))
            A("dve", ["T4", "T5"], ["T4"], lambda e: e.tensor_tensor(out=ACC[:], in0=ACC[:], in1=TMP[:], op=ALU.add))
            if debug and dbi is not None:
                for di, (src, kk_, n) in enumerate([(OCMP, "OCMP", 512), (OSEL, "OSEL", 520), (OWIN, "OWIN", 520),
                                                    (OR32, "OR32", 512), (ACC, "T4", 512)]):
                    A("pool", [kk_], [], lambda e, di=di, src=src, n=n: e.dma_start(
                        out=dbg[dbi * 128:(dbi + 1) * 128, di, 0:n], in_=src[:].rearrange("p h d -> p (h d)")),
                      dma=True, final=True)
            head_rms(ACC[:], "T4", 8)
            A("dve", ["T4", "st8"], ["T4"], lambda e: e.tensor_tensor(
                out=ACC[:], in0=ACC[:], in1=st8[:].unsqueeze(2).to_broadcast([128, 8, 64]), op=ALU.mult))
            A("dve", ["T4", "gvec"], ["OC"], lambda e: e.tensor_tensor(
                out=OC[:, 0:512], in0=ACC[:].rearrange("p h d -> p (h d)"), in1=gvec[:, 256:768], op=ALU.mult))
            head_rms(OR32[:], "OR32", 8)
            A("dve", ["OR32", "st8"], ["OR32"], lambda e: e.tensor_tensor(
                out=OR32[:], in0=OR32[:], in1=st8[:].unsqueeze(2).to_broadcast([128, 8, 64]), op=ALU.mult))
            A("dve", ["OR32", "gvec"], ["OR32"], lambda e: e.tensor_tensor(
                out=OR32[:].rearrange("p h d -> p (h d)"), in0=OR32[:].rearrange("p h d -> p (h d)"),
                in1=gvec[:, 768:1280], op=ALU.mult))
            A("dve", ["OR32", "RGS"], ["OC"], lambda e: e.tensor_tensor(
                out=OC[:, 512:1024], in0=OR32[:].rearrange("p h d -> p (h d)"), in1=RGS[:], op=ALU.mult))
            for k in range(8):
                A("pe", ["OC", "idb"], ["K0"], lambda e, k=k: e.transpose(
                    out=K0[:, k * 128:(k + 1) * 128], in_=OC[:, k * 128:(k + 1) * 128], identity=idb[:]))
            A("dve", ["K0"], ["OCT"], lambda e: e.tensor_copy(out=OCT[:].rearrange("p k t -> p (k t)"), in_=K0[:]))
            for n in range(2):
                for k in range(8):
                    A("pe", ["OCT", "WO"], ["K%d" % (1 + n)], lambda e, n=n, k=k: e.matmul(
                        K[1 + n][:], lhsT=OCT[:, k, :], rhs=WO[:, k, n * 512:(n + 1) * 512], start=(k == 0), stop=(k == 7)))
                A("dve", ["K%d" % (1 + n), kx], ["H"], lambda e, n=n, X=X: e.tensor_tensor(
                    out=H[:, n * 512:(n + 1) * 512], in0=K[1 + n][:], in1=X[:, n * 512:(n + 1) * 512], op=ALU.add))
            A("pool", ["H"], [ykey], lambda e: e.dma_start(out=ydst, in_=H[:]), dma=True, final=True)

        def tile_1b(v):
            p = v % 2
            own = (v % 2 == 0) and STAGE >= 3
            i_own = v // 2
            X, XT, kx, kxt = prep_x(xv[v * 128:(v + 1) * 128, :], p)
            Zt, kz = Z[p], "Z0"
            mm(XT, kxt, 1, 512, C_KV)
            mm(XT, kxt, 2, 256, C_KV + 512)
            A("act", ["K1"], [kz], lambda e, Zt=Zt: e.activation(out=Zt[:, 0:512], in_=K[1][:, 0:512], func=AF.Copy))
            A("act", ["K2"], [kz], lambda e, Zt=Zt: e.activation(out=Zt[:, 512:768], in_=K[2][:, 0:256], func=AF.Copy))
            mm(XT, kxt, 1, 512, C_RF)
            mm(XT, kxt, 2, 512, C_RI)
            Z5 = k_norms(Zt, kz)
            A("pool", [kz], [], lambda e, v=v, Zt=Zt: e.dma_start(out=kvp[v * 128:(v + 1) * 128, :], in_=Zt[:, 0:512]),
              dma=True, final=True)
            if v >= 28:
                A("pool", [kz], [], lambda e, v=v, Zt=Zt: e.dma_start(
                    out=winp[(v - 28) * 128:(v - 27) * 128, :], in_=Zt[:, 512:768]), dma=True, final=True)
            A("dve", [kz], ["zb"], lambda e, Z5=Z5: e.tensor_copy(out=zb[:].rearrange("p (a g) d -> p a g d", a=2),
                                                                  in_=Z5[:, 1:3, 0, :, :]))
            for a4 in range(4):
                A("pe", ["zb", "idb"], ["K0"], lambda e, a4=a4: e.transpose(
                    out=K0[0:64, a4 * 128:(a4 + 1) * 128], in_=zb[:, a4, :], identity=idb[:]))
            A("dve", ["K0"], ["KTs"], lambda e, v=v: e.tensor_copy(
                out=KTs[0:64, :, v * 128:(v + 1) * 128], in_=K0[0:64, 0:256].rearrange("p (g t) -> p g t", g=2)))
            sl = v % 6
            A("dve", ["K0"], ["KTw"], lambda e, sl=sl: e.tensor_copy(
                out=KTw[0:64, :, sl * 128:(sl + 1) * 128], in_=K0[0:64, 256:512].rearrange("p (g t) -> p g t", g=2)))
            A("sp", [], ["H"], lambda e, v=v: e.dma_start(out=stgq[64:68, 0:128], in_=kpos_d[:, v * 128:(v + 1) * 128]),
              dma=True)
            A("dve", ["H"], ["KTw"], lambda e, sl=sl: e.tensor_copy(
                out=KTw[64:68, :, sl * 128:(sl + 1) * 128], in_=stgq[64:68, 0:128].unsqueeze(1).to_broadcast([4, 2, 128])))
            A("dve", [kz], ["VVs"], lambda e, v=v, Z5=Z5: e.tensor_copy(out=VVs[:, v, :, 0:64], in_=Z5[:, 1, 1, :, :]))
            A("dve", [kz], ["VVw"], lambda e, sl=sl, Z5=Z5: e.tensor_copy(out=VVw[:, sl, :, 0:64], in_=Z5[:, 2, 1, :, :]))
            rnn_gates(1, 2, False)
            state_tot()
            if own:
                mm(XT, kxt, 1, 512, C_Q)
                mm(XT, kxt, 2, 24, C_G)
                A("act", ["K1"], ["ZQ"], lambda e: e.activation(out=ZQ[:].rearrange("p h d -> p (h d)"), in_=K[1][:],
                                                                 func=AF.Copy))
                A("act", ["K2"], ["gsig"], lambda e: e.activation(out=gsig[:].rearrange("p h b -> p (h b)"),
                                                                   in_=K[2][:, 0:24], func=AF.Exp, scale=-1.0))
                A("dve", ["gsig"], ["gsig"], lambda e: e.tensor_scalar_add(out=gsig[:], in0=gsig[:], scalar1=1.0))
                A("dve", ["gsig"], ["gsig"], lambda e: e.reciprocal(out=gsig[:], in_=gsig[:]))
                mm(XT, kxt, 1, 512, C_RQ)
                silu_from(1, QS, "QS")
                mm(XT, kxt, 2, 512, C_RG)
                silu_from(2, RGS, "RGS")
                head_rms(ZQ[:], "ZQ", 8)
                A("dve", ["ZQ", "st8"], ["ZQ"], lambda e: e.tensor_tensor(
                    out=ZQ[:], in0=ZQ[:], in1=st8[:].unsqueeze(2).to_broadcast([128, 8, 64]), op=ALU.mult))
                A("dve", ["ZQ", "qgs"], ["QN"], lambda e: e.tensor_tensor(
                    out=QN[:], in0=ZQ[:], in1=qgs[:].unsqueeze(1).to_broadcast([128, 8, 64]), op=ALU.mult))
                for h in range(8):
                    A("pe", ["QN", "idb"], ["K0"], lambda e, h=h: e.transpose(
                        out=K0[0:64, h * 128:(h + 1) * 128], in_=QN[:, h, :], identity=idb[:]))
                A("dve", ["K0"], ["QTa"], lambda e: e.tensor_copy(out=QTa[0:64, :, :].rearrange("p h t -> p (h t)"),
                                                                  in_=K0[0:64, :]))
                A("sp", [], ["H"], lambda e, i_own=i_own: e.dma_start(out=stgq[64:68, :], in_=qpos_d[i_own]), dma=True)
                A("dve", ["H"], ["QTa"], lambda e: e.tensor_copy(out=QTa[64:68, :, :].rearrange("p h t -> p (h t)"),
                                                                    in_=stgq[64:68, :]))
                if SUB < 1:
                    return
                A("pe", ["tm", "T3"], ["K3"], lambda e: e.matmul(K[3][:], lhsT=tm[:, 0, :], rhs=T[3][:], start=True, stop=True))
                A("act", ["K3"], ["T4"], lambda e: e.activation(out=T[4][:], in_=K[3][:], func=AF.Exp))
                A("act", ["K3"], ["T5"], lambda e: e.activation(out=T[5][:], in_=K[3][:], func=AF.Exp, scale=-1.0))
                A("dve", ["QS", "T4"], ["QD"], lambda e: e.tensor_tensor(out=QD[:], in0=QS[:], in1=T[4][:], op=ALU.mult))
                A("dve", ["T2", "T5"], ["KDI"], lambda e: e.tensor_tensor(out=KDI[:], in0=T[2][:], in1=T[5][:], op=ALU.mult))
                for hp in range(4):
                    A("pe", ["QD", "idb"], ["K0"], lambda e, hp=hp: e.transpose(
                        out=K0[:, hp * 128:(hp + 1) * 128], in_=QD[:, hp * 128:(hp + 1) * 128], identity=idb[:]))
                    A("pe", ["KDI", "idb"], ["K0"], lambda e, hp=hp: e.transpose(
                        out=K0[:, (4 + hp) * 128:(5 + hp) * 128], in_=KDI[:, hp * 128:(hp + 1) * 128], identity=idb[:]))
                A("dve", ["K0"], ["qkT"], lambda e: e.tensor_copy(out=qkT[:].rearrange("p a t -> p (a t)"), in_=K0[:]))
                if SUB < 2:
                    return
                pA = K[5][:].rearrange("p (h t) -> p h t", h=8)
                for c in range(2):
                    for h in [0, 2, 4, 6, 1, 3, 5, 7]:
                        hp, ee = h // 2, h % 2
                        A("pe", ["qkT"], ["K5"], lambda e, c=c, h=h, hp=hp, ee=ee: e.matmul(
                            pA[64 * c:64 * c + 64, h, :], lhsT=qkT[64 * ee:64 * ee + 64, 4 + hp, c * 64:(c + 1) * 64],
                            rhs=qkT[64 * ee:64 * ee + 64, hp, c * 64:(c + 1) * 64], start=True, stop=True), rg=64 * ee)
                if "a" not in SKIP:
                    A("dve", ["K5", "cst"], ["AT"], lambda e: e.tensor_tensor(
                        out=AT[:], in0=pA, in1=tri2.unsqueeze(1).to_broadcast([128, 8, 64]), op=ALU.mult))
                if "s" not in SKIP:
                    A("dve", ["St"], ["Sb0"], lambda e: e.tensor_copy(out=Sb[0][:], in_=St[:]))
            state_update(0)
            if own:
                A("dve", ["St"], ["Sb1"], lambda e: e.tensor_copy(out=Sb[1][:], in_=St[:]))
            state_update(1)
            if not own:
                return
            if SUB < 3:
                return
            po = K[6][:].rearrange("p (h d) -> p h d", h=8)
            for c in range(2):
                for h in range(8):
                    A("pe", ["AT", "VB"], ["K6"], lambda e, c=c, h=h: e.matmul(
                        po[64 * c:64 * c + 64, h, :], lhsT=AT[64 * c:64 * c + 64, h, :],
                        rhs=VB[64 * c:64 * c + 64, h * 64:(h + 1) * 64], start=(h == 0), stop=False), rg=64 * c)
                order = [h for h in range(8) if h % 2 == c] + [h for h in range(8) if h % 2 != c]
                for j, h in enumerate(order):
                    hp, ee = h // 2, h % 2
                    A("pe", ["qkT", "Sb%d" % c], ["K6"], lambda e, c=c, h=h, hp=hp, ee=ee, j=j: e.matmul(
                        po[64 * c:64 * c + 64, h, :], lhsT=qkT[64 * ee:64 * ee + 64, hp, c * 64:(c + 1) * 64],
                        rhs=Sb[c][64 * ee:64 * ee + 64, hp, :], start=False, stop=(j == 7)), rg=64 * ee)
            A("act", ["K6"], ["OR32"], lambda e: e.activation(out=OR32[:].rearrange("p h d -> p (h d)"), in_=K[6][:],
                                                               func=AF.Copy))
            if STAGE < 4:
                return
            nb = 2 * v + 2
            pc4 = K[3][:, 0:264].rearrange("p (h j) -> p h j", h=4)
            pocmp = K[4][:, 0:256].rearrange("p (h d) -> p h d", h=4)
            for g in range(2):
                for hh in range(4):
                    A("pe", ["QTa", "kcT"], ["K3"], lambda e, g=g, hh=hh: e.matmul(
                        pc4[:, hh, 0:nb], lhsT=QTa[0:64, g * 4 + hh, :], rhs=kcT[0:64, g, 0:nb], start=True, stop=True))
                A("dve", ["K3", "cst"], ["LG"], lambda e, g=g: e.tensor_tensor(
                    out=LG[:, :, 0:nb], in0=pc4[:, :, 0:nb], in1=acore[:, g * 4:(g + 1) * 4, 0:nb], op=ALU.add))
                A("dve", ["LG", "cst"], ["LG"], lambda e: e.tensor_tensor(
                    out=LG[:, :, 2 * v:2 * v + 2], in0=LG[:, :, 2 * v:2 * v + 2],
                    in1=fix2.unsqueeze(1).to_broadcast([128, 4, 2]), op=ALU.add))
                A("dve", ["LG"], ["st4"], lambda e: e.tensor_reduce(out=st4[:], in_=LG[:, :, 0:nb], axis=AX.X, op=ALU.max))
                A("dve", ["LG", "st4"], ["LG"], lambda e: e.tensor_tensor(
                    out=LG[:, :, 0:nb], in0=LG[:, :, 0:nb], in1=st4[:].unsqueeze(2).to_broadcast([128, 4, nb]),
                    op=ALU.subtract))
                A("act", ["LG"], ["EX"], lambda e: e.activation(out=EX[:, :, 0:nb], in_=LG[:, :, 0:nb], func=AF.Exp))
                A("dve", ["EX"], ["st4b"], lambda e: e.tensor_reduce(out=st4b[:], in_=EX[:, :, 0:nb], axis=AX.X, op=ALU.add))
                A("dve", ["st4b"], ["st4b"], lambda e: e.reciprocal(out=st4b[:], in_=st4b[:]))
                A("dve", ["st4b", "cst"], ["st4b"], lambda e: e.tensor_scalar(
                    out=st4b[:], in0=st4b[:], scalar1=rowvalid[:, i_own:i_own + 1], scalar2=None, op0=ALU.mult))
                A("dve", ["EX", "st4b"], ["EX"], lambda e: e.tensor_tensor(
                    out=EX[:, :, 0:nb], in0=EX[:, :, 0:nb], in1=st4b[:].unsqueeze(2).to_broadcast([128, 4, nb]),
                    op=ALU.mult))
                A("dve", ["EX"], ["PB"], lambda e: e.tensor_copy(out=PB[:, :, 0:nb], in_=EX[:, :, 0:nb]))
                A("dve", ["EX"], ["SC"], lambda e: e.tensor_reduce(
                    out=SC[:, 0:nb], in_=EX[:, :, 0:nb].rearrange("p h j -> p j h"), axis=AX.X, op=ALU.add))
                if nb < 66:
                    A("dve", [], ["SC"], lambda e: e.memset(SC[:, nb:66], -1.0))
                A("dve", ["SC", "cst"], ["SC"], lambda e: e.scalar_tensor_tensor(
                    out=SC[:, 2 * v:2 * v + 1], in0=SC[:, 2 * v:2 * v + 1], scalar=mhi, in1=mlo5,
                    op0=ALU.mult, op1=ALU.add))
                A("dve", ["cst"], ["SC"], lambda e: e.tensor_copy(out=SC[:, 2 * v + 1:2 * v + 2], in_=c1col))
                A("dve", ["SC", "cst"], ["SC"], lambda e: e.tensor_tensor(out=SC[:], in0=SC[:], in1=keepc, op=ALU.mult))
                A("dve", ["SC", "cst"], ["SC"], lambda e: e.tensor_tensor(out=SC[:], in0=SC[:], in1=addc, op=ALU.add))
                A("dve", ["SC"], ["m8"], lambda e: e.max(out=m8[:], in_=SC[:]))
                A("dve", ["SC", "m8"], ["WK"], lambda e: e.match_replace(out=WK[:], in_to_replace=m8[:], in_values=SC[:],
                                                                          imm_value=-2.0))
                A("dve", ["WK"], ["m8"], lambda e: e.max(out=m8[:], in_=WK[:]))
                A("dve", ["m8"], ["thr"], lambda e: e.tensor_reduce(out=thr[:], in_=m8[:], axis=AX.X, op=ALU.min))
                A("dve", ["SC", "thr"], ["SEL"], lambda e: e.tensor_scalar(
                    out=SEL[:], in0=SC[:], scalar1=thr[:, 0:1], scalar2=None, op0=ALU.is_ge))
                A("dve", ["SC"], ["WK"], lambda e: e.tensor_single_scalar(out=WK[:], in_=SC[:], scalar=0.0, op=ALU.is_ge))
                A("dve", ["SEL", "WK"], ["SEL"], lambda e: e.tensor_tensor(out=SEL[:], in0=SEL[:], in1=WK[:], op=ALU.mult))
                A("dve", ["SEL"], ["MBF"], lambda e: e.tensor_scalar(
                    out=MBF[:], in0=SEL[:], scalar1=-1.0, scalar2=BIG, op0=ALU.add, op1=ALU.mult))
                A("pe", ["MBF", "idb"], ["K0"], lambda e: e.transpose(out=K0[0:66, 0:128], in_=MBF[:], identity=idb[:]))
                for hh in range(4):
                    A("pe", ["PB", "idb"], ["K0"], lambda e, hh=hh: e.transpose(
                        out=K0[0:nb, (1 + hh) * 128:(2 + hh) * 128], in_=PB[:, hh, 0:nb], identity=idb[:]))
                A("dve", ["K0"], ["RHSM%d" % g], lambda e, g=g: e.tensor_copy(
                    out=RHSM_[g][0:66], in_=K0[0:66, 0:128].unsqueeze(1).to_broadcast([66, 4, 128])))
                A("dve", ["K0"], ["PT4"], lambda e: e.tensor_copy(
                    out=PT4[0:nb, :, :], in_=K0[0:nb, 128:640].rearrange("p (h t) -> p h t", h=4)))
                for hh in range(4):
                    A("pe", ["PT4", "vcb"], ["K4"], lambda e, g=g, hh=hh: e.matmul(
                        pocmp[:, hh, :], lhsT=PT4[0:nb, hh, :], rhs=vcb[0:nb, g, :], start=True, stop=True))
                A("act", ["K4"], ["OCMP"], lambda e, g=g: e.activation(out=OCMP[:, g * 4:(g + 1) * 4, :], in_=pocmp,
                                                                        func=AF.Copy))
            if STAGE < 5:
                return
            posb = [(K[7][:, 0:260].rearrange("p (h d) -> p h d", h=4), "K7"),
                    (K[4][:, 0:260].rearrange("p (h d) -> p h d", h=4), "K4")]
            its = []
            gi = 0
            for g in range(2):
                for br in range(2):
                    kts = list(range(0, v + 1)) if br == 0 else list(range(max(0, v - 4), v + 1))
                    for kt in kts:
                        extra = []
                        if br == 0:
                            extra.append((Eexp[:, kt * 128:(kt + 1) * 128],
                                          RHSM_[g][:].rearrange("p h t -> p (h t)"), ["Eexp", "RHSM%d" % g]))
                            if kt == v:
                                extra.append((idb[:], MSK[:, 0, :], ["idb", "MSK"]))
                            lhs_k, rk = KTs[:, g, kt * 128:(kt + 1) * 128], ["KTs"]
                            rv, kv_ = VVs[:, kt, g, :], "VVs"
                        else:
                            if kt == v:
                                extra.append((idb[:], MSK[:, 0, :], ["idb", "MSK"]))
                            elif kt == v - 4:
                                extra.append((idb[:], MSK[:, 3 if kt == 0 else 1, :], ["idb", "MSK"]))
                            elif kt == 0:
                                extra.append((idb[:], MSK[:, 2, :], ["idb", "MSK"]))
                            slk = kt % 6
                            lhs_k, rk = KTw[:, g, slk * 128:(slk + 1) * 128], ["KTw"]
                            rv, kv_ = VVw[:, slk, g, :], "VVw"
                        its.append(dict(g=g, br=br, first=(kt == kts[0]), last=(kt == kts[-1]), extra=extra, lhs_k=lhs_k,
                                        rk=rk, rv=rv, kv=kv_, gi=gi))
                    gi += 1

            SBK = [5, 6, 1]

            def s_stage(i):
                d = its[i]
                bank = SBK[i % 3]
                kb = "K%d" % bank
                g, extra = d["g"], d["extra"]
                A("pe", d["rk"] + ["QTa"], [kb], lambda e, bank=bank, lhs_k=d["lhs_k"], g=g, ne=len(extra): e.matmul(
                    K[bank][:], lhsT=lhs_k, rhs=QTa[:, g * 4:(g + 1) * 4, :].rearrange("p h t -> p (h t)"),
                    start=True, stop=(ne == 0)))
                for xi, (l_, r_, ks_) in enumerate(extra):
                    A("pe", ks_, [kb], lambda e, bank=bank, l_=l_, r_=r_, last=(xi == len(extra) - 1): e.matmul(
                        K[bank][:], lhsT=l_, rhs=r_, start=False, stop=last))

            def ep_stage(i):
                d = its[i]
                bank = SBK[i % 3]
                kb = "K%d" % bank
                PTt, kpt = PTb[i % 3], "PTb%d" % (i % 3)
                pos, kpos_ = posb[d["gi"] % 2]
                A("act", [kb], [kpt], lambda e, bank=bank, PTt=PTt: e.activation(out=PTt[:], in_=K[bank][:], func=AF.Exp))
                for hh in range(4):
                    A("pe", [kpt, d["kv"]], [kpos_], lambda e, hh=hh, PTt=PTt, rv=d["rv"], pos=pos,
                      first=(d["first"] and hh == 0), last=(d["last"] and hh == 3): e.matmul(
                          pos[:, hh, :], lhsT=PTt[:, hh * 128:(hh + 1) * 128], rhs=rv, start=first, stop=last))
                if d["last"]:
                    dst, kd = (OSEL, "OSEL") if d["br"] == 0 else (OWIN, "OWIN")
                    A("act", [kpos_], [kd], lambda e, dst=dst, g=d["g"], pos=pos: e.activation(
                        out=dst[:, g * 4:(g + 1) * 4, :], in_=pos, func=AF.Copy))

            s_stage(0)
            if len(its) > 1:
                s_stage(1)
            for i in range(len(its)):
                if i + 2 < len(its):
                    s_stage(i + 2)
                ep_stage(i)
            if STAGE < 6:
                return
            finish(X, kx, yp[i_own * 128:(i_own + 1) * 128, :], "yp%d" % i_own, i_own)

        for v_ in range(NT if STAGE >= 2 else 0):
            tile_1b(v_)
        A("pool", ["St"], [], lambda e: e.dma_start(out=rnnp.rearrange("(a e) k v -> (e k) a v", e=2), in_=St[:]),
          dma=True, final=True)

        SKEYS = ["W1s", "SELM", "S0bf", "KN", "ZV", "QTs", "ATs", "ATs", "XcTb", "PGB0", "PGB1", "hs_s", "KCa", "vca",
                 "QTb", "PTc", "PTs", "PTw", "RHSMs", "WPB", "OcT", "OsT", "OwT", "IDX", "MSKs", "kcbs"]
        _so = [0]

        def sw(n, parts=128):
            a_ = _so[0]
            _so[0] += n
            assert _so[0] <= 26816
            return BIGW[0:parts, a_:a_ + n]

        W1s = sw(8192).rearrange("p (a j h) -> p a j h", a=2, j=32)
        SELM = sw(128).rearrange("p (a j) -> p a j", a=2)
        S0bf = sw(4096).rearrange("p (a d) -> p a d", a=64)
        KN = sw(512, 64).rearrange("p (a t) -> p a t", a=4)
        ZV = sw(256).rearrange("p (a g d) -> p a g d", a=2, g=2)
        QTs = sw(1024, 64).rearrange("p (h t) -> p h t", h=8)
        ATs = sw(1024).rearrange("p (h t) -> p h t", h=8)
        oiT = ATs[0:64]
        XcTb = sw(4096).rearrange("p (a n) -> p a n", a=4)
        _pgb = sw(1024)
        PGB = [_pgb[:, 0:512], _pgb[:, 512:1024]]
        hs_s = sw(128).rearrange("p (a j) -> p a j", a=4)
        KCa = sw(64, 68).rearrange("p (g j) -> p g j", g=2)
        vca = sw(130, 32).rearrange("p (g d) -> p g d", g=2)
        QTb = sw(64, 128).rearrange("p (h t) -> p h t", h=8)
        PTc = sw(32, 32)
        PTs = sw(544)
        PTw = sw(160)
        RHSMs = sw(32, 128).rearrange("p (h t) -> p h t", h=4)
        WPB = _pgb.rearrange("p (t c) -> p t c", t=4)
        OcT = sw(1024, 65).rearrange("p (h t) -> p h t", h=8)
        OsT = sw(1024, 65).rearrange("p (h t) -> p h t", h=8)
        OwT = sw(1024, 65).rearrange("p (h t) -> p h t", h=8)
        IDX = sw(1024).bitcast(mybir.dt.int32)
        MSKs = sw(64).rearrange("p (m n) -> p m n", m=2)
        kcbs = sw(128, 32).rearrange("p (g d) -> p g d", g=2)
        csts = sb("csts", [128, 74])
        selT = csts[0:32, 0:8]
        keepS = csts[0:8, 8:41]
        addS = csts[0:8, 41:74]
        iotac = sb("iotac", [128, 1])

        A("dve", [], AKEYS + ["fdummy"] + ["S0_%d" % b for b in range(16)], lambda e: e.memset(fdummy[:], 0.0))
        for b in range(16):
            A("pool", [], ["S0_%d" % b], lambda e, b=b: e.dma_start(
                out=S0[:, b * 4:(b + 1) * 4, :], in_=srnn[b].rearrange("(a e) k v -> (e k) a v", e=2)), dma=True)
        for b in range(16):
            A("pool", [], [], lambda e, b=b: e.dma_start(out=wins[b, 0:504, :], in_=cwin[b, 8:512, :]), dma=True, final=True)
        X, XT, kx, kxt = prep_x(xsm[:, :], 0)
        Zt, kz = Z[0], "Z0"
        mm(XT, kxt, 1, 384, C_KV)
        mm(XT, kxt, 2, 384, C_KV + 384)
        A("act", ["K1"], [kz], lambda e: e.activation(out=Zt[:, 0:384], in_=K[1][:, 0:384], func=AF.Copy))
        A("act", ["K2"], [kz], lambda e: e.activation(out=Zt[:, 384:768], in_=K[2][:, 0:384], func=AF.Copy))
        mm(XT, kxt, 1, 512, C_RF)
        mm(XT, kxt, 2, 512, C_RI)
        Z5 = k_norms(Zt, kz)
        A("pool", [kz], [], lambda e: e.dma_start(out=kvs[:, :], in_=Zt[:, 0:512]), dma=True, final=True)
        for b in range(16):
            A("pool", [kz], [], lambda e, b=b: e.dma_start(out=wins[b, 504:512, :], in_=Zt[b * 8:(b + 1) * 8, 512:768]),
              dma=True, final=True)
        rnn_gates(1, 2, True)
        mm(XT, kxt, 1, 512, C_Q)
        mm(XT, kxt, 2, 24, C_G)
        A("act", ["K1"], ["ZQ"], lambda e: e.activation(out=ZQ[:].rearrange("p h d -> p (h d)"), in_=K[1][:], func=AF.Copy))
        A("act", ["K2"], ["gsig"], lambda e: e.activation(out=gsig[:].rearrange("p h b -> p (h b)"), in_=K[2][:, 0:24],
                                                           func=AF.Exp, scale=-1.0))
        A("dve", ["gsig"], ["gsig"], lambda e: e.tensor_scalar_add(out=gsig[:], in0=gsig[:], scalar1=1.0))
        A("dve", ["gsig"], ["gsig"], lambda e: e.reciprocal(out=gsig[:], in_=gsig[:]))
        mm(XT, kxt, 1, 512, C_RQ)
        silu_from(1, QS, "QS")
        mm(XT, kxt, 2, 512, C_RG)
        silu_from(2, RGS, "RGS")
        A("dve", [], WK_ + SKEYS + ["fdummy"], lambda e: e.memset(fdummy[:], 0.0))
        A("sp", [], ["csts"], lambda e: e.dma_start(out=csts[:], in_=csts_d), dma=True)
        A("sp", [], ["iotac"], lambda e: e.dma_start(out=iotac[:], in_=iota_d), dma=True)
        A("dve", [kz], ["zb"], lambda e: e.tensor_copy(out=zb[:].rearrange("p (a g) d -> p a g d", a=2), in_=Z5[:, 1:3, 0, :, :]))
        for a4 in range(4):
            A("pe", ["zb", "idb"], ["K0"], lambda e, a4=a4: e.transpose(
                out=K0[0:64, a4 * 128:(a4 + 1) * 128], in_=zb[:, a4, :], identity=idb[:]))
        A("dve", ["K0"], ["KN"], lambda e: e.tensor_copy(out=KN[:].rearrange("p a t -> p (a t)"), in_=K0[0:64, 0:512]))
        A("dve", [kz], ["ZV"], lambda e: e.tensor_copy(out=ZV[:], in_=Z5[:, 1:3, 1, :, :]))
        head_rms(ZQ[:], "ZQ", 8)
        A("dve", ["ZQ", "st8"], ["ZQ"], lambda e: e.tensor_tensor(
            out=ZQ[:], in0=ZQ[:], in1=st8[:].unsqueeze(2).to_broadcast([128, 8, 64]), op=ALU.mult))
        A("dve", ["ZQ", "qgs"], ["QN"], lambda e: e.tensor_tensor(
            out=QN[:], in0=ZQ[:], in1=qgs[:].unsqueeze(1).to_broadcast([128, 8, 64]), op=ALU.mult))
        for h in range(8):
            A("pe", ["QN", "idb"], ["K0"], lambda e, h=h: e.transpose(
                out=K0[0:64, h * 128:(h + 1) * 128], in_=QN[:, h, :], identity=idb[:]))
        A("dve", ["K0"], ["QTs"], lambda e: e.tensor_copy(out=QTs[:].rearrange("p h t -> p (h t)"), in_=K0[0:64, :]))
        A("pe", ["tm", "T3"], ["K3"], lambda e: e.matmul(K[3][:], lhsT=tm[:, 2, :], rhs=T[3][:], start=True, stop=True))
        A("act", ["K3"], ["T4"], lambda e: e.activation(out=T[4][:], in_=K[3][:], func=AF.Exp))
        A("act", ["K3"], ["T5"], lambda e: e.activation(out=T[5][:], in_=K[3][:], func=AF.Exp, scale=-1.0))
        A("dve", ["QS", "T4"], ["QD"], lambda e: e.tensor_tensor(out=QD[:], in0=QS[:], in1=T[4][:], op=ALU.mult))
        A("dve", ["T2", "T5"], ["KDI"], lambda e: e.tensor_tensor(out=KDI[:], in0=T[2][:], in1=T[5][:], op=ALU.mult))
        for hp in range(4):
            A("pe", ["QD", "idb"], ["K0"], lambda e, hp=hp: e.transpose(
                out=K0[:, hp * 128:(hp + 1) * 128], in_=QD[:, hp * 128:(hp + 1) * 128], identity=idb[:]))
            A("pe", ["KDI", "idb"], ["K0"], lambda e, hp=hp: e.transpose(
                out=K0[:, (4 + hp) * 128:(5 + hp) * 128], in_=KDI[:, hp * 128:(hp + 1) * 128], identity=idb[:]))
        A("dve", ["K0"], ["qkT"], lambda e: e.tensor_copy(out=qkT[:].rearrange("p a t -> p (a t)"), in_=K0[:]))
        for ee in range(2):
            for hp in range(4):
                A("pe", ["qkT"], ["K%d" % (5 + ee)], lambda e, ee=ee, hp=hp: e.matmul(
                    K[5 + ee][:, hp * 128:(hp + 1) * 128], lhsT=qkT[64 * ee:64 * ee + 64, 4 + hp, :],
                    rhs=qkT[64 * ee:64 * ee + 64, hp, :], start=True, stop=True), rg=64 * ee)
        for ee in range(2):
            A("dve", ["K%d" % (5 + ee), "tm"], ["ATs"], lambda e, ee=ee: e.tensor_tensor(
                out=ATs[:, ee * 4:(ee + 1) * 4, :], in0=K[5 + ee][:].rearrange("p (a t) -> p a t", a=4),
                in1=tm[:, 2, :].unsqueeze(1).to_broadcast([128, 4, 128]), op=ALU.mult))
        A("dve", ["S0_%d" % b for b in range(16)], ["S0bf"], lambda e: e.tensor_copy(out=S0bf[:, 0:32, :], in_=S0[:, 0:32, :]))
        A("act", ["S0_%d" % b for b in range(16)], ["S0bf"], lambda e: e.activation(out=S0bf[:, 32:64, :], in_=S0[:, 32:64, :],
                                                                                     func=AF.Copy))
        pos_ = K[7][:].rearrange("p (h d) -> p h d", h=8)
        for h in range(8):
            hp, ee = h // 2, h % 2
            A("pe", ["ATs", "VB"], ["K7"], lambda e, h=h, hp=hp, ee=ee: e.matmul(
                pos_[:, h, :], lhsT=ATs[:, ee * 4 + hp, :], rhs=VB[:, h * 64:(h + 1) * 64], start=True, stop=True))
        for ee in range(2):
            for b in range(16):
                for hp in range(4):
                    A("pe", ["qkT", "S0bf"], ["K%d" % (5 + ee)], lambda e, ee=ee, b=b, hp=hp: e.matmul(
                        K[5 + ee][0:64, hp * 128 + b * 8:hp * 128 + b * 8 + 8], lhsT=S0bf[64 * ee:64 * ee + 64, b * 4 + hp, :],
                        rhs=qkT[64 * ee:64 * ee + 64, hp, b * 8:(b + 1) * 8], start=True, stop=True), rg=64 * ee)
        for ee in range(2):
            A("act", ["K%d" % (5 + ee)], ["ATs"], lambda e, ee=ee: e.activation(
                out=oiT[:, ee * 4:(ee + 1) * 4, :].rearrange("p a t -> p (a t)"), in_=K[5 + ee][0:64, :], func=AF.Copy))
        for h in range(8):
            hp, ee = h // 2, h % 2
            A("pe", ["ATs", "idb"], ["K0"], lambda e, h=h, hp=hp, ee=ee: e.transpose(
                out=K0[:, h * 64:(h + 1) * 64], in_=oiT[:, ee * 4 + hp, :], identity=idb[0:64, 0:64]))
        A("act", ["K0"], ["T5"], lambda e: e.activation(out=T[5][:], in_=K0[:, 0:512], func=AF.Copy))
        A("dve", ["K7", "T5"], ["OR32"], lambda e: e.tensor_tensor(out=OR32[:].rearrange("p h d -> p (h d)"), in0=K[7][:],
                                                                  in1=T[5][:], op=ALU.add))
        G = T[3]
        ptots = K[5][:, 0:64].rearrange("p (a b) -> p a b", a=4)
        for h in range(8):
            hp, ee = h // 2, h % 2
            A("pe", ["T3", "rmask"], ["K5"], lambda e, h=h, hp=hp, ee=ee: e.matmul(
                ptots[64 * ee:64 * ee + 64, hp, :], lhsT=G[:, h * 64:(h + 1) * 64], rhs=rmask[:], start=True, stop=True))
        A("act", ["K5"], ["etots"], lambda e: e.activation(out=etots[:], in_=ptots, func=AF.Exp))
        for b in range(16):
            q = b % 2
            A("dve", ["KD", "rmask"], ["PTb%d" % q], lambda e, b=b, q=q: e.tensor_scalar(
                out=kdm[q][:], in0=KD[:], scalar1=rmask[:, b:b + 1], scalar2=None, op0=ALU.mult))
            for h in range(8):
                hp, ee = h // 2, h % 2
                A("pe", ["PTb%d" % q, "VB"], ["K4"], lambda e, h=h, q=q, hp=hp, ee=ee: e.matmul(
                    pst[64 * ee:64 * ee + 64, hp, :], lhsT=kdm[q][:, h * 64:(h + 1) * 64], rhs=VB[:, h * 64:(h + 1) * 64],
                    start=True, stop=True))
            Sbv = S0[:, b * 4:(b + 1) * 4, :]
            A("dve", ["S0_%d" % b, "etots", "S0bf"], ["S0_%d" % b], lambda e, b=b, Sbv=Sbv: e.tensor_tensor(
                out=Sbv, in0=Sbv, in1=etots[:, :, b:b + 1].to_broadcast([128, 4, 64]), op=ALU.mult))
            A("dve", ["S0_%d" % b, "K4"], ["S0_%d" % b], lambda e, Sbv=Sbv: e.tensor_tensor(out=Sbv, in0=Sbv, in1=pst, op=ALU.add))
            A("pool", ["S0_%d" % b], ["rnns_o"], lambda e, b=b, Sbv=Sbv: e.dma_start(
                out=rnns[b].rearrange("(a e) k v -> (e k) a v", e=2), in_=Sbv), dma=True, final=True)
        if STAGE >= 8:
            A("dve", ["rnns_o"] + ["S0_%d" % b for b in range(16)], ["KTs", "fdummy"], lambda e: e.memset(fdummy[:], 0.0))
            A("dve", [], ["KTs"], lambda e: e.memset(KTs[64:128, :, 0:2176], 0.0))
            for c0 in range(0, 17 * 128, 1024):
                n = min(1024, 17 * 128 - c0)
                A("sp", [], ["H"], lambda e, c0=c0, n=n: e.dma_start(out=stgq[64:68, 0:n], in_=kpos_d[:, c0:c0 + n]), dma=True)
                for g in range(2):
                    A("dve", ["H"], ["KTs"], lambda e, c0=c0, n=n, g=g: e.tensor_copy(
                        out=KTs[64:68, g, c0:c0 + n], in_=stgq[64:68, 0:n]))
            A("sp", [], ["H"], lambda e: e.dma_start(out=stgq[64:68, 0:640], in_=kpos_d[:, 1536:2176]), dma=True)
            A("dve", ["H"], ["KTw"], lambda e: e.tensor_copy(
                out=KTw[64:68, :, 0:640], in_=stgq[64:68, 0:640].unsqueeze(1).to_broadcast([4, 2, 640])))
            A("dve", [], ["QTb"], lambda e: e.memset(QTb[64:128, :, :], 0.0))
            A("dve", [], ["RHSMs"], lambda e: e.memset(RHSMs[:, :, :], 0.0))
            A("sp", [], ["H"], lambda e: e.dma_start(out=stgq[64:68, 0:64], in_=sq_d[0:4, :]), dma=True)
            A("dve", ["H"], ["QTb"], lambda e: e.tensor_copy(out=QTb[64:68, :, :].rearrange("p h t -> p (h t)"),
                                                             in_=stgq[64:68, 0:64]))
            A("sp", [], ["H"], lambda e: e.dma_start(out=stgq[64:68, 0:32], in_=sq_d[4:8, 0:32]), dma=True)
            A("dve", ["H"], ["KCa"], lambda e: e.tensor_copy(
                out=KCa[64:68, :, :], in_=stgq[64:68, 0:32].unsqueeze(1).to_broadcast([4, 2, 32])))
            A("sp", [], ["H"], lambda e: e.dma_start(out=stgq[:, 0:192], in_=sm_d), dma=True)
            A("dve", ["H"], ["SELM"], lambda e: e.tensor_copy(out=SELM[:].rearrange("p a j -> p (a j)"), in_=stgq[:, 0:128]))
            A("dve", ["H"], ["MSKs"], lambda e: e.tensor_copy(out=MSKs[:].rearrange("p m n -> p (m n)"), in_=stgq[:, 128:192]))
            for typ in range(2):
                for jq in range(4):
                    A("sp", [], ["H"], lambda e, typ=typ, jq=jq: e.dma_start(
                        out=stgq[:].rearrange("p (j h) -> p j h", j=8),
                        in_=w1_d[typ, jq * 1024:(jq + 1) * 1024, :].rearrange("(j p) h -> p j h", p=128)), dma=True)
                    A("dve", ["H"], ["W1s"], lambda e, typ=typ, jq=jq: e.tensor_copy(
                        out=W1s[:, typ, jq * 8:(jq + 1) * 8, :], in_=stgq[:].rearrange("p (j h) -> p j h", j=8)))
            A("dve", [], ["KTs"], lambda e: e.memset(KTs[0:64, :, 2048:2176], 0.0))
            A("dve", [], ["VVs"], lambda e: e.memset(VVs[:, 16, :, 0:64], 0.0))
            A("dve", [], ["KTw"], lambda e: e.memset(KTw[0:64, :, 512:640], 0.0))
            A("dve", [], ["VVw"], lambda e: e.memset(VVw[:, 4, :, 0:64], 0.0))
            A("dve", [], ["vca"], lambda e: e.memset(vca[:, :, 64:65], 1.0))
            A("sp", [], ["IDX"], lambda e: e.dma_start(out=IDX[:, 0:256], in_=ptrep_d), dma=True)
            A("dve", ["IDX"], ["T0"], lambda e: e.tensor_copy(out=T[0][:, 0:256], in_=IDX[:, 0:256]))
            A("dve", ["T0", "iotac"], ["T0"], lambda e: e.tensor_scalar(out=T[0][:, 0:256], in0=T[0][:, 0:256], scalar1=128.0,
                                                                       scalar2=iotac[:, 0:1], op0=ALU.mult, op1=ALU.add))
            A("dve", ["T0"], ["IDX"], lambda e: e.tensor_copy(out=IDX[:, 256:512], in_=T[0][:, 0:256]))
            PG = [T[0], T[1], T[2], T[3]]
            K1v = K[1][:, 0:256].rearrange("p (a t) -> p a t", a=4)
            K3v = K[3][:, 0:256].rearrange("p (a d) -> p a d", a=4)
            WP = xs[1]
            pgc = [0]

            def sample_seq(b):
                for j in range(16):
                    q = pgc[0] % 4
                    q2 = pgc[0] % 2
                    pgc[0] += 1
                    n = b * 16 + j
                    pg, kpg, pgb, kpgb = PG[q], "T%d" % q, PGB[q2], "PGB%d" % q2
                    A("pool", ["IDX"], [kpg], lambda e, pg=pg, n=n: e.indirect_dma_start(
                        out=pg[:], out_offset=None, in_=cache_d,
                        in_offset=bass.IndirectOffsetOnAxis(ap=IDX[:, 256 + n:257 + n], axis=0)), dma=True)
                    if q2 == 0:
                        A("act", [kpg], [kpgb], lambda e, pg=pg, pgb=pgb: e.activation(out=pgb, in_=pg[:], func=AF.Copy))
                    else:
                        A("dve", [kpg], [kpgb], lambda e, pg=pg, pgb=pgb: e.tensor_copy(out=pgb, in_=pg[:]))
                    for tg in range(4):
                        for par in range(2):
                            A("pe", [kpgb, "SELM"], ["K1"], lambda e, tg=tg, par=par, pgb=pgb: e.matmul(
                                K1v[64 * par:64 * par + 64, tg, :], lhsT=pgb[:, tg * 64:(tg + 1) * 64], rhs=SELM[:, par, :],
                                start=True, stop=True))
                    for a in range(2):
                        A("dve", ["K1", "peT"], ["XcTb"], lambda e, a=a, j=j: e.tensor_tensor(
                            out=XcTb[:, 2 * a:2 * a + 2, j * 64:(j + 1) * 64].rearrange("p g (b j) -> p g b j", b=2),
                            in0=K1v[:, 2 * a:2 * a + 2, :].rearrange("p g (b j) -> p g b j", b=2),
                            in1=peT[:, a, :].unsqueeze(1).unsqueeze(1).to_broadcast([128, 2, 2, 32]), op=ALU.add))
                    for g in range(2):
                        A("pe", [kpgb, "idb"], ["K0"], lambda e, g=g, pgb=pgb: e.transpose(
                            out=K0[0:64, g * 128:(g + 1) * 128], in_=pgb[:, 256 + g * 64:256 + (g + 1) * 64], identity=idb[:]))
                    A("dve", ["K0"], ["KTs"], lambda e, j=j: e.tensor_copy(
                        out=KTs[0:64, :, j * 128:(j + 1) * 128], in_=K0[0:64, 0:256].rearrange("p (g t) -> p g t", g=2)))
                    A("act", [kpg], ["VVs"], lambda e, pg=pg, j=j: e.activation(
                        out=VVs[:, j, :, 0:64], in_=pg[:, 384:512].rearrange("p (g d) -> p g d", g=2), func=AF.Copy))
                A("dve", ["KN"], ["KTs"], lambda e: e.tensor_copy(out=KTs[0:64, :, 2048:2056], in_=KN[:, 0:2, b * 8:(b + 1) * 8]))
                for g in range(2):
                    A("sp", ["ZV"], ["VVs"], lambda e, g=g: e.dma_start(out=VVs[0:8, 16, g, 0:64], in_=ZV[b * 8:(b + 1) * 8, 0, g, :]),
                      dma=True)
                    A("sp", ["ZV"], ["VVw"], lambda e, g=g: e.dma_start(out=VVw[0:8, 4, g, 0:64], in_=ZV[b * 8:(b + 1) * 8, 1, g, :]),
                      dma=True)
                for tg in range(4):
                    typ = tg // 2
                    for j in range(32):
                        A("pe", ["XcTb", "W1s"], ["K2"], lambda e, tg=tg, typ=typ, j=j: e.matmul(
                            K[2][:, 0:32], lhsT=W1s[:, typ, j, :], rhs=XcTb[:, tg, j:1024:32], start=(j == 0), stop=(j == 31)))
                    A("act", ["K2"], ["T4"], lambda e: e.activation(out=T[4][:, 0:32], in_=K[2][:, 0:32], func=AF.Exp, scale=-1.0))
                    A("dve", ["T4"], ["T4"], lambda e: e.tensor_scalar_add(out=T[4][:, 0:32], in0=T[4][:, 0:32], scalar1=1.0))
                    A("dve", ["T4"], ["T4"], lambda e: e.reciprocal(out=T[4][:, 0:32], in_=T[4][:, 0:32]))
                    A("dve", ["T4", "K2"], ["hs_s"], lambda e, tg=tg: e.tensor_tensor(out=hs_s[:, tg, :], in0=K[2][:, 0:32],
                                                                                   in1=T[4][:, 0:32], op=ALU.mult))
                    A("pe", ["hs_s", "W2b"], ["K3"], lambda e, tg=tg, typ=typ: e.matmul(
                        K3v[0:32, tg, :], lhsT=hs_s[:, tg, :], rhs=W2b[:, typ, :], start=True, stop=True))
                A("act", ["K3"], ["kcr"], lambda e: e.activation(out=kcr[0:32], in_=K3v[0:32, 0:2, :], func=AF.Copy))
                A("act", ["K3"], ["vca"], lambda e: e.activation(out=vca[:, :, 0:64], in_=K3v[0:32, 2:4, :], func=AF.Copy))
                A("dve", ["kcr"], ["LG"], lambda e: e.tensor_tensor(out=LG[0:32, 0:2, 0:64], in0=kcr[0:32], in1=kcr[0:32], op=ALU.mult))
                A("dve", ["LG"], ["st4"], lambda e: e.tensor_reduce(out=st4[0:32, 0:2], in_=LG[0:32, 0:2, 0:64], axis=AX.X, op=ALU.add))
                A("act", ["st4"], ["st4"], lambda e: e.activation(out=st4[0:32, 0:2], in_=st4[0:32, 0:2], func=AF.Ln,
                                                                  scale=1.0 / 64, bias=EPS))
                A("act", ["st4"], ["st4"], lambda e: e.activation(out=st4[0:32, 0:2], in_=st4[0:32, 0:2], func=AF.Exp, scale=-0.5))
                A("dve", ["kcr", "st4"], ["kcr"], lambda e: e.tensor_tensor(
                    out=kcr[0:32], in0=kcr[0:32], in1=st4[0:32, 0:2].unsqueeze(2).to_broadcast([32, 2, 64]), op=ALU.mult))
                A("dve", ["kcr", "gvec"], ["kcbs"], lambda e: e.tensor_tensor(
                    out=kcbs[:], in0=kcr[0:32], in1=gvec[0:32, 0:64].unsqueeze(1).to_broadcast([32, 2, 64]), op=ALU.mult))
                for g in range(2):
                    A("pe", ["kcbs", "idb"], ["K0"], lambda e, g=g: e.transpose(
                        out=K0[0:64, 512 + g * 32:512 + (g + 1) * 32], in_=kcbs[:, g, :], identity=idb[0:32, 0:32]))
                A("dve", ["K0"], ["KCa"], lambda e: e.tensor_copy(
                    out=KCa[0:64, :, :], in_=K0[0:64, 512:576].rearrange("p (g j) -> p g j", g=2)))
                A("sp", [], ["xs1"], lambda e: e.dma_start(out=WP[:].rearrange("p (t c) -> p t c", t=4),
                                                           in_=cwin[b].rearrange("(t p) c -> p t c", p=128)), dma=True)
                A("dve", ["xs1"], ["PGB0", "PGB1"], lambda e: e.tensor_copy(out=WPB[:].rearrange("p t c -> p (t c)"), in_=WP[:]))
                for tw in range(4):
                    for g in range(2):
                        A("pe", ["PGB0", "PGB1", "idb"], ["K0"], lambda e, tw=tw, g=g: e.transpose(
                            out=K0[0:64, (tw * 2 + g) * 128:(tw * 2 + g + 1) * 128], in_=WPB[:, tw, g * 64:(g + 1) * 64],
                            identity=idb[:]))
                A("dve", ["K0"], ["KTw"], lambda e: e.tensor_copy(
                    out=KTw[0:64, :, 0:512].rearrange("p g (t k) -> p g t k", t=4),
                    in_=K0[0:64, :].rearrange("p (t g k) -> p g t k", t=4, g=2)))
                A("act", ["xs1"], ["VVw"], lambda e: e.activation(
                    out=VVw[:, 0:4, :, 0:64], in_=WP[:].rearrange("p (t c) -> p t c", t=4)[:, :, 128:256].rearrange(
                        "p t (g d) -> p t g d", g=2), func=AF.Copy))
                A("dve", ["KN"], ["KTw"], lambda e: e.tensor_copy(out=KTw[0:64, :, 512:520], in_=KN[:, 2:4, b * 8:(b + 1) * 8]))
                A("dve", ["QTs"], ["QTb"], lambda e: e.tensor_copy(out=QTb[0:64, :, :], in_=QTs[:, :, b * 8:(b + 1) * 8]))
                for g in range(2):
                    qrhs = QTb[:, g * 4:(g + 1) * 4, :].rearrange("p h t -> p (h t)")
                    qrhs68 = QTb[0:68, g * 4:(g + 1) * 4, :].rearrange("p h t -> p (h t)")
                    A("pe", ["KCa", "QTb"], ["K3"], lambda e, g=g, qrhs68=qrhs68: e.matmul(
                        K[3][0:32, 0:32], lhsT=KCa[0:68, g, :], rhs=qrhs68, start=True, stop=True))
                    A("act", ["K3"], ["PTc"], lambda e: e.activation(out=PTc, in_=K[3][0:32, 0:32], func=AF.Exp))
                    A("pe", ["vca", "PTc"], ["K7"], lambda e, g=g: e.matmul(K[7][0:65, 0:32], lhsT=vca[:, g, :], rhs=PTc,
                                                                         start=True, stop=True))
                    A("act", ["K7"], ["OcT"], lambda e, g=g: e.activation(
                        out=OcT[:, g * 4:(g + 1) * 4, b * 8:(b + 1) * 8], in_=K[7][0:65, 0:32].rearrange("p (h t) -> p h t", h=4),
                        func=AF.Copy))
                    A("pe", ["PTc", "idb"], ["K0"], lambda e: e.transpose(out=K0[0:32, 640:672], in_=PTc, identity=idb[0:32, 0:32]))
                    A("dve", ["K0"], ["EX"], lambda e: e.tensor_copy(out=EX[0:32, 0, 0:32], in_=K0[0:32, 640:672]))
                    A("dve", ["EX"], ["st4b"], lambda e: e.tensor_reduce(out=st4b[0:32, 0:1], in_=EX[0:32, 0, 0:32], axis=AX.X, op=ALU.add))
                    A("dve", ["st4b"], ["st4b"], lambda e: e.reciprocal(out=st4b[0:32, 0:1], in_=st4b[0:32, 0:1]))
                    A("dve", ["EX", "st4b"], ["EX"], lambda e: e.tensor_scalar(
                        out=EX[0:32, 0, 0:32], in0=EX[0:32, 0, 0:32], scalar1=st4b[0:32, 0:1], scalar2=None, op0=ALU.mult))
                    A("pe", ["csts", "EX"], ["K3"], lambda e: e.matmul(K[3][0:8, 64:96], lhsT=selT, rhs=EX[0:32, 0, 0:32],
                                                                      start=True, stop=True))
                    A("dve", [], ["SC"], lambda e: e.memset(SC[0:8, 0:33], 0.0))
                    A("dve", ["K3"], ["SC"], lambda e: e.tensor_copy(out=SC[0:8, 0:32], in_=K[3][0:8, 64:96]))
                    A("dve", ["SC", "csts"], ["SC"], lambda e: e.tensor_tensor(out=SC[0:8, 0:33], in0=SC[0:8, 0:33], in1=keepS, op=ALU.mult))
                    A("dve", ["SC", "csts"], ["SC"], lambda e: e.tensor_tensor(out=SC[0:8, 0:33], in0=SC[0:8, 0:33], in1=addS, op=ALU.add))
                    A("dve", ["SC"], ["m8"], lambda e: e.max(out=m8[0:8], in_=SC[0:8, 0:33]))
                    A("dve", ["SC", "m8"], ["WK"], lambda e: e.match_replace(out=WK[0:8, 0:33], in_to_replace=m8[0:8],
                                                                              in_values=SC[0:8, 0:33], imm_value=-2.0))
                    A("dve", ["WK"], ["m8"], lambda e: e.max(out=m8[0:8], in_=WK[0:8, 0:33]))
                    A("dve", ["m8"], ["thr"], lambda e: e.tensor_reduce(out=thr[0:8], in_=m8[0:8], axis=AX.X, op=ALU.min))
                    A("dve", ["SC", "thr"], ["SEL"], lambda e: e.tensor_scalar(
                        out=SEL[0:8, 0:33], in0=SC[0:8, 0:33], scalar1=thr[0:8, 0:1], scalar2=None, op0=ALU.is_ge))
                    A("dve", ["SEL"], ["MBF"], lambda e: e.tensor_scalar(
                        out=MBF[0:8, 0:33], in0=SEL[0:8, 0:33], scalar1=-1.0, scalar2=BIG, op0=ALU.add, op1=ALU.mult))
                    A("pe", ["MBF", "idb"], ["K0"], lambda e: e.transpose(out=K0[0:33, 704:712], in_=MBF[0:8, 0:33],
                                                                          identity=idb[0:8, 0:8]))
                    A("dve", ["K0"], ["RHSMs"], lambda e: e.tensor_copy(
                        out=RHSMs[0:33], in_=K0[0:33, 704:712].unsqueeze(1).to_broadcast([33, 4, 8])))
                    for kt in range(17):
                        dst = K[5][:, kt * 32:(kt + 1) * 32] if kt < 16 else K[6][:, 0:32]
                        kb = "K5" if kt < 16 else "K6"
                        A("pe", ["KTs", "QTb"], [kb], lambda e, g=g, kt=kt, dst=dst, qrhs=qrhs: e.matmul(
                            dst, lhsT=KTs[:, g, kt * 128:(kt + 1) * 128], rhs=qrhs, start=True, stop=False))
                        A("pe", ["Eexp", "RHSMs"], [kb], lambda e, kt=kt, dst=dst: e.matmul(
                            dst, lhsT=Eexp[:, kt * 128:(kt + 1) * 128], rhs=RHSMs[:].rearrange("p h t -> p (h t)"),
                            start=False, stop=(kt < 16)))
                        if kt == 16:
                            A("pe", ["idb", "MSKs"], [kb], lambda e, dst=dst: e.matmul(dst, lhsT=idb[:], rhs=MSKs[:, 0, :],
                                                                                       start=False, stop=True))
                    A("act", ["K5"], ["PTs"], lambda e: e.activation(out=PTs[:, 0:512], in_=K[5][:], func=AF.Exp))
                    A("act", ["K6"], ["PTs"], lambda e: e.activation(out=PTs[:, 512:544], in_=K[6][:, 0:32], func=AF.Exp))
                    for kt in range(17):
                        A("pe", ["PTs", "VVs"], ["K7"], lambda e, g=g, kt=kt: e.matmul(
                            K[7][0:65, 32:64], lhsT=VVs[:, kt, g, :], rhs=PTs[:, kt * 32:(kt + 1) * 32],
                            start=(kt == 0), stop=(kt == 16)))
                    A("act", ["K7"], ["OsT"], lambda e, g=g: e.activation(
                        out=OsT[:, g * 4:(g + 1) * 4, b * 8:(b + 1) * 8], in_=K[7][0:65, 32:64].rearrange("p (h t) -> p h t", h=4),
                        func=AF.Copy))
                    for s_ in range(5):
                        dst = K[4][:, s_ * 32:(s_ + 1) * 32]
                        msk = {0: 1, 4: 0}.get(s_)
                        A("pe", ["KTw", "QTb"], ["K4"], lambda e, g=g, s_=s_, dst=dst, qrhs=qrhs, msk=msk: e.matmul(
                            dst, lhsT=KTw[:, g, s_ * 128:(s_ + 1) * 128], rhs=qrhs, start=True, stop=(msk is None)))
                        if msk is not None:
                            A("pe", ["idb", "MSKs"], ["K4"], lambda e, dst=dst, msk=msk: e.matmul(
                                dst, lhsT=idb[:], rhs=MSKs[:, msk, :], start=False, stop=True))
                    A("act", ["K4"], ["PTw"], lambda e: e.activation(out=PTw, in_=K[4][:, 0:160], func=AF.Exp))
                    for s_ in range(5):
                        A("pe", ["PTw", "VVw"], ["K7"], lambda e, g=g, s_=s_: e.matmul(
                            K[7][0:65, 64:96], lhsT=VVw[:, s_, g, :], rhs=PTw[:, s_ * 32:(s_ + 1) * 32],
                            start=(s_ == 0), stop=(s_ == 4)))
                    A("act", ["K7"], ["OwT"], lambda e, g=g: e.activation(
                        out=OwT[:, g * 4:(g + 1) * 4, b * 8:(b + 1) * 8], in_=K[7][0:65, 64:96].rearrange("p (h t) -> p h t", h=4),
                        func=AF.Copy))

            for b_ in range(16):
                sample_seq(b_)
            for src, ks, dst, kd in ((OcT, "OcT", OSEL, "OSEL"), (OsT, "OsT", OSEL, "OSEL"), (OwT, "OwT", OWIN, "OWIN")):
                for h in range(8):
                    A("pe", [ks, "idb"], ["K0"], lambda e, src=src, h=h: e.transpose(
                        out=K0[:, h * 66:h * 66 + 65], in_=src[:, h, :], identity=idb[0:65, 0:65]))
                A("act", ["K0"], [kd], lambda e, dst=dst: e.activation(
                    out=dst[:], in_=K0[:, 0:528].rearrange("p (h d) -> p h d", h=8)[:, :, 0:65], func=AF.Copy))
                if ks == "OcT":
                    A("dve", ["OSEL"], ["st8"], lambda e: e.reciprocal(out=st8[:], in_=OSEL[:, :, 64]))
                    A("dve", ["OSEL", "st8"], ["OCMP"], lambda e: e.tensor_tensor(
                        out=OCMP[:], in0=OSEL[:, :, 0:64], in1=st8[:].unsqueeze(2).to_broadcast([128, 8, 64]), op=ALU.mult))
            finish(xs[0], "xs0", ys[:, :], "ys", NOWN if debug else None)
        BKEYS = WK_ + AKEYS + ["Eexp", "MSK", "xT0", "xT1", "OC", "OCT", "KD", "VB", "QD", "KDI", "WUD", "QN", "qkT", "AT", "Sb0", "Sb1", "PB", "hs", "RHSM0", "RHSM1", "PT4"] + \
            ["S0_%d" % b_ for b_ in range(16)] + SKEYS
        if STAGE >= 7:
            A("dve", [], BKEYS + ["fdummy"], lambda e: e.memset(fdummy[:], 0.0))
            A("sp", [], ["lnmc"], lambda e: e.dma_start(out=lnmc[:], in_=lnmcol), dma=True)
            stg = [(H, "H"), (xs[1], "xs1")]
            cnt_ = [0]

            def wload(src, dst, scale_ap):
                st, ks = stg[cnt_[0] % 2]
                use_act = (cnt_[0] % 2 == 0)
                cnt_[0] += 1
                A("sp", [], [ks], lambda e: e.dma_start(out=st[:], in_=src), dma=True)
                rk = [ks] + (["lnmc"] if scale_ap is not None else [])
                if use_act:
                    if scale_ap is not None:
                        A("act", rk, ["WUD"], lambda e: e.activation(out=dst, in_=st[:], func=AF.Copy, scale=scale_ap))
                    else:
                        A("act", rk, ["WUD"], lambda e: e.activation(out=dst, in_=st[:], func=AF.Copy))
                else:
                    if scale_ap is not None:
                        A("dve", rk, ["WUD"], lambda e: e.tensor_scalar(out=dst, in0=st[:], scalar1=scale_ap, scalar2=None,
                                                                         op0=ALU.mult))
                    else:
                        A("dve", rk, ["WUD"], lambda e: e.tensor_copy(out=dst, in_=st[:]))

            for k in range(8):
                for q4 in range(4):
                    wload(w_up[k * 128:(k + 1) * 128, q4 * 1024:(q4 + 1) * 1024], WU[:, k, q4 * 1024:(q4 + 1) * 1024],
                          lnmc[:, k:k + 1])
            for f in range(32):
                wload(w_down[f * 128:(f + 1) * 128, :], WD[:, f, :], None)

            def mlp_tile(hsrc, ydst, ky, idx):
                X, kx = xs[idx % 2], "xs%d" % (idx % 2)
                A("sp", [ky], [kx], lambda e: e.dma_start(out=X[:], in_=hsrc), dma=True)
                A("act", [kx], ["OSEL", "ss"], lambda e: e.activation(
                    out=OSEL[:].rearrange("p h d -> p (h d)")[:, 0:512], in_=X[:, 0:512], func=AF.Square, accum_out=ss[:]))
                A("act", [kx], ["OSEL", "rstd"], lambda e: e.activation(
                    out=OSEL[:].rearrange("p h d -> p (h d)")[:, 0:512], in_=X[:, 512:1024], func=AF.Square,
                    accum_out=rstd[:]))
                A("dve", ["ss", "rstd"], ["ss"], lambda e: e.tensor_tensor(out=ss[:], in0=ss[:], in1=rstd[:], op=ALU.add))
                A("act", ["ss"], ["rstd"], lambda e: e.activation(out=rstd[:], in_=ss[:], func=AF.Ln, scale=1.0 / D, bias=EPS))
                A("act", ["rstd"], ["rstd"], lambda e: e.activation(out=rstd[:], in_=rstd[:], func=AF.Exp, scale=-0.5))
                A("dve", [kx, "rstd"], ["xb"], lambda e: e.tensor_scalar(out=xb[:], in0=X[:], scalar1=rstd[:, 0:1],
                                                                         scalar2=None, op0=ALU.mult))
                for k in range(8):
                    A("pe", ["xb", "idb"], ["K0"], lambda e, k=k: e.transpose(
                        out=K0[:, k * 128:(k + 1) * 128], in_=xb[:, k * 128:(k + 1) * 128], identity=idb[:]))
                A("dve", ["K0"], ["QTa"], lambda e: e.tensor_copy(out=QTa[:].rearrange("p k t -> p (k t)"), in_=K0[:]))
                for rnd in range(4):
                    ub = (rnd % 2) * 8
                    ku = "uT%d" % (rnd % 2)
                    for j in range(8):
                        f = rnd * 8 + j
                        bank = 1 + (f % 4)
                        kb = "K%d" % bank
                        R, kr = PTb[f % 2], "PTb%d" % (f % 2)
                        for k in range(8):
                            A("pe", ["QTa", "WUD"], [kb], lambda e, f=f, k=k, bank=bank: e.matmul(
                                K[bank][:, 0:128], lhsT=WU[:, k, f * 128:(f + 1) * 128], rhs=QTa[:, k, :],
                                start=(k == 0), stop=(k == 7)))
                        A("act", [kb], [kr], lambda e, bank=bank, R=R: e.activation(out=R[:, 0:128], in_=K[bank][:, 0:128],
                                                                                    func=AF.Relu))
                        A("dve", [kr], [ku], lambda e, j=j, ub=ub, R=R: e.tensor_tensor(
                            out=uT[:, ub + j, :], in0=R[:, 0:128], in1=R[:, 0:128], op=ALU.mult))
                    for n in range(2):
                        bank = 5 + n
                        kb = "K%d" % bank
                        for j in range(8):
                            f = rnd * 8 + j
                            A("pe", [ku, "WUD"], [kb], lambda e, f=f, j=j, ub=ub, n=n, bank=bank: e.matmul(
                                K[bank][:], lhsT=uT[:, ub + j, :], rhs=WD[:, f, n * 512:(n + 1) * 512],
                                start=(f == 0), stop=(f == 31)))
                for n in range(2):
                    bank = 5 + n
                    kb = "K%d" % bank
                    A("dve", [kb, kx], ["H"], lambda e, n=n, bank=bank: e.tensor_tensor(
                        out=H[:, n * 512:(n + 1) * 512], in0=K[bank][:], in1=X[:, n * 512:(n + 1) * 512], op=ALU.add))
                A("pool", ["H"], [ky], lambda e: e.dma_start(out=ydst, in_=H[:]), dma=True, final=True)

            for i_ in range(NOWN):
                mlp_tile(yp[i_ * 128:(i_ + 1) * 128, :], yp[i_ * 128:(i_ + 1) * 128, :], "yp%d" % i_, i_)
            if STAGE >= 8:
                mlp_tile(ys[:, :], ys[:, :], "ys", NOWN)
        S.emit()
    return nc


def make_sample_consts():
    f32 = np.float32
    slope = 2.0 ** (-(np.arange(8) + 1.0))
    csts = np.zeros((128, 74), f32)
    for hh in range(4):
        for t in range(8):
            csts[hh * 8 + t, t] = 1.0
    keep = np.ones(33, f32)
    add = np.zeros(33, f32)
    keep[0] = keep[32] = 0.0
    add[0] = add[32] = 5.0
    csts[:, 8:41] = keep[None]
    csts[:, 41:74] = add[None]
    iota = np.arange(128, dtype=f32).reshape(128, 1)
    sq = np.zeros((8, 64), f32)
    for h in range(8):
        for t in range(8):
            sq[0, h * 8 + t] = 128.0 * slope[h]
            sq[1, h * 8 + t] = slope[h]
            sq[2, h * 8 + t] = -128.0 * 16.0 * slope[h]
            sq[3, h * 8 + t] = -slope[h] * t
    j = np.arange(32)
    sq[4, 0:32] = j // 2
    sq[5, 0:32] = 64 * (j % 2) + 63
    sq[6, 0:32] = 1.0
    sq[7, 0:32] = 1.0
    sm = np.zeros((128, 192), f32)
    r = np.arange(128)
    for par in range(2):
        for jj in range(64):
            sm[2 * jj + par, par * 64 + jj] = 1.0
    kk = r[:, None]
    tt = np.arange(8)[None, :]
    mn = np.where(kk <= tt, 0.0, -BIG).astype(f32)
    mw = np.where(kk > tt, 0.0, -BIG).astype(f32)
    sm[:, 128:160] = np.tile(mn, (1, 4))
    sm[:, 160:192] = np.tile(mw, (1, 4))
    return csts, iota, sq, sm


def make_consts(half):
    f32 = np.float32
    t = np.arange(128)
    slope = 2.0 ** (-(np.arange(8) + 1.0))
    cst = np.zeros((128, 747), f32)
    ac = np.zeros((128, 8, 66), f32)
    ac[:] = (slope[:, None] * 64.0 * np.arange(66)[None, :])[None]
    if half == 1:
        ac[:, :, 0:2] -= BIG
    cst[:, 0:528] = ac.reshape(128, 528)
    cst[:, 528] = np.where(t >= 63, 0.0, -BIG)
    cst[:, 529] = np.where(t == 127, 0.0, -BIG)
    rv = np.ones((128, NOWN), f32)
    if half == 0:
        rv[:63, 0] = 0.0
    cst[:, 530:547] = rv
    keep = np.ones(66, f32)
    add = np.zeros(66, f32)
    j0 = 2 * half
    keep[j0], add[j0] = 0.0, 5.0
    if half == 1:
        keep[0:2], add[0:2] = 0.0, -1.0
    cst[:, 547:613] = keep[None]
    cst[:, 613:679] = add[None]
    cst[:, 679] = np.where(t < 64, 5.0, 0.0)
    cst[:, 680] = np.where(t < 64, 0.0, 1.0)
    cst[:, 681] = np.where(t < 64, -1.0, 5.0)
    s_ = t % 64
    cst[:, 682:746] = (s_[:, None] <= np.arange(64)[None, :]).astype(f32)
    kk, tt = t[:, None], t[None, :]
    tri = np.where(kk <= tt, 0.0, -BIG).astype(f32)
    anti = np.where(kk > tt, 0.0, -BIG).astype(f32)
    full = np.zeros((128, 128), f32) if half == 0 else np.full((128, 128), -BIG, f32)
    m0anti = anti if half == 0 else np.full((128, 128), -BIG, f32)
    masks = np.stack([np.tile(m, (1, 4)) for m in (tri, anti, full, m0anti)], axis=1).astype(f32)
    eexp = np.zeros((66, NT * 128), f32)
    n = np.arange(NT * 128)
    eexp[n // 64, n] = 1.0
    kpos = np.stack([(n // 128).astype(f32), (n % 128).astype(f32), np.ones_like(n, f32), np.ones_like(n, f32)], 0)
    qpos = np.zeros((NOWN, 4, 2, 4, 128), f32)
    for i in range(NOWN):
        v = 2 * i
        for g in range(2):
            for hh in range(4):
                s = slope[g * 4 + hh]
                qpos[i, 0, g, hh, :] = 128.0 * s
                qpos[i, 1, g, hh, :] = s
                qpos[i, 2, g, hh, :] = -128.0 * v * s
                qpos[i, 3, g, hh, :] = -s * t
    return cst, masks, eexp, kpos.astype(f32), qpos.reshape(NOWN, 4, 1024)


_NC = None


def kernel(x_prompt, x_sample, cache_kv, cache_win, state_rnn, page_table, ln_mix, w_in, q_norm, k_norm,
           cmp_pe, cmp_w1, cmp_w2, attn_out_norm, rnn_lb_logits, rnn_out_norm, w_out, ln_mlp, w_up, w_down,
           _debug=False):
    global _NC
    f32 = np.float32
    x_prompt = np.asarray(x_prompt, f32)
    x_sample = np.asarray(x_sample, f32)
    cache_win = np.asarray(cache_win, f32)
    state_rnn = np.asarray(state_rnn, f32)
    k_norm = np.asarray(k_norm, f32)
    cmp_pe = np.asarray(cmp_pe, f32)
    if _NC is None or _NC[0] != _debug:
        _NC = (_debug, build_nc(_debug))
    nc = _NC[1]
    t = np.arange(128)

    def tmats(c):
        same = (t[:, None] // c == t[None, :] // c)
        return ((t[:, None] <= t[None, :]) & same).astype(f32), ((t[:, None] > t[None, :]) & same).astype(f32)

    tc_p, tr_p = tmats(64)
    tc_s, tr_s = tmats(8)
    rowmask = (t[:, None] // 8 == np.arange(16)[None, :]).astype(f32)
    gv = np.concatenate([k_norm[0, 0], k_norm[0, 1], k_norm[0, 2], np.asarray(q_norm, f32)[0],
                         np.asarray(attn_out_norm, f32)[0], np.asarray(rnn_out_norm, f32)[0]])
    peT = np.ascontiguousarray(cmp_pe[0].reshape(2, 32, 2, 64).transpose(2, 3, 0, 1).reshape(128, 2, 32))
    common = {
        "w_in": np.ascontiguousarray(np.asarray(w_in, f32)[0]),
        "w_out": np.ascontiguousarray(np.asarray(w_out, f32)[0]),
        "lncol": np.ascontiguousarray(np.asarray(ln_mix, f32)[0].reshape(8, 128).T),
        "gvecd": np.ascontiguousarray(np.broadcast_to(gv[None], (128, 1280))),
        "lbl": np.ascontiguousarray(np.broadcast_to(np.asarray(rnn_lb_logits, f32)[None], (128, 2, 512))),
        "ident": np.eye(128, dtype=f32),
        "w_up": np.ascontiguousarray(np.asarray(w_up, f32)[0]),
        "w_down": np.ascontiguousarray(np.asarray(w_down, f32)[0]),
        "lnmcol": np.ascontiguousarray(np.asarray(ln_mlp, f32)[0].reshape(8, 128).T),
        "tmat": np.ascontiguousarray(np.stack([tc_p, tr_p, tc_s, tr_s], axis=1)),
        "rowmask": rowmask,
        "peTd": peT,
        "cmp_w1": np.ascontiguousarray(np.asarray(cmp_w1, f32)[0]),
        "cmp_w2": np.ascontiguousarray(np.asarray(cmp_w2, f32)[0].transpose(1, 0, 2)),
    }
    zt = np.zeros((128, D), f32)
    cc = [make_consts(0), make_consts(1)]
    csts, iota, sqd, smd = make_sample_consts()
    cache2 = np.ascontiguousarray(np.asarray(cache_kv, f32)[0].reshape(2560 * 128, 512))
    ptab = np.asarray(page_table).astype(np.int32)
    common.update({"cache": cache2, "cstsd": csts, "iotad": iota, "sqd": sqd, "smd": smd})
    in_maps = []
    for c in range(8):
        s, half = c // 2, c % 2
        xvv = np.concatenate([x_prompt[s], zt], 0) if half == 0 else np.concatenate([zt, x_prompt[s]], 0)
        m = dict(common)
        m["xv"] = np.ascontiguousarray(xvv)
        m["xsm"] = np.ascontiguousarray(x_sample[16 * c:16 * c + 16].reshape(128, D))
        m["cwin"] = np.ascontiguousarray(cache_win[0, 16 * c:16 * c + 16].reshape(16, 512, 256))
        m["srnn"] = np.ascontiguousarray(state_rnn[0, 16 * c:16 * c + 16])
        m["ptrep"] = np.ascontiguousarray(np.broadcast_to(ptab[16 * c:16 * c + 16].reshape(1, 256), (128, 256)))
        m["cstd"], m["masks4"], m["eexp"], m["kposrows"], m["qposrows"] = cc[half]
        in_maps.append(m)
    res = run_bass_kernel_spmd(nc, in_maps, core_ids=list(range(8))).results

    y_prompt = np.zeros((4, 4096, D), f32)
    y_sample = np.zeros((128, 8, D), f32)
    kv_prompt = np.zeros((1, 4, 4096, 4, 2, 64), f32)
    kv_sample = np.zeros((1, 128, 8, 4, 2, 64), f32)
    win_prompt = np.zeros((1, 4, 512, 2, 2, 64), f32)
    win_sample = np.zeros((1, 128, 512, 2, 2, 64), f32)
    rnn_prompt = np.zeros((1, 4, 8, 64, 64), f32)
    rnn_sample = np.zeros((1, 128, 8, 64, 64), f32)
    for c in range(8):
        r = res[c]
        s, half = c // 2, c % 2
        ypc = r["yp"].reshape(NOWN, 128, D)
        yps = y_prompt[s].reshape(32, 128, D)
        if half == 0:
            yps[0::2] = ypc[0:16]
            kv_prompt[0, s] = r["kvp"][:4096].reshape(4096, 4, 2, 64)
            win_prompt[0, s] = r["winp"][:512].reshape(512, 2, 2, 64)
        else:
            yps[1::2] = ypc[1:17]
            rnn_prompt[0, s] = r["rnnp"]
        kv_sample[0, 16 * c:16 * c + 16] = r["kvs"].reshape(16, 8, 4, 2, 64)
        y_sample[16 * c:16 * c + 16] = r["ys"].reshape(16, 8, D)
        win_sample[0, 16 * c:16 * c + 16] = r["wins"].reshape(16, 512, 2, 2, 64)
        rnn_sample[0, 16 * c:16 * c + 16] = r["rnns"]
    if _debug:
        kernel._dbg = [res[c]["dbg"] for c in range(8)]
    return (y_prompt, y_sample, kv_prompt, kv_sample, win_prompt, win_sample, rnn_prompt, rnn_sample)
```

```python
import contextlib
import numpy as np
import concourse.bass as bass
import concourse.mybir as mybir
from concourse.bass_utils import run_bass_kernel_spmd

F32 = mybir.dt.float32
BF16 = mybir.dt.bfloat16
AF = mybir.ActivationFunctionType
ALU = mybir.AluOpType
AX = mybir.AxisListType

NT = 33
D = 1024
INW = 3352
C_Q, C_KV, C_G, C_RQ, C_RF, C_RI, C_RG = 0, 512, 1280, 1304, 1816, 2328, 2840
EPS = 1e-6

ENGS = ("pe", "act", "dve", "pool", "sp")
DMA_POOL = 8


class Op:
    __slots__ = ("eng", "fn", "deps", "dma", "idx", "needed", "token")

    def __init__(self, eng, fn, deps, dma, idx):
        self.eng, self.fn, self.deps, self.dma, self.idx = eng, fn, deps, dma, idx
        self.needed = False
        self.token = None


class Sched:
    def __init__(self, nc):
        self.nc = nc
        self.ops = []
        self.last_w = {}
        self.readers = {}
        self.final = []

    def add(self, eng, fn, r=(), w=(), dma=False, final=False, rg=0):
        idx = len(self.ops)
        deps = {}
        if eng == "pe":
            prev = getattr(self, "prev_pe", None)
            if prev is not None and prev[1] != rg:
                deps[prev[0]] = "force"
            self.prev_pe = (idx, rg)
        for k in r:
            lw = self.last_w.get(k)
            if lw is not None and deps.get(lw) != "force":
                deps[lw] = True
        for k in w:
            lw = self.last_w.get(k)
            if lw is not None and deps.get(lw) != "force":
                deps[lw] = True
            for rd in self.readers.get(k, ()):
                if rd not in deps:
                    deps[rd] = False
        for k in r:
            self.readers.setdefault(k, []).append(idx)
        for k in w:
            self.last_w[k] = idx
            self.readers[k] = []
        self.ops.append(Op(eng, fn, deps, dma, idx))
        if final:
            self.final.append(idx)
        return idx

    def emit(self):
        nc = self.nc
        ops = self.ops
        for op in ops:
            nd = {}
            for d, strong in op.deps.items():
                p = ops[d]
                if p.eng == op.eng and not p.dma:
                    if op.eng == "pe" and strong != "force":
                        continue
                nd[d] = strong
            best = {}
            keep = {}
            for d, strong in nd.items():
                p = ops[d]
                if p.dma:
                    keep[d] = strong
                elif p.eng not in best or d > best[p.eng]:
                    best[p.eng] = d
            for d in best.values():
                keep[d] = True
            op.deps = keep
            for d in keep:
                ops[d].needed = True
        for f in self.final:
            ops[f].needed = True
        cnt = {e: 0 for e in ENGS}
        dcnt = {e: 0 for e in ENGS}
        with contextlib.ExitStack() as es:
            NEP = 8
            EPOCH = 1500
            csem = {e: [es.enter_context(nc.semaphore("c_%s%d" % (e, i))) for i in range(NEP if e in ("pe", "dve", "act") else 1)]
                    for e in ENGS}
            dsem = {e: [es.enter_context(nc.semaphore("d_%s%d" % (e, i))) for i in range(DMA_POOL)]
                    for e in ("sp", "pool", "act")}
            for op in ops:
                if op.dma:
                    j = dcnt[op.eng]
                    dcnt[op.eng] += 1
                    op.token = (dsem[op.eng][j % DMA_POOL], 16 * (j // DMA_POOL + 1))
                elif op.needed:
                    ep = cnt[op.eng] // EPOCH
                    assert ep < len(csem[op.eng]), (op.eng, cnt[op.eng])
                    op.token = (csem[op.eng][ep], cnt[op.eng] % EPOCH + 1)
                    cnt[op.eng] += 1
            per = {e: [op for op in ops if op.eng == e] for e in ENGS}
            final_tokens = [ops[f].token for f in self.final]

            def run(engname, eng):
                known = {}
                for op in per[engname]:
                    for d in op.deps:
                        sem, val = ops[d].token
                        if known.get(id(sem), 0) >= val:
                            continue
                        eng.wait_ge(sem, val)
                        known[id(sem)] = val
                    if op.dma:
                        sem, val = op.token
                        if val > 16 and known.get(id(sem), 0) < val - 16:
                            eng.wait_ge(sem, val - 16)
                            known[id(sem)] = val - 16
                        op.fn(eng).then_inc(sem, 16)
                    else:
                        ins = op.fn(eng)
                        if op.needed:
                            ins.then_inc(op.token[0], 1)
                if engname == "sp":
                    for sem, val in final_tokens:
                        if known.get(id(sem), 0) >= val:
                            continue
                        eng.wait_ge(sem, val)
                        known[id(sem)] = val

            with nc.Block() as block:
                @block.tensor
                def _(e):
                    run("pe", e)

                @block.scalar
                def _(e):
                    run("act", e)

                @block.vector
                def _(e):
                    run("dve", e)

                @block.gpsimd
                def _(e):
                    run("pool", e)

                @block.sync
                def _(e):
                    run("sp", e)


import os
BIG = 30000.0
STAGE = int(os.environ.get("KSTAGE", "9"))
SUB = int(os.environ.get("KSUB", "9"))
SKIP = os.environ.get("KSKIP", "")
NOWN = 17
SCALE = 0.125
DEBUG = False


def build_nc(debug=False):
    nc = bass.Bass("TRN2", target_bir_lowering=False)

    def din(name, shape, dt=F32):
        return nc.dram_tensor(name, list(shape), dt, kind="ExternalInput").ap()

    def dout(name, shape, dt=F32):
        return nc.dram_tensor(name, list(shape), dt, kind="ExternalOutput").ap()

    xv = din("xv", [NT * 128, D])
    xsm = din("xsm", [128, D])
    w_in = din("w_in", [D, INW])
    w_out = din("w_out", [D, D])
    w_up = din("w_up", [D, 4 * D])
    w_down = din("w_down", [4 * D, D])
    lnmcol = din("lnmcol", [128, 8])
    lncol = din("lncol", [128, 8])
    gvec_d = din("gvecd", [128, 1280])
    lbl = din("lbl", [128, 2, 512])
    ident = din("ident", [128, 128])
    tmat = din("tmat", [128, 4, 128])
    rowmask = din("rowmask", [128, 16])
    cst_d = din("cstd", [128, 747])
    masks_d = din("masks4", [128, 4, 512])
    eexp_d = din("eexp", [66, NT * 128])
    kpos_d = din("kposrows", [4, NT * 128])
    qpos_d = din("qposrows", [NOWN, 4, 1024])
    pet_d = din("peTd", [128, 2, 32])
    w1_d = din("cmp_w1", [2, 4096, 128])
    w2_d = din("cmp_w2", [128, 2, 64])
    cwin = din("cwin", [16, 512, 256])
    cache_d = din("cache", [327680, 512])
    ptrep_d = din("ptrep", [128, 256], mybir.dt.int32)
    csts_d = din("cstsd", [128, 74])
    iota_d = din("iotad", [128, 1])
    sq_d = din("sqd", [8, 64])
    sm_d = din("smd", [128, 192])
    srnn = din("srnn", [16, 8, 64, 64])

    yp = dout("yp", [NOWN * 128, D])
    ys = dout("ys", [128, D])
    kvp = dout("kvp", [NT * 128, 512])
    kvs = dout("kvs", [128, 512])
    winp = dout("winp", [5 * 128, 256])
    wins = dout("wins", [16, 512, 256])
    rnnp = dout("rnnp", [8, 64, 64])
    rnns = dout("rnns", [16, 8, 64, 64])
    if debug:
        dbg = dout("dbg", [(NOWN + 1) * 128, 5, 520])

    S = Sched(nc)
    with contextlib.ExitStack() as es:
        def sb(name, shape, dt=F32):
            return es.enter_context(nc.sbuf_tensor(name, list(shape), dt))

        def ps(name, shape, dt=F32):
            return es.enter_context(nc.psum_tensor(name, list(shape), dt))

        def A(eng, r, w, fn, **kw):
            S.add(eng, fn, r=r, w=w, **kw)

        BIGW = sb("BIGW", [128, 65536], BF16)
        W = BIGW[:, 0:26816].rearrange("p (k n) -> p k n", k=8)
        ARENA = BIGW[:, 26816:47808]
        WU = BIGW[:, 0:32768].rearrange("p (k n) -> p k n", k=8)
        WD = BIGW[:, 32768:65536].rearrange("p (f n) -> p f n", f=32)
        _bo = [47808]

        def bw(n, parts=128):
            a = _bo[0]
            _bo[0] += n
            assert _bo[0] <= 65536
            return BIGW[0:parts, a:a + n]
        XcT = ARENA[:, 0:8448].rearrange("p (a n) -> p a n", a=4)
        W1b = ARENA[:, 8448:16640].rearrange("p (a j h) -> p a j h", a=2, j=32)
        KTs = ARENA[:, 0:8448].rearrange("p (g n) -> p g n", g=2)
        VVs = ARENA[:, 8448:12738].rearrange("p (t g d) -> p t g d", t=NT, g=2)
        WO = ARENA[:, 12800:20992].rearrange("p (k n) -> p k n", k=8)
        AKEYS = ["XcT", "W1b", "KTs", "VVs", "WO"]
        KTw = sb("KTw", [128, 2, 768], BF16)
        VVw = sb("VVw", [128, 6, 2, 65], BF16)
        Eexp = bw(NT * 128, 128)
        MSK = bw(2048).rearrange("p (m n) -> p m n", m=4)
        cst = sb("cst", [128, 747])
        gvec = sb("gvec", [128, 1280])
        qgs = sb("qgs", [128, 64])
        lnc = sb("lnc", [128, 8])
        lbb = sb("lbb", [128, 512])
        omlb = sb("omlb", [128, 512])
        idb = sb("idb", [128, 128], BF16)
        tm = sb("tm", [128, 4, 128])
        rmask = sb("rmask", [128, 16])
        ones = sb("ones", [128, 1])
        peT = sb("peT", [128, 2, 32])
        W2b = sb("W2b", [128, 2, 64], BF16)
        kcT = sb("kcT", [64, 2, 66], BF16)
        vcb = sb("vcb", [66, 2, 64], BF16)
        fdummy = sb("fdummy", [128, 1])
        acore = cst[:, 0:528].rearrange("p (h j) -> p h j", h=8)
        fix2 = cst[:, 528:530]
        rowvalid = cst[:, 530:547]
        keepc = cst[:, 547:613]
        addc = cst[:, 613:679]
        mlo5 = cst[:, 679:680]
        mhi = cst[:, 680:681]
        c1col = cst[:, 681:682]
        tri2 = cst[:, 682:746]
        xs = [sb("xs%d" % i, [128, D]) for i in range(2)]
        xb = sb("xb", [128, D], BF16)
        xT = [bw(1024).rearrange("p (k t) -> p k t", k=8) for i in range(2)]
        ss = sb("ss", [128, 1])
        rstd = sb("rstd", [128, 1])
        Z = [sb("Z0", [128, 768])] * 2
        zb = sb("zb", [128, 4, 64], BF16)
        sq4 = sb("sq4", [128, 2, 2, 64])
        ms4 = sb("ms4", [128, 2, 2])
        T = [sb("T%d" % i, [128, 512]) for i in range(6)]
        QS = sb("QS", [128, 512])
        RGS = sb("RGS", [128, 512])
        KD = bw(512)
        VB = bw(512)
        QD = bw(512)
        KDI = bw(512)
        qkT = bw(1024).rearrange("p (a t) -> p a t", a=8)
        AT = bw(512).rearrange("p (h t) -> p h t", h=8)
        Sb = [bw(256).rearrange("p (a d) -> p a d", a=4) for i in range(2)]
        ET = sb("ET", [128, 2, 4])
        St = sb("St", [128, 4, 64])
        ZQ = sb("ZQ", [128, 8, 64])
        QN = bw(512).rearrange("p (h d) -> p h d", h=8)
        st8 = sb("st8", [128, 8])
        QTa = sb("QTa", [128, 8, 128], BF16)
        gsig = sb("gsig", [128, 8, 3])
        LG = sb("LG", [128, 4, 66])
        EX = sb("EX", [128, 4, 66])
        PB = bw(264).rearrange("p (h j) -> p h j", h=4)
        st4 = sb("st4", [128, 4])
        st4b = sb("st4b", [128, 4])
        SC = sb("SC", [128, 66])
        WK = sb("WK", [128, 66])
        SEL = sb("SEL", [128, 66])
        m8 = sb("m8", [128, 8])
        thr = sb("thr", [128, 1])
        MBF = sb("MBF", [128, 66], BF16)
        RHSM_ = [bw(512, 128).rearrange("p (h t) -> p h t", h=4) for i in range(2)]
        PT4 = bw(512, 66).rearrange("p (h t) -> p h t", h=4)
        PTb = [sb("PTb%d" % i, [128, 512], BF16) for i in range(3)]
        OSEL = sb("OSEL", [128, 8, 65])
        OWIN = sb("OWIN", [128, 8, 65])
        OCMP = sb("OCMP", [128, 8, 64])
        OR32 = sb("OR32", [128, 8, 64])
        g8 = sb("g8", [128, 8, 3])
        OC = bw(1024)
        OCT = bw(1024).rearrange("p (k t) -> p k t", k=8)
        uT = sb("uT", [128, 16, 128], BF16)
        lnmc = sb("lnmc", [128, 8])
        H = sb("H", [128, D])
        stgq = H
        TMP = T[5][:].rearrange("p (h d) -> p h d", h=8)
        ACC = T[4][:].rearrange("p (h d) -> p h d", h=8)
        junk = OCT
        hs = bw(264).rearrange("p (h j) -> p h j", h=4)
        kcr = sb("kcr", [66, 2, 64])
        kcbf = sb("kcbf", [66, 2, 64], BF16)
        S0 = ARENA[:, 0:8192].bitcast(F32).rearrange("p (a d) -> p a d", a=64)
        etots = sb("etots", [128, 4, 16])
        kdm = PTb

        K0 = ps("K0", [128, 1024], BF16)
        K = [None] + [ps("K%d" % i, [128, 512]) for i in range(1, 8)]

        def ld(dst, src, key, eng="sp"):
            S.add(eng, lambda e: e.dma_start(out=dst, in_=src), w=[key], dma=True)

        ld(lnc[:], lncol, "lnc")
        ld(gvec[:], gvec_d, "gvec")
        ld(cst[:], cst_d, "cst")
        ld(tm[:], tmat, "tm")
        ld(rmask[:], rowmask, "rmask")
        ld(peT[:], pet_d, "peT")
        ld(T[0][:], lbl[:, 0, :], "T0")
        ld(T[1][:], lbl[:, 1, :], "T1")
        ld(T[2][:, 0:128], ident, "T2")
        ld(T[3][:, 0:128], w2_d.rearrange("p a d -> p (a d)"), "T3")
        A("dve", ["T2"], ["idb"], lambda e: e.tensor_copy(out=idb[:], in_=T[2][:, 0:128]))
        A("dve", ["T3"], ["W2b"], lambda e: e.tensor_copy(out=W2b[:].rearrange("p a d -> p (a d)"), in_=T[3][:, 0:128]))
        A("dve", [], ["ones"], lambda e: e.memset(ones[:], 1.0))
        A("dve", [], ["St"], lambda e: e.memset(St[:], 0.0))
        A("dve", ["gvec"], ["qgs"], lambda e: e.tensor_scalar_mul(out=qgs[:], in0=gvec[:, 192:256], scalar1=SCALE))
        A("dve", ["T0", "T1"], ["lbb"], lambda e: e.tensor_sub(out=lbb[:], in0=T[1][:], in1=T[0][:]))
        A("act", ["lbb"], ["lbb"], lambda e: e.activation(out=lbb[:], in_=lbb[:], func=AF.Exp))
        A("dve", ["lbb"], ["lbb"], lambda e: e.tensor_scalar_add(out=lbb[:], in0=lbb[:], scalar1=1.0))
        A("dve", ["lbb"], ["lbb"], lambda e: e.reciprocal(out=lbb[:], in_=lbb[:]))
        A("dve", ["lbb"], ["omlb"], lambda e: e.tensor_scalar(out=omlb[:], in0=lbb[:], scalar1=-1.0, scalar2=1.0,
                                                              op0=ALU.mult, op1=ALU.add))

        wstg = [(H, "H"), (xs[0], "xs0"), (xs[1], "xs1")]
        wi_ = 0
        for k in range(8):
            for hf in range(4):
                c0, c1 = hf * 838, (hf + 1) * 838
                st_, ks_ = wstg[wi_ % 3]
                A("sp", [], [ks_], lambda e, k=k, c0=c0, c1=c1, st_=st_: e.dma_start(
                    out=st_[:, 0:838], in_=w_in[k * 128:(k + 1) * 128, c0:c1]), dma=True)
                if wi_ % 2 == 0:
                    A("act", [ks_, "lnc"], ["W%d" % k], lambda e, k=k, c0=c0, c1=c1, st_=st_: e.activation(
                        out=W[:, k, c0:c1], in_=st_[:, 0:838], func=AF.Copy, scale=lnc[:, k:k + 1]))
                else:
                    A("dve", [ks_, "lnc"], ["W%d" % k], lambda e, k=k, c0=c0, c1=c1, st_=st_: e.tensor_scalar(
                        out=W[:, k, c0:c1], in0=st_[:, 0:838], scalar1=lnc[:, k:k + 1], scalar2=None, op0=ALU.mult))
                wi_ += 1
        WK_ = ["W%d" % k for k in range(8)]

        def prep_x(xsrc, p):
            X, XT = xs[p], xT[p]
            kx, kxt = "xs%d" % p, "xT%d" % p
            A("sp", [], [kx], lambda e: e.dma_start(out=X[:], in_=xsrc), dma=True)
            A("act", [kx], ["OCT", "ss"], lambda e: e.activation(out=OCT[:].rearrange("p k t -> p (k t)"), in_=X[:], func=AF.Square, accum_out=ss[:]))
            A("act", ["ss"], ["rstd"], lambda e: e.activation(out=rstd[:], in_=ss[:], func=AF.Ln, scale=1.0 / D, bias=EPS))
            A("act", ["rstd"], ["rstd"], lambda e: e.activation(out=rstd[:], in_=rstd[:], func=AF.Exp, scale=-0.5))
            A("dve", [kx, "rstd"], ["xb"], lambda e: e.tensor_scalar(out=xb[:], in0=X[:], scalar1=rstd[:, 0:1],
                                                                     scalar2=None, op0=ALU.mult))
            for k in range(8):
                A("pe", ["xb", "idb"], ["K0"], lambda e, k=k: e.transpose(
                    out=K0[:, k * 128:(k + 1) * 128], in_=xb[:, k * 128:(k + 1) * 128], identity=idb[:]))
            A("dve", ["K0"], [kxt], lambda e: e.tensor_copy(out=XT[:].rearrange("p k t -> p (k t)"), in_=K0[:]))
            return X, XT, kx, kxt

        def mm(XT, kxt, bank, n, c0):
            for k in range(8):
                A("pe", [kxt] + WK_, ["K%d" % bank], lambda e, k=k: e.matmul(
                    K[bank][:, 0:n], lhsT=XT[:, k, :], rhs=W[:, k, c0:c0 + n], start=(k == 0), stop=(k == 7)))

        A("dve", [], AKEYS + ["fdummy"], lambda e: e.memset(fdummy[:], 0.0))
        for typ in range(2):
            for jq in range(4):
                A("sp", [], ["H"], lambda e, typ=typ, jq=jq: e.dma_start(
                    out=stgq[:].rearrange("p (j h) -> p j h", j=8),
                    in_=w1_d[typ, jq * 1024:(jq + 1) * 1024, :].rearrange("(j p) h -> p j h", p=128)), dma=True)
                A("dve", ["H"], ["W1b"], lambda e, typ=typ, jq=jq: e.tensor_copy(
                    out=W1b[:, typ, jq * 8:(jq + 1) * 8, :], in_=stgq[:].rearrange("p (j h) -> p j h", j=8)))
        def tile_1a(v):
            p = v % 2
            X, XT, kx, kxt = prep_x(xv[v * 128:(v + 1) * 128, :], p)
            K1v = K[1][:, 0:256].rearrange("p (a t) -> p a t", a=4)
            for tg in range(4):
                c0 = C_KV + tg * 64
                for par in range(2):
                    for k in range(8):
                        A("pe", [kxt] + WK_, ["K1"], lambda e, tg=tg, par=par, k=k, c0=c0: e.matmul(
                            K1v[64 * par:64 * par + 64, tg, :], lhsT=W[:, k, c0:c0 + 64], rhs=XT[:, k, par:128:2],
                            start=(k == 0), stop=(k == 7)))
            for a in range(2):
                A("dve", ["K1", "peT"], ["XcT"], lambda e, v=v, K1v=K1v, a=a: e.tensor_tensor(
                    out=XcT[:, 2 * a:2 * a + 2, v * 64:(v + 1) * 64].rearrange("p g (b j) -> p g b j", b=2),
                    in0=K1v[:, 2 * a:2 * a + 2, :].rearrange("p g (b j) -> p g b j", b=2),
                    in1=peT[:, a, :].unsqueeze(1).unsqueeze(1).to_broadcast([128, 2, 2, 32]), op=ALU.add))
        for v_ in range(NT if STAGE >= 1 else 0):
            tile_1a(v_)
        K3v = K[3][:, 0:256].rearrange("p (a d) -> p a d", a=4)
        for tg in range(4):
            typ = tg // 2
            for j in range(32):
                A("pe", ["XcT", "W1b"], ["K2"], lambda e, tg=tg, typ=typ, j=j: e.matmul(
                    K[2][:, 0:66], lhsT=W1b[:, typ, j, :], rhs=XcT[:, tg, j:2112:32], start=(j == 0), stop=(j == 31)))
            A("act", ["K2"], ["T4"], lambda e: e.activation(out=T[4][:, 0:66], in_=K[2][:, 0:66], func=AF.Exp, scale=-1.0))
            A("dve", ["T4"], ["T4"], lambda e: e.tensor_scalar_add(out=T[4][:, 0:66], in0=T[4][:, 0:66], scalar1=1.0))
            A("dve", ["T4"], ["T4"], lambda e: e.reciprocal(out=T[4][:, 0:66], in_=T[4][:, 0:66]))
            A("dve", ["T4", "K2"], ["hs"], lambda e, tg=tg: e.tensor_tensor(out=hs[:, tg, :], in0=K[2][:, 0:66],
                                                                           in1=T[4][:, 0:66], op=ALU.mult))
            A("pe", ["hs", "W2b"], ["K3"], lambda e, tg=tg, typ=typ: e.matmul(
                K3v[0:66, tg, :], lhsT=hs[:, tg, :], rhs=W2b[:, typ, :], start=True, stop=True))
        A("act", ["K3"], ["kcr"], lambda e: e.activation(out=kcr[:], in_=K3v[0:66, 0:2, :], func=AF.Copy))
        A("act", ["K3"], ["vcb"], lambda e: e.activation(out=vcb[:], in_=K3v[0:66, 2:4, :], func=AF.Copy))
        A("dve", ["kcr"], ["T5"], lambda e: e.tensor_tensor(out=T[5][0:66, 0:128].rearrange("p (g d) -> p g d", g=2),
                                                            in0=kcr[:], in1=kcr[:], op=ALU.mult))
        A("dve", ["T5"], ["st4"], lambda e: e.tensor_reduce(out=st4[0:66, 0:2],
                                                            in_=T[5][0:66, 0:128].rearrange("p (g d) -> p g d", g=2),
                                                            axis=AX.X, op=ALU.add))
        A("act", ["st4"], ["st4"], lambda e: e.activation(out=st4[0:66, 0:2], in_=st4[0:66, 0:2], func=AF.Ln,
                                                          scale=1.0 / 64, bias=EPS))
        A("act", ["st4"], ["st4"], lambda e: e.activation(out=st4[0:66, 0:2], in_=st4[0:66, 0:2], func=AF.Exp, scale=-0.5))
        A("dve", ["kcr", "st4"], ["kcr"], lambda e: e.tensor_tensor(
            out=kcr[:], in0=kcr[:], in1=st4[0:66, 0:2].unsqueeze(2).to_broadcast([66, 2, 64]), op=ALU.mult))
        A("dve", ["kcr", "gvec"], ["kcbf"], lambda e: e.tensor_tensor(
            out=kcbf[:], in0=kcr[:], in1=gvec[0:66, 0:64].unsqueeze(1).to_broadcast([66, 2, 64]), op=ALU.mult))
        for g in range(2):
            A("pe", ["kcbf", "idb"], ["K0"], lambda e, g=g: e.transpose(
                out=K0[0:64, g * 128:g * 128 + 66], in_=kcbf[:, g, :], identity=idb[0:66, 0:66]))
        A("dve", ["K0"], ["kcT"], lambda e: e.tensor_copy(
            out=kcT[:], in_=K0[0:64, 0:256].rearrange("p (g n) -> p g n", g=2)[:, :, 0:66]))
        A("dve", [], AKEYS + ["fdummy"], lambda e: e.memset(fdummy[:], 0.0))

        A("dve", [], ["KTs"], lambda e: e.memset(KTs[64:128, :, :], 0.0))
        for c0 in range(0, NT * 128, 1024):
            n = min(1024, NT * 128 - c0)
            A("sp", [], ["H"], lambda e, c0=c0, n=n: e.dma_start(out=stgq[0:66, 0:n], in_=eexp_d[:, c0:c0 + n]), dma=True)
            A("dve", [], ["Eexp"], lambda e, c0=c0, n=n: e.memset(Eexp[64:128, c0:c0 + n], 0.0))
            A("dve", ["H"], ["Eexp"], lambda e, c0=c0, n=n: e.tensor_copy(out=Eexp[0:66, c0:c0 + n], in_=stgq[0:66, 0:n]))
            A("sp", [], ["H"], lambda e, c0=c0, n=n: e.dma_start(out=stgq[64:68, 0:n], in_=kpos_d[:, c0:c0 + n]), dma=True)
            for g in range(2):
                A("dve", ["H"], ["KTs"], lambda e, c0=c0, n=n, g=g: e.tensor_copy(
                    out=KTs[64:68, g, c0:c0 + n], in_=stgq[64:68, 0:n]))
        for m in range(4):
            A("sp", [], ["H"], lambda e, m=m: e.dma_start(out=stgq[:, 0:512], in_=masks_d[:, m, :]), dma=True)
            A("dve", ["H"], ["MSK"], lambda e, m=m: e.tensor_copy(out=MSK[:, m, :], in_=stgq[:, 0:512]))
        for k in range(8):
            A("sp", [], ["H"], lambda e, k=k: e.dma_start(out=stgq[:], in_=w_out[k * 128:(k + 1) * 128, :]), dma=True)
            A("dve", ["H"], ["WO"], lambda e, k=k: e.tensor_copy(out=WO[:, k, :], in_=stgq[:]))
        A("dve", [], ["KTw"], lambda e: e.memset(KTw[64:128, :, :], 0.0))
        A("dve", [], ["QTa"], lambda e: e.memset(QTa[64:128, :, :], 0.0))
        for g_ in range(2):
            A("dve", [], ["RHSM%d" % g_], lambda e, g_=g_: e.memset(RHSM_[g_][64:128, :, :], 0.0))
        A("dve", [], ["VVs"], lambda e: e.memset(VVs[:, :, :, 64:65], 1.0))
        A("dve", [], ["VVw"], lambda e: e.memset(VVw[:, :, :, 64:65], 1.0))

        def rnn_gates(bf, bi, sample):
            kf, ki = "K%d" % bf, "K%d" % bi
            e1, uu, kk, G = T[0], T[1], T[2], T[3]
            A("act", [kf], ["T0"], lambda e: e.activation(out=e1[:], in_=K[bf][:], func=AF.Exp, scale=-1.0))
            A("act", [ki], ["VB"], lambda e: e.activation(out=VB[:], in_=K[bi][:], func=AF.Copy))
            A("dve", ["T0"], ["T0"], lambda e: e.tensor_scalar_add(out=e1[:], in0=e1[:], scalar1=1.0))
            A("dve", ["T0"], ["T0"], lambda e: e.reciprocal(out=e1[:], in_=e1[:]))
            A("dve", ["T0", "omlb"], ["T1"], lambda e: e.tensor_tensor(out=uu[:], in0=e1[:], in1=omlb[:], op=ALU.mult))
            A("dve", ["T1", "lbb"], ["T0"], lambda e: e.tensor_tensor(out=e1[:], in0=uu[:], in1=lbb[:], op=ALU.add))
            A("dve", ["T1", "omlb"], ["T2"], lambda e: e.tensor_tensor(out=kk[:], in0=omlb[:], in1=uu[:], op=ALU.subtract))
            A("act", ["T0"], ["T3"], lambda e: e.activation(out=G[:], in_=e1[:], func=AF.Ln))
            mi = 3 if sample else 1
            A("pe", ["tm", "T3"], ["K3"], lambda e: e.matmul(K[3][:], lhsT=tm[:, mi, :], rhs=G[:], start=True, stop=True))
            A("act", ["K3"], ["T4"], lambda e: e.activation(out=T[4][:], in_=K[3][:], func=AF.Exp))
            A("dve", ["T2", "T4"], ["KD"], lambda e: e.tensor_tensor(out=KD[:], in0=kk[:], in1=T[4][:], op=ALU.mult))

        pst = K[4][:, 0:256].rearrange("p (a d) -> p a d", a=4)
        ptot = K[4][:, 256:264].rearrange("p (c a) -> p c a", c=2)

        def state_tot():
            G = T[3]
            for c in range(2):
                for h in range(8):
                    hp, ee = h // 2, h % 2
                    A("pe", ["T3", "ones"], ["K4"], lambda e, c=c, h=h, hp=hp, ee=ee: e.matmul(
                        ptot[64 * ee:64 * ee + 64, c, hp:hp + 1], lhsT=G[c * 64:(c + 1) * 64, h * 64:(h + 1) * 64],
                        rhs=ones[c * 64:(c + 1) * 64, 0:1], start=True, stop=True), rg=64 * c)
            A("act", ["K4"], ["ET"], lambda e: e.activation(out=ET[:], in_=ptot, func=AF.Exp))

        def state_update(c):
            for h in range(8):
                hp, ee = h // 2, h % 2
                A("pe", ["KD", "VB"], ["K4"], lambda e, c=c, h=h, hp=hp, ee=ee: e.matmul(
                    pst[64 * ee:64 * ee + 64, hp, :], lhsT=KD[c * 64:(c + 1) * 64, h * 64:(h + 1) * 64],
                    rhs=VB[c * 64:(c + 1) * 64, h * 64:(h + 1) * 64], start=True, stop=True), rg=64 * c)
            A("dve", ["St", "ET"], ["St"], lambda e, c=c: e.tensor_tensor(
                out=St[:], in0=St[:], in1=ET[:, c, :].unsqueeze(2).to_broadcast([128, 4, 64]), op=ALU.mult))
            A("dve", ["St", "K4"], ["St"], lambda e: e.tensor_tensor(out=St[:], in0=St[:], in1=pst, op=ALU.add))

        def k_norms(Zt, kz):
            Z5 = Zt[:].rearrange("p (a b g d) -> p a b g d", a=3, b=2, g=2)
            KS = Z5[:, 1:3, 0, :, :]
            A("dve", [kz], ["sq4"], lambda e: e.tensor_tensor(out=sq4[:], in0=KS, in1=KS, op=ALU.mult))
            A("dve", ["sq4"], ["ms4"], lambda e: e.tensor_reduce(out=ms4[:], in_=sq4[:], axis=AX.X, op=ALU.add))
            A("act", ["ms4"], ["ms4"], lambda e: e.activation(out=ms4[:], in_=ms4[:], func=AF.Ln, scale=1.0 / 64, bias=EPS))
            A("act", ["ms4"], ["ms4"], lambda e: e.activation(out=ms4[:], in_=ms4[:], func=AF.Exp, scale=-0.5))
            A("dve", [kz, "ms4"], [kz], lambda e: e.tensor_tensor(
                out=KS, in0=KS, in1=ms4[:].unsqueeze(3).to_broadcast([128, 2, 2, 64]), op=ALU.mult))
            A("dve", [kz, "gvec"], [kz], lambda e: e.tensor_tensor(
                out=KS, in0=KS, in1=gvec[:, 64:192].rearrange("p (a d) -> p a d", a=2).unsqueeze(2).to_broadcast(
                    [128, 2, 2, 64]), op=ALU.mult))
            return Z5

        def silu_from(bank, dst, kdst):
            kb = "K%d" % bank
            A("act", [kb], ["T5"], lambda e: e.activation(out=T[5][:], in_=K[bank][:], func=AF.Exp, scale=-1.0))
            A("dve", ["T5"], ["T5"], lambda e: e.tensor_scalar_add(out=T[5][:], in0=T[5][:], scalar1=1.0))
            A("dve", ["T5"], ["T5"], lambda e: e.reciprocal(out=T[5][:], in_=T[5][:]))
            A("dve", ["T5", kb], [kdst], lambda e: e.tensor_tensor(out=dst[:], in0=K[bank][:], in1=T[5][:], op=ALU.mult))

        def head_rms(src3, ksrc, n):
            A("dve", [ksrc], ["T5"], lambda e: e.tensor_tensor(out=TMP[:, 0:n, :], in0=src3, in1=src3, op=ALU.mult))
            A("dve", ["T5"], ["st8"], lambda e: e.tensor_reduce(out=st8[:, 0:n], in_=TMP[:, 0:n, :], axis=AX.X, op=ALU.add))
            A("act", ["st8"], ["st8"], lambda e: e.activation(out=st8[:, 0:n], in_=st8[:, 0:n], func=AF.Ln,
                                                              scale=1.0 / 64, bias=EPS))
            A("act", ["st8"], ["st8"], lambda e: e.activation(out=st8[:, 0:n], in_=st8[:, 0:n], func=AF.Exp, scale=-0.5))

        def finish(X, kx, ydst, ykey, dbi):
            A("dve", ["OSEL"], ["st8"], lambda e: e.tensor_scalar_max(out=st8[:], in0=OSEL[:, :, 64], scalar1=1e-30))
            A("dve", ["st8"], ["st8"], lambda e: e.reciprocal(out=st8[:], in_=st8[:]))
            A("dve", ["st8", "gsig"], ["g8"], lambda e: e.tensor_tensor(out=g8[:, :, 1], in0=gsig[:, :, 1], in1=st8[:],
                                                                         op=ALU.mult))
            A("dve", ["OWIN"], ["st8"], lambda e: e.tensor_scalar_max(out=st8[:], in0=OWIN[:, :, 64], scalar1=1e-30))
            A("dve", ["st8"], ["st8"], lambda e: e.reciprocal(out=st8[:], in_=st8[:]))
            A("dve", ["st8", "gsig"], ["g8"], lambda e: e.tensor_tensor(out=g8[:, :, 2], in0=gsig[:, :, 2], in1=st8[:],
                                                                         op=ALU.mult))
            A("dve", ["OCMP", "gsig"], ["T4"], lambda e: e.tensor_tensor(
                out=ACC[:], in0=OCMP[:], in1=gsig[:, :, 0:1].to_broadcast([128, 8, 64]), op=ALU.mult))
            A("dve", ["OSEL", "g8"], ["T5"], lambda e: e.tensor_tensor(
                out=TMP[:], in0=OSEL[:, :, 0:64], in1=g8[:, :, 1:2].to_broadcast([128, 8, 64]), op=ALU.mult))
            A("dve", ["T4", "T5"], ["T4"], lambda e: e.tensor_tensor(out=ACC[:], in0=ACC[:], in1=TMP[:], op=ALU.add))
            A("dve", ["OWIN", "g8"], ["T5"], lambda e: e.tensor_tensor(
                out=TMP[:], in0=OWIN[:, :, 0:64], in1=g8[:, :, 2:3].to_broadcast([128, 8, 64]), op=ALU.mult))
            A("dve", ["T4", "T5"], ["T4"], lambda e: e.tensor_tensor(out=ACC[:], in0=ACC[:], in1=TMP[:], op=ALU.add))
            if debug and dbi is not None:
                for di, (src, kk_, n) in enumerate([(OCMP, "OCMP", 512), (OSEL, "OSEL", 520), (OWIN, "OWIN", 520),
                                                    (OR32, "OR32", 512), (ACC, "T4", 512)]):
                    A("pool", [kk_], [], lambda e, di=di, src=src, n=n: e.dma_start(
                        out=dbg[dbi * 128:(dbi + 1) * 128, di, 0:n], in_=src[:].rearrange("p h d -> p (h d)")),
                      dma=True, final=True)
            head_rms(ACC[:], "T4", 8)
            A("dve", ["T4", "st8"], ["T4"], lambda e: e.tensor_tensor(
                out=ACC[:], in0=ACC[:], in1=st8[:].unsqueeze(2).to_broadcast([128, 8, 64]), op=ALU.mult))
            A("dve", ["T4", "gvec"], ["OC"], lambda e: e.tensor_tensor(
                out=OC[:, 0:512], in0=ACC[:].rearrange("p h d -> p (h d)"), in1=gvec[:, 256:768], op=ALU.mult))
            head_rms(OR32[:], "OR32", 8)
            A("dve", ["OR32", "st8"], ["OR32"], lambda e: e.tensor_tensor(
                out=OR32[:], in0=OR32[:], in1=st8[:].unsqueeze(2).to_broadcast([128, 8, 64]), op=ALU.mult))
            A("dve", ["OR32", "gvec"], ["OR32"], lambda e: e.tensor_tensor(
                out=OR32[:].rearrange("p h d -> p (h d)"), in0=OR32[:].rearrange("p h d -> p (h d)"),
                in1=gvec[:, 768:1280], op=ALU.mult))
            A("dve", ["OR32", "RGS"], ["OC"], lambda e: e.tensor_tensor(
                out=OC[:, 512:1024], in0=OR32[:].rearrange("p h d -> p (h d)"), in1=RGS[:], op=ALU.mult))
            for k in range(8):
                A("pe", ["OC", "idb"], ["K0"], lambda e, k=k: e.transpose(
                    out=K0[:, k * 128:(k + 1) * 128], in_=OC[:, k * 128:(k + 1) * 128], identity=idb[:]))
            A("dve", ["K0"], ["OCT"], lambda e: e.tensor_copy(out=OCT[:].rearrange("p k t -> p (k t)"), in_=K0[:]))
            for n in range(2):
                for k in range(8):
                    A("pe", ["OCT", "WO"], ["K%d" % (1 + n)], lambda e, n=n, k=k: e.matmul(
                        K[1 + n][:], lhsT=OCT[:, k, :], rhs=WO[:, k, n * 512:(n + 1) * 512], start=(k == 0), stop=(k == 7)))
                A("dve", ["K%d" % (1 + n), kx], ["H"], lambda e, n=n, X=X: e.tensor_tensor(
                    out=H[:, n * 512:(n + 1) * 512], in0=K[1 + n][:], in1=X[:, n * 512:(n + 1) * 512], op=ALU.add))
            A("pool", ["H"], [ykey], lambda e: e.dma_start(out=ydst, in_=H[:]), dma=True, final=True)

        def tile_1b(v):
            p = v % 2
            own = (v % 2 == 0) and STAGE >= 3
            i_own = v // 2
            X, XT, kx, kxt = prep_x(xv[v * 128:(v + 1) * 128, :], p)
            Zt, kz = Z[p], "Z0"
            mm(XT, kxt, 1, 512, C_KV)
            mm(XT, kxt, 2, 256, C_KV + 512)
            A("act", ["K1"], [kz], lambda e, Zt=Zt: e.activation(out=Zt[:, 0:512], in_=K[1][:, 0:512], func=AF.Copy))
            A("act", ["K2"], [kz], lambda e, Zt=Zt: e.activation(out=Zt[:, 512:768], in_=K[2][:, 0:256], func=AF.Copy))
            mm(XT, kxt, 1, 512, C_RF)
            mm(XT, kxt, 2, 512, C_RI)
            Z5 = k_norms(Zt, kz)
            A("pool", [kz], [], lambda e, v=v, Zt=Zt: e.dma_start(out=kvp[v * 128:(v + 1) * 128, :], in_=Zt[:, 0:512]),
              dma=True, final=True)
            if v >= 28:
                A("pool", [kz], [], lambda e, v=v, Zt=Zt: e.dma_start(
                    out=winp[(v - 28) * 128:(v - 27) * 128, :], in_=Zt[:, 512:768]), dma=True, final=True)
            A("dve", [kz], ["zb"], lambda e, Z5=Z5: e.tensor_copy(out=zb[:].rearrange("p (a g) d -> p a g d", a=2),
                                                                  in_=Z5[:, 1:3, 0, :, :]))
            for a4 in range(4):
                A("pe", ["zb", "idb"], ["K0"], lambda e, a4=a4: e.transpose(
                    out=K0[0:64, a4 * 128:(a4 + 1) * 128], in_=zb[:, a4, :], identity=idb[:]))
            A("dve", ["K0"], ["KTs"], lambda e, v=v: e.tensor_copy(
                out=KTs[0:64, :, v * 128:(v + 1) * 128], in_=K0[0:64, 0:256].rearrange("p (g t) -> p g t", g=2)))
            sl = v % 6
            A("dve", ["K0"], ["KTw"], lambda e, sl=sl: e.tensor_copy(
                out=KTw[0:64, :, sl * 128:(sl + 1) * 128], in_=K0[0:64, 256:512].rearrange("p (g t) -> p g t", g=2)))
            A("sp", [], ["H"], lambda e, v=v: e.dma_start(out=stgq[64:68, 0:128], in_=kpos_d[:, v * 128:(v + 1) * 128]),
              dma=True)
            A("dve", ["H"], ["KTw"], lambda e, sl=sl: e.tensor_copy(
                out=KTw[64:68, :, sl * 128:(sl + 1) * 128], in_=stgq[64:68, 0:128].unsqueeze(1).to_broadcast([4, 2, 128])))
            A("dve", [kz], ["VVs"], lambda e, v=v, Z5=Z5: e.tensor_copy(out=VVs[:, v, :, 0:64], in_=Z5[:, 1, 1, :, :]))
            A("dve", [kz], ["VVw"], lambda e, sl=sl, Z5=Z5: e.tensor_copy(out=VVw[:, sl, :, 0:64], in_=Z5[:, 2, 1, :, :]))
            rnn_gates(1, 2, False)
            state_tot()
            if own:
                mm(XT, kxt, 1, 512, C_Q)
                mm(XT, kxt, 2, 24, C_G)
                A("act", ["K1"], ["ZQ"], lambda e: e.activation(out=ZQ[:].rearrange("p h d -> p (h d)"), in_=K[1][:],
                                                                 func=AF.Copy))
                A("act", ["K2"], ["gsig"], lambda e: e.activation(out=gsig[:].rearrange("p h b -> p (h b)"),
                                                                   in_=K[2][:, 0:24], func=AF.Exp, scale=-1.0))
                A("dve", ["gsig"], ["gsig"], lambda e: e.tensor_scalar_add(out=gsig[:], in0=gsig[:], scalar1=1.0))
                A("dve", ["gsig"], ["gsig"], lambda e: e.reciprocal(out=gsig[:], in_=gsig[:]))
                mm(XT, kxt, 1, 512, C_RQ)
                silu_from(1, QS, "QS")
                mm(XT, kxt, 2, 512, C_RG)
                silu_from(2, RGS, "RGS")
                head_rms(ZQ[:], "ZQ", 8)
                A("dve", ["ZQ", "st8"], ["ZQ"], lambda e: e.tensor_tensor(
                    out=ZQ[:], in0=ZQ[:], in1=st8[:].unsqueeze(2).to_broadcast([128, 8, 64]), op=ALU.mult))
                A("dve", ["ZQ", "qgs"], ["QN"], lambda e: e.tensor_tensor(
                    out=QN[:], in0=ZQ[:], in1=qgs[:].unsqueeze(1).to_broadcast([128, 8, 64]), op=ALU.mult))
                for h in range(8):
                    A("pe", ["QN", "idb"], ["K0"], lambda e, h=h: e.transpose(
                        out=K0[0:64, h * 128:(h + 1) * 128], in_=QN[:, h, :], identity=idb[:]))
                A("dve", ["K0"], ["QTa"], lambda e: e.tensor_copy(out=QTa[0:64, :, :].rearrange("p h t -> p (h t)"),
                                                                  in_=K0[0:64, :]))
                A("sp", [], ["H"], lambda e, i_own=i_own: e.dma_start(out=stgq[64:68, :], in_=qpos_d[i_own]), dma=True)
                A("dve", ["H"], ["QTa"], lambda e: e.tensor_copy(out=QTa[64:68, :, :].rearrange("p h t -> p (h t)"),
                                                                    in_=stgq[64:68, :]))
                if SUB < 1:
                    return
                A("pe", ["tm", "T3"], ["K3"], lambda e: e.matmul(K[3][:], lhsT=tm[:, 0, :], rhs=T[3][:], start=True, stop=True))
                A("act", ["K3"], ["T4"], lambda e: e.activation(out=T[4][:], in_=K[3][:], func=AF.Exp))
                A("act", ["K3"], ["T5"], lambda e: e.activation(out=T[5][:], in_=K[3][:], func=AF.Exp, scale=-1.0))
                A("dve", ["QS", "T4"], ["QD"], lambda e: e.tensor_tensor(out=QD[:], in0=QS[:], in1=T[4][:], op=ALU.mult))
                A("dve", ["T2", "T5"], ["KDI"], lambda e: e.tensor_tensor(out=KDI[:], in0=T[2][:], in1=T[5][:], op=ALU.mult))
                for hp in range(4):
                    A("pe", ["QD", "idb"], ["K0"], lambda e, hp=hp: e.transpose(
                        out=K0[:, hp * 128:(hp + 1) * 128], in_=QD[:, hp * 128:(hp + 1) * 128], identity=idb[:]))
                    A("pe", ["KDI", "idb"], ["K0"], lambda e, hp=hp: e.transpose(
                        out=K0[:, (4 + hp) * 128:(5 + hp) * 128], in_=KDI[:, hp * 128:(hp + 1) * 128], identity=idb[:]))
                A("dve", ["K0"], ["qkT"], lambda e: e.tensor_copy(out=qkT[:].rearrange("p a t -> p (a t)"), in_=K0[:]))
                if SUB < 2:
                    return
                pA = K[5][:].rearrange("p (h t) -> p h t", h=8)
                for c in range(2):
                    for h in [0, 2, 4, 6, 1, 3, 5, 7]:
                        hp, ee = h // 2, h % 2
                        A("pe", ["qkT"], ["K5"], lambda e, c=c, h=h, hp=hp, ee=ee: e.matmul(
                            pA[64 * c:64 * c + 64, h, :], lhsT=qkT[64 * ee:64 * ee + 64, 4 + hp, c * 64:(c + 1) * 64],
                            rhs=qkT[64 * ee:64 * ee + 64, hp, c * 64:(c + 1) * 64], start=True, stop=True), rg=64 * ee)
                if "a" not in SKIP:
                    A("dve", ["K5", "cst"], ["AT"], lambda e: e.tensor_tensor(
                        out=AT[:], in0=pA, in1=tri2.unsqueeze(1).to_broadcast([128, 8, 64]), op=ALU.mult))
                if "s" not in SKIP:
                    A("dve", ["St"], ["Sb0"], lambda e: e.tensor_copy(out=Sb[0][:], in_=St[:]))
            state_update(0)
            if own:
                A("dve", ["St"], ["Sb1"], lambda e: e.tensor_copy(out=Sb[1][:], in_=St[:]))
            state_update(1)
            if not own:
                return
            if SUB < 3:
                return
            po = K[6][:].rearrange("p (h d) -> p h d", h=8)
            for c in range(2):
                for h in range(8):
                    A("pe", ["AT", "VB"], ["K6"], lambda e, c=c, h=h: e.matmul(
                        po[64 * c:64 * c + 64, h, :], lhsT=AT[64 * c:64 * c + 64, h, :],
                        rhs=VB[64 * c:64 * c + 64, h * 64:(h + 1) * 64], start=(h == 0), stop=False), rg=64 * c)
                order = [h for h in range(8) if h % 2 == c] + [h for h in range(8) if h % 2 != c]
                for j, h in enumerate(order):
                    hp, ee = h // 2, h % 2
                    A("pe", ["qkT", "Sb%d" % c], ["K6"], lambda e, c=c, h=h, hp=hp, ee=ee, j=j: e.matmul(
                        po[64 * c:64 * c + 64, h, :], lhsT=qkT[64 * ee:64 * ee + 64, hp, c * 64:(c + 1) * 64],
                        rhs=Sb[c][64 * ee:64 * ee + 64, hp, :], start=False, stop=(j == 7)), rg=64 * ee)
            A("act", ["K6"], ["OR32"], lambda e: e.activation(out=OR32[:].rearrange("p h d -> p (h d)"), in_=K[6][:],
                                                               func=AF.Copy))
            if STAGE < 4:
                return
            nb = 2 * v + 2
            pc4 = K[3][:, 0:264].rearrange("p (h j) -> p h j", h=4)
            pocmp = K[4][:, 0:256].rearrange("p (h d) -> p h d", h=4)
            for g in range(2):
                for hh in range(4):
                    A("pe", ["QTa", "kcT"], ["K3"], lambda e, g=g, hh=hh: e.matmul(
                        pc4[:, hh, 0:nb], lhsT=QTa[0:64, g * 4 + hh, :], rhs=kcT[0:64, g, 0:nb], start=True, stop=True))
                A("dve", ["K3", "cst"], ["LG"], lambda e, g=g: e.tensor_tensor(
                    out=LG[:, :, 0:nb], in0=pc4[:, :, 0:nb], in1=acore[:, g * 4:(g + 1) * 4, 0:nb], op=ALU.add))
                A("dve", ["LG", "cst"], ["LG"], lambda e: e.tensor_tensor(
                    out=LG[:, :, 2 * v:2 * v + 2], in0=LG[:, :, 2 * v:2 * v + 2],
                    in1=fix2.unsqueeze(1).to_broadcast([128, 4, 2]), op=ALU.add))
                A("dve", ["LG"], ["st4"], lambda e: e.tensor_reduce(out=st4[:], in_=LG[:, :, 0:nb], axis=AX.X, op=ALU.max))
                A("dve", ["LG", "st4"], ["LG"], lambda e: e.tensor_tensor(
                    out=LG[:, :, 0:nb], in0=LG[:, :, 0:nb], in1=st4[:].unsqueeze(2).to_broadcast([128, 4, nb]),
                    op=ALU.subtract))
                A("act", ["LG"], ["EX"], lambda e: e.activation(out=EX[:, :, 0:nb], in_=LG[:, :, 0:nb], func=AF.Exp))
                A("dve", ["EX"], ["st4b"], lambda e: e.tensor_reduce(out=st4b[:], in_=EX[:, :, 0:nb], axis=AX.X, op=ALU.add))
                A("dve", ["st4b"], ["st4b"], lambda e: e.reciprocal(out=st4b[:], in_=st4b[:]))
                A("dve", ["st4b", "cst"], ["st4b"], lambda e: e.tensor_scalar(
                    out=st4b[:], in0=st4b[:], scalar1=rowvalid[:, i_own:i_own + 1], scalar2=None, op0=ALU.mult))
                A("dve", ["EX", "st4b"], ["EX"], lambda e: e.tensor_tensor(
                    out=EX[:, :, 0:nb], in0=EX[:, :, 0:nb], in1=st4b[:].unsqueeze(2).to_broadcast([128, 4, nb]),
                    op=ALU.mult))
                A("dve", ["EX"], ["PB"], lambda e: e.tensor_copy(out=PB[:, :, 0:nb], in_=EX[:, :, 0:nb]))
                A("dve", ["EX"], ["SC"], lambda e: e.tensor_reduce(
                    out=SC[:, 0:nb], in_=EX[:, :, 0:nb].rearrange("p h j -> p j h"), axis=AX.X, op=ALU.add))
                if nb < 66:
                    A("dve", [], ["SC"], lambda e: e.memset(SC[:, nb:66], -1.0))
                A("dve", ["SC", "cst"], ["SC"], lambda e: e.scalar_tensor_tensor(
                    out=SC[:, 2 * v:2 * v + 1], in0=SC[:, 2 * v:2 * v + 1], scalar=mhi, in1=mlo5,
                    op0=ALU.mult, op1=ALU.add))
                A("dve", ["cst"], ["SC"], lambda e: e.tensor_copy(out=SC[:, 2 * v + 1:2 * v + 2], in_=c1col))
                A("dve", ["SC", "cst"], ["SC"], lambda e: e.tensor_tensor(out=SC[:], in0=SC[:], in1=keepc, op=ALU.mult))
                A("dve", ["SC", "cst"], ["SC"], lambda e: e.tensor_tensor(out=SC[:], in0=SC[:], in1=addc, op=ALU.add))
                A("dve", ["SC"], ["m8"], lambda e: e.max(out=m8[:], in_=SC[:]))
                A("dve", ["SC", "m8"], ["WK"], lambda e: e.match_replace(out=WK[:], in_to_replace=m8[:], in_values=SC[:],
                                                                          imm_value=-2.0))
                A("dve", ["WK"], ["m8"], lambda e: e.max(out=m8[:], in_=WK[:]))
                A("dve", ["m8"], ["thr"], lambda e: e.tensor_reduce(out=thr[:], in_=m8[:], axis=AX.X, op=ALU.min))
                A("dve", ["SC", "thr"], ["SEL"], lambda e: e.tensor_scalar(
                    out=SEL[:], in0=SC[:], scalar1=thr[:, 0:1], scalar2=None, op0=ALU.is_ge))
                A("dve", ["SC"], ["WK"], lambda e: e.tensor_single_scalar(out=WK[:], in_=SC[:], scalar=0.0, op=ALU.is_ge))
                A("dve", ["SEL", "WK"], ["SEL"], lambda e: e.tensor_tensor(out=SEL[:], in0=SEL[:], in1=WK[:], op=ALU.mult))
                A("dve", ["SEL"], ["MBF"], lambda e: e.tensor_scalar(
                    out=MBF[:], in0=SEL[:], scalar1=-1.0, scalar2=BIG, op0=ALU.add, op1=ALU.mult))
                A("pe", ["MBF", "idb"], ["K0"], lambda e: e.transpose(out=K0[0:66, 0:128], in_=MBF[:], identity=idb[:]))
                for hh in range(4):
                    A("pe", ["PB", "idb"], ["K0"], lambda e, hh=hh: e.transpose(
                        out=K0[0:nb, (1 + hh) * 128:(2 + hh) * 128], in_=PB[:, hh, 0:nb], identity=idb[:]))
                A("dve", ["K0"], ["RHSM%d" % g], lambda e, g=g: e.tensor_copy(
                    out=RHSM_[g][0:66], in_=K0[0:66, 0:128].unsqueeze(1).to_broadcast([66, 4, 128])))
                A("dve", ["K0"], ["PT4"], lambda e: e.tensor_copy(
                    out=PT4[0:nb, :, :], in_=K0[0:nb, 128:640].rearrange("p (h t) -> p h t", h=4)))
                for hh in range(4):
                    A("pe", ["PT4", "vcb"], ["K4"], lambda e, g=g, hh=hh: e.matmul(
                        pocmp[:, hh, :], lhsT=PT4[0:nb, hh, :], rhs=vcb[0:nb, g, :], start=True, stop=True))
                A("act", ["K4"], ["OCMP"], lambda e, g=g: e.activation(out=OCMP[:, g * 4:(g + 1) * 4, :], in_=pocmp,
                                                                        func=AF.Copy))
            if STAGE < 5:
                return
            posb = [(K[7][:, 0:260].rearrange("p (h d) -> p h d", h=4), "K7"),
                    (K[4][:, 0:260].rearrange("p (h d) -> p h d", h=4), "K4")]
            its = []
            gi = 0
            for g in range(2):
                for br in range(2):
                    kts = list(range(0, v + 1)) if br == 0 else list(range(max(0, v - 4), v + 1))
                    for kt in kts:
                        extra = []
                        if br == 0:
                            extra.append((Eexp[:, kt * 128:(kt + 1) * 128],
                                          RHSM_[g][:].rearrange("p h t -> p (h t)"), ["Eexp", "RHSM%d" % g]))
                            if kt == v:
                                extra.append((idb[:], MSK[:, 0, :], ["idb", "MSK"]))
                            lhs_k, rk = KTs[:, g, kt * 128:(kt + 1) * 128], ["KTs"]
                            rv, kv_ = VVs[:, kt, g, :], "VVs"
                        else:
                            if kt == v:
                                extra.append((idb[:], MSK[:, 0, :], ["idb", "MSK"]))
                            elif kt == v - 4:
                                extra.append((idb[:], MSK[:, 3 if kt == 0 else 1, :], ["idb", "MSK"]))
                            elif kt == 0:
                                extra.append((idb[:], MSK[:, 2, :], ["idb", "MSK"]))
                            slk = kt % 6
                            lhs_k, rk = KTw[:, g, slk * 128:(slk + 1) * 128], ["KTw"]
                            rv, kv_ = VVw[:, slk, g, :], "VVw"
                        its.append(dict(g=g, br=br, first=(kt == kts[0]), last=(kt == kts[-1]), extra=extra, lhs_k=lhs_k,
                                        rk=rk, rv=rv, kv=kv_, gi=gi))
                    gi += 1

            SBK = [5, 6, 1]

            def s_stage(i):
                d = its[i]
                bank = SBK[i % 3]
                kb = "K%d" % bank
                g, extra = d["g"], d["extra"]
                A("pe", d["rk"] + ["QTa"], [kb], lambda e, bank=bank, lhs_k=d["lhs_k"], g=g, ne=len(extra): e.matmul(
                    K[bank][:], lhsT=lhs_k, rhs=QTa[:, g * 4:(g + 1) * 4, :].rearrange("p h t -> p (h t)"),
                    start=True, stop=(ne == 0)))
                for xi, (l_, r_, ks_) in enumerate(extra):
                    A("pe", ks_, [kb], lambda e, bank=bank, l_=l_, r_=r_, last=(xi == len(extra) - 1): e.matmul(
                        K[bank][:], lhsT=l_, rhs=r_, start=False, stop=last))

            def ep_stage(i):
                d = its[i]
                bank = SBK[i % 3]
                kb = "K%d" % bank
                PTt, kpt = PTb[i % 3], "PTb%d" % (i % 3)
                pos, kpos_ = posb[d["gi"] % 2]
                A("act", [kb], [kpt], lambda e, bank=bank, PTt=PTt: e.activation(out=PTt[:], in_=K[bank][:], func=AF.Exp))
                for hh in range(4):
                    A("pe", [kpt, d["kv"]], [kpos_], lambda e, hh=hh, PTt=PTt, rv=d["rv"], pos=pos,
                      first=(d["first"] and hh == 0), last=(d["last"] and hh == 3): e.matmul(
                          pos[:, hh, :], lhsT=PTt[:, hh * 128:(hh + 1) * 128], rhs=rv, start=first, stop=last))
                if d["last"]:
                    dst, kd = (OSEL, "OSEL") if d["br"] == 0 else (OWIN, "OWIN")
                    A("act", [kpos_], [kd], lambda e, dst=dst, g=d["g"], pos=pos: e.activation(
                        out=dst[:, g * 4:(g + 1) * 4, :], in_=pos, func=AF.Copy))

            s_stage(0)
            if len(its) > 1:
                s_stage(1)
            for i in range(len(its)):
                if i + 2 < len(its):
                    s_stage(i + 2)
                ep_stage(i)
            if STAGE < 6:
                return
            finish(X, kx, yp[i_own * 128:(i_own + 1) * 128, :], "yp%d" % i_own, i_own)

        for v_ in range(NT if STAGE >= 2 else 0):
            tile_1b(v_)
        A("pool", ["St"], [], lambda e: e.dma_start(out=rnnp.rearrange("(a e) k v -> (e k) a v", e=2), in_=St[:]),
          dma=True, final=True)

        SKEYS = ["W1s", "SELM", "S0bf", "KN", "ZV", "QTs", "ATs", "ATs", "XcTb", "PGB0", "PGB1", "hs_s", "KCa", "vca",
                 "QTb", "PTc", "PTs", "PTw", "RHSMs", "WPB", "OcT", "OsT", "OwT", "IDX", "MSKs", "kcbs"]
        _so = [0]

        def sw(n, parts=128):
            a_ = _so[0]
            _so[0] += n
            assert _so[0] <= 26816
            return BIGW[0:parts, a_:a_ + n]

        W1s = sw(8192).rearrange("p (a j h) -> p a j h", a=2, j=32)
        SELM = sw(128).rearrange("p (a j) -> p a j", a=2)
        S0bf = sw(4096).rearrange("p (a d) -> p a d", a=64)
        KN = sw(512, 64).rearrange("p (a t) -> p a t", a=4)
        ZV = sw(256).rearrange("p (a g d) -> p a g d", a=2, g=2)
        QTs = sw(1024, 64).rearrange("p (h t) -> p h t", h=8)
        ATs = sw(1024).rearrange("p (h t) -> p h t", h=8)
        oiT = ATs[0:64]
        XcTb = sw(4096).rearrange("p (a n) -> p a n", a=4)
        _pgb = sw(1024)
        PGB = [_pgb[:, 0:512], _pgb[:, 512:1024]]
        hs_s = sw(128).rearrange("p (a j) -> p a j", a=4)
        KCa = sw(64, 68).rearrange("p (g j) -> p g j", g=2)
        vca = sw(130, 32).rearrange("p (g d) -> p g d", g=2)
        QTb = sw(64, 128).rearrange("p (h t) -> p h t", h=8)
        PTc = sw(32, 32)
        PTs = sw(544)
        PTw = sw(160)
        RHSMs = sw(32, 128).rearrange("p (h t) -> p h t", h=4)
        WPB = _pgb.rearrange("p (t c) -> p t c", t=4)
        OcT = sw(1024, 65).rearrange("p (h t) -> p h t", h=8)
        OsT = sw(1024, 65).rearrange("p (h t) -> p h t", h=8)
        OwT = sw(1024, 65).rearrange("p (h t) -> p h t", h=8)
        IDX = sw(1024).bitcast(mybir.dt.int32)
        MSKs = sw(64).rearrange("p (m n) -> p m n", m=2)
        kcbs = sw(128, 32).rearrange("p (g d) -> p g d", g=2)
        csts = sb("csts", [128, 74])
        selT = csts[0:32, 0:8]
        keepS = csts[0:8, 8:41]
        addS = csts[0:8, 41:74]
        iotac = sb("iotac", [128, 1])

        A("dve", [], AKEYS + ["fdummy"] + ["S0_%d" % b for b in range(16)], lambda e: e.memset(fdummy[:], 0.0))
        for b in range(16):
            A("pool", [], ["S0_%d" % b], lambda e, b=b: e.dma_start(
                out=S0[:, b * 4:(b + 1) * 4, :], in_=srnn[b].rearrange("(a e) k v -> (e k) a v", e=2)), dma=True)
        for b in range(16):
            A("pool", [], [], lambda e, b=b: e.dma_start(out=wins[b, 0:504, :], in_=cwin[b, 8:512, :]), dma=True, final=True)
        X, XT, kx, kxt = prep_x(xsm[:, :], 0)
        Zt, kz = Z[0], "Z0"
        mm(XT, kxt, 1, 384, C_KV)
        mm(XT, kxt, 2, 384, C_KV + 384)
        A("act", ["K1"], [kz], lambda e: e.activation(out=Zt[:, 0:384], in_=K[1][:, 0:384], func=AF.Copy))
        A("act", ["K2"], [kz], lambda e: e.activation(out=Zt[:, 384:768], in_=K[2][:, 0:384], func=AF.Copy))
        mm(XT, kxt, 1, 512, C_RF)
        mm(XT, kxt, 2, 512, C_RI)
        Z5 = k_norms(Zt, kz)
        A("pool", [kz], [], lambda e: e.dma_start(out=kvs[:, :], in_=Zt[:, 0:512]), dma=True, final=True)
        for b in range(16):
            A("pool", [kz], [], lambda e, b=b: e.dma_start(out=wins[b, 504:512, :], in_=Zt[b * 8:(b + 1) * 8, 512:768]),
              dma=True, final=True)
        rnn_gates(1, 2, True)
        mm(XT, kxt, 1, 512, C_Q)
        mm(XT, kxt, 2, 24, C_G)
        A("act", ["K1"], ["ZQ"], lambda e: e.activation(out=ZQ[:].rearrange("p h d -> p (h d)"), in_=K[1][:], func=AF.Copy))
        A("act", ["K2"], ["gsig"], lambda e: e.activation(out=gsig[:].rearrange("p h b -> p (h b)"), in_=K[2][:, 0:24],
                                                           func=AF.Exp, scale=-1.0))
        A("dve", ["gsig"], ["gsig"], lambda e: e.tensor_scalar_add(out=gsig[:], in0=gsig[:], scalar1=1.0))
        A("dve", ["gsig"], ["gsig"], lambda e: e.reciprocal(out=gsig[:], in_=gsig[:]))
        mm(XT, kxt, 1, 512, C_RQ)
        silu_from(1, QS, "QS")
        mm(XT, kxt, 2, 512, C_RG)
        silu_from(2, RGS, "RGS")
        A("dve", [], WK_ + SKEYS + ["fdummy"], lambda e: e.memset(fdummy[:], 0.0))
        A("sp", [], ["csts"], lambda e: e.dma_start(out=csts[:], in_=csts_d), dma=True)
        A("sp", [], ["iotac"], lambda e: e.dma_start(out=iotac[:], in_=iota_d), dma=True)
        A("dve", [kz], ["zb"], lambda e: e.tensor_copy(out=zb[:].rearrange("p (a g) d -> p a g d", a=2), in_=Z5[:, 1:3, 0, :, :]))
        for a4 in range(4):
            A("pe", ["zb", "idb"], ["K0"], lambda e, a4=a4: e.transpose(
                out=K0[0:64, a4 * 128:(a4 + 1) * 128], in_=zb[:, a4, :], identity=idb[:]))
        A("dve", ["K0"], ["KN"], lambda e: e.tensor_copy(out=KN[:].rearrange("p a t -> p (a t)"), in_=K0[0:64, 0:512]))
        A("dve", [kz], ["ZV"], lambda e: e.tensor_copy(out=ZV[:], in_=Z5[:, 1:3, 1, :, :]))
        head_rms(ZQ[:], "ZQ", 8)
        A("dve", ["ZQ", "st8"], ["ZQ"], lambda e: e.tensor_tensor(
            out=ZQ[:], in0=ZQ[:], in1=st8[:].unsqueeze(2).to_broadcast([128, 8, 64]), op=ALU.mult))
        A("dve", ["ZQ", "qgs"], ["QN"], lambda e: e.tensor_tensor(
            out=QN[:], in0=ZQ[:], in1=qgs[:].unsqueeze(1).to_broadcast([128, 8, 64]), op=ALU.mult))
        for h in range(8):
            A("pe", ["QN", "idb"], ["K0"], lambda e, h=h: e.transpose(
                out=K0[0:64, h * 128:(h + 1) * 128], in_=QN[:, h, :], identity=idb[:]))
        A("dve", ["K0"], ["QTs"], lambda e: e.tensor_copy(out=QTs[:].rearrange("p h t -> p (h t)"), in_=K0[0:64, :]))
        A("pe", ["tm", "T3"], ["K3"], lambda e: e.matmul(K[3][:], lhsT=tm[:, 2, :], rhs=T[3][:], start=True, stop=True))
        A("act", ["K3"], ["T4"], lambda e: e.activation(out=T[4][:], in_=K[3][:], func=AF.Exp))
        A("act", ["K3"], ["T5"], lambda e: e.activation(out=T[5][:], in_=K[3][:], func=AF.Exp, scale=-1.0))
        A("dve", ["QS", "T4"], ["QD"], lambda e: e.tensor_tensor(out=QD[:], in0=QS[:], in1=T[4][:], op=ALU.mult))
        A("dve", ["T2", "T5"], ["KDI"], lambda e: e.tensor_tensor(out=KDI[:], in0=T[2][:], in1=T[5][:], op=ALU.mult))
        for hp in range(4):
            A("pe", ["QD", "idb"], ["K0"], lambda e, hp=hp: e.transpose(
                out=K0[:, hp * 128:(hp + 1) * 128], in_=QD[:, hp * 128:(hp + 1) * 128], identity=idb[:]))
            A("pe", ["KDI", "idb"], ["K0"], lambda e, hp=hp: e.transpose(
                out=K0[:, (4 + hp) * 128:(5 + hp) * 128], in_=KDI[:, hp * 128:(hp + 1) * 128], identity=idb[:]))
        A("dve", ["K0"], ["qkT"], lambda e: e.tensor_copy(out=qkT[:].rearrange("p a t -> p (a t)"), in_=K0[:]))
        for ee in range(2):
            for hp in range(4):
                A("pe", ["qkT"], ["K%d" % (5 + ee)], lambda e, ee=ee, hp=hp: e.matmul(
                    K[5 + ee][:, hp * 128:(hp + 1) * 128], lhsT=qkT[64 * ee:64 * ee + 64, 4 + hp, :],
                    rhs=qkT[64 * ee:64 * ee + 64, hp, :], start=True, stop=True), rg=64 * ee)
        for ee in range(2):
            A("dve", ["K%d" % (5 + ee), "tm"], ["ATs"], lambda e, ee=ee: e.tensor_tensor(
                out=ATs[:, ee * 4:(ee + 1) * 4, :], in0=K[5 + ee][:].rearrange("p (a t) -> p a t", a=4),
                in1=tm[:, 2, :].unsqueeze(1).to_broadcast([128, 4, 128]), op=ALU.mult))
        A("dve", ["S0_%d" % b for b in range(16)], ["S0bf"], lambda e: e.tensor_copy(out=S0bf[:, 0:32, :], in_=S0[:, 0:32, :]))
        A("act", ["S0_%d" % b for b in range(16)], ["S0bf"], lambda e: e.activation(out=S0bf[:, 32:64, :], in_=S0[:, 32:64, :],
                                                                                     func=AF.Copy))
        pos_ = K[7][:].rearrange("p (h d) -> p h d", h=8)
        for h in range(8):
            hp, ee = h // 2, h % 2
            A("pe", ["ATs", "VB"], ["K7"], lambda e, h=h, hp=hp, ee=ee: e.matmul(
                pos_[:, h, :], lhsT=ATs[:, ee * 4 + hp, :], rhs=VB[:, h * 64:(h + 1) * 64], start=True, stop=True))
        for ee in range(2):
            for b in range(16):
                for hp in range(4):
                    A("pe", ["qkT", "S0bf"], ["K%d" % (5 + ee)], lambda e, ee=ee, b=b, hp=hp: e.matmul(
                        K[5 + ee][0:64, hp * 128 + b * 8:hp * 128 + b * 8 + 8], lhsT=S0bf[64 * ee:64 * ee + 64, b * 4 + hp, :],
                        rhs=qkT[64 * ee:64 * ee + 64, hp, b * 8:(b + 1) * 8], start=True, stop=True), rg=64 * ee)
        for ee in range(2):
            A("act", ["K%d" % (5 + ee)], ["ATs"], lambda e, ee=ee: e.activation(
                out=oiT[:, ee * 4:(ee + 1) * 4, :].rearrange("p a t -> p (a t)"), in_=K[5 + ee][0:64, :], func=AF.Copy))
        for h in range(8):
            hp, ee = h // 2, h % 2
            A("pe", ["ATs", "idb"], ["K0"], lambda e, h=h, hp=hp, ee=ee: e.transpose(
                out=K0[:, h * 64:(h + 1) * 64], in_=oiT[:, ee * 4 + hp, :], identity=idb[0:64, 0:64]))
        A("act", ["K0"], ["T5"], lambda e: e.activation(out=T[5][:], in_=K0[:, 0:512], func=AF.Copy))
        A("dve", ["K7", "T5"], ["OR32"], lambda e: e.tensor_tensor(out=OR32[:].rearrange("p h d -> p (h d)"), in0=K[7][:],
                                                                  in1=T[5][:], op=ALU.add))
        G = T[3]
        ptots = K[5][:, 0:64].rearrange("p (a b) -> p a b", a=4)
        for h in range(8):
            hp, ee = h // 2, h % 2
            A("pe", ["T3", "rmask"], ["K5"], lambda e, h=h, hp=hp, ee=ee: e.matmul(
                ptots[64 * ee:64 * ee + 64, hp, :], lhsT=G[:, h * 64:(h + 1) * 64], rhs=rmask[:], start=True, stop=True))
        A("act", ["K5"], ["etots"], lambda e: e.activation(out=etots[:], in_=ptots, func=AF.Exp))
        for b in range(16):
            q = b % 2
            A("dve", ["KD", "rmask"], ["PTb%d" % q], lambda e, b=b, q=q: e.tensor_scalar(
                out=kdm[q][:], in0=KD[:], scalar1=rmask[:, b:b + 1], scalar2=None, op0=ALU.mult))
            for h in range(8):
                hp, ee = h // 2, h % 2
                A("pe", ["PTb%d" % q, "VB"], ["K4"], lambda e, h=h, q=q, hp=hp, ee=ee: e.matmul(
                    pst[64 * ee:64 * ee + 64, hp, :], lhsT=kdm[q][:, h * 64:(h + 1) * 64], rhs=VB[:, h * 64:(h + 1) * 64],
                    start=True, stop=True))
            Sbv = S0[:, b * 4:(b + 1) * 4, :]
            A("dve", ["S0_%d" % b, "etots", "S0bf"], ["S0_%d" % b], lambda e, b=b, Sbv=Sbv: e.tensor_tensor(
                out=Sbv, in0=Sbv, in1=etots[:, :, b:b + 1].to_broadcast([128, 4, 64]), op=ALU.mult))
            A("dve", ["S0_%d" % b, "K4"], ["S0_%d" % b], lambda e, Sbv=Sbv: e.tensor_tensor(out=Sbv, in0=Sbv, in1=pst, op=ALU.add))
            A("pool", ["S0_%d" % b], ["rnns_o"], lambda e, b=b, Sbv=Sbv: e.dma_start(
                out=rnns[b].rearrange("(a e) k v -> (e k) a v", e=2), in_=Sbv), dma=True, final=True)
        if STAGE >= 8:
            A("dve", ["rnns_o"] + ["S0_%d" % b for b in range(16)], ["KTs", "fdummy"], lambda e: e.memset(fdummy[:], 0.0))
            A("dve", [], ["KTs"], lambda e: e.memset(KTs[64:128, :, 0:2176], 0.0))
            for c0 in range(0, 17 * 128, 1024):
                n = min(1024, 17 * 128 - c0)
                A("sp", [], ["H"], lambda e, c0=c0, n=n: e.dma_start(out=stgq[64:68, 0:n], in_=kpos_d[:, c0:c0 + n]), dma=True)
                for g in range(2):
                    A("dve", ["H"], ["KTs"], lambda e, c0=c0, n=n, g=g: e.tensor_copy(
                        out=KTs[64:68, g, c0:c0 + n], in_=stgq[64:68, 0:n]))
            A("sp", [], ["H"], lambda e: e.dma_start(out=stgq[64:68, 0:640], in_=kpos_d[:, 1536:2176]), dma=True)
            A("dve", ["H"], ["KTw"], lambda e: e.tensor_copy(
                out=KTw[64:68, :, 0:640], in_=stgq[64:68, 0:640].unsqueeze(1).to_broadcast([4, 2, 640])))
            A("dve", [], ["QTb"], lambda e: e.memset(QTb[64:128, :, :], 0.0))
            A("dve", [], ["RHSMs"], lambda e: e.memset(RHSMs[:, :, :], 0.0))
            A("sp", [], ["H"], lambda e: e.dma_start(out=stgq[64:68, 0:64], in_=sq_d[0:4, :]), dma=True)
            A("dve", ["H"], ["QTb"], lambda e: e.tensor_copy(out=QTb[64:68, :, :].rearrange("p h t -> p (h t)"),
                                                             in_=stgq[64:68, 0:64]))
            A("sp", [], ["H"], lambda e: e.dma_start(out=stgq[64:68, 0:32], in_=sq_d[4:8, 0:32]), dma=True)
            A("dve", ["H"], ["KCa"], lambda e: e.tensor_copy(
                out=KCa[64:68, :, :], in_=stgq[64:68, 0:32].unsqueeze(1).to_broadcast([4, 2, 32])))
            A("sp", [], ["H"], lambda e: e.dma_start(out=stgq[:, 0:192], in_=sm_d), dma=True)
            A("dve", ["H"], ["SELM"], lambda e: e.tensor_copy(out=SELM[:].rearrange("p a j -> p (a j)"), in_=stgq[:, 0:128]))
            A("dve", ["H"], ["MSKs"], lambda e: e.tensor_copy(out=MSKs[:].rearrange("p m n -> p (m n)"), in_=stgq[:, 128:192]))
            for typ in range(2):
                for jq in range(4):
                    A("sp", [], ["H"], lambda e, typ=typ, jq=jq: e.dma_start(
                        out=stgq[:].rearrange("p (j h) -> p j h", j=8),
                        in_=w1_d[typ, jq * 1024:(jq + 1) * 1024, :].rearrange("(j p) h -> p j h", p=128)), dma=True)
                    A("dve", ["H"], ["W1s"], lambda e, typ=typ, jq=jq: e.tensor_copy(
                        out=W1s[:, typ, jq * 8:(jq + 1) * 8, :], in_=stgq[:].rearrange("p (j h) -> p j h", j=8)))
            A("dve", [], ["KTs"], lambda e: e.memset(KTs[0:64, :, 2048:2176], 0.0))
            A("dve", [], ["VVs"], lambda e: e.memset(VVs[:, 16, :, 0:64], 0.0))
            A("dve", [], ["KTw"], lambda e: e.memset(KTw[0:64, :, 512:640], 0.0))
            A("dve", [], ["VVw"], lambda e: e.memset(VVw[:, 4, :, 0:64], 0.0))
            A("dve", [], ["vca"], lambda e: e.memset(vca[:, :, 64:65], 1.0))
            A("sp", [], ["IDX"], lambda e: e.dma_start(out=IDX[:, 0:256], in_=ptrep_d), dma=True)
            A("dve", ["IDX"], ["T0"], lambda e: e.tensor_copy(out=T[0][:, 0:256], in_=IDX[:, 0:256]))
            A("dve", ["T0", "iotac"], ["T0"], lambda e: e.tensor_scalar(out=T[0][:, 0:256], in0=T[0][:, 0:256], scalar1=128.0,
                                                                       scalar2=iotac[:, 0:1], op0=ALU.mult, op1=ALU.add))
            A("dve", ["T0"], ["IDX"], lambda e: e.tensor_copy(out=IDX[:, 256:512], in_=T[0][:, 0:256]))
            PG = [T[0], T[1], T[2], T[3]]
            K1v = K[1][:, 0:256].rearrange("p (a t) -> p a t", a=4)
            K3v = K[3][:, 0:256].rearrange("p (a d) -> p a d", a=4)
            WP = xs[1]
            pgc = [0]

            def sample_seq(b):
                for j in range(16):
                    q = pgc[0] % 4
                    q2 = pgc[0] % 2
                    pgc[0] += 1
                    n = b * 16 + j
                    pg, kpg, pgb, kpgb = PG[q], "T%d" % q, PGB[q2], "PGB%d" % q2
                    A("pool", ["IDX"], [kpg], lambda e, pg=pg, n=n: e.indirect_dma_start(
                        out=pg[:], out_offset=None, in_=cache_d,
                        in_offset=bass.IndirectOffsetOnAxis(ap=IDX[:, 256 + n:257 + n], axis=0)), dma=True)
                    if q2 == 0:
                        A("act", [kpg], [kpgb], lambda e, pg=pg, pgb=pgb: e.activation(out=pgb, in_=pg[:], func=AF.Copy))
                    else:
                        A("dve", [kpg], [kpgb], lambda e, pg=pg, pgb=pgb: e.tensor_copy(out=pgb, in_=pg[:]))
                    for tg in range(4):
                        for par in range(2):
                            A("pe", [kpgb, "SELM"], ["K1"], lambda e, tg=tg, par=par, pgb=pgb: e.matmul(
                                K1v[64 * par:64 * par + 64, tg, :], lhsT=pgb[:, tg * 64:(tg + 1) * 64], rhs=SELM[:, par, :],
                                start=True, stop=True))
                    for a in range(2):
                        A("dve", ["K1", "peT"], ["XcTb"], lambda e, a=a, j=j: e.tensor_tensor(
                            out=XcTb[:, 2 * a:2 * a + 2, j * 64:(j + 1) * 64].rearrange("p g (b j) -> p g b j", b=2),
                            in0=K1v[:, 2 * a:2 * a + 2, :].rearrange("p g (b j) -> p g b j", b=2),
                            in1=peT[:, a, :].unsqueeze(1).unsqueeze(1).to_broadcast([128, 2, 2, 32]), op=ALU.add))
                    for g in range(2):
                        A("pe", [kpgb, "idb"], ["K0"], lambda e, g=g, pgb=pgb: e.transpose(
                            out=K0[0:64, g * 128:(g + 1) * 128], in_=pgb[:, 256 + g * 64:256 + (g + 1) * 64], identity=idb[:]))
                    A("dve", ["K0"], ["KTs"], lambda e, j=j: e.tensor_copy(
                        out=KTs[0:64, :, j * 128:(j + 1) * 128], in_=K0[0:64, 0:256].rearrange("p (g t) -> p g t", g=2)))
                    A("act", [kpg], ["VVs"], lambda e, pg=pg, j=j: e.activation(
                        out=VVs[:, j, :, 0:64], in_=pg[:, 384:512].rearrange("p (g d) -> p g d", g=2), func=AF.Copy))
                A("dve", ["KN"], ["KTs"], lambda e: e.tensor_copy(out=KTs[0:64, :, 2048:2056], in_=KN[:, 0:2, b * 8:(b + 1) * 8]))
                for g in range(2):
                    A("sp", ["ZV"], ["VVs"], lambda e, g=g: e.dma_start(out=VVs[0:8, 16, g, 0:64], in_=ZV[b * 8:(b + 1) * 8, 0, g, :]),
                      dma=True)
                    A("sp", ["ZV"], ["VVw"], lambda e, g=g: e.dma_start(out=VVw[0:8, 4, g, 0:64], in_=ZV[b * 8:(b + 1) * 8, 1, g, :]),
                      dma=True)
                for tg in range(4):
                    typ = tg // 2
                    for j in range(32):
                        A("pe", ["XcTb", "W1s"], ["K2"], lambda e, tg=tg, typ=typ, j=j: e.matmul(
                            K[2][:, 0:32], lhsT=W1s[:, typ, j, :], rhs=XcTb[:, tg, j:1024:32], start=(j == 0), stop=(j == 31)))
                    A("act", ["K2"], ["T4"], lambda e: e.activation(out=T[4][:, 0:32], in_=K[2][:, 0:32], func=AF.Exp, scale=-1.0))
                    A("dve", ["T4"], ["T4"], lambda e: e.tensor_scalar_add(out=T[4][:, 0:32], in0=T[4][:, 0:32], scalar1=1.0))
                    A("dve", ["T4"], ["T4"], lambda e: e.reciprocal(out=T[4][:, 0:32], in_=T[4][:, 0:32]))
                    A("dve", ["T4", "K2"], ["hs_s"], lambda e, tg=tg: e.tensor_tensor(out=hs_s[:, tg, :], in0=K[2][:, 0:32],
                                                                                   in1=T[4][:, 0:32], op=ALU.mult))
                    A("pe", ["hs_s", "W2b"], ["K3"], lambda e, tg=tg, typ=typ: e.matmul(
                        K3v[0:32, tg, :], lhsT=hs_s[:, tg, :], rhs=W2b[:, typ, :], start=True, stop=True))
                A("act", ["K3"], ["kcr"], lambda e: e.activation(out=kcr[0:32], in_=K3v[0:32, 0:2, :], func=AF.Copy))
                A("act", ["K3"], ["vca"], lambda e: e.activation(out=vca[:, :, 0:64], in_=K3v[0:32, 2:4, :], func=AF.Copy))
                A("dve", ["kcr"], ["LG"], lambda e: e.tensor_tensor(out=LG[0:32, 0:2, 0:64], in0=kcr[0:32], in1=kcr[0:32], op=ALU.mult))
                A("dve", ["LG"], ["st4"], lambda e: e.tensor_reduce(out=st4[0:32, 0:2], in_=LG[0:32, 0:2, 0:64], axis=AX.X, op=ALU.add))
                A("act", ["st4"], ["st4"], lambda e: e.activation(out=st4[0:32, 0:2], in_=st4[0:32, 0:2], func=AF.Ln,
                                                                  scale=1.0 / 64, bias=EPS))
                A("act", ["st4"], ["st4"], lambda e: e.activation(out=st4[0:32, 0:2], in_=st4[0:32, 0:2], func=AF.Exp, scale=-0.5))
                A("dve", ["kcr", "st4"], ["kcr"], lambda e: e.tensor_tensor(
                    out=kcr[0:32], in0=kcr[0:32], in1=st4[0:32, 0:2].unsqueeze(2).to_broadcast([32, 2, 64]), op=ALU.mult))
                A("dve", ["kcr", "gvec"], ["kcbs"], lambda e: e.tensor_tensor(
                    out=kcbs[:], in0=kcr[0:32], in1=gvec[0:32, 0:64].unsqueeze(1).to_broadcast([32, 2, 64]), op=ALU.mult))
                for g in range(2):
                    A("pe", ["kcbs", "idb"], ["K0"], lambda e, g=g: e.transpose(
                        out=K0[0:64, 512 + g * 32:512 + (g + 1) * 32], in_=kcbs[:, g, :], identity=idb[0:32, 0:32]))
                A("dve", ["K0"], ["KCa"], lambda e: e.tensor_copy(
                    out=KCa[0:64, :, :], in_=K0[0:64, 512:576].rearrange("p (g j) -> p g j", g=2)))
                A("sp", [], ["xs1"], lambda e: e.dma_start(out=WP[:].rearrange("p (t c) -> p t c", t=4),
                                                           in_=cwin[b].rearrange("(t p) c -> p t c", p=128)), dma=True)
                A("dve", ["xs1"], ["PGB0", "PGB1"], lambda e: e.tensor_copy(out=WPB[:].rearrange("p t c -> p (t c)"), in_=WP[:]))
                for tw in range(4):
                    for g in range(2):
                        A("pe", ["PGB0", "PGB1", "idb"], ["K0"], lambda e, tw=tw, g=g: e.transpose(
                            out=K0[0:64, (tw * 2 + g) * 128:(tw * 2 + g + 1) * 128], in_=WPB[:, tw, g * 64:(g + 1) * 64],
                            identity=idb[:]))
                A("dve", ["K0"], ["KTw"], lambda e: e.tensor_copy(
                    out=KTw[0:64, :, 0:512].rearrange("p g (t k) -> p g t k", t=4),
                    in_=K0[0:64, :].rearrange("p (t g k) -> p g t k", t=4, g=2)))
                A("act", ["xs1"], ["VVw"], lambda e: e.activation(
                    out=VVw[:, 0:4, :, 0:64], in_=WP[:].rearrange("p (t c) -> p t c", t=4)[:, :, 128:256].rearrange(
                        "p t (g d) -> p t g d", g=2), func=AF.Copy))
                A("dve", ["KN"], ["KTw"], lambda e: e.tensor_copy(out=KTw[0:64, :, 512:520], in_=KN[:, 2:4, b * 8:(b + 1) * 8]))
                A("dve", ["QTs"], ["QTb"], lambda e: e.tensor_copy(out=QTb[0:64, :, :], in_=QTs[:, :, b * 8:(b + 1) * 8]))
                for g in range(2):
                    qrhs = QTb[:, g * 4:(g + 1) * 4, :].rearrange("p h t -> p (h t)")
                    qrhs68 = QTb[0:68, g * 4:(g + 1) * 4, :].rearrange("p h t -> p (h t)")
                    A("pe", ["KCa", "QTb"], ["K3"], lambda e, g=g, qrhs68=qrhs68: e.matmul(
                        K[3][0:32, 0:32], lhsT=KCa[0:68, g, :], rhs=qrhs68, start=True, stop=True))
                    A("act", ["K3"], ["PTc"], lambda e: e.activation(out=PTc, in_=K[3][0:32, 0:32], func=AF.Exp))
                    A("pe", ["vca", "PTc"], ["K7"], lambda e, g=g: e.matmul(K[7][0:65, 0:32], lhsT=vca[:, g, :], rhs=PTc,
                                                                         start=True, stop=True))
                    A("act", ["K7"], ["OcT"], lambda e, g=g: e.activation(
                        out=OcT[:, g * 4:(g + 1) * 4, b * 8:(b + 1) * 8], in_=K[7][0:65, 0:32].rearrange("p (h t) -> p h t", h=4),
                        func=AF.Copy))
                    A("pe", ["PTc", "idb"], ["K0"], lambda e: e.transpose(out=K0[0:32, 640:672], in_=PTc, identity=idb[0:32, 0:32]))
                    A("dve", ["K0"], ["EX"], lambda e: e.tensor_copy(out=EX[0:32, 0, 0:32], in_=K0[0:32, 640:672]))
                    A("dve", ["EX"], ["st4b"], lambda e: e.tensor_reduce(out=st4b[0:32, 0:1], in_=EX[0:32, 0, 0:32], axis=AX.X, op=ALU.add))
                    A("dve", ["st4b"], ["st4b"], lambda e: e.reciprocal(out=st4b[0:32, 0:1], in_=st4b[0:32, 0:1]))
                    A("dve", ["EX", "st4b"], ["EX"], lambda e: e.tensor_scalar(
                        out=EX[0:32, 0, 0:32], in0=EX[0:32, 0, 0:32], scalar1=st4b[0:32, 0:1], scalar2=None, op0=ALU.mult))
                    A("pe", ["csts", "EX"], ["K3"], lambda e: e.matmul(K[3][0:8, 64:96], lhsT=selT, rhs=EX[0:32, 0, 0:32],
                                                                      start=True, stop=True))
                    A("dve", [], ["SC"], lambda e: e.memset(SC[0:8, 0:33], 0.0))
                    A("dve", ["K3"], ["SC"], lambda e: e.tensor_copy(out=SC[0:8, 0:32], in_=K[3][0:8, 64:96]))
                    A("dve", ["SC", "csts"], ["SC"], lambda e: e.tensor_tensor(out=SC[0:8, 0:33], in0=SC[0:8, 0:33], in1=keepS, op=ALU.mult))
                    A("dve", ["SC", "csts"], ["SC"], lambda e: e.tensor_tensor(out=SC[0:8, 0:33], in0=SC[0:8, 0:33], in1=addS, op=ALU.add))
                    A("dve", ["SC"], ["m8"], lambda e: e.max(out=m8[0:8], in_=SC[0:8, 0:33]))
                    A("dve", ["SC", "m8"], ["WK"], lambda e: e.match_replace(out=WK[0:8, 0:33], in_to_replace=m8[0:8],
                                                                              in_values=SC[0:8, 0:33], imm_value=-2.0))
                    A("dve", ["WK"], ["m8"], lambda e: e.max(out=m8[0:8], in_=WK[0:8, 0:33]))
                    A("dve", ["m8"], ["thr"], lambda e: e.tensor_reduce(out=thr[0:8], in_=m8[0:8], axis=AX.X, op=ALU.min))
                    A("dve", ["SC", "thr"], ["SEL"], lambda e: e.tensor_scalar(
                        out=SEL[0:8, 0:33], in0=SC[0:8, 0:33], scalar1=thr[0:8, 0:1], scalar2=None, op0=ALU.is_ge))
                    A("dve", ["SEL"], ["MBF"], lambda e: e.tensor_scalar(
                        out=MBF[0:8, 0:33], in0=SEL[0:8, 0:33], scalar1=-1.0, scalar2=BIG, op0=ALU.add, op1=ALU.mult))
                    A("pe", ["MBF", "idb"], ["K0"], lambda e: e.transpose(out=K0[0:33, 704:712], in_=MBF[0:8, 0:33],
                                                                          identity=idb[0:8, 0:8]))
                    A("dve", ["K0"], ["RHSMs"], lambda e: e.tensor_copy(
                        out=RHSMs[0:33], in_=K0[0:33, 704:712].unsqueeze(1).to_broadcast([33, 4, 8])))
                    for kt in range(17):
                        dst = K[5][:, kt * 32:(kt + 1) * 32] if kt < 16 else K[6][:, 0:32]
                        kb = "K5" if kt < 16 else "K6"
                        A("pe", ["KTs", "QTb"], [kb], lambda e, g=g, kt=kt, dst=dst, qrhs=qrhs: e.matmul(
                            dst, lhsT=KTs[:, g, kt * 128:(kt + 1) * 128], rhs=qrhs, start=True, stop=False))
                        A("pe", ["Eexp", "RHSMs"], [kb], lambda e, kt=kt, dst=dst: e.matmul(
                            dst, lhsT=Eexp[:, kt * 128:(kt + 1) * 128], rhs=RHSMs[:].rearrange("p h t -> p (h t)"),
                            start=False, stop=(kt < 16)))
                        if kt == 16:
                            A("pe", ["idb", "MSKs"], [kb], lambda e, dst=dst: e.matmul(dst, lhsT=idb[:], rhs=MSKs[:, 0, :],
                                                                                       start=False, stop=True))
                    A("act", ["K5"], ["PTs"], lambda e: e.activation(out=PTs[:, 0:512], in_=K[5][:], func=AF.Exp))
                    A("act", ["K6"], ["PTs"], lambda e: e.activation(out=PTs[:, 512:544], in_=K[6][:, 0:32], func=AF.Exp))
                    for kt in range(17):
                        A("pe", ["PTs", "VVs"], ["K7"], lambda e, g=g, kt=kt: e.matmul(
                            K[7][0:65, 32:64], lhsT=VVs[:, kt, g, :], rhs=PTs[:, kt * 32:(kt + 1) * 32],
                            start=(kt == 0), stop=(kt == 16)))
                    A("act", ["K7"], ["OsT"], lambda e, g=g: e.activation(
                        out=OsT[:, g * 4:(g + 1) * 4, b * 8:(b + 1) * 8], in_=K[7][0:65, 32:64].rearrange("p (h t) -> p h t", h=4),
                        func=AF.Copy))
                    for s_ in range(5):
                        dst = K[4][:, s_ * 32:(s_ + 1) * 32]
                        msk = {0: 1, 4: 0}.get(s_)
                        A("pe", ["KTw", "QTb"], ["K4"], lambda e, g=g, s_=s_, dst=dst, qrhs=qrhs, msk=msk: e.matmul(
                            dst, lhsT=KTw[:, g, s_ * 128:(s_ + 1) * 128], rhs=qrhs, start=True, stop=(msk is None)))
                        if msk is not None:
                            A("pe", ["idb", "MSKs"], ["K4"], lambda e, dst=dst, msk=msk: e.matmul(
                                dst, lhsT=idb[:], rhs=MSKs[:, msk, :], start=False, stop=True))
                    A("act", ["K4"], ["PTw"], lambda e: e.activation(out=PTw, in_=K[4][:, 0:160], func=AF.Exp))
                    for s_ in range(5):
                        A("pe", ["PTw", "VVw"], ["K7"], lambda e, g=g, s_=s_: e.matmul(
                            K[7][0:65, 64:96], lhsT=VVw[:, s_, g, :], rhs=PTw[:, s_ * 32:(s_ + 1) * 32],
                            start=(s_ == 0), stop=(s_ == 4)))
                    A("act", ["K7"], ["OwT"], lambda e, g=g: e.activation(
                        out=OwT[:, g * 4:(g + 1) * 4, b * 8:(b + 1) * 8], in_=K[7][0:65, 64:96].rearrange("p (h t) -> p h t", h=4),
                        func=AF.Copy))

            for b_ in range(16):
                sample_seq(b_)
            for src, ks, dst, kd in ((OcT, "OcT", OSEL, "OSEL"), (OsT, "OsT", OSEL, "OSEL"), (OwT, "OwT", OWIN, "OWIN")):
                for h in range(8):
                    A("pe", [ks, "idb"], ["K0"], lambda e, src=src, h=h: e.transpose(
                        out=K0[:, h * 66:h * 66 + 65], in_=src[:, h, :], identity=idb[0:65, 0:65]))
                A("act", ["K0"], [kd], lambda e, dst=dst: e.activation(
                    out=dst[:], in_=K0[:, 0:528].rearrange("p (h d) -> p h d", h=8)[:, :, 0:65], func=AF.Copy))
                if ks == "OcT":
                    A("dve", ["OSEL"], ["st8"], lambda e: e.reciprocal(out=st8[:], in_=OSEL[:, :, 64]))
                    A("dve", ["OSEL", "st8"], ["OCMP"], lambda e: e.tensor_tensor(
                        out=OCMP[:], in0=OSEL[:, :, 0:64], in1=st8[:].unsqueeze(2).to_broadcast([128, 8, 64]), op=ALU.mult))
            finish(xs[0], "xs0", ys[:, :], "ys", NOWN if debug else None)
        BKEYS = WK_ + AKEYS + ["Eexp", "MSK", "xT0", "xT1", "OC", "OCT", "KD", "VB", "QD", "KDI", "WUD", "QN", "qkT", "AT", "Sb0", "Sb1", "PB", "hs", "RHSM0", "RHSM1", "PT4"] + \
            ["S0_%d" % b_ for b_ in range(16)] + SKEYS
        if STAGE >= 7:
            A("dve", [], BKEYS + ["fdummy"], lambda e: e.memset(fdummy[:], 0.0))
            A("sp", [], ["lnmc"], lambda e: e.dma_start(out=lnmc[:], in_=lnmcol), dma=True)
            stg = [(T[i_], "T%d" % i_) for i_ in range(6)]
            cnt_ = [0]

            def wload(src, dst, scale_ap):
                st, ks = stg[cnt_[0] % 6]
                use_act = (cnt_[0] % 2 == 0)
                cnt_[0] += 1
                A("sp", [], [ks], lambda e: e.dma_start(out=st[:], in_=src), dma=True)
                rk = [ks] + (["lnmc"] if scale_ap is not None else [])
                if use_act:
                    if scale_ap is not None:
                        A("act", rk, ["WUD"], lambda e: e.activation(out=dst, in_=st[:], func=AF.Copy, scale=scale_ap))
                    else:
                        A("act", rk, ["WUD"], lambda e: e.activation(out=dst, in_=st[:], func=AF.Copy))
                else:
                    if scale_ap is not None:
                        A("dve", rk, ["WUD"], lambda e: e.tensor_scalar(out=dst, in0=st[:], scalar1=scale_ap, scalar2=None,
                                                                         op0=ALU.mult))
                    else:
                        A("dve", rk, ["WUD"], lambda e: e.tensor_copy(out=dst, in_=st[:]))

            for k in range(8):
                for q8 in range(8):
                    wload(w_up[k * 128:(k + 1) * 128, q8 * 512:(q8 + 1) * 512], WU[:, k, q8 * 512:(q8 + 1) * 512],
                          lnmc[:, k:k + 1])
            for f in range(32):
                for q2 in range(2):
                    wload(w_down[f * 128:(f + 1) * 128, q2 * 512:(q2 + 1) * 512], WD[:, f, q2 * 512:(q2 + 1) * 512], None)

            def mlp_tile(hsrc, ydst, ky, idx):
                X, kx = xs[idx % 2], "xs%d" % (idx % 2)
                A("sp", [ky], [kx], lambda e: e.dma_start(out=X[:], in_=hsrc), dma=True)
                A("act", [kx], ["OSEL", "ss"], lambda e: e.activation(
                    out=OSEL[:].rearrange("p h d -> p (h d)")[:, 0:512], in_=X[:, 0:512], func=AF.Square, accum_out=ss[:]))
                A("act", [kx], ["OSEL", "rstd"], lambda e: e.activation(
                    out=OSEL[:].rearrange("p h d -> p (h d)")[:, 0:512], in_=X[:, 512:1024], func=AF.Square,
                    accum_out=rstd[:]))
                A("dve", ["ss", "rstd"], ["ss"], lambda e: e.tensor_tensor(out=ss[:], in0=ss[:], in1=rstd[:], op=ALU.add))
                A("act", ["ss"], ["rstd"], lambda e: e.activation(out=rstd[:], in_=ss[:], func=AF.Ln, scale=1.0 / D, bias=EPS))
                A("act", ["rstd"], ["rstd"], lambda e: e.activation(out=rstd[:], in_=rstd[:], func=AF.Exp, scale=-0.5))
                A("dve", [kx, "rstd"], ["xb"], lambda e: e.tensor_scalar(out=xb[:], in0=X[:], scalar1=rstd[:, 0:1],
                                                                         scalar2=None, op0=ALU.mult))
                for k in range(8):
                    A("pe", ["xb", "idb"], ["K0"], lambda e, k=k: e.transpose(
                        out=K0[:, k * 128:(k + 1) * 128], in_=xb[:, k * 128:(k + 1) * 128], identity=idb[:]))
                A("dve", ["K0"], ["QTa"], lambda e: e.tensor_copy(out=QTa[:].rearrange("p k t -> p (k t)"), in_=K0[:]))
                for rnd in range(4):
                    ub = (rnd % 2) * 8
                    ku = "uT%d" % (rnd % 2)
                    for j in range(8):
                        f = rnd * 8 + j
                        bank = 1 + (f % 4)
                        kb = "K%d" % bank
                        R, kr = PTb[f % 2], "PTb%d" % (f % 2)
                        for k in range(8):
                            A("pe", ["QTa", "WUD"], [kb], lambda e, f=f, k=k, bank=bank: e.matmul(
                                K[bank][:, 0:128], lhsT=WU[:, k, f * 128:(f + 1) * 128], rhs=QTa[:, k, :],
                                start=(k == 0), stop=(k == 7)))
                        A("act", [kb], [kr], lambda e, bank=bank, R=R: e.activation(out=R[:, 0:128], in_=K[bank][:, 0:128],
                                                                                    func=AF.Relu))
                        A("dve", [kr], [ku], lambda e, j=j, ub=ub, R=R: e.tensor_tensor(
                            out=uT[:, ub + j, :], in0=R[:, 0:128], in1=R[:, 0:128], op=ALU.mult))
                    for n in range(2):
                        bank = 5 + n
                        kb = "K%d" % bank
                        for j in range(8):
                            f = rnd * 8 + j
                            A("pe", [ku, "WUD"], [kb], lambda e, f=f, j=j, ub=ub, n=n, bank=bank: e.matmul(
                                K[bank][:], lhsT=uT[:, ub + j, :], rhs=WD[:, f, n * 512:(n + 1) * 512],
                                start=(f == 0), stop=(f == 31)))
                for n in range(2):
                    bank = 5 + n
                    kb = "K%d" % bank
                    A("dve", [kb, kx], ["H"], lambda e, n=n, bank=bank: e.tensor_tensor(
                        out=H[:, n * 512:(n + 1) * 512], in0=K[bank][:], in1=X[:, n * 512:(n + 1) * 512], op=ALU.add))
                A("pool", ["H"], [ky], lambda e: e.dma_start(out=ydst, in_=H[:]), dma=True, final=True)

            for i_ in range(NOWN):
                mlp_tile(yp[i_ * 128:(i_ + 1) * 128, :], yp[i_ * 128:(i_ + 1) * 128, :], "yp%d" % i_, i_)
            if STAGE >= 8:
                mlp_tile(ys[:, :], ys[:, :], "ys", NOWN)
        S.emit()
    return nc


def make_sample_consts():
    f32 = np.float32
    slope = 2.0 ** (-(np.arange(8) + 1.0))
    csts = np.zeros((128, 74), f32)
    for hh in range(4):
        for t in range(8):
            csts[hh * 8 + t, t] = 1.0
    keep = np.ones(33, f32)
    add = np.zeros(33, f32)
    keep[0] = keep[32] = 0.0
    add[0] = add[32] = 5.0
    csts[:, 8:41] = keep[None]
    csts[:, 41:74] = add[None]
    iota = np.arange(128, dtype=f32).reshape(128, 1)
    sq = np.zeros((8, 64), f32)
    for h in range(8):
        for t in range(8):
            sq[0, h * 8 + t] = 128.0 * slope[h]
            sq[1, h * 8 + t] = slope[h]
            sq[2, h * 8 + t] = -128.0 * 16.0 * slope[h]
            sq[3, h * 8 + t] = -slope[h] * t
    j = np.arange(32)
    sq[4, 0:32] = j // 2
    sq[5, 0:32] = 64 * (j % 2) + 63
    sq[6, 0:32] = 1.0
    sq[7, 0:32] = 1.0
    sm = np.zeros((128, 192), f32)
    r = np.arange(128)
    for par in range(2):
        for jj in range(64):
            sm[2 * jj + par, par * 64 + jj] = 1.0
    kk = r[:, None]
    tt = np.arange(8)[None, :]
    mn = np.where(kk <= tt, 0.0, -BIG).astype(f32)
    mw = np.where(kk > tt, 0.0, -BIG).astype(f32)
    sm[:, 128:160] = np.tile(mn, (1, 4))
    sm[:, 160:192] = np.tile(mw, (1, 4))
    return csts, iota, sq, sm


def make_consts(half):
    f32 = np.float32
    t = np.arange(128)
    slope = 2.0 ** (-(np.arange(8) + 1.0))
    cst = np.zeros((128, 747), f32)
    ac = np.zeros((128, 8, 66), f32)
    ac[:] = (slope[:, None] * 64.0 * np.arange(66)[None, :])[None]
    if half == 1:
        ac[:, :, 0:2] -= BIG
    cst[:, 0:528] = ac.reshape(128, 528)
    cst[:, 528] = np.where(t >= 63, 0.0, -BIG)
    cst[:, 529] = np.where(t == 127, 0.0, -BIG)
    rv = np.ones((128, NOWN), f32)
    if half == 0:
        rv[:63, 0] = 0.0
    cst[:, 530:547] = rv
    keep = np.ones(66, f32)
    add = np.zeros(66, f32)
    j0 = 2 * half
    keep[j0], add[j0] = 0.0, 5.0
    if half == 1:
        keep[0:2], add[0:2] = 0.0, -1.0
    cst[:, 547:613] = keep[None]
    cst[:, 613:679] = add[None]
    cst[:, 679] = np.where(t < 64, 5.0, 0.0)
    cst[:, 680] = np.where(t < 64, 0.0, 1.0)
    cst[:, 681] = np.where(t < 64, -1.0, 5.0)
    s_ = t % 64
    cst[:, 682:746] = (s_[:, None] <= np.arange(64)[None, :]).astype(f32)
    kk, tt = t[:, None], t[None, :]
    tri = np.where(kk <= tt, 0.0, -BIG).astype(f32)
    anti = np.where(kk > tt, 0.0, -BIG).astype(f32)
    full = np.zeros((128, 128), f32) if half == 0 else np.full((128, 128), -BIG, f32)
    m0anti = anti if half == 0 else np.full((128, 128), -BIG, f32)
    masks = np.stack([np.tile(m, (1, 4)) for m in (tri, anti, full, m0anti)], axis=1).astype(f32)
    eexp = np.zeros((66, NT * 128), f32)
    n = np.arange(NT * 128)
    eexp[n // 64, n] = 1.0
    kpos = np.stack([(n // 128).astype(f32), (n % 128).astype(f32), np.ones_like(n, f32), np.ones_like(n, f32)], 0)
    qpos = np.zeros((NOWN, 4, 2, 4, 128), f32)
    for i in range(NOWN):
        v = 2 * i
        for g in range(2):
            for hh in range(4):
                s = slope[g * 4 + hh]
                qpos[i, 0, g, hh, :] = 128.0 * s
                qpos[i, 1, g, hh, :] = s
                qpos[i, 2, g, hh, :] = -128.0 * v * s
                qpos[i, 3, g, hh, :] = -s * t
    return cst, masks, eexp, kpos.astype(f32), qpos.reshape(NOWN, 4, 1024)


_NC = None


def kernel(x_prompt, x_sample, cache_kv, cache_win, state_rnn, page_table, ln_mix, w_in, q_norm, k_norm,
           cmp_pe, cmp_w1, cmp_w2, attn_out_norm, rnn_lb_logits, rnn_out_norm, w_out, ln_mlp, w_up, w_down,
           _debug=False):
    global _NC
    f32 = np.float32
    x_prompt = np.asarray(x_prompt, f32)
    x_sample = np.asarray(x_sample, f32)
    cache_win = np.asarray(cache_win, f32)
    state_rnn = np.asarray(state_rnn, f32)
    k_norm = np.asarray(k_norm, f32)
    cmp_pe = np.asarray(cmp_pe, f32)
    if _NC is None or _NC[0] != _debug:
        _NC = (_debug, build_nc(_debug))
    nc = _NC[1]
    t = np.arange(128)

    def tmats(c):
        same = (t[:, None] // c == t[None, :] // c)
        return ((t[:, None] <= t[None, :]) & same).astype(f32), ((t[:, None] > t[None, :]) & same).astype(f32)

    tc_p, tr_p = tmats(64)
    tc_s, tr_s = tmats(8)
    rowmask = (t[:, None] // 8 == np.arange(16)[None, :]).astype(f32)
    gv = np.concatenate([k_norm[0, 0], k_norm[0, 1], k_norm[0, 2], np.asarray(q_norm, f32)[0],
                         np.asarray(attn_out_norm, f32)[0], np.asarray(rnn_out_norm, f32)[0]])
    peT = np.ascontiguousarray(cmp_pe[0].reshape(2, 32, 2, 64).transpose(2, 3, 0, 1).reshape(128, 2, 32))
    common = {
        "w_in": np.ascontiguousarray(np.asarray(w_in, f32)[0]),
        "w_out": np.ascontiguousarray(np.asarray(w_out, f32)[0]),
        "lncol": np.ascontiguousarray(np.asarray(ln_mix, f32)[0].reshape(8, 128).T),
        "gvecd": np.ascontiguousarray(np.broadcast_to(gv[None], (128, 1280))),
        "lbl": np.ascontiguousarray(np.broadcast_to(np.asarray(rnn_lb_logits, f32)[None], (128, 2, 512))),
        "ident": np.eye(128, dtype=f32),
        "w_up": np.ascontiguousarray(np.asarray(w_up, f32)[0]),
        "w_down": np.ascontiguousarray(np.asarray(w_down, f32)[0]),
        "lnmcol": np.ascontiguousarray(np.asarray(ln_mlp, f32)[0].reshape(8, 128).T),
        "tmat": np.ascontiguousarray(np.stack([tc_p, tr_p, tc_s, tr_s], axis=1)),
        "rowmask": rowmask,
        "peTd": peT,
        "cmp_w1": np.ascontiguousarray(np.asarray(cmp_w1, f32)[0]),
        "cmp_w2": np.ascontiguousarray(np.asarray(cmp_w2, f32)[0].transpose(1, 0, 2)),
    }
    zt = np.zeros((128, D), f32)
    cc = [make_consts(0), make_consts(1)]
    csts, iota, sqd, smd = make_sample_consts()
    cache2 = np.ascontiguousarray(np.asarray(cache_kv, f32)[0].reshape(2560 * 128, 512))
    ptab = np.asarray(page_table).astype(np.int32)
    common.update({"cache": cache2, "cstsd": csts, "iotad": iota, "sqd": sqd, "smd": smd})
    in_maps = []
    for c in range(8):
        s, half = c // 2, c % 2
        xvv = np.concatenate([x_prompt[s], zt], 0) if half == 0 else np.concatenate([zt, x_prompt[s]], 0)
        m = dict(common)
        m["xv"] = np.ascontiguousarray(xvv)
        m["xsm"] = np.ascontiguousarray(x_sample[16 * c:16 * c + 16].reshape(128, D))
        m["cwin"] = np.ascontiguousarray(cache_win[0, 16 * c:16 * c + 16].reshape(16, 512, 256))
        m["srnn"] = np.ascontiguousarray(state_rnn[0, 16 * c:16 * c + 16])
        m["ptrep"] = np.ascontiguousarray(np.broadcast_to(ptab[16 * c:16 * c + 16].reshape(1, 256), (128, 256)))
        m["cstd"], m["masks4"], m["eexp"], m["kposrows"], m["qposrows"] = cc[half]
        in_maps.append(m)
    res = run_bass_kernel_spmd(nc, in_maps, core_ids=list(range(8))).results

    y_prompt = np.zeros((4, 4096, D), f32)
    y_sample = np.zeros((128, 8, D), f32)
    kv_prompt = np.zeros((1, 4, 4096, 4, 2, 64), f32)
    kv_sample = np.zeros((1, 128, 8, 4, 2, 64), f32)
    win_prompt = np.zeros((1, 4, 512, 2, 2, 64), f32)
    win_sample = np.zeros((1, 128, 512, 2, 2, 64), f32)
    rnn_prompt = np.zeros((1, 4, 8, 64, 64), f32)
    rnn_sample = np.zeros((1, 128, 8, 64, 64), f32)
    for c in range(8):
        r = res[c]
        s, half = c // 2, c % 2
        ypc = r["yp"].reshape(NOWN, 128, D)
        yps = y_prompt[s].reshape(32, 128, D)
        if half == 0:
            yps[0::2] = ypc[0:16]
            kv_prompt[0, s] = r["kvp"][:4096].reshape(4096, 4, 2, 64)
            win_prompt[0, s] = r["winp"][:512].reshape(512, 2, 2, 64)
        else:
            yps[1::2] = ypc[1:17]
            rnn_prompt[0, s] = r["rnnp"]
        kv_sample[0, 16 * c:16 * c + 16] = r["kvs"].reshape(16, 8, 4, 2, 64)
        y_sample[16 * c:16 * c + 16] = r["ys"].reshape(16, 8, D)
        win_sample[0, 16 * c:16 * c + 16] = r["wins"].reshape(16, 512, 2, 2, 64)
        rnn_sample[0, 16 * c:16 * c + 16] = r["rnns"]
    if _debug:
        kernel._dbg = [res[c]["dbg"] for c in range(8)]
    return (y_prompt, y_sample, kv_prompt, kv_sample, win_prompt, win_sample, rnn_prompt, rnn_sample)
```

```python
import contextlib
import numpy as np
import concourse.bass as bass
import concourse.mybir as mybir
from concourse.bass_utils import run_bass_kernel_spmd

F32 = mybir.dt.float32
BF16 = mybir.dt.bfloat16
AF = mybir.ActivationFunctionType
ALU = mybir.AluOpType
AX = mybir.AxisListType

NT = 33
D = 1024
INW = 3352
C_Q, C_KV, C_G, C_RQ, C_RF, C_RI, C_RG = 0, 512, 1280, 1304, 1816, 2328, 2840
EPS = 1e-6

ENGS = ("pe", "act", "dve", "pool", "sp")
DMA_POOL = 8


class Op:
    __slots__ = ("eng", "fn", "deps", "dma", "idx", "needed", "token")

    def __init__(self, eng, fn, deps, dma, idx):
        self.eng, self.fn, self.deps, self.dma, self.idx = eng, fn, deps, dma, idx
        self.needed = False
        self.token = None


class Sched:
    def __init__(self, nc):
        self.nc = nc
        self.ops = []
        self.last_w = {}
        self.readers = {}
        self.final = []

    def add(self, eng, fn, r=(), w=(), dma=False, final=False, rg=0):
        idx = len(self.ops)
        deps = {}
        if eng == "pe":
            prev = getattr(self, "prev_pe", None)
            if prev is not None and prev[1] != rg:
                deps[prev[0]] = "force"
            self.prev_pe = (idx, rg)
        for k in r:
            lw = self.last_w.get(k)
            if lw is not None and deps.get(lw) != "force":
                deps[lw] = True
        for k in w:
            lw = self.last_w.get(k)
            if lw is not None and deps.get(lw) != "force":
                deps[lw] = True
            for rd in self.readers.get(k, ()):
                if rd not in deps:
                    deps[rd] = False
        for k in r:
            self.readers.setdefault(k, []).append(idx)
        for k in w:
            self.last_w[k] = idx
            self.readers[k] = []
        self.ops.append(Op(eng, fn, deps, dma, idx))
        if final:
            self.final.append(idx)
        return idx

    def emit(self):
        nc = self.nc
        ops = self.ops
        for op in ops:
            nd = {}
            for d, strong in op.deps.items():
                p = ops[d]
                if p.eng == op.eng and not p.dma:
                    if op.eng == "pe" and strong != "force":
                        continue
                nd[d] = strong
            best = {}
            keep = {}
            for d, strong in nd.items():
                p = ops[d]
                if p.dma:
                    keep[d] = strong
                elif p.eng not in best or d > best[p.eng]:
                    best[p.eng] = d
            for d in best.values():
                keep[d] = True
            op.deps = keep
            for d in keep:
                ops[d].needed = True
        for f in self.final:
            ops[f].needed = True
        cnt = {e: 0 for e in ENGS}
        dcnt = {e: 0 for e in ENGS}
        with contextlib.ExitStack() as es:
            NEP = 8
            EPOCH = 1500
            csem = {e: [es.enter_context(nc.semaphore("c_%s%d" % (e, i))) for i in range(NEP if e in ("pe", "dve", "act") else 1)]
                    for e in ENGS}
            dsem = {e: [es.enter_context(nc.semaphore("d_%s%d" % (e, i))) for i in range(DMA_POOL)]
                    for e in ("sp", "pool", "act")}
            for op in ops:
                if op.dma:
                    j = dcnt[op.eng]
                    dcnt[op.eng] += 1
                    op.token = (dsem[op.eng][j % DMA_POOL], 16 * (j // DMA_POOL + 1))
                elif op.needed:
                    ep = cnt[op.eng] // EPOCH
                    assert ep < len(csem[op.eng]), (op.eng, cnt[op.eng])
                    op.token = (csem[op.eng][ep], cnt[op.eng] % EPOCH + 1)
                    cnt[op.eng] += 1
            per = {e: [op for op in ops if op.eng == e] for e in ENGS}
            final_tokens = [ops[f].token for f in self.final]

            def run(engname, eng):
                known = {}
                for op in per[engname]:
                    for d in op.deps:
                        sem, val = ops[d].token
                        if known.get(id(sem), 0) >= val:
                            continue
                        eng.wait_ge(sem, val)
                        known[id(sem)] = val
                    if op.dma:
                        sem, val = op.token
                        if val > 16 and known.get(id(sem), 0) < val - 16:
                            eng.wait_ge(sem, val - 16)
                            known[id(sem)] = val - 16
                        op.fn(eng).then_inc(sem, 16)
                    else:
                        ins = op.fn(eng)
                        if op.needed:
                            ins.then_inc(op.token[0], 1)
                if engname == "sp":
                    for sem, val in final_tokens:
                        if known.get(id(sem), 0) >= val:
                            continue
                        eng.wait_ge(sem, val)
                        known[id(sem)] = val

            with nc.Block() as block:
                @block.tensor
                def _(e):
                    run("pe", e)

                @block.scalar
                def _(e):
                    run("act", e)

                @block.vector
                def _(e):
                    run("dve", e)

                @block.gpsimd
                def _(e):
                    run("pool", e)

                @block.sync
                def _(e):
                    run("sp", e)


import os
BIG = 30000.0
STAGE = int(os.environ.get("KSTAGE", "9"))
SUB = int(os.environ.get("KSUB", "9"))
SKIP = os.environ.get("KSKIP", "")
NOWN = 17
SCALE = 0.125
DEBUG = False


def build_nc(debug=False):
    nc = bass.Bass("TRN2", target_bir_lowering=False)

    def din(name, shape, dt=F32):
        return nc.dram_tensor(name, list(shape), dt, kind="ExternalInput").ap()

    def dout(name, shape, dt=F32):
        return nc.dram_tensor(name, list(shape), dt, kind="ExternalOutput").ap()

    xv = din("xv", [NT * 128, D])
    xsm = din("xsm", [128, D])
    w_in = din("w_in", [D, INW])
    w_out = din("w_out", [D, D])
    w_up = din("w_up", [D, 4 * D])
    w_down = din("w_down", [4 * D, D])
    lnmcol = din("lnmcol", [128, 8])
    lncol = din("lncol", [128, 8])
    gvec_d = din("gvecd", [128, 1280])
    lbl = din("lbl", [128, 2, 512])
    ident = din("ident", [128, 128])
    tmat = din("tmat", [128, 4, 128])
    rowmask = din("rowmask", [128, 16])
    cst_d = din("cstd", [128, 747])
    masks_d = din("masks4", [128, 4, 512])
    eexp_d = din("eexp", [66, NT * 128])
    kpos_d = din("kposrows", [4, NT * 128])
    qpos_d = din("qposrows", [NOWN, 4, 1024])
    pet_d = din("peTd", [128, 2, 32])
    w1_d = din("cmp_w1", [2, 4096, 128])
    w2_d = din("cmp_w2", [128, 2, 64])
    cwin = din("cwin", [16, 512, 256])
    cache_d = din("cache", [327680, 512])
    ptrep_d = din("ptrep", [128, 256], mybir.dt.int32)
    csts_d = din("cstsd", [128, 74])
    iota_d = din("iotad", [128, 1])
    sq_d = din("sqd", [8, 64])
    sm_d = din("smd", [128, 192])
    srnn = din("srnn", [16, 8, 64, 64])

    yp = dout("yp", [NOWN * 128, D])
    ys = dout("ys", [128, D])
    kvp = dout("kvp", [NT * 128, 512])
    kvs = dout("kvs", [128, 512])
    winp = dout("winp", [5 * 128, 256])
    wins = dout("wins", [16, 512, 256])
    rnnp = dout("rnnp", [8, 64, 64])
    rnns = dout("rnns", [16, 8, 64, 64])
    if debug:
        dbg = dout("dbg", [(NOWN + 1) * 128, 5, 520])

    S = Sched(nc)
    with contextlib.ExitStack() as es:
        def sb(name, shape, dt=F32):
            return es.enter_context(nc.sbuf_tensor(name, list(shape), dt))

        def ps(name, shape, dt=F32):
            return es.enter_context(nc.psum_tensor(name, list(shape), dt))

        def A(eng, r, w, fn, **kw):
            S.add(eng, fn, r=r, w=w, **kw)

        BIGW = sb("BIGW", [128, 65536], BF16)
        W = BIGW[:, 0:26816].rearrange("p (k n) -> p k n", k=8)
        ARENA = BIGW[:, 26816:47808]
        WU = BIGW[:, 0:32768].rearrange("p (k n) -> p k n", k=8)
        WD = BIGW[:, 32768:65536].rearrange("p (f n) -> p f n", f=32)
        _bo = [47808]

        def bw(n, parts=128):
            a = _bo[0]
            _bo[0] += n
            assert _bo[0] <= 65536
            return BIGW[0:parts, a:a + n]
        XcT = ARENA[:, 0:8448].rearrange("p (a n) -> p a n", a=4)
        W1b = ARENA[:, 8448:16640].rearrange("p (a j h) -> p a j h", a=2, j=32)
        KTs = ARENA[:, 0:8448].rearrange("p (g n) -> p g n", g=2)
        VVs = ARENA[:, 8448:12738].rearrange("p (t g d) -> p t g d", t=NT, g=2)
        WO = ARENA[:, 12800:20992].rearrange("p (k n) -> p k n", k=8)
        AKEYS = ["XcT", "W1b", "KTs", "VVs", "WO"]
        KTw = sb("KTw", [128, 2, 768], BF16)
        VVw = sb("VVw", [128, 6, 2, 65], BF16)
        Eexp = bw(NT * 128, 128)
        MSK = bw(2048).rearrange("p (m n) -> p m n", m=4)
        cst = sb("cst", [128, 747])
        gvec = sb("gvec", [128, 1280])
        qgs = sb("qgs", [128, 64])
        lnc = sb("lnc", [128, 8])
        lbb = sb("lbb", [128, 512])
        omlb = sb("omlb", [128, 512])
        idb = sb("idb", [128, 128], BF16)
        tm = sb("tm", [128, 4, 128])
        rmask = sb("rmask", [128, 16])
        ones = sb("ones", [128, 1])
        peT = sb("peT", [128, 2, 32])
        W2b = sb("W2b", [128, 2, 64], BF16)
        kcT = sb("kcT", [64, 2, 66], BF16)
        vcb = sb("vcb", [66, 2, 64], BF16)
        fdummy = sb("fdummy", [128, 1])
        acore = cst[:, 0:528].rearrange("p (h j) -> p h j", h=8)
        fix2 = cst[:, 528:530]
        rowvalid = cst[:, 530:547]
        keepc = cst[:, 547:613]
        addc = cst[:, 613:679]
        mlo5 = cst[:, 679:680]
        mhi = cst[:, 680:681]
        c1col = cst[:, 681:682]
        tri2 = cst[:, 682:746]
        xs = [sb("xs%d" % i, [128, D]) for i in range(2)]
        xb = sb("xb", [128, D], BF16)
        xT = [bw(1024).rearrange("p (k t) -> p k t", k=8) for i in range(2)]
        ss = sb("ss", [128, 1])
        rstd = sb("rstd", [128, 1])
        Z = [sb("Z0", [128, 768])] * 2
        zb = sb("zb", [128, 4, 64], BF16)
        sq4 = sb("sq4", [128, 2, 2, 64])
        ms4 = sb("ms4", [128, 2, 2])
        T = [sb("T%d" % i, [128, 512]) for i in range(6)]
        QS = sb("QS", [128, 512])
        RGS = sb("RGS", [128, 512])
        KD = bw(512)
        VB = bw(512)
        QD = bw(512)
        KDI = bw(512)
        qkT = bw(1024).rearrange("p (a t) -> p a t", a=8)
        AT = bw(512).rearrange("p (h t) -> p h t", h=8)
        Sb = [bw(256).rearrange("p (a d) -> p a d", a=4) for i in range(2)]
        ET = sb("ET", [128, 2, 4])
        St = sb("St", [128, 4, 64])
        ZQ = sb("ZQ", [128, 8, 64])
        QN = bw(512).rearrange("p (h d) -> p h d", h=8)
        st8 = sb("st8", [128, 8])
        QTa = sb("QTa", [128, 8, 128], BF16)
        gsig = sb("gsig", [128, 8, 3])
        LG = sb("LG", [128, 4, 66])
        EX = sb("EX", [128, 4, 66])
        PB = bw(264).rearrange("p (h j) -> p h j", h=4)
        st4 = sb("st4", [128, 4])
        st4b = sb("st4b", [128, 4])
        SC = sb("SC", [128, 66])
        WK = sb("WK", [128, 66])
        SEL = sb("SEL", [128, 66])
        m8 = sb("m8", [128, 8])
        thr = sb("thr", [128, 1])
        MBF = sb("MBF", [128, 66], BF16)
        RHSM_ = [bw(512, 128).rearrange("p (h t) -> p h t", h=4) for i in range(2)]
        PT4 = bw(512, 66).rearrange("p (h t) -> p h t", h=4)
        PTb = [sb("PTb%d" % i, [128, 512], BF16) for i in range(3)]
        OSEL = sb("OSEL", [128, 8, 65])
        OWIN = sb("OWIN", [128, 8, 65])
        OCMP = sb("OCMP", [128, 8, 64])
        OR32 = sb("OR32", [128, 8, 64])
        g8 = sb("g8", [128, 8, 3])
        OC = bw(1024)
        OCT = bw(1024).rearrange("p (k t) -> p k t", k=8)
        uT = sb("uT", [128, 16, 128], BF16)
        lnmc = sb("lnmc", [128, 8])
        H = sb("H", [128, D])
        stgq = H
        TMP = T[5][:].rearrange("p (h d) -> p h d", h=8)
        ACC = T[4][:].rearrange("p (h d) -> p h d", h=8)
        junk = OCT
        hs = bw(264).rearrange("p (h j) -> p h j", h=4)
        kcr = sb("kcr", [66, 2, 64])
        kcbf = sb("kcbf", [66, 2, 64], BF16)
        S0 = ARENA[:, 0:8192].bitcast(F32).rearrange("p (a d) -> p a d", a=64)
        etots = sb("etots", [128, 4, 16])
        kdm = PTb

        K0 = ps("K0", [128, 1024], BF16)
        K = [None] + [ps("K%d" % i, [128, 512]) for i in range(1, 8)]

        def ld(dst, src, key, eng="sp"):
            S.add(eng, lambda e: e.dma_start(out=dst, in_=src), w=[key], dma=True)

        ld(lnc[:], lncol, "lnc")
        ld(gvec[:], gvec_d, "gvec")
        ld(cst[:], cst_d, "cst")
        ld(tm[:], tmat, "tm")
        ld(rmask[:], rowmask, "rmask")
        ld(peT[:], pet_d, "peT")
        ld(T[0][:], lbl[:, 0, :], "T0")
        ld(T[1][:], lbl[:, 1, :], "T1")
        ld(T[2][:, 0:128], ident, "T2")
        ld(T[3][:, 0:128], w2_d.rearrange("p a d -> p (a d)"), "T3")
        A("dve", ["T2"], ["idb"], lambda e: e.tensor_copy(out=idb[:], in_=T[2][:, 0:128]))
        A("dve", ["T3"], ["W2b"], lambda e: e.tensor_copy(out=W2b[:].rearrange("p a d -> p (a d)"), in_=T[3][:, 0:128]))
        A("dve", [], ["ones"], lambda e: e.memset(ones[:], 1.0))
        A("dve", [], ["St"], lambda e: e.memset(St[:], 0.0))
        A("dve", ["gvec"], ["qgs"], lambda e: e.tensor_scalar_mul(out=qgs[:], in0=gvec[:, 192:256], scalar1=SCALE))
        A("dve", ["T0", "T1"], ["lbb"], lambda e: e.tensor_sub(out=lbb[:], in0=T[1][:], in1=T[0][:]))
        A("act", ["lbb"], ["lbb"], lambda e: e.activation(out=lbb[:], in_=lbb[:], func=AF.Exp))
        A("dve", ["lbb"], ["lbb"], lambda e: e.tensor_scalar_add(out=lbb[:], in0=lbb[:], scalar1=1.0))
        A("dve", ["lbb"], ["lbb"], lambda e: e.reciprocal(out=lbb[:], in_=lbb[:]))
        A("dve", ["lbb"], ["omlb"], lambda e: e.tensor_scalar(out=omlb[:], in0=lbb[:], scalar1=-1.0, scalar2=1.0,
                                                              op0=ALU.mult, op1=ALU.add))

        wstg = [(H, "H"), (xs[0], "xs0"), (xs[1], "xs1")]
        wi_ = 0
        for k in range(8):
            for hf in range(4):
                c0, c1 = hf * 838, (hf + 1) * 838
                st_, ks_ = wstg[wi_ % 3]
                A("sp", [], [ks_], lambda e, k=k, c0=c0, c1=c1, st_=st_: e.dma_start(
                    out=st_[:, 0:838], in_=w_in[k * 128:(k + 1) * 128, c0:c1]), dma=True)
                if wi_ % 2 == 0:
                    A("act", [ks_, "lnc"], ["W%d" % k], lambda e, k=k, c0=c0, c1=c1, st_=st_: e.activation(
                        out=W[:, k, c0:c1], in_=st_[:, 0:838], func=AF.Copy, scale=lnc[:, k:k + 1]))
                else:
                    A("dve", [ks_, "lnc"], ["W%d" % k], lambda e, k=k, c0=c0, c1=c1, st_=st_: e.tensor_scalar(
                        out=W[:, k, c0:c1], in0=st_[:, 0:838], scalar1=lnc[:, k:k + 1], scalar2=None, op0=ALU.mult))
                wi_ += 1
        WK_ = ["W%d" % k for k in range(8)]

        def prep_x(xsrc, p):
            X, XT = xs[p], xT[p]
            kx, kxt = "xs%d" % p, "xT%d" % p
            A("sp", [], [kx], lambda e: e.dma_start(out=X[:], in_=xsrc), dma=True)
            A("act", [kx], ["OCT", "ss"], lambda e: e.activation(out=OCT[:].rearrange("p k t -> p (k t)"), in_=X[:], func=AF.Square, accum_out=ss[:]))
            A("act", ["ss"], ["rstd"], lambda e: e.activation(out=rstd[:], in_=ss[:], func=AF.Ln, scale=1.0 / D, bias=EPS))
            A("act", ["rstd"], ["rstd"], lambda e: e.activation(out=rstd[:], in_=rstd[:], func=AF.Exp, scale=-0.5))
            A("dve", [kx, "rstd"], ["xb"], lambda e: e.tensor_scalar(out=xb[:], in0=X[:], scalar1=rstd[:, 0:1],
                                                                     scalar2=None, op0=ALU.mult))
            for k in range(8):
                A("pe", ["xb", "idb"], ["K0"], lambda e, k=k: e.transpose(
                    out=K0[:, k * 128:(k + 1) * 128], in_=xb[:, k * 128:(k + 1) * 128], identity=idb[:]))
            A("dve", ["K0"], [kxt], lambda e: e.tensor_copy(out=XT[:].rearrange("p k t -> p (k t)"), in_=K0[:]))
            return X, XT, kx, kxt

        def mm(XT, kxt, bank, n, c0):
            for k in range(8):
                A("pe", [kxt] + WK_, ["K%d" % bank], lambda e, k=k: e.matmul(
                    K[bank][:, 0:n], lhsT=XT[:, k, :], rhs=W[:, k, c0:c0 + n], start=(k == 0), stop=(k == 7)))

        A("dve", [], AKEYS + ["fdummy"], lambda e: e.memset(fdummy[:], 0.0))
        for typ in range(2):
            for jq in range(4):
                A("sp", [], ["H"], lambda e, typ=typ, jq=jq: e.dma_start(
                    out=stgq[:].rearrange("p (j h) -> p j h", j=8),
                    in_=w1_d[typ, jq * 1024:(jq + 1) * 1024, :].rearrange("(j p) h -> p j h", p=128)), dma=True)
                A("dve", ["H"], ["W1b"], lambda e, typ=typ, jq=jq: e.tensor_copy(
                    out=W1b[:, typ, jq * 8:(jq + 1) * 8, :], in_=stgq[:].rearrange("p (j h) -> p j h", j=8)))
        def tile_1a(v):
            p = v % 2
            X, XT, kx, kxt = prep_x(xv[v * 128:(v + 1) * 128, :], p)
            K1v = K[1][:, 0:256].rearrange("p (a t) -> p a t", a=4)
            for tg in range(4):
                c0 = C_KV + tg * 64
                for par in range(2):
                    for k in range(8):
                        A("pe", [kxt] + WK_, ["K1"], lambda e, tg=tg, par=par, k=k, c0=c0: e.matmul(
                            K1v[64 * par:64 * par + 64, tg, :], lhsT=W[:, k, c0:c0 + 64], rhs=XT[:, k, par:128:2],
                            start=(k == 0), stop=(k == 7)))
            for a in range(2):
                A("dve", ["K1", "peT"], ["XcT"], lambda e, v=v, K1v=K1v, a=a: e.tensor_tensor(
                    out=XcT[:, 2 * a:2 * a + 2, v * 64:(v + 1) * 64].rearrange("p g (b j) -> p g b j", b=2),
                    in0=K1v[:, 2 * a:2 * a + 2, :].rearrange("p g (b j) -> p g b j", b=2),
                    in1=peT[:, a, :].unsqueeze(1).unsqueeze(1).to_broadcast([128, 2, 2, 32]), op=ALU.add))
        for v_ in range(NT if STAGE >= 1 else 0):
            tile_1a(v_)
        K3v = K[3][:, 0:256].rearrange("p (a d) -> p a d", a=4)
        for tg in range(4):
            typ = tg // 2
            for j in range(32):
                A("pe", ["XcT", "W1b"], ["K2"], lambda e, tg=tg, typ=typ, j=j: e.matmul(
                    K[2][:, 0:66], lhsT=W1b[:, typ, j, :], rhs=XcT[:, tg, j:2112:32], start=(j == 0), stop=(j == 31)))
            A("act", ["K2"], ["T4"], lambda e: e.activation(out=T[4][:, 0:66], in_=K[2][:, 0:66], func=AF.Exp, scale=-1.0))
            A("dve", ["T4"], ["T4"], lambda e: e.tensor_scalar_add(out=T[4][:, 0:66], in0=T[4][:, 0:66], scalar1=1.0))
            A("dve", ["T4"], ["T4"], lambda e: e.reciprocal(out=T[4][:, 0:66], in_=T[4][:, 0:66]))
            A("dve", ["T4", "K2"], ["hs"], lambda e, tg=tg: e.tensor_tensor(out=hs[:, tg, :], in0=K[2][:, 0:66],
                                                                           in1=T[4][:, 0:66], op=ALU.mult))
            A("pe", ["hs", "W2b"], ["K3"], lambda e, tg=tg, typ=typ: e.matmul(
                K3v[0:66, tg, :], lhsT=hs[:, tg, :], rhs=W2b[:, typ, :], start=True, stop=True))
        A("act", ["K3"], ["kcr"], lambda e: e.activation(out=kcr[:], in_=K3v[0:66, 0:2, :], func=AF.Copy))
        A("act", ["K3"], ["vcb"], lambda e: e.activation(out=vcb[:], in_=K3v[0:66, 2:4, :], func=AF.Copy))
        A("dve", ["kcr"], ["T5"], lambda e: e.tensor_tensor(out=T[5][0:66, 0:128].rearrange("p (g d) -> p g d", g=2),
                                                            in0=kcr[:], in1=kcr[:], op=ALU.mult))
        A("dve", ["T5"], ["st4"], lambda e: e.tensor_reduce(out=st4[0:66, 0:2],
                                                            in_=T[5][0:66, 0:128].rearrange("p (g d) -> p g d", g=2),
                                                            axis=AX.X, op=ALU.add))
        A("act", ["st4"], ["st4"], lambda e: e.activation(out=st4[0:66, 0:2], in_=st4[0:66, 0:2], func=AF.Ln,
                                                          scale=1.0 / 64, bias=EPS))
        A("act", ["st4"], ["st4"], lambda e: e.activation(out=st4[0:66, 0:2], in_=st4[0:66, 0:2], func=AF.Exp, scale=-0.5))
        A("dve", ["kcr", "st4"], ["kcr"], lambda e: e.tensor_tensor(
            out=kcr[:], in0=kcr[:], in1=st4[0:66, 0:2].unsqueeze(2).to_broadcast([66, 2, 64]), op=ALU.mult))
        A("dve", ["kcr", "gvec"], ["kcbf"], lambda e: e.tensor_tensor(
            out=kcbf[:], in0=kcr[:], in1=gvec[0:66, 0:64].unsqueeze(1).to_broadcast([66, 2, 64]), op=ALU.mult))
        for g in range(2):
            A("pe", ["kcbf", "idb"], ["K0"], lambda e, g=g: e.transpose(
                out=K0[0:64, g * 128:g * 128 + 66], in_=kcbf[:, g, :], identity=idb[0:66, 0:66]))
        A("dve", ["K0"], ["kcT"], lambda e: e.tensor_copy(
            out=kcT[:], in_=K0[0:64, 0:256].rearrange("p (g n) -> p g n", g=2)[:, :, 0:66]))
        A("dve", [], AKEYS + ["fdummy"], lambda e: e.memset(fdummy[:], 0.0))

        A("dve", [], ["KTs"], lambda e: e.memset(KTs[64:128, :, :], 0.0))
        for c0 in range(0, NT * 128, 1024):
            n = min(1024, NT * 128 - c0)
            A("sp", [], ["H"], lambda e, c0=c0, n=n: e.dma_start(out=stgq[0:66, 0:n], in_=eexp_d[:, c0:c0 + n]), dma=True)
            A("dve", [], ["Eexp"], lambda e, c0=c0, n=n: e.memset(Eexp[64:128, c0:c0 + n], 0.0))
            A("dve", ["H"], ["Eexp"], lambda e, c0=c0, n=n: e.tensor_copy(out=Eexp[0:66, c0:c0 + n], in_=stgq[0:66, 0:n]))
            A("sp", [], ["H"], lambda e, c0=c0, n=n: e.dma_start(out=stgq[64:68, 0:n], in_=kpos_d[:, c0:c0 + n]), dma=True)
            for g in range(2):
                A("dve", ["H"], ["KTs"], lambda e, c0=c0, n=n, g=g: e.tensor_copy(
                    out=KTs[64:68, g, c0:c0 + n], in_=stgq[64:68, 0:n]))
        for m in range(4):
            A("sp", [], ["H"], lambda e, m=m: e.dma_start(out=stgq[:, 0:512], in_=masks_d[:, m, :]), dma=True)
            A("dve", ["H"], ["MSK"], lambda e, m=m: e.tensor_copy(out=MSK[:, m, :], in_=stgq[:, 0:512]))
        for k in range(8):
            A("sp", [], ["H"], lambda e, k=k: e.dma_start(out=stgq[:], in_=w_out[k * 128:(k + 1) * 128, :]), dma=True)
            A("dve", ["H"], ["WO"], lambda e, k=k: e.tensor_copy(out=WO[:, k, :], in_=stgq[:]))
        A("dve", [], ["KTw"], lambda e: e.memset(KTw[64:128, :, :], 0.0))
        A("dve", [], ["QTa"], lambda e: e.memset(QTa[64:128, :, :], 0.0))
        for g_ in range(2):
            A("dve", [], ["RHSM%d" % g_], lambda e, g_=g_: e.memset(RHSM_[g_][64:128, :, :], 0.0))
        A("dve", [], ["VVs"], lambda e: e.memset(VVs[:, :, :, 64:65], 1.0))
        A("dve", [], ["VVw"], lambda e: e.memset(VVw[:, :, :, 64:65], 1.0))

        def rnn_gates(bf, bi, sample):
            kf, ki = "K%d" % bf, "K%d" % bi
            e1, uu, kk, G = T[0], T[1], T[2], T[3]
            A("act", [kf], ["T0"], lambda e: e.activation(out=e1[:], in_=K[bf][:], func=AF.Exp, scale=-1.0))
            A("act", [ki], ["VB"], lambda e: e.activation(out=VB[:], in_=K[bi][:], func=AF.Copy))
            A("dve", ["T0"], ["T0"], lambda e: e.tensor_scalar_add(out=e1[:], in0=e1[:], scalar1=1.0))
            A("dve", ["T0"], ["T0"], lambda e: e.reciprocal(out=e1[:], in_=e1[:]))
            A("dve", ["T0", "omlb"], ["T1"], lambda e: e.tensor_tensor(out=uu[:], in0=e1[:], in1=omlb[:], op=ALU.mult))
            A("dve", ["T1", "lbb"], ["T0"], lambda e: e.tensor_tensor(out=e1[:], in0=uu[:], in1=lbb[:], op=ALU.add))
            A("dve", ["T1", "omlb"], ["T2"], lambda e: e.tensor_tensor(out=kk[:], in0=omlb[:], in1=uu[:], op=ALU.subtract))
            A("act", ["T0"], ["T3"], lambda e: e.activation(out=G[:], in_=e1[:], func=AF.Ln))
            mi = 3 if sample else 1
            A("pe", ["tm", "T3"], ["K3"], lambda e: e.matmul(K[3][:], lhsT=tm[:, mi, :], rhs=G[:], start=True, stop=True))
            A("act", ["K3"], ["T4"], lambda e: e.activation(out=T[4][:], in_=K[3][:], func=AF.Exp))
            A("dve", ["T2", "T4"], ["KD"], lambda e: e.tensor_tensor(out=KD[:], in0=kk[:], in1=T[4][:], op=ALU.mult))

        pst = K[4][:, 0:256].rearrange("p (a d) -> p a d", a=4)
        ptot = K[4][:, 256:264].rearrange("p (c a) -> p c a", c=2)

        def state_tot():
            G = T[3]
            for c in range(2):
                for h in range(8):
                    hp, ee = h // 2, h % 2
                    A("pe", ["T3", "ones"], ["K4"], lambda e, c=c, h=h, hp=hp, ee=ee: e.matmul(
                        ptot[64 * ee:64 * ee + 64, c, hp:hp + 1], lhsT=G[c * 64:(c + 1) * 64, h * 64:(h + 1) * 64],
                        rhs=ones[c * 64:(c + 1) * 64, 0:1], start=True, stop=True), rg=64 * c)
            A("act", ["K4"], ["ET"], lambda e: e.activation(out=ET[:], in_=ptot, func=AF.Exp))

        def state_update(c):
            for h in range(8):
                hp, ee = h // 2, h % 2
                A("pe", ["KD", "VB"], ["K4"], lambda e, c=c, h=h, hp=hp, ee=ee: e.matmul(
                    pst[64 * ee:64 * ee + 64, hp, :], lhsT=KD[c * 64:(c + 1) * 64, h * 64:(h + 1) * 64],
                    rhs=VB[c * 64:(c + 1) * 64, h * 64:(h + 1) * 64], start=True, stop=True), rg=64 * c)
            A("dve", ["St", "ET"], ["St"], lambda e, c=c: e.tensor_tensor(
                out=St[:], in0=St[:], in1=ET[:, c, :].unsqueeze(2).to_broadcast([128, 4, 64]), op=ALU.mult))
            A("dve", ["St", "K4"], ["St"], lambda e: e.tensor_tensor(out=St[:], in0=St[:], in1=pst, op=ALU.add))

        def k_norms(Zt, kz):
            Z5 = Zt[:].rearrange("p (a b g d) -> p a b g d", a=3, b=2, g=2)
            KS = Z5[:, 1:3, 0, :, :]
            A("dve", [kz], ["sq4"], lambda e: e.tensor_tensor(out=sq4[:], in0=KS, in1=KS, op=ALU.mult))
            A("dve", ["sq4"], ["ms4"], lambda e: e.tensor_reduce(out=ms4[:], in_=sq4[:], axis=AX.X, op=ALU.add))
            A("act", ["ms4"], ["ms4"], lambda e: e.activation(out=ms4[:], in_=ms4[:], func=AF.Ln, scale=1.0 / 64, bias=EPS))
            A("act", ["ms4"], ["ms4"], lambda e: e.activation(out=ms4[:], in_=ms4[:], func=AF.Exp, scale=-0.5))
            A("dve", [kz, "ms4"], [kz], lambda e: e.tensor_tensor(
                out=KS, in0=KS, in1=ms4[:].unsqueeze(3).to_broadcast([128, 2, 2, 64]), op=ALU.mult))
            A("dve", [kz, "gvec"], [kz], lambda e: e.tensor_tensor(
                out=KS, in0=KS, in1=gvec[:, 64:192].rearrange("p (a d) -> p a d", a=2).unsqueeze(2).to_broadcast(
                    [128, 2, 2, 64]), op=ALU.mult))
            return Z5

        def silu_from(bank, dst, kdst):
            kb = "K%d" % bank
            A("act", [kb], ["T5"], lambda e: e.activation(out=T[5][:], in_=K[bank][:], func=AF.Exp, scale=-1.0))
            A("dve", ["T5"], ["T5"], lambda e: e.tensor_scalar_add(out=T[5][:], in0=T[5][:], scalar1=1.0))
            A("dve", ["T5"], ["T5"], lambda e: e.reciprocal(out=T[5][:], in_=T[5][:]))
            A("dve", ["T5", kb], [kdst], lambda e: e.tensor_tensor(out=dst[:], in0=K[bank][:], in1=T[5][:], op=ALU.mult))

        def head_rms(src3, ksrc, n):
            A("dve", [ksrc], ["T5"], lambda e: e.tensor_tensor(out=TMP[:, 0:n, :], in0=src3, in1=src3, op=ALU.mult))
            A("dve", ["T5"], ["st8"], lambda e: e.tensor_reduce(out=st8[:, 0:n], in_=TMP[:, 0:n, :], axis=AX.X, op=ALU.add))
            A("act", ["st8"], ["st8"], lambda e: e.activation(out=st8[:, 0:n], in_=st8[:, 0:n], func=AF.Ln,
                                                              scale=1.0 / 64, bias=EPS))
            A("act", ["st8"], ["st8"], lambda e: e.activation(out=st8[:, 0:n], in_=st8[:, 0:n], func=AF.Exp, scale=-0.5))

        def finish(X, kx, ydst, ykey, dbi):
            A("dve", ["OSEL"], ["st8"], lambda e: e.tensor_scalar_max(out=st8[:], in0=OSEL[:, :, 64], scalar1=1e-30))
            A("dve", ["st8"], ["st8"], lambda e: e.reciprocal(out=st8[:], in_=st8[:]))
            A("dve", ["st8", "gsig"], ["g8"], lambda e: e.tensor_tensor(out=g8[:, :, 1], in0=gsig[:, :, 1], in1=st8[:],
                                                                         op=ALU.mult))
            A("dve", ["OWIN"], ["st8"], lambda e: e.tensor_scalar_max(out=st8[:], in0=OWIN[:, :, 64], scalar1=1e-30))
            A("dve", ["st8"], ["st8"], lambda e: e.reciprocal(out=st8[:], in_=st8[:]))
            A("dve", ["st8", "gsig"], ["g8"], lambda e: e.tensor_tensor(out=g8[:, :, 2], in0=gsig[:, :, 2], in1=st8[:],
                                                                         op=ALU.mult))
            A("dve", ["OCMP", "gsig"], ["T4"], lambda e: e.tensor_tensor(
                out=ACC[:], in0=OCMP[:], in1=gsig[:, :, 0:1].to_broadcast([128, 8, 64]), op=ALU.mult))
            A("dve", ["OSEL", "g8"], ["T5"], lambda e: e.tensor_tensor(
                out=TMP[:], in0=OSEL[:, :, 0:64], in1=g8[:, :, 1:2].to_broadcast([128, 8, 64]), op=ALU.mult))
            A("dve", ["T4", "T5"], ["T4"], lambda e: e.tensor_tensor(out=ACC[:], in0=ACC[:], in1=TMP[:], op=ALU.add))
            A("dve", ["OWIN", "g8"], ["T5"], lambda e: e.tensor_tensor(
                out=TMP[:], in0=OWIN[:, :, 0:64], in1=g8[:, :, 2:3].to_broadcast([128, 8, 64]), op=ALU.mult))
            A("dve", ["T4", "T5"], ["T4"], lambda e: e.tensor_tensor(out=ACC[:], in0=ACC[:], in1=TMP[:], op=ALU.add))
            if debug and dbi is not None:
                for di, (src, kk_, n) in enumerate([(OCMP, "OCMP", 512), (OSEL, "OSEL", 520), (OWIN, "OWIN", 520),
                                                    (OR32, "OR32", 512), (ACC, "T4", 512)]):
                    A("pool", [kk_], [], lambda e, di=di, src=src, n=n: e.dma_start(
                        out=dbg[dbi * 128:(dbi + 1) * 128, di, 0:n], in_=src[:].rearrange("p h d -> p (h d)")),
                      dma=True, final=True)
            head_rms(ACC[:], "T4", 8)
            A("dve", ["T4", "st8"], ["T4"], lambda e: e.tensor_tensor(
                out=ACC[:], in0=ACC[:], in1=st8[:].unsqueeze(2).to_broadcast([128, 8, 64]), op=ALU.mult))
            A("dve", ["T4", "gvec"], ["OC"], lambda e: e.tensor_tensor(
                out=OC[:, 0:512], in0=ACC[:].rearrange("p h d -> p (h d)"), in1=gvec[:, 256:768], op=ALU.mult))
            head_rms(OR32[:], "OR32", 8)
            A("dve", ["OR32", "st8"], ["OR32"], lambda e: e.tensor_tensor(
                out=OR32[:], in0=OR32[:], in1=st8[:].unsqueeze(2).to_broadcast([128, 8, 64]), op=ALU.mult))
            A("dve", ["OR32", "gvec"], ["OR32"], lambda e: e.tensor_tensor(
                out=OR32[:].rearrange("p h d -> p (h d)"), in0=OR32[:].rearrange("p h d -> p (h d)"),
                in1=gvec[:, 768:1280], op=ALU.mult))
            A("dve", ["OR32", "RGS"], ["OC"], lambda e: e.tensor_tensor(
                out=OC[:, 512:1024], in0=OR32[:].rearrange("p h d -> p (h d)"), in1=RGS[:], op=ALU.mult))
            for k in range(8):
                A("pe", ["OC", "idb"], ["K0"], lambda e, k=k: e.transpose(
                    out=K0[:, k * 128:(k + 1) * 128], in_=OC[:, k * 128:(k + 1) * 128], identity=idb[:]))
            A("dve", ["K0"], ["OCT"], lambda e: e.tensor_copy(out=OCT[:].rearrange("p k t -> p (k t)"), in_=K0[:]))
            for n in range(2):
                for k in range(8):
                    A("pe", ["OCT", "WO"], ["K%d" % (1 + n)], lambda e, n=n, k=k: e.matmul(
                        K[1 + n][:], lhsT=OCT[:, k, :], rhs=WO[:, k, n * 512:(n + 1) * 512], start=(k == 0), stop=(k == 7)))
                A("dve", ["K%d" % (1 + n), kx], ["H"], lambda e, n=n, X=X: e.tensor_tensor(
                    out=H[:, n * 512:(n + 1) * 512], in0=K[1 + n][:], in1=X[:, n * 512:(n + 1) * 512], op=ALU.add))
            A("pool", ["H"], [ykey], lambda e: e.dma_start(out=ydst, in_=H[:]), dma=True, final=True)

        def tile_1b(v):
            p = v % 2
            own = (v % 2 == 0) and STAGE >= 3
            i_own = v // 2
            X, XT, kx, kxt = prep_x(xv[v * 128:(v + 1) * 128, :], p)
            Zt, kz = Z[p], "Z0"
            mm(XT, kxt, 1, 512, C_KV)
            mm(XT, kxt, 2, 256, C_KV + 512)
            A("act", ["K1"], [kz], lambda e, Zt=Zt: e.activation(out=Zt[:, 0:512], in_=K[1][:, 0:512], func=AF.Copy))
            A("act", ["K2"], [kz], lambda e, Zt=Zt: e.activation(out=Zt[:, 512:768], in_=K[2][:, 0:256], func=AF.Copy))
            mm(XT, kxt, 1, 512, C_RF)
            mm(XT, kxt, 2, 512, C_RI)
            Z5 = k_norms(Zt, kz)
            A("pool", [kz], [], lambda e, v=v, Zt=Zt: e.dma_start(out=kvp[v * 128:(v + 1) * 128, :], in_=Zt[:, 0:512]),
              dma=True, final=True)
            if v >= 28:
                A("pool", [kz], [], lambda e, v=v, Zt=Zt: e.dma_start(
                    out=winp[(v - 28) * 128:(v - 27) * 128, :], in_=Zt[:, 512:768]), dma=True, final=True)
            A("dve", [kz], ["zb"], lambda e, Z5=Z5: e.tensor_copy(out=zb[:].rearrange("p (a g) d -> p a g d", a=2),
                                                                  in_=Z5[:, 1:3, 0, :, :]))
            for a4 in range(4):
                A("pe", ["zb", "idb"], ["K0"], lambda e, a4=a4: e.transpose(
                    out=K0[0:64, a4 * 128:(a4 + 1) * 128], in_=zb[:, a4, :], identity=idb[:]))
            A("dve", ["K0"], ["KTs"], lambda e, v=v: e.tensor_copy(
                out=KTs[0:64, :, v * 128:(v + 1) * 128], in_=K0[0:64, 0:256].rearrange("p (g t) -> p g t", g=2)))
            sl = v % 6
            A("dve", ["K0"], ["KTw"], lambda e, sl=sl: e.tensor_copy(
                out=KTw[0:64, :, sl * 128:(sl + 1) * 128], in_=K0[0:64, 256:512].rearrange("p (g t) -> p g t", g=2)))
            A("sp", [], ["H"], lambda e, v=v: e.dma_start(out=stgq[64:68, 0:128], in_=kpos_d[:, v * 128:(v + 1) * 128]),
              dma=True)
            A("dve", ["H"], ["KTw"], lambda e, sl=sl: e.tensor_copy(
                out=KTw[64:68, :, sl * 128:(sl + 1) * 128], in_=stgq[64:68, 0:128].unsqueeze(1).to_broadcast([4, 2, 128])))
            A("dve", [kz], ["VVs"], lambda e, v=v, Z5=Z5: e.tensor_copy(out=VVs[:, v, :, 0:64], in_=Z5[:, 1, 1, :, :]))
            A("dve", [kz], ["VVw"], lambda e, sl=sl, Z5=Z5: e.tensor_copy(out=VVw[:, sl, :, 0:64], in_=Z5[:, 2, 1, :, :]))
            rnn_gates(1, 2, False)
            state_tot()
            if own:
                mm(XT, kxt, 1, 512, C_Q)
                mm(XT, kxt, 2, 24, C_G)
                A("act", ["K1"], ["ZQ"], lambda e: e.activation(out=ZQ[:].rearrange("p h d -> p (h d)"), in_=K[1][:],
                                                                 func=AF.Copy))
                A("act", ["K2"], ["gsig"], lambda e: e.activation(out=gsig[:].rearrange("p h b -> p (h b)"),
                                                                   in_=K[2][:, 0:24], func=AF.Exp, scale=-1.0))
                A("dve", ["gsig"], ["gsig"], lambda e: e.tensor_scalar_add(out=gsig[:], in0=gsig[:], scalar1=1.0))
                A("dve", ["gsig"], ["gsig"], lambda e: e.reciprocal(out=gsig[:], in_=gsig[:]))
                mm(XT, kxt, 1, 512, C_RQ)
                silu_from(1, QS, "QS")
                mm(XT, kxt, 2, 512, C_RG)
                silu_from(2, RGS, "RGS")
                head_rms(ZQ[:], "ZQ", 8)
                A("dve", ["ZQ", "st8"], ["ZQ"], lambda e: e.tensor_tensor(
                    out=ZQ[:], in0=ZQ[:], in1=st8[:].unsqueeze(2).to_broadcast([128, 8, 64]), op=ALU.mult))
                A("dve", ["ZQ", "qgs"], ["QN"], lambda e: e.tensor_tensor(
                    out=QN[:], in0=ZQ[:], in1=qgs[:].unsqueeze(1).to_broadcast([128, 8, 64]), op=ALU.mult))
                for h in range(8):
                    A("pe", ["QN", "idb"], ["K0"], lambda e, h=h: e.transpose(
                        out=K0[0:64, h * 128:(h + 1) * 128], in_=QN[:, h, :], identity=idb[:]))
                A("dve", ["K0"], ["QTa"], lambda e: e.tensor_copy(out=QTa[0:64, :, :].rearrange("p h t -> p (h t)"),
                                                                  in_=K0[0:64, :]))
                A("sp", [], ["H"], lambda e, i_own=i_own: e.dma_start(out=stgq[64:68, :], in_=qpos_d[i_own]), dma=True)
                A("dve", ["H"], ["QTa"], lambda e: e.tensor_copy(out=QTa[64:68, :, :].rearrange("p h t -> p (h t)"),
                                                                    in_=stgq[64:68, :]))
                if SUB < 1:
                    return
                A("pe", ["tm", "T3"], ["K3"], lambda e: e.matmul(K[3][:], lhsT=tm[:, 0, :], rhs=T[3][:], start=True, stop=True))
                A("act", ["K3"], ["T4"], lambda e: e.activation(out=T[4][:], in_=K[3][:], func=AF.Exp))
                A("act", ["K3"], ["T5"], lambda e: e.activation(out=T[5][:], in_=K[3][:], func=AF.Exp, scale=-1.0))
                A("dve", ["QS", "T4"], ["QD"], lambda e: e.tensor_tensor(out=QD[:], in0=QS[:], in1=T[4][:], op=ALU.mult))
                A("dve", ["T2", "T5"], ["KDI"], lambda e: e.tensor_tensor(out=KDI[:], in0=T[2][:], in1=T[5][:], op=ALU.mult))
                for hp in range(4):
                    A("pe", ["QD", "idb"], ["K0"], lambda e, hp=hp: e.transpose(
                        out=K0[:, hp * 128:(hp + 1) * 128], in_=QD[:, hp * 128:(hp + 1) * 128], identity=idb[:]))
                    A("pe", ["KDI", "idb"], ["K0"], lambda e, hp=hp: e.transpose(
                        out=K0[:, (4 + hp) * 128:(5 + hp) * 128], in_=KDI[:, hp * 128:(hp + 1) * 128], identity=idb[:]))
                A("dve", ["K0"], ["qkT"], lambda e: e.tensor_copy(out=qkT[:].rearrange("p a t -> p (a t)"), in_=K0[:]))
                if SUB < 2:
                    return
                pA = K[5][:].rearrange("p (h t) -> p h t", h=8)
                for c in range(2):
                    for h in [0, 2, 4, 6, 1, 3, 5, 7]:
                        hp, ee = h // 2, h % 2
                        A("pe", ["qkT"], ["K5"], lambda e, c=c, h=h, hp=hp, ee=ee: e.matmul(
                            pA[64 * c:64 * c + 64, h, :], lhsT=qkT[64 * ee:64 * ee + 64, 4 + hp, c * 64:(c + 1) * 64],
                            rhs=qkT[64 * ee:64 * ee + 64, hp, c * 64:(c + 1) * 64], start=True, stop=True), rg=64 * ee)
                if "a" not in SKIP:
                    A("dve", ["K5", "cst"], ["AT"], lambda e: e.tensor_tensor(
                        out=AT[:], in0=pA, in1=tri2.unsqueeze(1).to_broadcast([128, 8, 64]), op=ALU.mult))
                if "s" not in SKIP:
                    A("dve", ["St"], ["Sb0"], lambda e: e.tensor_copy(out=Sb[0][:], in_=St[:]))
            state_update(0)
            if own:
                A("dve", ["St"], ["Sb1"], lambda e: e.tensor_copy(out=Sb[1][:], in_=St[:]))
            state_update(1)
            if not own:
                return
            if SUB < 3:
                return
            po = K[6][:].rearrange("p (h d) -> p h d", h=8)
            for c in range(2):
                for h in range(8):
                    A("pe", ["AT", "VB"], ["K6"], lambda e, c=c, h=h: e.matmul(
                        po[64 * c:64 * c + 64, h, :], lhsT=AT[64 * c:64 * c + 64, h, :],
                        rhs=VB[64 * c:64 * c + 64, h * 64:(h + 1) * 64], start=(h == 0), stop=False), rg=64 * c)
                order = [h for h in range(8) if h % 2 == c] + [h for h in range(8) if h % 2 != c]
                for j, h in enumerate(order):
                    hp, ee = h // 2, h % 2
                    A("pe", ["qkT", "Sb%d" % c], ["K6"], lambda e, c=c, h=h, hp=hp, ee=ee, j=j: e.matmul(
                        po[64 * c:64 * c + 64, h, :], lhsT=qkT[64 * ee:64 * ee + 64, hp, c * 64:(c + 1) * 64],
                        rhs=Sb[c][64 * ee:64 * ee + 64, hp, :], start=False, stop=(j == 7)), rg=64 * ee)
            A("act", ["K6"], ["OR32"], lambda e: e.activation(out=OR32[:].rearrange("p h d -> p (h d)"), in_=K[6][:],
                                                               func=AF.Copy))
            if STAGE < 4:
                return
            nb = 2 * v + 2
            pc4 = K[3][:, 0:264].rearrange("p (h j) -> p h j", h=4)
            pocmp = K[4][:, 0:256].rearrange("p (h d) -> p h d", h=4)
            for g in range(2):
                for hh in range(4):
                    A("pe", ["QTa", "kcT"], ["K3"], lambda e, g=g, hh=hh: e.matmul(
                        pc4[:, hh, 0:nb], lhsT=QTa[0:64, g * 4 + hh, :], rhs=kcT[0:64, g, 0:nb], start=True, stop=True))
                A("dve", ["K3", "cst"], ["LG"], lambda e, g=g: e.tensor_tensor(
                    out=LG[:, :, 0:nb], in0=pc4[:, :, 0:nb], in1=acore[:, g * 4:(g + 1) * 4, 0:nb], op=ALU.add))
                A("dve", ["LG", "cst"], ["LG"], lambda e: e.tensor_tensor(
                    out=LG[:, :, 2 * v:2 * v + 2], in0=LG[:, :, 2 * v:2 * v + 2],
                    in1=fix2.unsqueeze(1).to_broadcast([128, 4, 2]), op=ALU.add))
                A("dve", ["LG"], ["st4"], lambda e: e.tensor_reduce(out=st4[:], in_=LG[:, :, 0:nb], axis=AX.X, op=ALU.max))
                A("dve", ["LG", "st4"], ["LG"], lambda e: e.tensor_tensor(
                    out=LG[:, :, 0:nb], in0=LG[:, :, 0:nb], in1=st4[:].unsqueeze(2).to_broadcast([128, 4, nb]),
                    op=ALU.subtract))
                A("act", ["LG"], ["EX"], lambda e: e.activation(out=EX[:, :, 0:nb], in_=LG[:, :, 0:nb], func=AF.Exp))
                A("dve", ["EX"], ["st4b"], lambda e: e.tensor_reduce(out=st4b[:], in_=EX[:, :, 0:nb], axis=AX.X, op=ALU.add))
                A("dve", ["st4b"], ["st4b"], lambda e: e.reciprocal(out=st4b[:], in_=st4b[:]))
                A("dve", ["st4b", "cst"], ["st4b"], lambda e: e.tensor_scalar(
                    out=st4b[:], in0=st4b[:], scalar1=rowvalid[:, i_own:i_own + 1], scalar2=None, op0=ALU.mult))
                A("dve", ["EX", "st4b"], ["EX"], lambda e: e.tensor_tensor(
                    out=EX[:, :, 0:nb], in0=EX[:, :, 0:nb], in1=st4b[:].unsqueeze(2).to_broadcast([128, 4, nb]),
                    op=ALU.mult))
                A("dve", ["EX"], ["PB"], lambda e: e.tensor_copy(out=PB[:, :, 0:nb], in_=EX[:, :, 0:nb]))
                A("dve", ["EX"], ["SC"], lambda e: e.tensor_reduce(
                    out=SC[:, 0:nb], in_=EX[:, :, 0:nb].rearrange("p h j -> p j h"), axis=AX.X, op=ALU.add))
                if nb < 66:
                    A("dve", [], ["SC"], lambda e: e.memset(SC[:, nb:66], -1.0))
                A("dve", ["SC", "cst"], ["SC"], lambda e: e.scalar_tensor_tensor(
                    out=SC[:, 2 * v:2 * v + 1], in0=SC[:, 2 * v:2 * v + 1], scalar=mhi, in1=mlo5,
                    op0=ALU.mult, op1=ALU.add))
                A("dve", ["cst"], ["SC"], lambda e: e.tensor_copy(out=SC[:, 2 * v + 1:2 * v + 2], in_=c1col))
                A("dve", ["SC", "cst"], ["SC"], lambda e: e.tensor_tensor(out=SC[:], in0=SC[:], in1=keepc, op=ALU.mult))
                A("dve", ["SC", "cst"], ["SC"], lambda e: e.tensor_tensor(out=SC[:], in0=SC[:], in1=addc, op=ALU.add))
                A("dve", ["SC"], ["m8"], lambda e: e.max(out=m8[:], in_=SC[:]))
                A("dve", ["SC", "m8"], ["WK"], lambda e: e.match_replace(out=WK[:], in_to_replace=m8[:], in_values=SC[:],
                                                                          imm_value=-2.0))
                A("dve", ["WK"], ["m8"], lambda e: e.max(out=m8[:], in_=WK[:]))
                A("dve", ["m8"], ["thr"], lambda e: e.tensor_reduce(out=thr[:], in_=m8[:], axis=AX.X, op=ALU.min))
                A("dve", ["SC", "thr"], ["SEL"], lambda e: e.tensor_scalar(
                    out=SEL[:], in0=SC[:], scalar1=thr[:, 0:1], scalar2=None, op0=ALU.is_ge))
                A("dve", ["SC"], ["WK"], lambda e: e.tensor_single_scalar(out=WK[:], in_=SC[:], scalar=0.0, op=ALU.is_ge))
                A("dve", ["SEL", "WK"], ["SEL"], lambda e: e.tensor_tensor(out=SEL[:], in0=SEL[:], in1=WK[:], op=ALU.mult))
                A("dve", ["SEL"], ["MBF"], lambda e: e.tensor_scalar(
                    out=MBF[:], in0=SEL[:], scalar1=-1.0, scalar2=BIG, op0=ALU.add, op1=ALU.mult))
                A("pe", ["MBF", "idb"], ["K0"], lambda e: e.transpose(out=K0[0:66, 0:128], in_=MBF[:], identity=idb[:]))
                for hh in range(4):
                    A("pe", ["PB", "idb"], ["K0"], lambda e, hh=hh: e.transpose(
                        out=K0[0:nb, (1 + hh) * 128:(2 + hh) * 128], in_=PB[:, hh, 0:nb], identity=idb[:]))
                A("dve", ["K0"], ["RHSM%d" % g], lambda e, g=g: e.tensor_copy(
                    out=RHSM_[g][0:66], in_=K0[0:66, 0:128].unsqueeze(1).to_broadcast([66, 4, 128])))
                A("dve", ["K0"], ["PT4"], lambda e: e.tensor_copy(
                    out=PT4[0:nb, :, :], in_=K0[0:nb, 128:640].rearrange("p (h t) -> p h t", h=4)))
                for hh in range(4):
                    A("pe", ["PT4", "vcb"], ["K4"], lambda e, g=g, hh=hh: e.matmul(
                        pocmp[:, hh, :], lhsT=PT4[0:nb, hh, :], rhs=vcb[0:nb, g, :], start=True, stop=True))
                A("act", ["K4"], ["OCMP"], lambda e, g=g: e.activation(out=OCMP[:, g * 4:(g + 1) * 4, :], in_=pocmp,
                                                                        func=AF.Copy))
            if STAGE < 5:
                return
            posb = [(K[7][:, 0:260].rearrange("p (h d) -> p h d", h=4), "K7"),
                    (K[4][:, 0:260].rearrange("p (h d) -> p h d", h=4), "K4")]
            its = []
            gi = 0
            for g in range(2):
                for br in range(2):
                    kts = list(range(0, v + 1)) if br == 0 else list(range(max(0, v - 4), v + 1))
                    for kt in kts:
                        extra = []
                        if br == 0:
                            extra.append((Eexp[:, kt * 128:(kt + 1) * 128],
                                          RHSM_[g][:].rearrange("p h t -> p (h t)"), ["Eexp", "RHSM%d" % g]))
                            if kt == v:
                                extra.append((idb[:], MSK[:, 0, :], ["idb", "MSK"]))
                            lhs_k, rk = KTs[:, g, kt * 128:(kt + 1) * 128], ["KTs"]
                            rv, kv_ = VVs[:, kt, g, :], "VVs"
                        else:
                            if kt == v:
                                extra.append((idb[:], MSK[:, 0, :], ["idb", "MSK"]))
                            elif kt == v - 4:
                                extra.append((idb[:], MSK[:, 3 if kt == 0 else 1, :], ["idb", "MSK"]))
                            elif kt == 0:
                                extra.append((idb[:], MSK[:, 2, :], ["idb", "MSK"]))
                            slk = kt % 6
                            lhs_k, rk = KTw[:, g, slk * 128:(slk + 1) * 128], ["KTw"]
                            rv, kv_ = VVw[:, slk, g, :], "VVw"
                        its.append(dict(g=g, br=br, first=(kt == kts[0]), last=(kt == kts[-1]), extra=extra, lhs_k=lhs_k,
                                        rk=rk, rv=rv, kv=kv_, gi=gi))
                    gi += 1

            SBK = [5, 6, 1]

            def s_stage(i):
                d = its[i]
                bank = SBK[i % 3]
                kb = "K%d" % bank
                g, extra = d["g"], d["extra"]
                A("pe", d["rk"] + ["QTa"], [kb], lambda e, bank=bank, lhs_k=d["lhs_k"], g=g, ne=len(extra): e.matmul(
                    K[bank][:], lhsT=lhs_k, rhs=QTa[:, g * 4:(g + 1) * 4, :].rearrange("p h t -> p (h t)"),
                    start=True, stop=(ne == 0)))
                for xi, (l_, r_, ks_) in enumerate(extra):
                    A("pe", ks_, [kb], lambda e, bank=bank, l_=l_, r_=r_, last=(xi == len(extra) - 1): e.matmul(
                        K[bank][:], lhsT=l_, rhs=r_, start=False, stop=last))

            def ep_stage(i):
                d = its[i]
                bank = SBK[i % 3]
                kb = "K%d" % bank
                PTt, kpt = PTb[i % 3], "PTb%d" % (i % 3)
                pos, kpos_ = posb[d["gi"] % 2]
                A("act", [kb], [kpt], lambda e, bank=bank, PTt=PTt: e.activation(out=PTt[:], in_=K[bank][:], func=AF.Exp))
                for hh in range(4):
                    A("pe", [kpt, d["kv"]], [kpos_], lambda e, hh=hh, PTt=PTt, rv=d["rv"], pos=pos,
                      first=(d["first"] and hh == 0), last=(d["last"] and hh == 3): e.matmul(
                          pos[:, hh, :], lhsT=PTt[:, hh * 128:(hh + 1) * 128], rhs=rv, start=first, stop=last))
                if d["last"]:
                    dst, kd = (OSEL, "OSEL") if d["br"] == 0 else (OWIN, "OWIN")
                    A("act", [kpos_], [kd], lambda e, dst=dst, g=d["g"], pos=pos: e.activation(
                        out=dst[:, g * 4:(g + 1) * 4, :], in_=pos, func=AF.Copy))

            s_stage(0)
            if len(its) > 1:
                s_stage(1)
            for i in range(len(its)):
                if i + 2 < len(its):
                    s_stage(i + 2)
                ep_stage(i)
            if STAGE < 6:
                return
            finish(X, kx, yp[i_own * 128:(i_own + 1) * 128, :], "yp%d" % i_own, i_own)

        for v_ in range(NT if STAGE >= 2 else 0):
            tile_1b(v_)
        A("pool", ["St"], [], lambda e: e.dma_start(out=rnnp.rearrange("(a e) k v -> (e k) a v", e=2), in_=St[:]),
          dma=True, final=True)

        SKEYS = ["W1s", "SELM", "S0bf", "KN", "ZV", "QTs", "ATs", "ATs", "XcTb", "PGB0", "PGB1", "hs_s", "KCa", "vca",
                 "QTb", "PTc", "PTs", "PTw", "RHSMs", "WPB", "OcT", "OsT", "OwT", "IDX", "MSKs", "kcbs"]
        _so = [0]

        def sw(n, parts=128):
            a_ = _so[0]
            _so[0] += n
            assert _so[0] <= 26816
            return BIGW[0:parts, a_:a_ + n]

        W1s = sw(8192).rearrange("p (a j h) -> p a j h", a=2, j=32)
        SELM = sw(128).rearrange("p (a j) -> p a j", a=2)
        S0bf = sw(4096).rearrange("p (a d) -> p a d", a=64)
        KN = sw(512, 64).rearrange("p (a t) -> p a t", a=4)
        ZV = sw(256).rearrange("p (a g d) -> p a g d", a=2, g=2)
        QTs = sw(1024, 64).rearrange("p (h t) -> p h t", h=8)
        ATs = sw(1024).rearrange("p (h t) -> p h t", h=8)
        oiT = ATs[0:64]
        XcTb = sw(4096).rearrange("p (a n) -> p a n", a=4)
        _pgb = sw(1024)
        PGB = [_pgb[:, 0:512], _pgb[:, 512:1024]]
        hs_s = sw(128).rearrange("p (a j) -> p a j", a=4)
        KCa = sw(64, 68).rearrange("p (g j) -> p g j", g=2)
        vca = sw(130, 32).rearrange("p (g d) -> p g d", g=2)
        QTb = sw(64, 128).rearrange("p (h t) -> p h t", h=8)
        PTc = sw(32, 32)
        PTs = sw(544)
        PTw = sw(160)
        RHSMs = sw(32, 128).rearrange("p (h t) -> p h t", h=4)
        WPB = _pgb.rearrange("p (t c) -> p t c", t=4)
        OcT = sw(1024, 65).rearrange("p (h t) -> p h t", h=8)
        OsT = sw(1024, 65).rearrange("p (h t) -> p h t", h=8)
        OwT = sw(1024, 65).rearrange("p (h t) -> p h t", h=8)
        IDX = sw(1024).bitcast(mybir.dt.int32)
        MSKs = sw(64).rearrange("p (m n) -> p m n", m=2)
        kcbs = sw(128, 32).rearrange("p (g d) -> p g d", g=2)
        csts = sb("csts", [128, 74])
        selT = csts[0:32, 0:8]
        keepS = csts[0:8, 8:41]
        addS = csts[0:8, 41:74]
        iotac = sb("iotac", [128, 1])

        A("dve", [], AKEYS + ["fdummy"] + ["S0_%d" % b for b in range(16)], lambda e: e.memset(fdummy[:], 0.0))
        for b in range(16):
            A("pool", [], ["S0_%d" % b], lambda e, b=b: e.dma_start(
                out=S0[:, b * 4:(b + 1) * 4, :], in_=srnn[b].rearrange("(a e) k v -> (e k) a v", e=2)), dma=True)
        for b in range(16):
            A("pool", [], [], lambda e, b=b: e.dma_start(out=wins[b, 0:504, :], in_=cwin[b, 8:512, :]), dma=True, final=True)
        X, XT, kx, kxt = prep_x(xsm[:, :], 0)
        Zt, kz = Z[0], "Z0"
        mm(XT, kxt, 1, 384, C_KV)
        mm(XT, kxt, 2, 384, C_KV + 384)
        A("act", ["K1"], [kz], lambda e: e.activation(out=Zt[:, 0:384], in_=K[1][:, 0:384], func=AF.Copy))
        A("act", ["K2"], [kz], lambda e: e.activation(out=Zt[:, 384:768], in_=K[2][:, 0:384], func=AF.Copy))
        mm(XT, kxt, 1, 512, C_RF)
        mm(XT, kxt, 2, 512, C_RI)
        Z5 = k_norms(Zt, kz)
        A("pool", [kz], [], lambda e: e.dma_start(out=kvs[:, :], in_=Zt[:, 0:512]), dma=True, final=True)
        for b in range(16):
            A("pool", [kz], [], lambda e, b=b: e.dma_start(out=wins[b, 504:512, :], in_=Zt[b * 8:(b + 1) * 8, 512:768]),
              dma=True, final=True)
        rnn_gates(1, 2, True)
        mm(XT, kxt, 1, 512, C_Q)
        mm(XT, kxt, 2, 24, C_G)
        A("act", ["K1"], ["ZQ"], lambda e: e.activation(out=ZQ[:].rearrange("p h d -> p (h d)"), in_=K[1][:], func=AF.Copy))
        A("act", ["K2"], ["gsig"], lambda e: e.activation(out=gsig[:].rearrange("p h b -> p (h b)"), in_=K[2][:, 0:24],
                                                           func=AF.Exp, scale=-1.0))
        A("dve", ["gsig"], ["gsig"], lambda e: e.tensor_scalar_add(out=gsig[:], in0=gsig[:], scalar1=1.0))
        A("dve", ["gsig"], ["gsig"], lambda e: e.reciprocal(out=gsig[:], in_=gsig[:]))
        mm(XT, kxt, 1, 512, C_RQ)
        silu_from(1, QS, "QS")
        mm(XT, kxt, 2, 512, C_RG)
        silu_from(2, RGS, "RGS")
        A("dve", [], WK_ + SKEYS + ["fdummy"], lambda e: e.memset(fdummy[:], 0.0))
        A("sp", [], ["csts"], lambda e: e.dma_start(out=csts[:], in_=csts_d), dma=True)
        A("sp", [], ["iotac"], lambda e: e.dma_start(out=iotac[:], in_=iota_d), dma=True)
        A("dve", [kz], ["zb"], lambda e: e.tensor_copy(out=zb[:].rearrange("p (a g) d -> p a g d", a=2), in_=Z5[:, 1:3, 0, :, :]))
        for a4 in range(4):
            A("pe", ["zb", "idb"], ["K0"], lambda e, a4=a4: e.transpose(
                out=K0[0:64, a4 * 128:(a4 + 1) * 128], in_=zb[:, a4, :], identity=idb[:]))
        A("dve", ["K0"], ["KN"], lambda e: e.tensor_copy(out=KN[:].rearrange("p a t -> p (a t)"), in_=K0[0:64, 0:512]))
        A("dve", [kz], ["ZV"], lambda e: e.tensor_copy(out=ZV[:], in_=Z5[:, 1:3, 1, :, :]))
        head_rms(ZQ[:], "ZQ", 8)
        A("dve", ["ZQ", "st8"], ["ZQ"], lambda e: e.tensor_tensor(
            out=ZQ[:], in0=ZQ[:], in1=st8[:].unsqueeze(2).to_broadcast([128, 8, 64]), op=ALU.mult))
        A("dve", ["ZQ", "qgs"], ["QN"], lambda e: e.tensor_tensor(
            out=QN[:], in0=ZQ[:], in1=qgs[:].unsqueeze(1).to_broadcast([128, 8, 64]), op=ALU.mult))
        for h in range(8):
            A("pe", ["QN", "idb"], ["K0"], lambda e, h=h: e.transpose(
                out=K0[0:64, h * 128:(h + 1) * 128], in_=QN[:, h, :], identity=idb[:]))
        A("dve", ["K0"], ["QTs"], lambda e: e.tensor_copy(out=QTs[:].rearrange("p h t -> p (h t)"), in_=K0[0:64, :]))
        A("pe", ["tm", "T3"], ["K3"], lambda e: e.matmul(K[3][:], lhsT=tm[:, 2, :], rhs=T[3][:], start=True, stop=True))
        A("act", ["K3"], ["T4"], lambda e: e.activation(out=T[4][:], in_=K[3][:], func=AF.Exp))
        A("act", ["K3"], ["T5"], lambda e: e.activation(out=T[5][:], in_=K[3][:], func=AF.Exp, scale=-1.0))
        A("dve", ["QS", "T4"], ["QD"], lambda e: e.tensor_tensor(out=QD[:], in0=QS[:], in1=T[4][:], op=ALU.mult))
        A("dve", ["T2", "T5"], ["KDI"], lambda e: e.tensor_tensor(out=KDI[:], in0=T[2][:], in1=T[5][:], op=ALU.mult))
        for hp in range(4):
            A("pe", ["QD", "idb"], ["K0"], lambda e, hp=hp: e.transpose(
                out=K0[:, hp * 128:(hp + 1) * 128], in_=QD[:, hp * 128:(hp + 1) * 128], identity=idb[:]))
            A("pe", ["KDI", "idb"], ["K0"], lambda e, hp=hp: e.transpose(
                out=K0[:, (4 + hp) * 128:(5 + hp) * 128], in_=KDI[:, hp * 128:(hp + 1) * 128], identity=idb[:]))
        A("dve", ["K0"], ["qkT"], lambda e: e.tensor_copy(out=qkT[:].rearrange("p a t -> p (a t)"), in_=K0[:]))
        for ee in range(2):
            for hp in range(4):
                A("pe", ["qkT"], ["K%d" % (5 + ee)], lambda e, ee=ee, hp=hp: e.matmul(
                    K[5 + ee][:, hp * 128:(hp + 1) * 128], lhsT=qkT[64 * ee:64 * ee + 64, 4 + hp, :],
                    rhs=qkT[64 * ee:64 * ee + 64, hp, :], start=True, stop=True), rg=64 * ee)
        for ee in range(2):
            A("dve", ["K%d" % (5 + ee), "tm"], ["ATs"], lambda e, ee=ee: e.tensor_tensor(
                out=ATs[:, ee * 4:(ee + 1) * 4, :], in0=K[5 + ee][:].rearrange("p (a t) -> p a t", a=4),
                in1=tm[:, 2, :].unsqueeze(1).to_broadcast([128, 4, 128]), op=ALU.mult))
        A("dve", ["S0_%d" % b for b in range(16)], ["S0bf"], lambda e: e.tensor_copy(out=S0bf[:, 0:32, :], in_=S0[:, 0:32, :]))
        A("act", ["S0_%d" % b for b in range(16)], ["S0bf"], lambda e: e.activation(out=S0bf[:, 32:64, :], in_=S0[:, 32:64, :],
                                                                                     func=AF.Copy))
        pos_ = K[7][:].rearrange("p (h d) -> p h d", h=8)
        for h in range(8):
            hp, ee = h // 2, h % 2
            A("pe", ["ATs", "VB"], ["K7"], lambda e, h=h, hp=hp, ee=ee: e.matmul(
                pos_[:, h, :], lhsT=ATs[:, ee * 4 + hp, :], rhs=VB[:, h * 64:(h + 1) * 64], start=True, stop=True))
        for ee in range(2):
            for b in range(16):
                for hp in range(4):
                    A("pe", ["qkT", "S0bf"], ["K%d" % (5 + ee)], lambda e, ee=ee, b=b, hp=hp: e.matmul(
                        K[5 + ee][0:64, hp * 128 + b * 8:hp * 128 + b * 8 + 8], lhsT=S0bf[64 * ee:64 * ee + 64, b * 4 + hp, :],
                        rhs=qkT[64 * ee:64 * ee + 64, hp, b * 8:(b + 1) * 8], start=True, stop=True), rg=64 * ee)
        for ee in range(2):
            A("act", ["K%d" % (5 + ee)], ["ATs"], lambda e, ee=ee: e.activation(
                out=oiT[:, ee * 4:(ee + 1) * 4, :].rearrange("p a t -> p (a t)"), in_=K[5 + ee][0:64, :], func=AF.Copy))
        for h in range(8):
            hp, ee = h // 2, h % 2
            A("pe", ["ATs", "idb"], ["K0"], lambda e, h=h, hp=hp, ee=ee: e.transpose(
                out=K0[:, h * 64:(h + 1) * 64], in_=oiT[:, ee * 4 + hp, :], identity=idb[0:64, 0:64]))
        A("act", ["K0"], ["T5"], lambda e: e.activation(out=T[5][:], in_=K0[:, 0:512], func=AF.Copy))
        A("dve", ["K7", "T5"], ["OR32"], lambda e: e.tensor_tensor(out=OR32[:].rearrange("p h d -> p (h d)"), in0=K[7][:],
                                                                  in1=T[5][:], op=ALU.add))
        G = T[3]
        ptots = K[5][:, 0:64].rearrange("p (a b) -> p a b", a=4)
        for h in range(8):
            hp, ee = h // 2, h % 2
            A("pe", ["T3", "rmask"], ["K5"], lambda e, h=h, hp=hp, ee=ee: e.matmul(
                ptots[64 * ee:64 * ee + 64, hp, :], lhsT=G[:, h * 64:(h + 1) * 64], rhs=rmask[:], start=True, stop=True))
        A("act", ["K5"], ["etots"], lambda e: e.activation(out=etots[:], in_=ptots, func=AF.Exp))
        for b in range(16):
            q = b % 2
            A("dve", ["KD", "rmask"], ["PTb%d" % q], lambda e, b=b, q=q: e.tensor_scalar(
                out=kdm[q][:], in0=KD[:], scalar1=rmask[:, b:b + 1], scalar2=None, op0=ALU.mult))
            for h in range(8):
                hp, ee = h // 2, h % 2
                A("pe", ["PTb%d" % q, "VB"], ["K4"], lambda e, h=h, q=q, hp=hp, ee=ee: e.matmul(
                    pst[64 * ee:64 * ee + 64, hp, :], lhsT=kdm[q][:, h * 64:(h + 1) * 64], rhs=VB[:, h * 64:(h + 1) * 64],
                    start=True, stop=True))
            Sbv = S0[:, b * 4:(b + 1) * 4, :]
            A("dve", ["S0_%d" % b, "etots", "S0bf"], ["S0_%d" % b], lambda e, b=b, Sbv=Sbv: e.tensor_tensor(
                out=Sbv, in0=Sbv, in1=etots[:, :, b:b + 1].to_broadcast([128, 4, 64]), op=ALU.mult))
            A("dve", ["S0_%d" % b, "K4"], ["S0_%d" % b], lambda e, Sbv=Sbv: e.tensor_tensor(out=Sbv, in0=Sbv, in1=pst, op=ALU.add))
            A("pool", ["S0_%d" % b], ["rnns_o"], lambda e, b=b, Sbv=Sbv: e.dma_start(
                out=rnns[b].rearrange("(a e) k v -> (e k) a v", e=2), in_=Sbv), dma=True, final=True)
        if STAGE >= 8:
            A("dve", ["rnns_o"] + ["S0_%d" % b for b in range(16)], ["KTs", "fdummy"], lambda e: e.memset(fdummy[:], 0.0))
            A("dve", [], ["KTs"], lambda e: e.memset(KTs[64:128, :, 0:2176], 0.0))
            for c0 in range(0, 17 * 128, 1024):
                n = min(1024, 17 * 128 - c0)
                A("sp", [], ["H"], lambda e, c0=c0, n=n: e.dma_start(out=stgq[64:68, 0:n], in_=kpos_d[:, c0:c0 + n]), dma=True)
                for g in range(2):
                    A("dve", ["H"], ["KTs"], lambda e, c0=c0, n=n, g=g: e.tensor_copy(
                        out=KTs[64:68, g, c0:c0 + n], in_=stgq[64:68, 0:n]))
            A("sp", [], ["H"], lambda e: e.dma_start(out=stgq[64:68, 0:640], in_=kpos_d[:, 1536:2176]), dma=True)
            A("dve", ["H"], ["KTw"], lambda e: e.tensor_copy(
                out=KTw[64:68, :, 0:640], in_=stgq[64:68, 0:640].unsqueeze(1).to_broadcast([4, 2, 640])))
            A("dve", [], ["QTb"], lambda e: e.memset(QTb[64:128, :, :], 0.0))
            A("dve", [], ["RHSMs"], lambda e: e.memset(RHSMs[:, :, :], 0.0))
            A("sp", [], ["H"], lambda e: e.dma_start(out=stgq[64:68, 0:64], in_=sq_d[0:4, :]), dma=True)
            A("dve", ["H"], ["QTb"], lambda e: e.tensor_copy(out=QTb[64:68, :, :].rearrange("p h t -> p (h t)"),
                                                             in_=stgq[64:68, 0:64]))
            A("sp", [], ["H"], lambda e: e.dma_start(out=stgq[64:68, 0:32], in_=sq_d[4:8, 0:32]), dma=True)
            A("dve", ["H"], ["KCa"], lambda e: e.tensor_copy(
                out=KCa[64:68, :, :], in_=stgq[64:68, 0:32].unsqueeze(1).to_broadcast([4, 2, 32])))
            A("sp", [], ["H"], lambda e: e.dma_start(out=stgq[:, 0:192], in_=sm_d), dma=True)
            A("dve", ["H"], ["SELM"], lambda e: e.tensor_copy(out=SELM[:].rearrange("p a j -> p (a j)"), in_=stgq[:, 0:128]))
            A("dve", ["H"], ["MSKs"], lambda e: e.tensor_copy(out=MSKs[:].rearrange("p m n -> p (m n)"), in_=stgq[:, 128:192]))
            for typ in range(2):
                for jq in range(4):
                    A("sp", [], ["H"], lambda e, typ=typ, jq=jq: e.dma_start(
                        out=stgq[:].rearrange("p (j h) -> p j h", j=8),
                        in_=w1_d[typ, jq * 1024:(jq + 1) * 1024, :].rearrange("(j p) h -> p j h", p=128)), dma=True)
                    A("dve", ["H"], ["W1s"], lambda e, typ=typ, jq=jq: e.tensor_copy(
                        out=W1s[:, typ, jq * 8:(jq + 1) * 8, :], in_=stgq[:].rearrange("p (j h) -> p j h", j=8)))
            A("dve", [], ["KTs"], lambda e: e.memset(KTs[0:64, :, 2048:2176], 0.0))
            A("dve", [], ["VVs"], lambda e: e.memset(VVs[:, 16, :, 0:64], 0.0))
            A("dve", [], ["KTw"], lambda e: e.memset(KTw[0:64, :, 512:640], 0.0))
            A("dve", [], ["VVw"], lambda e: e.memset(VVw[:, 4, :, 0:64], 0.0))
            A("dve", [], ["vca"], lambda e: e.memset(vca[:, :, 64:65], 1.0))
            A("sp", [], ["IDX"], lambda e: e.dma_start(out=IDX[:, 0:256], in_=ptrep_d), dma=True)
            A("dve", ["IDX"], ["T0"], lambda e: e.tensor_copy(out=T[0][:, 0:256], in_=IDX[:, 0:256]))
            A("dve", ["T0", "iotac"], ["T0"], lambda e: e.tensor_scalar(out=T[0][:, 0:256], in0=T[0][:, 0:256], scalar1=128.0,
                                                                       scalar2=iotac[:, 0:1], op0=ALU.mult, op1=ALU.add))
            A("dve", ["T0"], ["IDX"], lambda e: e.tensor_copy(out=IDX[:, 256:512], in_=T[0][:, 0:256]))
            PG = [T[0], T[1], T[2], T[3]]
            K1v = K[1][:, 0:256].rearrange("p (a t) -> p a t", a=4)
            K3v = K[3][:, 0:256].rearrange("p (a d) -> p a d", a=4)
            WP = xs[1]
            pgc = [0]

            def sample_seq(b):
                for j in range(16):
                    q = pgc[0] % 4
                    q2 = pgc[0] % 2
                    pgc[0] += 1
                    n = b * 16 + j
                    pg, kpg, pgb, kpgb = PG[q], "T%d" % q, PGB[q2], "PGB%d" % q2
                    A("pool", ["IDX"], [kpg], lambda e, pg=pg, n=n: e.indirect_dma_start(
                        out=pg[:], out_offset=None, in_=cache_d,
                        in_offset=bass.IndirectOffsetOnAxis(ap=IDX[:, 256 + n:257 + n], axis=0)), dma=True)
                    if q2 == 0:
                        A("act", [kpg], [kpgb], lambda e, pg=pg, pgb=pgb: e.activation(out=pgb, in_=pg[:], func=AF.Copy))
                    else:
                        A("dve", [kpg], [kpgb], lambda e, pg=pg, pgb=pgb: e.tensor_copy(out=pgb, in_=pg[:]))
                    for tg in range(4):
                        for par in range(2):
                            A("pe", [kpgb, "SELM"], ["K1"], lambda e, tg=tg, par=par, pgb=pgb: e.matmul(
                                K1v[64 * par:64 * par + 64, tg, :], lhsT=pgb[:, tg * 64:(tg + 1) * 64], rhs=SELM[:, par, :],
                                start=True, stop=True))
                    for a in range(2):
                        A("dve", ["K1", "peT"], ["XcTb"], lambda e, a=a, j=j: e.tensor_tensor(
                            out=XcTb[:, 2 * a:2 * a + 2, j * 64:(j + 1) * 64].rearrange("p g (b j) -> p g b j", b=2),
                            in0=K1v[:, 2 * a:2 * a + 2, :].rearrange("p g (b j) -> p g b j", b=2),
                            in1=peT[:, a, :].unsqueeze(1).unsqueeze(1).to_broadcast([128, 2, 2, 32]), op=ALU.add))
                    for g in range(2):
                        A("pe", [kpgb, "idb"], ["K0"], lambda e, g=g, pgb=pgb: e.transpose(
                            out=K0[0:64, g * 128:(g + 1) * 128], in_=pgb[:, 256 + g * 64:256 + (g + 1) * 64], identity=idb[:]))
                    A("dve", ["K0"], ["KTs"], lambda e, j=j: e.tensor_copy(
                        out=KTs[0:64, :, j * 128:(j + 1) * 128], in_=K0[0:64, 0:256].rearrange("p (g t) -> p g t", g=2)))
                    A("act", [kpg], ["VVs"], lambda e, pg=pg, j=j: e.activation(
                        out=VVs[:, j, :, 0:64], in_=pg[:, 384:512].rearrange("p (g d) -> p g d", g=2), func=AF.Copy))
                A("dve", ["KN"], ["KTs"], lambda e: e.tensor_copy(out=KTs[0:64, :, 2048:2056], in_=KN[:, 0:2, b * 8:(b + 1) * 8]))
                for g in range(2):
                    A("sp", ["ZV"], ["VVs"], lambda e, g=g: e.dma_start(out=VVs[0:8, 16, g, 0:64], in_=ZV[b * 8:(b + 1) * 8, 0, g, :]),
                      dma=True)
                    A("sp", ["ZV"], ["VVw"], lambda e, g=g: e.dma_start(out=VVw[0:8, 4, g, 0:64], in_=ZV[b * 8:(b + 1) * 8, 1, g, :]),
                      dma=True)
                for tg in range(4):
                    typ = tg // 2
                    for j in range(32):
                        A("pe", ["XcTb", "W1s"], ["K2"], lambda e, tg=tg, typ=typ, j=j: e.matmul(
                            K[2][:, 0:32], lhsT=W1s[:, typ, j, :], rhs=XcTb[:, tg, j:1024:32], start=(j == 0), stop=(j == 31)))
                    A("act", ["K2"], ["T4"], lambda e: e.activation(out=T[4][:, 0:32], in_=K[2][:, 0:32], func=AF.Exp, scale=-1.0))
                    A("dve", ["T4"], ["T4"], lambda e: e.tensor_scalar_add(out=T[4][:, 0:32], in0=T[4][:, 0:32], scalar1=1.0))
                    A("dve", ["T4"], ["T4"], lambda e: e.reciprocal(out=T[4][:, 0:32], in_=T[4][:, 0:32]))
                    A("dve", ["T4", "K2"], ["hs_s"], lambda e, tg=tg: e.tensor_tensor(out=hs_s[:, tg, :], in0=K[2][:, 0:32],
                                                                                   in1=T[4][:, 0:32], op=ALU.mult))
                    A("pe", ["hs_s", "W2b"], ["K3"], lambda e, tg=tg, typ=typ: e.matmul(
                        K3v[0:32, tg, :], lhsT=hs_s[:, tg, :], rhs=W2b[:, typ, :], start=True, stop=True))
                A("act", ["K3"], ["kcr"], lambda e: e.activation(out=kcr[0:32], in_=K3v[0:32, 0:2, :], func=AF.Copy))
                A("act", ["K3"], ["vca"], lambda e: e.activation(out=vca[:, :, 0:64], in_=K3v[0:32, 2:4, :], func=AF.Copy))
                A("dve", ["kcr"], ["LG"], lambda e: e.tensor_tensor(out=LG[0:32, 0:2, 0:64], in0=kcr[0:32], in1=kcr[0:32], op=ALU.mult))
                A("dve", ["LG"], ["st4"], lambda e: e.tensor_reduce(out=st4[0:32, 0:2], in_=LG[0:32, 0:2, 0:64], axis=AX.X, op=ALU.add))
                A("act", ["st4"], ["st4"], lambda e: e.activation(out=st4[0:32, 0:2], in_=st4[0:32, 0:2], func=AF.Ln,
                                                                  scale=1.0 / 64, bias=EPS))
                A("act", ["st4"], ["st4"], lambda e: e.activation(out=st4[0:32, 0:2], in_=st4[0:32, 0:2], func=AF.Exp, scale=-0.5))
                A("dve", ["kcr", "st4"], ["kcr"], lambda e: e.tensor_tensor(
                    out=kcr[0:32], in0=kcr[0:32], in1=st4[0:32, 0:2].unsqueeze(2).to_broadcast([32, 2, 64]), op=ALU.mult))
                A("dve", ["kcr", "gvec"], ["kcbs"], lambda e: e.tensor_tensor(
                    out=kcbs[:], in0=kcr[0:32], in1=gvec[0:32, 0:64].unsqueeze(1).to_broadcast([32, 2, 64]), op=ALU.mult))
                for g in range(2):
                    A("pe", ["kcbs", "idb"], ["K0"], lambda e, g=g: e.transpose(
                        out=K0[0:64, 512 + g * 32:512 + (g + 1) * 32], in_=kcbs[:, g, :], identity=idb[0:32, 0:32]))
                A("dve", ["K0"], ["KCa"], lambda e: e.tensor_copy(
                    out=KCa[0:64, :, :], in_=K0[0:64, 512:576].rearrange("p (g j) -> p g j", g=2)))
                A("sp", [], ["xs1"], lambda e: e.dma_start(out=WP[:].rearrange("p (t c) -> p t c", t=4),
                                                           in_=cwin[b].rearrange("(t p) c -> p t c", p=128)), dma=True)
                A("dve", ["xs1"], ["PGB0", "PGB1"], lambda e: e.tensor_copy(out=WPB[:].rearrange("p t c -> p (t c)"), in_=WP[:]))
                for tw in range(4):
                    for g in range(2):
                        A("pe", ["PGB0", "PGB1", "idb"], ["K0"], lambda e, tw=tw, g=g: e.transpose(
                            out=K0[0:64, (tw * 2 + g) * 128:(tw * 2 + g + 1) * 128], in_=WPB[:, tw, g * 64:(g + 1) * 64],
                            identity=idb[:]))
                A("dve", ["K0"], ["KTw"], lambda e: e.tensor_copy(
                    out=KTw[0:64, :, 0:512].rearrange("p g (t k) -> p g t k", t=4),
                    in_=K0[0:64, :].rearrange("p (t g k) -> p g t k", t=4, g=2)))
                A("act", ["xs1"], ["VVw"], lambda e: e.activation(
                    out=VVw[:, 0:4, :, 0:64], in_=WP[:].rearrange("p (t c) -> p t c", t=4)[:, :, 128:256].rearrange(
                        "p t (g d) -> p t g d", g=2), func=AF.Copy))
                A("dve", ["KN"], ["KTw"], lambda e: e.tensor_copy(out=KTw[0:64, :, 512:520], in_=KN[:, 2:4, b * 8:(b + 1) * 8]))
                A("dve", ["QTs"], ["QTb"], lambda e: e.tensor_copy(out=QTb[0:64, :, :], in_=QTs[:, :, b * 8:(b + 1) * 8]))
                for g in range(2):
                    qrhs = QTb[:, g * 4:(g + 1) * 4, :].rearrange("p h t -> p (h t)")
                    qrhs68 = QTb[0:68, g * 4:(g + 1) * 4, :].rearrange("p h t -> p (h t)")
                    A("pe", ["KCa", "QTb"], ["K3"], lambda e, g=g, qrhs68=qrhs68: e.matmul(
                        K[3][0:32, 0:32], lhsT=KCa[0:68, g, :], rhs=qrhs68, start=True, stop=True))
                    A("act", ["K3"], ["PTc"], lambda e: e.activation(out=PTc, in_=K[3][0:32, 0:32], func=AF.Exp))
                    A("pe", ["vca", "PTc"], ["K7"], lambda e, g=g: e.matmul(K[7][0:65, 0:32], lhsT=vca[:, g, :], rhs=PTc,
                                                                         start=True, stop=True))
                    A("act", ["K7"], ["OcT"], lambda e, g=g: e.activation(
                        out=OcT[:, g * 4:(g + 1) * 4, b * 8:(b + 1) * 8], in_=K[7][0:65, 0:32].rearrange("p (h t) -> p h t", h=4),
                        func=AF.Copy))
                    A("pe", ["PTc", "idb"], ["K0"], lambda e: e.transpose(out=K0[0:32, 640:672], in_=PTc, identity=idb[0:32, 0:32]))
                    A("dve", ["K0"], ["EX"], lambda e: e.tensor_copy(out=EX[0:32, 0, 0:32], in_=K0[0:32, 640:672]))
                    A("dve", ["EX"], ["st4b"], lambda e: e.tensor_reduce(out=st4b[0:32, 0:1], in_=EX[0:32, 0, 0:32], axis=AX.X, op=ALU.add))
                    A("dve", ["st4b"], ["st4b"], lambda e: e.reciprocal(out=st4b[0:32, 0:1], in_=st4b[0:32, 0:1]))
                    A("dve", ["EX", "st4b"], ["EX"], lambda e: e.tensor_scalar(
                        out=EX[0:32, 0, 0:32], in0=EX[0:32, 0, 0:32], scalar1=st4b[0:32, 0:1], scalar2=None, op0=ALU.mult))
                    A("pe", ["csts", "EX"], ["K3"], lambda e: e.matmul(K[3][0:8, 64:96], lhsT=selT, rhs=EX[0:32, 0, 0:32],
                                                                      start=True, stop=True))
                    A("dve", [], ["SC"], lambda e: e.memset(SC[0:8, 0:33], 0.0))
                    A("dve", ["K3"], ["SC"], lambda e: e.tensor_copy(out=SC[0:8, 0:32], in_=K[3][0:8, 64:96]))
                    A("dve", ["SC", "csts"], ["SC"], lambda e: e.tensor_tensor(out=SC[0:8, 0:33], in0=SC[0:8, 0:33], in1=keepS, op=ALU.mult))
                    A("dve", ["SC", "csts"], ["SC"], lambda e: e.tensor_tensor(out=SC[0:8, 0:33], in0=SC[0:8, 0:33], in1=addS, op=ALU.add))
                    A("dve", ["SC"], ["m8"], lambda e: e.max(out=m8[0:8], in_=SC[0:8, 0:33]))
                    A("dve", ["SC", "m8"], ["WK"], lambda e: e.match_replace(out=WK[0:8, 0:33], in_to_replace=m8[0:8],
                                                                              in_values=SC[0:8, 0:33], imm_value=-2.0))
                    A("dve", ["WK"], ["m8"], lambda e: e.max(out=m8[0:8], in_=WK[0:8, 0:33]))
                    A("dve", ["m8"], ["thr"], lambda e: e.tensor_reduce(out=thr[0:8], in_=m8[0:8], axis=AX.X, op=ALU.min))
                    A("dve", ["SC", "thr"], ["SEL"], lambda e: e.tensor_scalar(
                        out=SEL[0:8, 0:33], in0=SC[0:8, 0:33], scalar1=thr[0:8, 0:1], scalar2=None, op0=ALU.is_ge))
                    A("dve", ["SEL"], ["MBF"], lambda e: e.tensor_scalar(
                        out=MBF[0:8, 0:33], in0=SEL[0:8, 0:33], scalar1=-1.0, scalar2=BIG, op0=ALU.add, op1=ALU.mult))
                    A("pe", ["MBF", "idb"], ["K0"], lambda e: e.transpose(out=K0[0:33, 704:712], in_=MBF[0:8, 0:33],
                                                                          identity=idb[0:8, 0:8]))
                    A("dve", ["K0"], ["RHSMs"], lambda e: e.tensor_copy(
                        out=RHSMs[0:33], in_=K0[0:33, 704:712].unsqueeze(1).to_broadcast([33, 4, 8])))
                    for kt in range(17):
                        dst = K[5][:, kt * 32:(kt + 1) * 32] if kt < 16 else K[6][:, 0:32]
                        kb = "K5" if kt < 16 else "K6"
                        A("pe", ["KTs", "QTb"], [kb], lambda e, g=g, kt=kt, dst=dst, qrhs=qrhs: e.matmul(
                            dst, lhsT=KTs[:, g, kt * 128:(kt + 1) * 128], rhs=qrhs, start=True, stop=False))
                        A("pe", ["Eexp", "RHSMs"], [kb], lambda e, kt=kt, dst=dst: e.matmul(
                            dst, lhsT=Eexp[:, kt * 128:(kt + 1) * 128], rhs=RHSMs[:].rearrange("p h t -> p (h t)"),
                            start=False, stop=(kt < 16)))
                        if kt == 16:
                            A("pe", ["idb", "MSKs"], [kb], lambda e, dst=dst: e.matmul(dst, lhsT=idb[:], rhs=MSKs[:, 0, :],
                                                                                       start=False, stop=True))
                    A("act", ["K5"], ["PTs"], lambda e: e.activation(out=PTs[:, 0:512], in_=K[5][:], func=AF.Exp))
                    A("act", ["K6"], ["PTs"], lambda e: e.activation(out=PTs[:, 512:544], in_=K[6][:, 0:32], func=AF.Exp))
                    for kt in range(17):
                        A("pe", ["PTs", "VVs"], ["K7"], lambda e, g=g, kt=kt: e.matmul(
                            K[7][0:65, 32:64], lhsT=VVs[:, kt, g, :], rhs=PTs[:, kt * 32:(kt + 1) * 32],
                            start=(kt == 0), stop=(kt == 16)))
                    A("act", ["K7"], ["OsT"], lambda e, g=g: e.activation(
                        out=OsT[:, g * 4:(g + 1) * 4, b * 8:(b + 1) * 8], in_=K[7][0:65, 32:64].rearrange("p (h t) -> p h t", h=4),
                        func=AF.Copy))
                    for s_ in range(5):
                        dst = K[4][:, s_ * 32:(s_ + 1) * 32]
                        msk = {0: 1, 4: 0}.get(s_)
                        A("pe", ["KTw", "QTb"], ["K4"], lambda e, g=g, s_=s_, dst=dst, qrhs=qrhs, msk=msk: e.matmul(
                            dst, lhsT=KTw[:, g, s_ * 128:(s_ + 1) * 128], rhs=qrhs, start=True, stop=(msk is None)))
                        if msk is not None:
                            A("pe", ["idb", "MSKs"], ["K4"], lambda e, dst=dst, msk=msk: e.matmul(
                                dst, lhsT=idb[:], rhs=MSKs[:, msk, :], start=False, stop=True))
                    A("act", ["K4"], ["PTw"], lambda e: e.activation(out=PTw, in_=K[4][:, 0:160], func=AF.Exp))
                    for s_ in range(5):
                        A("pe", ["PTw", "VVw"], ["K7"], lambda e, g=g, s_=s_: e.matmul(
                            K[7][0:65, 64:96], lhsT=VVw[:, s_, g, :], rhs=PTw[:, s_ * 32:(s_ + 1) * 32],
                            start=(s_ == 0), stop=(s_ == 4)))
                    A("act", ["K7"], ["OwT"], lambda e, g=g: e.activation(
                        out=OwT[:, g * 4:(g + 1) * 4, b * 8:(b + 1) * 8], in_=K[7][0:65, 64:96].rearrange("p (h t) -> p h t", h=4),
                        func=AF.Copy))

            for b_ in range(16):
                sample_seq(b_)
            for src, ks, dst, kd in ((OcT, "OcT", OSEL, "OSEL"), (OsT, "OsT", OSEL, "OSEL"), (OwT, "OwT", OWIN, "OWIN")):
                for h in range(8):
                    A("pe", [ks, "idb"], ["K0"], lambda e, src=src, h=h: e.transpose(
                        out=K0[:, h * 66:h * 66 + 65], in_=src[:, h, :], identity=idb[0:65, 0:65]))
                A("act", ["K0"], [kd], lambda e, dst=dst: e.activation(
                    out=dst[:], in_=K0[:, 0:528].rearrange("p (h d) -> p h d", h=8)[:, :, 0:65], func=AF.Copy))
                if ks == "OcT":
                    A("dve", ["OSEL"], ["st8"], lambda e: e.reciprocal(out=st8[:], in_=OSEL[:, :, 64]))
                    A("dve", ["OSEL", "st8"], ["OCMP"], lambda e: e.tensor_tensor(
                        out=OCMP[:], in0=OSEL[:, :, 0:64], in1=st8[:].unsqueeze(2).to_broadcast([128, 8, 64]), op=ALU.mult))
            finish(xs[0], "xs0", ys[:, :], "ys", NOWN if debug else None)
        BKEYS = WK_ + AKEYS + ["Eexp", "MSK", "xT0", "xT1", "OC", "OCT", "KD", "VB", "QD", "KDI", "WUD", "QN", "qkT", "AT", "Sb0", "Sb1", "PB", "hs", "RHSM0", "RHSM1", "PT4"] + \
            ["S0_%d" % b_ for b_ in range(16)] + SKEYS
        if STAGE >= 7:
            A("dve", [], BKEYS + ["fdummy"], lambda e: e.memset(fdummy[:], 0.0))
            A("sp", [], ["lnmc"], lambda e: e.dma_start(out=lnmc[:], in_=lnmcol), dma=True)
            stg = [(T[i_], "T%d" % i_) for i_ in range(6)]
            cnt_ = [0]

            def wload(src, dst, scale_ap):
                st, ks = stg[cnt_[0] % 6]
                use_act = (cnt_[0] % 2 == 0)
                cnt_[0] += 1
                A("sp", [], [ks], lambda e: e.dma_start(out=st[:], in_=src), dma=True)
                rk = [ks] + (["lnmc"] if scale_ap is not None else [])
                if use_act:
                    if scale_ap is not None:
                        A("act", rk, ["WUD"], lambda e: e.activation(out=dst, in_=st[:], func=AF.Copy, scale=scale_ap))
                    else:
                        A("act", rk, ["WUD"], lambda e: e.activation(out=dst, in_=st[:], func=AF.Copy))
                else:
                    if scale_ap is not None:
                        A("dve", rk, ["WUD"], lambda e: e.tensor_scalar(out=dst, in0=st[:], scalar1=scale_ap, scalar2=None,
                                                                         op0=ALU.mult))
                    else:
                        A("dve", rk, ["WUD"], lambda e: e.tensor_copy(out=dst, in_=st[:]))

            for k in range(8):
                for q8 in range(8):
                    wload(w_up[k * 128:(k + 1) * 128, q8 * 512:(q8 + 1) * 512], WU[:, k, q8 * 512:(q8 + 1) * 512],
                          lnmc[:, k:k + 1])
            for f in range(32):
                for q2 in range(2):
                    wload(w_down[f * 128:(f + 1) * 128, q2 * 512:(q2 + 1) * 512], WD[:, f, q2 * 512:(q2 + 1) * 512], None)

            def mlp_tile(hsrc, ydst, ky, idx):
                X, kx = xs[idx % 2], "xs%d" % (idx % 2)
                A("sp", [ky], [kx], lambda e: e.dma_start(out=X[:], in_=hsrc), dma=True)
                A("act", [kx], ["OSEL", "ss"], lambda e: e.activation(
                    out=OSEL[:].rearrange("p h d -> p (h d)")[:, 0:512], in_=X[:, 0:512], func=AF.Square, accum_out=ss[:]))
                A("act", [kx], ["OSEL", "rstd"], lambda e: e.activation(
                    out=OSEL[:].rearrange("p h d -> p (h d)")[:, 0:512], in_=X[:, 512:1024], func=AF.Square,
                    accum_out=rstd[:]))
                A("dve", ["ss", "rstd"], ["ss"], lambda e: e.tensor_tensor(out=ss[:], in0=ss[:], in1=rstd[:], op=ALU.add))
                A("act", ["ss"], ["rstd"], lambda e: e.activation(out=rstd[:], in_=ss[:], func=AF.Ln, scale=1.0 / D, bias=EPS))
                A("act", ["rstd"], ["rstd"], lambda e: e.activation(out=rstd[:], in_=rstd[:], func=AF.Exp, scale=-0.5))
                A("dve", [kx, "rstd"], ["xb"], lambda e: e.tensor_scalar(out=xb[:], in0=X[:], scalar1=rstd[:, 0:1],
                                                                         scalar2=None, op0=ALU.mult))
                for k in range(8):
                    A("pe", ["xb", "idb"], ["K0"], lambda e, k=k: e.transpose(
                        out=K0[:, k * 128:(k + 1) * 128], in_=xb[:, k * 128:(k + 1) * 128], identity=idb[:]))
                A("dve", ["K0"], ["QTa"], lambda e: e.tensor_copy(out=QTa[:].rearrange("p k t -> p (k t)"), in_=K0[:]))
                def up_round(rnd):
                    ub = (rnd % 2) * 8
                    ku = "uT%d" % (rnd % 2)
                    for j in range(8):
                        f = rnd * 8 + j
                        bank = 1 + (f % 4)
                        kb = "K%d" % bank
                        R, kr = PTb[f % 2], "PTb%d" % (f % 2)
                        for k in range(8):
                            A("pe", ["QTa", "WUD"], [kb], lambda e, f=f, k=k, bank=bank: e.matmul(
                                K[bank][:, 0:128], lhsT=WU[:, k, f * 128:(f + 1) * 128], rhs=QTa[:, k, :],
                                start=(k == 0), stop=(k == 7)))
                        A("act", [kb], [kr], lambda e, bank=bank, R=R: e.activation(out=R[:, 0:128], in_=K[bank][:, 0:128],
                                                                                    func=AF.Relu))
                        A("dve", [kr], [ku], lambda e, j=j, ub=ub, R=R: e.tensor_tensor(
                            out=uT[:, ub + j, :], in0=R[:, 0:128], in1=R[:, 0:128], op=ALU.mult))

                def down_round(rnd):
                    ub = (rnd % 2) * 8
                    ku = "uT%d" % (rnd % 2)
                    for n in range(2):
                        bank = 5 + n
                        kb = "K%d" % bank
                        for j in range(8):
                            f = rnd * 8 + j
                            A("pe", [ku, "WUD"], [kb], lambda e, f=f, j=j, ub=ub, n=n, bank=bank: e.matmul(
                                K[bank][:], lhsT=uT[:, ub + j, :], rhs=WD[:, f, n * 512:(n + 1) * 512],
                                start=(f == 0), stop=(f == 31)))

                up_round(0)
                up_round(1)
                down_round(0)
                up_round(2)
                down_round(1)
                up_round(3)
                down_round(2)
                down_round(3)
                for n in range(2):
                    bank = 5 + n
                    kb = "K%d" % bank
                    A("dve", [kb, kx], ["H"], lambda e, n=n, bank=bank: e.tensor_tensor(
                        out=H[:, n * 512:(n + 1) * 512], in0=K[bank][:], in1=X[:, n * 512:(n + 1) * 512], op=ALU.add))
                A("pool", ["H"], [ky], lambda e: e.dma_start(out=ydst, in_=H[:]), dma=True, final=True)

            for i_ in range(NOWN):
                mlp_tile(yp[i_ * 128:(i_ + 1) * 128, :], yp[i_ * 128:(i_ + 1) * 128, :], "yp%d" % i_, i_)
            if STAGE >= 8:
                mlp_tile(ys[:, :], ys[:, :], "ys", NOWN)
        S.emit()
    return nc


def make_sample_consts():
    f32 = np.float32
    slope = 2.0 ** (-(np.arange(8) + 1.0))
    csts = np.zeros((128, 74), f32)
    for hh in range(4):
        for t in range(8):
            csts[hh * 8 + t, t] = 1.0
    keep = np.ones(33, f32)
    add = np.zeros(33, f32)
    keep[0] = keep[32] = 0.0
    add[0] = add[32] = 5.0
    csts[:, 8:41] = keep[None]
    csts[:, 41:74] = add[None]
    iota = np.arange(128, dtype=f32).reshape(128, 1)
    sq = np.zeros((8, 64), f32)
    for h in range(8):
        for t in range(8):
            sq[0, h * 8 + t] = 128.0 * slope[h]
            sq[1, h * 8 + t] = slope[h]
            sq[2, h * 8 + t] = -128.0 * 16.0 * slope[h]
            sq[3, h * 8 + t] = -slope[h] * t
    j = np.arange(32)
    sq[4, 0:32] = j // 2
    sq[5, 0:32] = 64 * (j % 2) + 63
    sq[6, 0:32] = 1.0
    sq[7, 0:32] = 1.0
    sm = np.zeros((128, 192), f32)
    r = np.arange(128)
    for par in range(2):
        for jj in range(64):
            sm[2 * jj + par, par * 64 + jj] = 1.0
    kk = r[:, None]
    tt = np.arange(8)[None, :]
    mn = np.where(kk <= tt, 0.0, -BIG).astype(f32)
    mw = np.where(kk > tt, 0.0, -BIG).astype(f32)
    sm[:, 128:160] = np.tile(mn, (1, 4))
    sm[:, 160:192] = np.tile(mw, (1, 4))
    return csts, iota, sq, sm


def make_consts(half):
    f32 = np.float32
    t = np.arange(128)
    slope = 2.0 ** (-(np.arange(8) + 1.0))
    cst = np.zeros((128, 747), f32)
    ac = np.zeros((128, 8, 66), f32)
    ac[:] = (slope[:, None] * 64.0 * np.arange(66)[None, :])[None]
    if half == 1:
        ac[:, :, 0:2] -= BIG
    cst[:, 0:528] = ac.reshape(128, 528)
    cst[:, 528] = np.where(t >= 63, 0.0, -BIG)
    cst[:, 529] = np.where(t == 127, 0.0, -BIG)
    rv = np.ones((128, NOWN), f32)
    if half == 0:
        rv[:63, 0] = 0.0
    cst[:, 530:547] = rv
    keep = np.ones(66, f32)
    add = np.zeros(66, f32)
    j0 = 2 * half
    keep[j0], add[j0] = 0.0, 5.0
    if half == 1:
        keep[0:2], add[0:2] = 0.0, -1.0
    cst[:, 547:613] = keep[None]
    cst[:, 613:679] = add[None]
    cst[:, 679] = np.where(t < 64, 5.0, 0.0)
    cst[:, 680] = np.where(t < 64, 0.0, 1.0)
    cst[:, 681] = np.where(t < 64, -1.0, 5.0)
    s_ = t % 64
    cst[:, 682:746] = (s_[:, None] <= np.arange(64)[None, :]).astype(f32)
    kk, tt = t[:, None], t[None, :]
    tri = np.where(kk <= tt, 0.0, -BIG).astype(f32)
    anti = np.where(kk > tt, 0.0, -BIG).astype(f32)
    full = np.zeros((128, 128), f32) if half == 0 else np.full((128, 128), -BIG, f32)
    m0anti = anti if half == 0 else np.full((128, 128), -BIG, f32)
    masks = np.stack([np.tile(m, (1, 4)) for m in (tri, anti, full, m0anti)], axis=1).astype(f32)
    eexp = np.zeros((66, NT * 128), f32)
    n = np.arange(NT * 128)
    eexp[n // 64, n] = 1.0
    kpos = np.stack([(n // 128).astype(f32), (n % 128).astype(f32), np.ones_like(n, f32), np.ones_like(n, f32)], 0)
    qpos = np.zeros((NOWN, 4, 2, 4, 128), f32)
    for i in range(NOWN):
        v = 2 * i
        for g in range(2):
            for hh in range(4):
                s = slope[g * 4 + hh]
                qpos[i, 0, g, hh, :] = 128.0 * s
                qpos[i, 1, g, hh, :] = s
                qpos[i, 2, g, hh, :] = -128.0 * v * s
                qpos[i, 3, g, hh, :] = -s * t
    return cst, masks, eexp, kpos.astype(f32), qpos.reshape(NOWN, 4, 1024)


_NC = None


def kernel(x_prompt, x_sample, cache_kv, cache_win, state_rnn, page_table, ln_mix, w_in, q_norm, k_norm,
           cmp_pe, cmp_w1, cmp_w2, attn_out_norm, rnn_lb_logits, rnn_out_norm, w_out, ln_mlp, w_up, w_down,
           _debug=False):
    global _NC
    f32 = np.float32
    x_prompt = np.asarray(x_prompt, f32)
    x_sample = np.asarray(x_sample, f32)
    cache_win = np.asarray(cache_win, f32)
    state_rnn = np.asarray(state_rnn, f32)
    k_norm = np.asarray(k_norm, f32)
    cmp_pe = np.asarray(cmp_pe, f32)
    if _NC is None or _NC[0] != _debug:
        _NC = (_debug, build_nc(_debug))
    nc = _NC[1]
    t = np.arange(128)

    def tmats(c):
        same = (t[:, None] // c == t[None, :] // c)
        return ((t[:, None] <= t[None, :]) & same).astype(f32), ((t[:, None] > t[None, :]) & same).astype(f32)

    tc_p, tr_p = tmats(64)
    tc_s, tr_s = tmats(8)
    rowmask = (t[:, None] // 8 == np.arange(16)[None, :]).astype(f32)
    gv = np.concatenate([k_norm[0, 0], k_norm[0, 1], k_norm[0, 2], np.asarray(q_norm, f32)[0],
                         np.asarray(attn_out_norm, f32)[0], np.asarray(rnn_out_norm, f32)[0]])
    peT = np.ascontiguousarray(cmp_pe[0].reshape(2, 32, 2, 64).transpose(2, 3, 0, 1).reshape(128, 2, 32))
    common = {
        "w_in": np.ascontiguousarray(np.asarray(w_in, f32)[0]),
        "w_out": np.ascontiguousarray(np.asarray(w_out, f32)[0]),
        "lncol": np.ascontiguousarray(np.asarray(ln_mix, f32)[0].reshape(8, 128).T),
        "gvecd": np.ascontiguousarray(np.broadcast_to(gv[None], (128, 1280))),
        "lbl": np.ascontiguousarray(np.broadcast_to(np.asarray(rnn_lb_logits, f32)[None], (128, 2, 512))),
        "ident": np.eye(128, dtype=f32),
        "w_up": np.ascontiguousarray(np.asarray(w_up, f32)[0]),
        "w_down": np.ascontiguousarray(np.asarray(w_down, f32)[0]),
        "lnmcol": np.ascontiguousarray(np.asarray(ln_mlp, f32)[0].reshape(8, 128).T),
        "tmat": np.ascontiguousarray(np.stack([tc_p, tr_p, tc_s, tr_s], axis=1)),
        "rowmask": rowmask,
        "peTd": peT,
        "cmp_w1": np.ascontiguousarray(np.asarray(cmp_w1, f32)[0]),
        "cmp_w2": np.ascontiguousarray(np.asarray(cmp_w2, f32)[0].transpose(1, 0, 2)),
    }
    zt = np.zeros((128, D), f32)
    cc = [make_consts(0), make_consts(1)]
    csts, iota, sqd, smd = make_sample_consts()
    cache2 = np.ascontiguousarray(np.asarray(cache_kv, f32)[0].reshape(2560 * 128, 512))
    ptab = np.asarray(page_table).astype(np.int32)
    common.update({"cache": cache2, "cstsd": csts, "iotad": iota, "sqd": sqd, "smd": smd})
    in_maps = []
    for c in range(8):
        s, half = c // 2, c % 2
        xvv = np.concatenate([x_prompt[s], zt], 0) if half == 0 else np.concatenate([zt, x_prompt[s]], 0)
        m = dict(common)
        m["xv"] = np.ascontiguousarray(xvv)
        m["xsm"] = np.ascontiguousarray(x_sample[16 * c:16 * c + 16].reshape(128, D))
        m["cwin"] = np.ascontiguousarray(cache_win[0, 16 * c:16 * c + 16].reshape(16, 512, 256))
        m["srnn"] = np.ascontiguousarray(state_rnn[0, 16 * c:16 * c + 16])
        m["ptrep"] = np.ascontiguousarray(np.broadcast_to(ptab[16 * c:16 * c + 16].reshape(1, 256), (128, 256)))
        m["cstd"], m["masks4"], m["eexp"], m["kposrows"], m["qposrows"] = cc[half]
        in_maps.append(m)
    res = run_bass_kernel_spmd(nc, in_maps, core_ids=list(range(8))).results

    y_prompt = np.zeros((4, 4096, D), f32)
    y_sample = np.zeros((128, 8, D), f32)
    kv_prompt = np.zeros((1, 4, 4096, 4, 2, 64), f32)
    kv_sample = np.zeros((1, 128, 8, 4, 2, 64), f32)
    win_prompt = np.zeros((1, 4, 512, 2, 2, 64), f32)
    win_sample = np.zeros((1, 128, 512, 2, 2, 64), f32)
    rnn_prompt = np.zeros((1, 4, 8, 64, 64), f32)
    rnn_sample = np.zeros((1, 128, 8, 64, 64), f32)
    for c in range(8):
        r = res[c]
        s, half = c // 2, c % 2
        ypc = r["yp"].reshape(NOWN, 128, D)
        yps = y_prompt[s].reshape(32, 128, D)
        if half == 0:
            yps[0::2] = ypc[0:16]
            kv_prompt[0, s] = r["kvp"][:4096].reshape(4096, 4, 2, 64)
            win_prompt[0, s] = r["winp"][:512].reshape(512, 2, 2, 64)
        else:
            yps[1::2] = ypc[1:17]
            rnn_prompt[0, s] = r["rnnp"]
        kv_sample[0, 16 * c:16 * c + 16] = r["kvs"].reshape(16, 8, 4, 2, 64)
        y_sample[16 * c:16 * c + 16] = r["ys"].reshape(16, 8, D)
        win_sample[0, 16 * c:16 * c + 16] = r["wins"].reshape(16, 512, 2, 2, 64)
        rnn_sample[0, 16 * c:16 * c + 16] = r["rnns"]
    if _debug:
        kernel._dbg = [res[c]["dbg"] for c in range(8)]
    return (y_prompt, y_sample, kv_prompt, kv_sample, win_prompt, win_sample, rnn_prompt, rnn_sample)
```

```python
import contextlib
import numpy as np
import concourse.bass as bass
import concourse.mybir as mybir
from concourse.bass_utils import run_bass_kernel_spmd

F32 = mybir.dt.float32
BF16 = mybir.dt.bfloat16
AF = mybir.ActivationFunctionType
ALU = mybir.AluOpType
AX = mybir.AxisListType

NT = 33
D = 1024
INW = 3352
C_Q, C_KV, C_G, C_RQ, C_RF, C_RI, C_RG = 0, 512, 1280, 1304, 1816, 2328, 2840
EPS = 1e-6

ENGS = ("pe", "act", "dve", "pool", "sp")
DMA_POOL = 8


class Op:
    __slots__ = ("eng", "fn", "deps", "dma", "idx", "needed", "token")

    def __init__(self, eng, fn, deps, dma, idx):
        self.eng, self.fn, self.deps, self.dma, self.idx = eng, fn, deps, dma, idx
        self.needed = False
        self.token = None


class Sched:
    def __init__(self, nc):
        self.nc = nc
        self.ops = []
        self.last_w = {}
        self.readers = {}
        self.final = []

    def add(self, eng, fn, r=(), w=(), dma=False, final=False, rg=0):
        idx = len(self.ops)
        deps = {}
        if eng == "pe":
            prev = getattr(self, "prev_pe", None)
            if prev is not None and prev[1] != rg:
                deps[prev[0]] = "force"
            self.prev_pe = (idx, rg)
        for k in r:
            lw = self.last_w.get(k)
            if lw is not None and deps.get(lw) != "force":
                deps[lw] = True
        for k in w:
            lw = self.last_w.get(k)
            if lw is not None and deps.get(lw) != "force":
                deps[lw] = True
            for rd in self.readers.get(k, ()):
                if rd not in deps:
                    deps[rd] = False
        for k in r:
            self.readers.setdefault(k, []).append(idx)
        for k in w:
            self.last_w[k] = idx
            self.readers[k] = []
        self.ops.append(Op(eng, fn, deps, dma, idx))
        if final:
            self.final.append(idx)
        return idx

    def emit(self):
        nc = self.nc
        ops = self.ops
        for op in ops:
            nd = {}
            for d, strong in op.deps.items():
                p = ops[d]
                if p.eng == op.eng and not p.dma:
                    if op.eng == "pe" and strong != "force":
                        continue
                nd[d] = strong
            best = {}
            keep = {}
            for d, strong in nd.items():
                p = ops[d]
                if p.dma:
                    keep[d] = strong
                elif p.eng not in best or d > best[p.eng]:
                    best[p.eng] = d
            for d in best.values():
                keep[d] = True
            op.deps = keep
            for d in keep:
                ops[d].needed = True
        for f in self.final:
            ops[f].needed = True
        cnt = {e: 0 for e in ENGS}
        dcnt = {e: 0 for e in ENGS}
        with contextlib.ExitStack() as es:
            NEP = 8
            EPOCH = 1500
            csem = {e: [es.enter_context(nc.semaphore("c_%s%d" % (e, i))) for i in range(NEP if e in ("pe", "dve", "act") else 1)]
                    for e in ENGS}
            dsem = {e: [es.enter_context(nc.semaphore("d_%s%d" % (e, i))) for i in range(DMA_POOL)]
                    for e in ("sp", "pool", "act")}
            for op in ops:
                if op.dma:
                    j = dcnt[op.eng]
                    dcnt[op.eng] += 1
                    op.token = (dsem[op.eng][j % DMA_POOL], 16 * (j // DMA_POOL + 1))
                elif op.needed:
                    ep = cnt[op.eng] // EPOCH
                    assert ep < len(csem[op.eng]), (op.eng, cnt[op.eng])
                    op.token = (csem[op.eng][ep], cnt[op.eng] % EPOCH + 1)
                    cnt[op.eng] += 1
            per = {e: [op for op in ops if op.eng == e] for e in ENGS}
            final_tokens = [ops[f].token for f in self.final]

            def run(engname, eng):
                known = {}
                for op in per[engname]:
                    for d in op.deps:
                        sem, val = ops[d].token
                        if known.get(id(sem), 0) >= val:
                            continue
                        eng.wait_ge(sem, val)
                        known[id(sem)] = val
                    if op.dma:
                        sem, val = op.token
                        if val > 16 and known.get(id(sem), 0) < val - 16:
                            eng.wait_ge(sem, val - 16)
                            known[id(sem)] = val - 16
                        op.fn(eng).then_inc(sem, 16)
                    else:
                        ins = op.fn(eng)
                        if op.needed:
                            ins.then_inc(op.token[0], 1)
                if engname == "sp":
                    for sem, val in final_tokens:
                        if known.get(id(sem), 0) >= val:
                            continue
                        eng.wait_ge(sem, val)
                        known[id(sem)] = val

            with nc.Block() as block:
                @block.tensor
                def _(e):
                    run("pe", e)

                @block.scalar
                def _(e):
                    run("act", e)

                @block.vector
                def _(e):
                    run("dve", e)

                @block.gpsimd
                def _(e):
                    run("pool", e)

                @block.sync
                def _(e):
                    run("sp", e)


import os
BIG = 30000.0
STAGE = int(os.environ.get("KSTAGE", "9"))
SUB = int(os.environ.get("KSUB", "9"))
SKIP = os.environ.get("KSKIP", "")
NOWN = 17
SCALE = 0.125
DEBUG = False


def build_nc(debug=False):
    nc = bass.Bass("TRN2", target_bir_lowering=False)

    def din(name, shape, dt=F32):
        return nc.dram_tensor(name, list(shape), dt, kind="ExternalInput").ap()

    def dout(name, shape, dt=F32):
        return nc.dram_tensor(name, list(shape), dt, kind="ExternalOutput").ap()

    xv = din("xv", [NT * 128, D])
    xsm = din("xsm", [128, D])
    w_in = din("w_in", [D, INW])
    w_out = din("w_out", [D, D])
    w_up = din("w_up", [D, 4 * D])
    w_down = din("w_down", [4 * D, D])
    lnmcol = din("lnmcol", [128, 8])
    lncol = din("lncol", [128, 8])
    gvec_d = din("gvecd", [128, 1280])
    lbl = din("lbl", [128, 2, 512])
    ident = din("ident", [128, 128])
    tmat = din("tmat", [128, 4, 128])
    rowmask = din("rowmask", [128, 16])
    cst_d = din("cstd", [128, 747])
    masks_d = din("masks4", [128, 4, 512])
    eexp_d = din("eexp", [66, NT * 128])
    kpos_d = din("kposrows", [4, NT * 128])
    qpos_d = din("qposrows", [NOWN, 4, 1024])
    pet_d = din("peTd", [128, 2, 32])
    w1_d = din("cmp_w1", [2, 4096, 128])
    w2_d = din("cmp_w2", [128, 2, 64])
    cwin = din("cwin", [16, 512, 256])
    cache_d = din("cache", [327680, 512])
    ptrep_d = din("ptrep", [128, 256], mybir.dt.int32)
    csts_d = din("cstsd", [128, 74])
    iota_d = din("iotad", [128, 1])
    sq_d = din("sqd", [8, 64])
    sm_d = din("smd", [128, 192])
    srnn = din("srnn", [16, 8, 64, 64])

    yp = dout("yp", [NOWN * 128, D])
    ys = dout("ys", [128, D])
    kvp = dout("kvp", [NT * 128, 512])
    kvs = dout("kvs", [128, 512])
    winp = dout("winp", [5 * 128, 256])
    wins = dout("wins", [16, 512, 256])
    rnnp = dout("rnnp", [8, 64, 64])
    rnns = dout("rnns", [16, 8, 64, 64])
    if debug:
        dbg = dout("dbg", [(NOWN + 1) * 128, 5, 520])

    S = Sched(nc)
    with contextlib.ExitStack() as es:
        def sb(name, shape, dt=F32):
            return es.enter_context(nc.sbuf_tensor(name, list(shape), dt))

        def ps(name, shape, dt=F32):
            return es.enter_context(nc.psum_tensor(name, list(shape), dt))

        def A(eng, r, w, fn, **kw):
            S.add(eng, fn, r=r, w=w, **kw)

        BIGW = sb("BIGW", [128, 65536], BF16)
        W = BIGW[:, 0:26816].rearrange("p (k n) -> p k n", k=8)
        ARENA = BIGW[:, 26816:47808]
        WU = BIGW[:, 0:32768].rearrange("p (k n) -> p k n", k=8)
        WD = BIGW[:, 32768:65536].rearrange("p (f n) -> p f n", f=32)
        _bo = [47808]

        def bw(n, parts=128):
            a = _bo[0]
            _bo[0] += n
            assert _bo[0] <= 65536
            return BIGW[0:parts, a:a + n]
        XcT = ARENA[:, 0:8448].rearrange("p (a n) -> p a n", a=4)
        W1b = ARENA[:, 8448:16640].rearrange("p (a j h) -> p a j h", a=2, j=32)
        KTs = ARENA[:, 0:8448].rearrange("p (g n) -> p g n", g=2)
        VVs = ARENA[:, 8448:12738].rearrange("p (t g d) -> p t g d", t=NT, g=2)
        WO = ARENA[:, 12800:20992].rearrange("p (k n) -> p k n", k=8)
        AKEYS = ["XcT", "W1b", "KTs", "VVs", "WO"]
        KTw = sb("KTw", [128, 2, 768], BF16)
        VVw = sb("VVw", [128, 6, 2, 65], BF16)
        Eexp = bw(NT * 128, 128)
        MSK = bw(2048).rearrange("p (m n) -> p m n", m=4)
        cst = sb("cst", [128, 747])
        gvec = sb("gvec", [128, 1280])
        qgs = sb("qgs", [128, 64])
        lnc = sb("lnc", [128, 8])
        lbb = sb("lbb", [128, 512])
        omlb = sb("omlb", [128, 512])
        idb = sb("idb", [128, 128], BF16)
        tm = sb("tm", [128, 4, 128])
        rmask = sb("rmask", [128, 16])
        ones = sb("ones", [128, 1])
        peT = sb("peT", [128, 2, 32])
        W2b = sb("W2b", [128, 2, 64], BF16)
        kcT = sb("kcT", [64, 2, 66], BF16)
        vcb = sb("vcb", [66, 2, 64], BF16)
        fdummy = sb("fdummy", [128, 1])
        acore = cst[:, 0:528].rearrange("p (h j) -> p h j", h=8)
        fix2 = cst[:, 528:530]
        rowvalid = cst[:, 530:547]
        keepc = cst[:, 547:613]
        addc = cst[:, 613:679]
        mlo5 = cst[:, 679:680]
        mhi = cst[:, 680:681]
        c1col = cst[:, 681:682]
        tri2 = cst[:, 682:746]
        xs = [sb("xs%d" % i, [128, D]) for i in range(2)]
        xb = sb("xb", [128, D], BF16)
        xT = [bw(1024).rearrange("p (k t) -> p k t", k=8) for i in range(2)]
        ss = sb("ss", [128, 1])
        rstd = sb("rstd", [128, 1])
        Z = [sb("Z0", [128, 768])] * 2
        zb = sb("zb", [128, 4, 64], BF16)
        sq4 = sb("sq4", [128, 2, 2, 64])
        ms4 = sb("ms4", [128, 2, 2])
        T = [sb("T%d" % i, [128, 512]) for i in range(6)]
        QS = sb("QS", [128, 512])
        RGS = sb("RGS", [128, 512])
        KD = bw(512)
        VB = bw(512)
        QD = bw(512)
        KDI = bw(512)
        qkT = bw(1024).rearrange("p (a t) -> p a t", a=8)
        AT = bw(512).rearrange("p (h t) -> p h t", h=8)
        Sb = [bw(256).rearrange("p (a d) -> p a d", a=4) for i in range(2)]
        ET = sb("ET", [128, 2, 4])
        St = sb("St", [128, 4, 64])
        ZQ = sb("ZQ", [128, 8, 64])
        QN = bw(512).rearrange("p (h d) -> p h d", h=8)
        st8 = sb("st8", [128, 8])
        QTa = sb("QTa", [128, 8, 128], BF16)
        gsig = sb("gsig", [128, 8, 3])
        LG = sb("LG", [128, 4, 66])
        EX = sb("EX", [128, 4, 66])
        PB = bw(264).rearrange("p (h j) -> p h j", h=4)
        st4 = sb("st4", [128, 4])
        st4b = sb("st4b", [128, 4])
        SC = sb("SC", [128, 66])
        WK = sb("WK", [128, 66])
        SEL = sb("SEL", [128, 66])
        m8 = sb("m8", [128, 8])
        thr = sb("thr", [128, 1])
        MBF = sb("MBF", [128, 66], BF16)
        RHSM_ = [bw(512, 128).rearrange("p (h t) -> p h t", h=4) for i in range(2)]
        PT4 = bw(512, 66).rearrange("p (h t) -> p h t", h=4)
        PTb = [sb("PTb%d" % i, [128, 512], BF16) for i in range(3)]
        OSEL = sb("OSEL", [128, 8, 65])
        OWIN = sb("OWIN", [128, 8, 65])
        OCMP = sb("OCMP", [128, 8, 64])
        OR32 = sb("OR32", [128, 8, 64])
        g8 = sb("g8", [128, 8, 3])
        OC = bw(1024)
        OCT = bw(1024).rearrange("p (k t) -> p k t", k=8)
        uT = sb("uT", [128, 16, 128], BF16)
        lnmc = sb("lnmc", [128, 8])
        H = sb("H", [128, D])
        stgq = H
        TMP = T[5][:].rearrange("p (h d) -> p h d", h=8)
        ACC = T[4][:].rearrange("p (h d) -> p h d", h=8)
        junk = OCT
        hs = bw(264).rearrange("p (h j) -> p h j", h=4)
        kcr = sb("kcr", [66, 2, 64])
        kcbf = sb("kcbf", [66, 2, 64], BF16)
        S0 = ARENA[:, 0:8192].bitcast(F32).rearrange("p (a d) -> p a d", a=64)
        etots = sb("etots", [128, 4, 16])
        kdm = PTb

        K0 = ps("K0", [128, 1024], BF16)
        K = [None] + [ps("K%d" % i, [128, 512]) for i in range(1, 8)]

        def ld(dst, src, key, eng="sp"):
            S.add(eng, lambda e: e.dma_start(out=dst, in_=src), w=[key], dma=True)

        ld(lnc[:], lncol, "lnc")
        ld(gvec[:], gvec_d, "gvec")
        ld(cst[:], cst_d, "cst")
        ld(tm[:], tmat, "tm")
        ld(rmask[:], rowmask, "rmask")
        ld(peT[:], pet_d, "peT")
        ld(T[0][:], lbl[:, 0, :], "T0")
        ld(T[1][:], lbl[:, 1, :], "T1")
        ld(T[2][:, 0:128], ident, "T2")
        ld(T[3][:, 0:128], w2_d.rearrange("p a d -> p (a d)"), "T3")
        A("dve", ["T2"], ["idb"], lambda e: e.tensor_copy(out=idb[:], in_=T[2][:, 0:128]))
        A("dve", ["T3"], ["W2b"], lambda e: e.tensor_copy(out=W2b[:].rearrange("p a d -> p (a d)"), in_=T[3][:, 0:128]))
        A("dve", [], ["ones"], lambda e: e.memset(ones[:], 1.0))
        A("dve", [], ["St"], lambda e: e.memset(St[:], 0.0))
        A("dve", ["gvec"], ["qgs"], lambda e: e.tensor_scalar_mul(out=qgs[:], in0=gvec[:, 192:256], scalar1=SCALE))
        A("dve", ["T0", "T1"], ["lbb"], lambda e: e.tensor_sub(out=lbb[:], in0=T[1][:], in1=T[0][:]))
        A("act", ["lbb"], ["lbb"], lambda e: e.activation(out=lbb[:], in_=lbb[:], func=AF.Exp))
        A("dve", ["lbb"], ["lbb"], lambda e: e.tensor_scalar_add(out=lbb[:], in0=lbb[:], scalar1=1.0))
        A("dve", ["lbb"], ["lbb"], lambda e: e.reciprocal(out=lbb[:], in_=lbb[:]))
        A("dve", ["lbb"], ["omlb"], lambda e: e.tensor_scalar(out=omlb[:], in0=lbb[:], scalar1=-1.0, scalar2=1.0,
                                                              op0=ALU.mult, op1=ALU.add))

        wstg = [(H, "H"), (xs[0], "xs0"), (xs[1], "xs1")]
        wi_ = 0
        for k in range(8):
            for hf in range(4):
                c0, c1 = hf * 838, (hf + 1) * 838
                st_, ks_ = wstg[wi_ % 3]
                A("sp", [], [ks_], lambda e, k=k, c0=c0, c1=c1, st_=st_: e.dma_start(
                    out=st_[:, 0:838], in_=w_in[k * 128:(k + 1) * 128, c0:c1]), dma=True)
                if wi_ % 2 == 0:
                    A("act", [ks_, "lnc"], ["W%d" % k], lambda e, k=k, c0=c0, c1=c1, st_=st_: e.activation(
                        out=W[:, k, c0:c1], in_=st_[:, 0:838], func=AF.Copy, scale=lnc[:, k:k + 1]))
                else:
                    A("dve", [ks_, "lnc"], ["W%d" % k], lambda e, k=k, c0=c0, c1=c1, st_=st_: e.tensor_scalar(
                        out=W[:, k, c0:c1], in0=st_[:, 0:838], scalar1=lnc[:, k:k + 1], scalar2=None, op0=ALU.mult))
                wi_ += 1
        WK_ = ["W%d" % k for k in range(8)]

        def prep_x(xsrc, p):
            X, XT = xs[p], xT[p]
            kx, kxt = "xs%d" % p, "xT%d" % p
            A("sp", [], [kx], lambda e: e.dma_start(out=X[:], in_=xsrc), dma=True)
            A("act", [kx], ["OCT", "ss"], lambda e: e.activation(out=OCT[:].rearrange("p k t -> p (k t)"), in_=X[:], func=AF.Square, accum_out=ss[:]))
            A("act", ["ss"], ["rstd"], lambda e: e.activation(out=rstd[:], in_=ss[:], func=AF.Ln, scale=1.0 / D, bias=EPS))
            A("act", ["rstd"], ["rstd"], lambda e: e.activation(out=rstd[:], in_=rstd[:], func=AF.Exp, scale=-0.5))
            A("dve", [kx, "rstd"], ["xb"], lambda e: e.tensor_scalar(out=xb[:], in0=X[:], scalar1=rstd[:, 0:1],
                                                                     scalar2=None, op0=ALU.mult))
            for k in range(8):
                A("pe", ["xb", "idb"], ["K0"], lambda e, k=k: e.transpose(
                    out=K0[:, k * 128:(k + 1) * 128], in_=xb[:, k * 128:(k + 1) * 128], identity=idb[:]))
            A("dve", ["K0"], [kxt], lambda e: e.tensor_copy(out=XT[:].rearrange("p k t -> p (k t)"), in_=K0[:]))
            return X, XT, kx, kxt

        def mm(XT, kxt, bank, n, c0):
            for k in range(8):
                A("pe", [kxt] + WK_, ["K%d" % bank], lambda e, k=k: e.matmul(
                    K[bank][:, 0:n], lhsT=XT[:, k, :], rhs=W[:, k, c0:c0 + n], start=(k == 0), stop=(k == 7)))

        A("dve", [], AKEYS + ["fdummy"], lambda e: e.memset(fdummy[:], 0.0))
        for typ in range(2):
            for jq in range(4):
                A("sp", [], ["H"], lambda e, typ=typ, jq=jq: e.dma_start(
                    out=stgq[:].rearrange("p (j h) -> p j h", j=8),
                    in_=w1_d[typ, jq * 1024:(jq + 1) * 1024, :].rearrange("(j p) h -> p j h", p=128)), dma=True)
                A("dve", ["H"], ["W1b"], lambda e, typ=typ, jq=jq: e.tensor_copy(
                    out=W1b[:, typ, jq * 8:(jq + 1) * 8, :], in_=stgq[:].rearrange("p (j h) -> p j h", j=8)))
        def tile_1a(v):
            p = v % 2
            X, XT, kx, kxt = prep_x(xv[v * 128:(v + 1) * 128, :], p)
            K1v = K[1][:, 0:256].rearrange("p (a t) -> p a t", a=4)
            for tg in range(4):
                c0 = C_KV + tg * 64
                for par in range(2):
                    for k in range(8):
                        A("pe", [kxt] + WK_, ["K1"], lambda e, tg=tg, par=par, k=k, c0=c0: e.matmul(
                            K1v[64 * par:64 * par + 64, tg, :], lhsT=W[:, k, c0:c0 + 64], rhs=XT[:, k, par:128:2],
                            start=(k == 0), stop=(k == 7)))
            for a in range(2):
                A("dve", ["K1", "peT"], ["XcT"], lambda e, v=v, K1v=K1v, a=a: e.tensor_tensor(
                    out=XcT[:, 2 * a:2 * a + 2, v * 64:(v + 1) * 64].rearrange("p g (b j) -> p g b j", b=2),
                    in0=K1v[:, 2 * a:2 * a + 2, :].rearrange("p g (b j) -> p g b j", b=2),
                    in1=peT[:, a, :].unsqueeze(1).unsqueeze(1).to_broadcast([128, 2, 2, 32]), op=ALU.add))
        for v_ in range(NT if STAGE >= 1 else 0):
            tile_1a(v_)
        K3v = K[3][:, 0:256].rearrange("p (a d) -> p a d", a=4)
        for tg in range(4):
            typ = tg // 2
            for j in range(32):
                A("pe", ["XcT", "W1b"], ["K2"], lambda e, tg=tg, typ=typ, j=j: e.matmul(
                    K[2][:, 0:66], lhsT=W1b[:, typ, j, :], rhs=XcT[:, tg, j:2112:32], start=(j == 0), stop=(j == 31)))
            A("act", ["K2"], ["T4"], lambda e: e.activation(out=T[4][:, 0:66], in_=K[2][:, 0:66], func=AF.Exp, scale=-1.0))
            A("dve", ["T4"], ["T4"], lambda e: e.tensor_scalar_add(out=T[4][:, 0:66], in0=T[4][:, 0:66], scalar1=1.0))
            A("dve", ["T4"], ["T4"], lambda e: e.reciprocal(out=T[4][:, 0:66], in_=T[4][:, 0:66]))
            A("dve", ["T4", "K2"], ["hs"], lambda e, tg=tg: e.tensor_tensor(out=hs[:, tg, :], in0=K[2][:, 0:66],
                                                                           in1=T[4][:, 0:66], op=ALU.mult))
            A("pe", ["hs", "W2b"], ["K3"], lambda e, tg=tg, typ=typ: e.matmul(
                K3v[0:66, tg, :], lhsT=hs[:, tg, :], rhs=W2b[:, typ, :], start=True, stop=True))
        A("act", ["K3"], ["kcr"], lambda e: e.activation(out=kcr[:], in_=K3v[0:66, 0:2, :], func=AF.Copy))
        A("act", ["K3"], ["vcb"], lambda e: e.activation(out=vcb[:], in_=K3v[0:66, 2:4, :], func=AF.Copy))
        A("dve", ["kcr"], ["T5"], lambda e: e.tensor_tensor(out=T[5][0:66, 0:128].rearrange("p (g d) -> p g d", g=2),
                                                            in0=kcr[:], in1=kcr[:], op=ALU.mult))
        A("dve", ["T5"], ["st4"], lambda e: e.tensor_reduce(out=st4[0:66, 0:2],
                                                            in_=T[5][0:66, 0:128].rearrange("p (g d) -> p g d", g=2),
                                                            axis=AX.X, op=ALU.add))
        A("act", ["st4"], ["st4"], lambda e: e.activation(out=st4[0:66, 0:2], in_=st4[0:66, 0:2], func=AF.Ln,
                                                          scale=1.0 / 64, bias=EPS))
        A("act", ["st4"], ["st4"], lambda e: e.activation(out=st4[0:66, 0:2], in_=st4[0:66, 0:2], func=AF.Exp, scale=-0.5))
        A("dve", ["kcr", "st4"], ["kcr"], lambda e: e.tensor_tensor(
            out=kcr[:], in0=kcr[:], in1=st4[0:66, 0:2].unsqueeze(2).to_broadcast([66, 2, 64]), op=ALU.mult))
        A("dve", ["kcr", "gvec"], ["kcbf"], lambda e: e.tensor_tensor(
            out=kcbf[:], in0=kcr[:], in1=gvec[0:66, 0:64].unsqueeze(1).to_broadcast([66, 2, 64]), op=ALU.mult))
        for g in range(2):
            A("pe", ["kcbf", "idb"], ["K0"], lambda e, g=g: e.transpose(
                out=K0[0:64, g * 128:g * 128 + 66], in_=kcbf[:, g, :], identity=idb[0:66, 0:66]))
        A("dve", ["K0"], ["kcT"], lambda e: e.tensor_copy(
            out=kcT[:], in_=K0[0:64, 0:256].rearrange("p (g n) -> p g n", g=2)[:, :, 0:66]))
        A("dve", [], AKEYS + ["fdummy"], lambda e: e.memset(fdummy[:], 0.0))

        A("dve", [], ["KTs"], lambda e: e.memset(KTs[64:128, :, :], 0.0))
        sst = [(H, "H"), (xs[0], "xs0"), (xs[1], "xs1")]
        si_ = [0]

        def nst():
            t_ = sst[si_[0] % 3]
            si_[0] += 1
            return t_

        for c0 in range(0, NT * 128, 1024):
            n = min(1024, NT * 128 - c0)
            st_, ks_ = nst()
            A("sp", [], [ks_], lambda e, c0=c0, n=n, st_=st_: e.dma_start(out=st_[0:66, 0:n], in_=eexp_d[:, c0:c0 + n]), dma=True)
            A("dve", [], ["Eexp"], lambda e, c0=c0, n=n: e.memset(Eexp[64:128, c0:c0 + n], 0.0))
            A("dve", [ks_], ["Eexp"], lambda e, c0=c0, n=n, st_=st_: e.tensor_copy(out=Eexp[0:66, c0:c0 + n], in_=st_[0:66, 0:n]))
            st_, ks_ = nst()
            A("sp", [], [ks_], lambda e, c0=c0, n=n, st_=st_: e.dma_start(out=st_[64:68, 0:n], in_=kpos_d[:, c0:c0 + n]), dma=True)
            for g in range(2):
                A("dve", [ks_], ["KTs"], lambda e, c0=c0, n=n, g=g, st_=st_: e.tensor_copy(
                    out=KTs[64:68, g, c0:c0 + n], in_=st_[64:68, 0:n]))
        for m in range(4):
            st_, ks_ = nst()
            A("sp", [], [ks_], lambda e, m=m, st_=st_: e.dma_start(out=st_[:, 0:512], in_=masks_d[:, m, :]), dma=True)
            A("dve", [ks_], ["MSK"], lambda e, m=m, st_=st_: e.tensor_copy(out=MSK[:, m, :], in_=st_[:, 0:512]))
        for k in range(8):
            st_, ks_ = nst()
            A("sp", [], [ks_], lambda e, k=k, st_=st_: e.dma_start(out=st_[:], in_=w_out[k * 128:(k + 1) * 128, :]), dma=True)
            if k % 2 == 0:
                A("act", [ks_], ["WO"], lambda e, k=k, st_=st_: e.activation(out=WO[:, k, :], in_=st_[:], func=AF.Copy))
            else:
                A("dve", [ks_], ["WO"], lambda e, k=k, st_=st_: e.tensor_copy(out=WO[:, k, :], in_=st_[:]))
        A("dve", [], ["KTw"], lambda e: e.memset(KTw[64:128, :, :], 0.0))
        A("dve", [], ["QTa"], lambda e: e.memset(QTa[64:128, :, :], 0.0))
        for g_ in range(2):
            A("dve", [], ["RHSM%d" % g_], lambda e, g_=g_: e.memset(RHSM_[g_][64:128, :, :], 0.0))
        A("dve", [], ["VVs"], lambda e: e.memset(VVs[:, :, :, 64:65], 1.0))
        A("dve", [], ["VVw"], lambda e: e.memset(VVw[:, :, :, 64:65], 1.0))

        def rnn_gates(bf, bi, sample):
            kf, ki = "K%d" % bf, "K%d" % bi
            e1, uu, kk, G = T[0], T[1], T[2], T[3]
            A("act", [kf], ["T0"], lambda e: e.activation(out=e1[:], in_=K[bf][:], func=AF.Exp, scale=-1.0))
            A("act", [ki], ["VB"], lambda e: e.activation(out=VB[:], in_=K[bi][:], func=AF.Copy))
            A("dve", ["T0"], ["T0"], lambda e: e.tensor_scalar_add(out=e1[:], in0=e1[:], scalar1=1.0))
            A("dve", ["T0"], ["T0"], lambda e: e.reciprocal(out=e1[:], in_=e1[:]))
            A("dve", ["T0", "omlb"], ["T1"], lambda e: e.tensor_tensor(out=uu[:], in0=e1[:], in1=omlb[:], op=ALU.mult))
            A("dve", ["T1", "lbb"], ["T0"], lambda e: e.tensor_tensor(out=e1[:], in0=uu[:], in1=lbb[:], op=ALU.add))
            A("dve", ["T1", "omlb"], ["T2"], lambda e: e.tensor_tensor(out=kk[:], in0=omlb[:], in1=uu[:], op=ALU.subtract))
            A("act", ["T0"], ["T3"], lambda e: e.activation(out=G[:], in_=e1[:], func=AF.Ln))
            mi = 3 if sample else 1
            A("pe", ["tm", "T3"], ["K3"], lambda e: e.matmul(K[3][:], lhsT=tm[:, mi, :], rhs=G[:], start=True, stop=True))
            A("act", ["K3"], ["T4"], lambda e: e.activation(out=T[4][:], in_=K[3][:], func=AF.Exp))
            A("dve", ["T2", "T4"], ["KD"], lambda e: e.tensor_tensor(out=KD[:], in0=kk[:], in1=T[4][:], op=ALU.mult))

        pst = K[4][:, 0:256].rearrange("p (a d) -> p a d", a=4)
        ptot = K[4][:, 256:264].rearrange("p (c a) -> p c a", c=2)

        def state_tot():
            G = T[3]
            for c in range(2):
                for h in range(8):
                    hp, ee = h // 2, h % 2
                    A("pe", ["T3", "ones"], ["K4"], lambda e, c=c, h=h, hp=hp, ee=ee: e.matmul(
                        ptot[64 * ee:64 * ee + 64, c, hp:hp + 1], lhsT=G[c * 64:(c + 1) * 64, h * 64:(h + 1) * 64],
                        rhs=ones[c * 64:(c + 1) * 64, 0:1], start=True, stop=True), rg=64 * c)
            A("act", ["K4"], ["ET"], lambda e: e.activation(out=ET[:], in_=ptot, func=AF.Exp))

        def state_update(c):
            for h in range(8):
                hp, ee = h // 2, h % 2
                A("pe", ["KD", "VB"], ["K4"], lambda e, c=c, h=h, hp=hp, ee=ee: e.matmul(
                    pst[64 * ee:64 * ee + 64, hp, :], lhsT=KD[c * 64:(c + 1) * 64, h * 64:(h + 1) * 64],
                    rhs=VB[c * 64:(c + 1) * 64, h * 64:(h + 1) * 64], start=True, stop=True), rg=64 * c)
            A("dve", ["St", "ET"], ["St"], lambda e, c=c: e.tensor_tensor(
                out=St[:], in0=St[:], in1=ET[:, c, :].unsqueeze(2).to_broadcast([128, 4, 64]), op=ALU.mult))
            A("dve", ["St", "K4"], ["St"], lambda e: e.tensor_tensor(out=St[:], in0=St[:], in1=pst, op=ALU.add))

        def k_norms(Zt, kz):
            Z5 = Zt[:].rearrange("p (a b g d) -> p a b g d", a=3, b=2, g=2)
            KS = Z5[:, 1:3, 0, :, :]
            A("dve", [kz], ["sq4"], lambda e: e.tensor_tensor(out=sq4[:], in0=KS, in1=KS, op=ALU.mult))
            A("dve", ["sq4"], ["ms4"], lambda e: e.tensor_reduce(out=ms4[:], in_=sq4[:], axis=AX.X, op=ALU.add))
            A("act", ["ms4"], ["ms4"], lambda e: e.activation(out=ms4[:], in_=ms4[:], func=AF.Ln, scale=1.0 / 64, bias=EPS))
            A("act", ["ms4"], ["ms4"], lambda e: e.activation(out=ms4[:], in_=ms4[:], func=AF.Exp, scale=-0.5))
            A("dve", [kz, "ms4"], [kz], lambda e: e.tensor_tensor(
                out=KS, in0=KS, in1=ms4[:].unsqueeze(3).to_broadcast([128, 2, 2, 64]), op=ALU.mult))
            A("dve", [kz, "gvec"], [kz], lambda e: e.tensor_tensor(
                out=KS, in0=KS, in1=gvec[:, 64:192].rearrange("p (a d) -> p a d", a=2).unsqueeze(2).to_broadcast(
                    [128, 2, 2, 64]), op=ALU.mult))
            return Z5

        def silu_from(bank, dst, kdst):
            kb = "K%d" % bank
            A("act", [kb], ["T5"], lambda e: e.activation(out=T[5][:], in_=K[bank][:], func=AF.Exp, scale=-1.0))
            A("dve", ["T5"], ["T5"], lambda e: e.tensor_scalar_add(out=T[5][:], in0=T[5][:], scalar1=1.0))
            A("dve", ["T5"], ["T5"], lambda e: e.reciprocal(out=T[5][:], in_=T[5][:]))
            A("dve", ["T5", kb], [kdst], lambda e: e.tensor_tensor(out=dst[:], in0=K[bank][:], in1=T[5][:], op=ALU.mult))

        def head_rms(src3, ksrc, n):
            A("dve", [ksrc], ["T5"], lambda e: e.tensor_tensor(out=TMP[:, 0:n, :], in0=src3, in1=src3, op=ALU.mult))
            A("dve", ["T5"], ["st8"], lambda e: e.tensor_reduce(out=st8[:, 0:n], in_=TMP[:, 0:n, :], axis=AX.X, op=ALU.add))
            A("act", ["st8"], ["st8"], lambda e: e.activation(out=st8[:, 0:n], in_=st8[:, 0:n], func=AF.Ln,
                                                              scale=1.0 / 64, bias=EPS))
            A("act", ["st8"], ["st8"], lambda e: e.activation(out=st8[:, 0:n], in_=st8[:, 0:n], func=AF.Exp, scale=-0.5))

        def finish(X, kx, ydst, ykey, dbi):
            A("dve", ["OSEL"], ["st8"], lambda e: e.tensor_scalar_max(out=st8[:], in0=OSEL[:, :, 64], scalar1=1e-30))
            A("dve", ["st8"], ["st8"], lambda e: e.reciprocal(out=st8[:], in_=st8[:]))
            A("dve", ["st8", "gsig"], ["g8"], lambda e: e.tensor_tensor(out=g8[:, :, 1], in0=gsig[:, :, 1], in1=st8[:],
                                                                         op=ALU.mult))
            A("dve", ["OWIN"], ["st8"], lambda e: e.tensor_scalar_max(out=st8[:], in0=OWIN[:, :, 64], scalar1=1e-30))
            A("dve", ["st8"], ["st8"], lambda e: e.reciprocal(out=st8[:], in_=st8[:]))
            A("dve", ["st8", "gsig"], ["g8"], lambda e: e.tensor_tensor(out=g8[:, :, 2], in0=gsig[:, :, 2], in1=st8[:],
                                                                         op=ALU.mult))
            A("dve", ["OCMP", "gsig"], ["T4"], lambda e: e.tensor_tensor(
                out=ACC[:], in0=OCMP[:], in1=gsig[:, :, 0:1].to_broadcast([128, 8, 64]), op=ALU.mult))
            A("dve", ["OSEL", "g8"], ["T5"], lambda e: e.tensor_tensor(
                out=TMP[:], in0=OSEL[:, :, 0:64], in1=g8[:, :, 1:2].to_broadcast([128, 8, 64]), op=ALU.mult))
            A("dve", ["T4", "T5"], ["T4"], lambda e: e.tensor_tensor(out=ACC[:], in0=ACC[:], in1=TMP[:], op=ALU.add))
            A("dve", ["OWIN", "g8"], ["T5"], lambda e: e.tensor_tensor(
                out=TMP[:], in0=OWIN[:, :, 0:64], in1=g8[:, :, 2:3].to_broadcast([128, 8, 64]), op=ALU.mult))
            A("dve", ["T4", "T5"], ["T4"], lambda e: e.tensor_tensor(out=ACC[:], in0=ACC[:], in1=TMP[:], op=ALU.add))
            if debug and dbi is not None:
                for di, (src, kk_, n) in enumerate([(OCMP, "OCMP", 512), (OSEL, "OSEL", 520), (OWIN, "OWIN", 520),
                                                    (OR32, "OR32", 512), (ACC, "T4", 512)]):
                    A("pool", [kk_], [], lambda e, di=di, src=src, n=n: e.dma_start(
                        out=dbg[dbi * 128:(dbi + 1) * 128, di, 0:n], in_=src[:].rearrange("p h d -> p (h d)")),
                      dma=True, final=True)
            head_rms(ACC[:], "T4", 8)
            A("dve", ["T4", "st8"], ["T4"], lambda e: e.tensor_tensor(
                out=ACC[:], in0=ACC[:], in1=st8[:].unsqueeze(2).to_broadcast([128, 8, 64]), op=ALU.mult))
            A("dve", ["T4", "gvec"], ["OC"], lambda e: e.tensor_tensor(
                out=OC[:, 0:512], in0=ACC[:].rearrange("p h d -> p (h d)"), in1=gvec[:, 256:768], op=ALU.mult))
            head_rms(OR32[:], "OR32", 8)
            A("dve", ["OR32", "st8"], ["OR32"], lambda e: e.tensor_tensor(
                out=OR32[:], in0=OR32[:], in1=st8[:].unsqueeze(2).to_broadcast([128, 8, 64]), op=ALU.mult))
            A("dve", ["OR32", "gvec"], ["OR32"], lambda e: e.tensor_tensor(
                out=OR32[:].rearrange("p h d -> p (h d)"), in0=OR32[:].rearrange("p h d -> p (h d)"),
                in1=gvec[:, 768:1280], op=ALU.mult))
            A("dve", ["OR32", "RGS"], ["OC"], lambda e: e.tensor_tensor(
                out=OC[:, 512:1024], in0=OR32[:].rearrange("p h d -> p (h d)"), in1=RGS[:], op=ALU.mult))
            for k in range(8):
                A("pe", ["OC", "idb"], ["K0"], lambda e, k=k: e.transpose(
                    out=K0[:, k * 128:(k + 1) * 128], in_=OC[:, k * 128:(k + 1) * 128], identity=idb[:]))
            A("dve", ["K0"], ["OCT"], lambda e: e.tensor_copy(out=OCT[:].rearrange("p k t -> p (k t)"), in_=K0[:]))
            for n in range(2):
                for k in range(8):
                    A("pe", ["OCT", "WO"], ["K%d" % (1 + n)], lambda e, n=n, k=k: e.matmul(
                        K[1 + n][:], lhsT=OCT[:, k, :], rhs=WO[:, k, n * 512:(n + 1) * 512], start=(k == 0), stop=(k == 7)))
                A("dve", ["K%d" % (1 + n), kx], ["H"], lambda e, n=n, X=X: e.tensor_tensor(
                    out=H[:, n * 512:(n + 1) * 512], in0=K[1 + n][:], in1=X[:, n * 512:(n + 1) * 512], op=ALU.add))
            A("pool", ["H"], [ykey], lambda e: e.dma_start(out=ydst, in_=H[:]), dma=True, final=True)

        def tile_1b(v):
            p = v % 2
            own = (v % 2 == 0) and STAGE >= 3
            i_own = v // 2
            X, XT, kx, kxt = prep_x(xv[v * 128:(v + 1) * 128, :], p)
            Zt, kz = Z[p], "Z0"
            mm(XT, kxt, 1, 512, C_KV)
            mm(XT, kxt, 2, 256, C_KV + 512)
            A("act", ["K1"], [kz], lambda e, Zt=Zt: e.activation(out=Zt[:, 0:512], in_=K[1][:, 0:512], func=AF.Copy))
            A("act", ["K2"], [kz], lambda e, Zt=Zt: e.activation(out=Zt[:, 512:768], in_=K[2][:, 0:256], func=AF.Copy))
            mm(XT, kxt, 3, 512, C_RF)
            mm(XT, kxt, 5, 512, C_RI)
            Z5 = k_norms(Zt, kz)
            A("pool", [kz], [], lambda e, v=v, Zt=Zt: e.dma_start(out=kvp[v * 128:(v + 1) * 128, :], in_=Zt[:, 0:512]),
              dma=True, final=True)
            if v >= 28:
                A("pool", [kz], [], lambda e, v=v, Zt=Zt: e.dma_start(
                    out=winp[(v - 28) * 128:(v - 27) * 128, :], in_=Zt[:, 512:768]), dma=True, final=True)
            A("dve", [kz], ["zb"], lambda e, Z5=Z5: e.tensor_copy(out=zb[:].rearrange("p (a g) d -> p a g d", a=2),
                                                                  in_=Z5[:, 1:3, 0, :, :]))
            for a4 in range(4):
                A("pe", ["zb", "idb"], ["K0"], lambda e, a4=a4: e.transpose(
                    out=K0[0:64, a4 * 128:(a4 + 1) * 128], in_=zb[:, a4, :], identity=idb[:]))
            A("dve", ["K0"], ["KTs"], lambda e, v=v: e.tensor_copy(
                out=KTs[0:64, :, v * 128:(v + 1) * 128], in_=K0[0:64, 0:256].rearrange("p (g t) -> p g t", g=2)))
            sl = v % 6
            A("dve", ["K0"], ["KTw"], lambda e, sl=sl: e.tensor_copy(
                out=KTw[0:64, :, sl * 128:(sl + 1) * 128], in_=K0[0:64, 256:512].rearrange("p (g t) -> p g t", g=2)))
            A("sp", [], ["H"], lambda e, v=v: e.dma_start(out=stgq[64:68, 0:128], in_=kpos_d[:, v * 128:(v + 1) * 128]),
              dma=True)
            A("dve", ["H"], ["KTw"], lambda e, sl=sl: e.tensor_copy(
                out=KTw[64:68, :, sl * 128:(sl + 1) * 128], in_=stgq[64:68, 0:128].unsqueeze(1).to_broadcast([4, 2, 128])))
            A("dve", [kz], ["VVs"], lambda e, v=v, Z5=Z5: e.tensor_copy(out=VVs[:, v, :, 0:64], in_=Z5[:, 1, 1, :, :]))
            A("dve", [kz], ["VVw"], lambda e, sl=sl, Z5=Z5: e.tensor_copy(out=VVw[:, sl, :, 0:64], in_=Z5[:, 2, 1, :, :]))
            rnn_gates(3, 5, False)
            state_tot()
            if own:
                mm(XT, kxt, 1, 512, C_Q)
                mm(XT, kxt, 2, 24, C_G)
                A("act", ["K1"], ["ZQ"], lambda e: e.activation(out=ZQ[:].rearrange("p h d -> p (h d)"), in_=K[1][:],
                                                                 func=AF.Copy))
                A("act", ["K2"], ["gsig"], lambda e: e.activation(out=gsig[:].rearrange("p h b -> p (h b)"),
                                                                   in_=K[2][:, 0:24], func=AF.Exp, scale=-1.0))
                A("dve", ["gsig"], ["gsig"], lambda e: e.tensor_scalar_add(out=gsig[:], in0=gsig[:], scalar1=1.0))
                A("dve", ["gsig"], ["gsig"], lambda e: e.reciprocal(out=gsig[:], in_=gsig[:]))
                mm(XT, kxt, 6, 512, C_RQ)
                silu_from(6, QS, "QS")
                mm(XT, kxt, 7, 512, C_RG)
                silu_from(7, RGS, "RGS")
                head_rms(ZQ[:], "ZQ", 8)
                A("dve", ["ZQ", "st8"], ["ZQ"], lambda e: e.tensor_tensor(
                    out=ZQ[:], in0=ZQ[:], in1=st8[:].unsqueeze(2).to_broadcast([128, 8, 64]), op=ALU.mult))
                A("dve", ["ZQ", "qgs"], ["QN"], lambda e: e.tensor_tensor(
                    out=QN[:], in0=ZQ[:], in1=qgs[:].unsqueeze(1).to_broadcast([128, 8, 64]), op=ALU.mult))
                for h in range(8):
                    A("pe", ["QN", "idb"], ["K0"], lambda e, h=h: e.transpose(
                        out=K0[0:64, h * 128:(h + 1) * 128], in_=QN[:, h, :], identity=idb[:]))
                A("dve", ["K0"], ["QTa"], lambda e: e.tensor_copy(out=QTa[0:64, :, :].rearrange("p h t -> p (h t)"),
                                                                  in_=K0[0:64, :]))
                A("sp", [], ["H"], lambda e, i_own=i_own: e.dma_start(out=stgq[64:68, :], in_=qpos_d[i_own]), dma=True)
                A("dve", ["H"], ["QTa"], lambda e: e.tensor_copy(out=QTa[64:68, :, :].rearrange("p h t -> p (h t)"),
                                                                    in_=stgq[64:68, :]))
                if SUB < 1:
                    return
                A("pe", ["tm", "T3"], ["K3"], lambda e: e.matmul(K[3][:], lhsT=tm[:, 0, :], rhs=T[3][:], start=True, stop=True))
                A("act", ["K3"], ["T4"], lambda e: e.activation(out=T[4][:], in_=K[3][:], func=AF.Exp))
                A("act", ["K3"], ["T5"], lambda e: e.activation(out=T[5][:], in_=K[3][:], func=AF.Exp, scale=-1.0))
                A("dve", ["QS", "T4"], ["QD"], lambda e: e.tensor_tensor(out=QD[:], in0=QS[:], in1=T[4][:], op=ALU.mult))
                A("dve", ["T2", "T5"], ["KDI"], lambda e: e.tensor_tensor(out=KDI[:], in0=T[2][:], in1=T[5][:], op=ALU.mult))
                for hp in range(4):
                    A("pe", ["QD", "idb"], ["K0"], lambda e, hp=hp: e.transpose(
                        out=K0[:, hp * 128:(hp + 1) * 128], in_=QD[:, hp * 128:(hp + 1) * 128], identity=idb[:]))
                    A("pe", ["KDI", "idb"], ["K0"], lambda e, hp=hp: e.transpose(
                        out=K0[:, (4 + hp) * 128:(5 + hp) * 128], in_=KDI[:, hp * 128:(hp + 1) * 128], identity=idb[:]))
                A("dve", ["K0"], ["qkT"], lambda e: e.tensor_copy(out=qkT[:].rearrange("p a t -> p (a t)"), in_=K0[:]))
                if SUB < 2:
                    return
                pA = K[5][:].rearrange("p (h t) -> p h t", h=8)
                for c in range(2):
                    for h in [0, 2, 4, 6, 1, 3, 5, 7]:
                        hp, ee = h // 2, h % 2
                        A("pe", ["qkT"], ["K5"], lambda e, c=c, h=h, hp=hp, ee=ee: e.matmul(
                            pA[64 * c:64 * c + 64, h, :], lhsT=qkT[64 * ee:64 * ee + 64, 4 + hp, c * 64:(c + 1) * 64],
                            rhs=qkT[64 * ee:64 * ee + 64, hp, c * 64:(c + 1) * 64], start=True, stop=True), rg=64 * ee)
                if "a" not in SKIP:
                    A("dve", ["K5", "cst"], ["AT"], lambda e: e.tensor_tensor(
                        out=AT[:], in0=pA, in1=tri2.unsqueeze(1).to_broadcast([128, 8, 64]), op=ALU.mult))
                if "s" not in SKIP:
                    A("dve", ["St"], ["Sb0"], lambda e: e.tensor_copy(out=Sb[0][:], in_=St[:]))
            state_update(0)
            if own:
                A("dve", ["St"], ["Sb1"], lambda e: e.tensor_copy(out=Sb[1][:], in_=St[:]))
            state_update(1)
            if not own:
                return
            if SUB < 3:
                return
            po = K[6][:].rearrange("p (h d) -> p h d", h=8)
            for c in range(2):
                for h in range(8):
                    A("pe", ["AT", "VB"], ["K6"], lambda e, c=c, h=h: e.matmul(
                        po[64 * c:64 * c + 64, h, :], lhsT=AT[64 * c:64 * c + 64, h, :],
                        rhs=VB[64 * c:64 * c + 64, h * 64:(h + 1) * 64], start=(h == 0), stop=False), rg=64 * c)
                order = [h for h in range(8) if h % 2 == c] + [h for h in range(8) if h % 2 != c]
                for j, h in enumerate(order):
                    hp, ee = h // 2, h % 2
                    A("pe", ["qkT", "Sb%d" % c], ["K6"], lambda e, c=c, h=h, hp=hp, ee=ee, j=j: e.matmul(
                        po[64 * c:64 * c + 64, h, :], lhsT=qkT[64 * ee:64 * ee + 64, hp, c * 64:(c + 1) * 64],
                        rhs=Sb[c][64 * ee:64 * ee + 64, hp, :], start=False, stop=(j == 7)), rg=64 * ee)
            A("act", ["K6"], ["OR32"], lambda e: e.activation(out=OR32[:].rearrange("p h d -> p (h d)"), in_=K[6][:],
                                                               func=AF.Copy))
            if STAGE < 4:
                return
            nb = 2 * v + 2
            pc4 = K[3][:, 0:264].rearrange("p (h j) -> p h j", h=4)
            pocmp = K[4][:, 0:256].rearrange("p (h d) -> p h d", h=4)
            for g in range(2):
                for hh in range(4):
                    A("pe", ["QTa", "kcT"], ["K3"], lambda e, g=g, hh=hh: e.matmul(
                        pc4[:, hh, 0:nb], lhsT=QTa[0:64, g * 4 + hh, :], rhs=kcT[0:64, g, 0:nb], start=True, stop=True))
                A("dve", ["K3", "cst"], ["LG"], lambda e, g=g: e.tensor_tensor(
                    out=LG[:, :, 0:nb], in0=pc4[:, :, 0:nb], in1=acore[:, g * 4:(g + 1) * 4, 0:nb], op=ALU.add))
                A("dve", ["LG", "cst"], ["LG"], lambda e: e.tensor_tensor(
                    out=LG[:, :, 2 * v:2 * v + 2], in0=LG[:, :, 2 * v:2 * v + 2],
                    in1=fix2.unsqueeze(1).to_broadcast([128, 4, 2]), op=ALU.add))
                A("dve", ["LG"], ["st4"], lambda e: e.tensor_reduce(out=st4[:], in_=LG[:, :, 0:nb], axis=AX.X, op=ALU.max))
                A("dve", ["LG", "st4"], ["LG"], lambda e: e.tensor_tensor(
                    out=LG[:, :, 0:nb], in0=LG[:, :, 0:nb], in1=st4[:].unsqueeze(2).to_broadcast([128, 4, nb]),
                    op=ALU.subtract))
                A("act", ["LG"], ["EX"], lambda e: e.activation(out=EX[:, :, 0:nb], in_=LG[:, :, 0:nb], func=AF.Exp))
                A("dve", ["EX"], ["st4b"], lambda e: e.tensor_reduce(out=st4b[:], in_=EX[:, :, 0:nb], axis=AX.X, op=ALU.add))
                A("dve", ["st4b"], ["st4b"], lambda e: e.reciprocal(out=st4b[:], in_=st4b[:]))
                A("dve", ["st4b", "cst"], ["st4b"], lambda e: e.tensor_scalar(
                    out=st4b[:], in0=st4b[:], scalar1=rowvalid[:, i_own:i_own + 1], scalar2=None, op0=ALU.mult))
                A("dve", ["EX", "st4b"], ["EX"], lambda e: e.tensor_tensor(
                    out=EX[:, :, 0:nb], in0=EX[:, :, 0:nb], in1=st4b[:].unsqueeze(2).to_broadcast([128, 4, nb]),
                    op=ALU.mult))
                A("dve", ["EX"], ["PB"], lambda e: e.tensor_copy(out=PB[:, :, 0:nb], in_=EX[:, :, 0:nb]))
                A("dve", ["EX"], ["SC"], lambda e: e.tensor_reduce(
                    out=SC[:, 0:nb], in_=EX[:, :, 0:nb].rearrange("p h j -> p j h"), axis=AX.X, op=ALU.add))
                if nb < 66:
                    A("dve", [], ["SC"], lambda e: e.memset(SC[:, nb:66], -1.0))
                A("dve", ["SC", "cst"], ["SC"], lambda e: e.scalar_tensor_tensor(
                    out=SC[:, 2 * v:2 * v + 1], in0=SC[:, 2 * v:2 * v + 1], scalar=mhi, in1=mlo5,
                    op0=ALU.mult, op1=ALU.add))
                A("dve", ["cst"], ["SC"], lambda e: e.tensor_copy(out=SC[:, 2 * v + 1:2 * v + 2], in_=c1col))
                A("dve", ["SC", "cst"], ["SC"], lambda e: e.tensor_tensor(out=SC[:], in0=SC[:], in1=keepc, op=ALU.mult))
                A("dve", ["SC", "cst"], ["SC"], lambda e: e.tensor_tensor(out=SC[:], in0=SC[:], in1=addc, op=ALU.add))
                A("dve", ["SC"], ["m8"], lambda e: e.max(out=m8[:], in_=SC[:]))
                A("dve", ["SC", "m8"], ["WK"], lambda e: e.match_replace(out=WK[:], in_to_replace=m8[:], in_values=SC[:],
                                                                          imm_value=-2.0))
                A("dve", ["WK"], ["m8"], lambda e: e.max(out=m8[:], in_=WK[:]))
                A("dve", ["m8"], ["thr"], lambda e: e.tensor_reduce(out=thr[:], in_=m8[:], axis=AX.X, op=ALU.min))
                A("dve", ["SC", "thr"], ["SEL"], lambda e: e.tensor_scalar(
                    out=SEL[:], in0=SC[:], scalar1=thr[:, 0:1], scalar2=None, op0=ALU.is_ge))
                A("dve", ["SC"], ["WK"], lambda e: e.tensor_single_scalar(out=WK[:], in_=SC[:], scalar=0.0, op=ALU.is_ge))
                A("dve", ["SEL", "WK"], ["SEL"], lambda e: e.tensor_tensor(out=SEL[:], in0=SEL[:], in1=WK[:], op=ALU.mult))
                A("dve", ["SEL"], ["MBF"], lambda e: e.tensor_scalar(
                    out=MBF[:], in0=SEL[:], scalar1=-1.0, scalar2=BIG, op0=ALU.add, op1=ALU.mult))
                A("pe", ["MBF", "idb"], ["K0"], lambda e: e.transpose(out=K0[0:66, 0:128], in_=MBF[:], identity=idb[:]))
                for hh in range(4):
                    A("pe", ["PB", "idb"], ["K0"], lambda e, hh=hh: e.transpose(
                        out=K0[0:nb, (1 + hh) * 128:(2 + hh) * 128], in_=PB[:, hh, 0:nb], identity=idb[:]))
                A("dve", ["K0"], ["RHSM%d" % g], lambda e, g=g: e.tensor_copy(
                    out=RHSM_[g][0:66], in_=K0[0:66, 0:128].unsqueeze(1).to_broadcast([66, 4, 128])))
                A("dve", ["K0"], ["PT4"], lambda e: e.tensor_copy(
                    out=PT4[0:nb, :, :], in_=K0[0:nb, 128:640].rearrange("p (h t) -> p h t", h=4)))
                for hh in range(4):
                    A("pe", ["PT4", "vcb"], ["K4"], lambda e, g=g, hh=hh: e.matmul(
                        pocmp[:, hh, :], lhsT=PT4[0:nb, hh, :], rhs=vcb[0:nb, g, :], start=True, stop=True))
                A("act", ["K4"], ["OCMP"], lambda e, g=g: e.activation(out=OCMP[:, g * 4:(g + 1) * 4, :], in_=pocmp,
                                                                        func=AF.Copy))
            if STAGE < 5:
                return
            posb = [(K[7][:, 0:260].rearrange("p (h d) -> p h d", h=4), "K7"),
                    (K[4][:, 0:260].rearrange("p (h d) -> p h d", h=4), "K4")]
            its = []
            gi = 0
            for g in range(2):
                for br in range(2):
                    kts = list(range(0, v + 1)) if br == 0 else list(range(max(0, v - 4), v + 1))
                    for kt in kts:
                        extra = []
                        if br == 0:
                            extra.append((Eexp[:, kt * 128:(kt + 1) * 128],
                                          RHSM_[g][:].rearrange("p h t -> p (h t)"), ["Eexp", "RHSM%d" % g]))
                            if kt == v:
                                extra.append((idb[:], MSK[:, 0, :], ["idb", "MSK"]))
                            lhs_k, rk = KTs[:, g, kt * 128:(kt + 1) * 128], ["KTs"]
                            rv, kv_ = VVs[:, kt, g, :], "VVs"
                        else:
                            if kt == v:
                                extra.append((idb[:], MSK[:, 0, :], ["idb", "MSK"]))
                            elif kt == v - 4:
                                extra.append((idb[:], MSK[:, 3 if kt == 0 else 1, :], ["idb", "MSK"]))
                            elif kt == 0:
                                extra.append((idb[:], MSK[:, 2, :], ["idb", "MSK"]))
                            slk = kt % 6
                            lhs_k, rk = KTw[:, g, slk * 128:(slk + 1) * 128], ["KTw"]
                            rv, kv_ = VVw[:, slk, g, :], "VVw"
                        its.append(dict(g=g, br=br, first=(kt == kts[0]), last=(kt == kts[-1]), extra=extra, lhs_k=lhs_k,
                                        rk=rk, rv=rv, kv=kv_, gi=gi))
                    gi += 1

            SBK = [5, 6, 1]

            def s_stage(i):
                d = its[i]
                bank = SBK[i % 3]
                kb = "K%d" % bank
                g, extra = d["g"], d["extra"]
                A("pe", d["rk"] + ["QTa"], [kb], lambda e, bank=bank, lhs_k=d["lhs_k"], g=g, ne=len(extra): e.matmul(
                    K[bank][:], lhsT=lhs_k, rhs=QTa[:, g * 4:(g + 1) * 4, :].rearrange("p h t -> p (h t)"),
                    start=True, stop=(ne == 0)))
                for xi, (l_, r_, ks_) in enumerate(extra):
                    A("pe", ks_, [kb], lambda e, bank=bank, l_=l_, r_=r_, last=(xi == len(extra) - 1): e.matmul(
                        K[bank][:], lhsT=l_, rhs=r_, start=False, stop=last))

            def ep_stage(i):
                d = its[i]
                bank = SBK[i % 3]
                kb = "K%d" % bank
                PTt, kpt = PTb[i % 3], "PTb%d" % (i % 3)
                pos, kpos_ = posb[d["gi"] % 2]
                A("act", [kb], [kpt], lambda e, bank=bank, PTt=PTt: e.activation(out=PTt[:], in_=K[bank][:], func=AF.Exp))
                for hh in range(4):
                    A("pe", [kpt, d["kv"]], [kpos_], lambda e, hh=hh, PTt=PTt, rv=d["rv"], pos=pos,
                      first=(d["first"] and hh == 0), last=(d["last"] and hh == 3): e.matmul(
                          pos[:, hh, :], lhsT=PTt[:, hh * 128:(hh + 1) * 128], rhs=rv, start=first, stop=last))
                if d["last"]:
                    dst, kd = (OSEL, "OSEL") if d["br"] == 0 else (OWIN, "OWIN")
                    A("act", [kpos_], [kd], lambda e, dst=dst, g=d["g"], pos=pos: e.activation(
                        out=dst[:, g * 4:(g + 1) * 4, :], in_=pos, func=AF.Copy))

            s_stage(0)
            if len(its) > 1:
                s_stage(1)
            for i in range(len(its)):
                if i + 2 < len(its):
                    s_stage(i + 2)
                ep_stage(i)
            if STAGE < 6:
                return
            finish(X, kx, yp[i_own * 128:(i_own + 1) * 128, :], "yp%d" % i_own, i_own)

        for v_ in range(NT if STAGE >= 2 else 0):
            tile_1b(v_)
        A("pool", ["St"], [], lambda e: e.dma_start(out=rnnp.rearrange("(a e) k v -> (e k) a v", e=2), in_=St[:]),
          dma=True, final=True)

        SKEYS = ["W1s", "SELM", "S0bf", "KN", "ZV", "QTs", "ATs", "ATs", "XcTb", "PGB0", "PGB1", "hs_s", "KCa", "vca",
                 "QTb", "PTc", "PTs", "PTw", "RHSMs", "WPB", "OcT", "OsT", "OwT", "IDX", "MSKs", "kcbs"]
        _so = [0]

        def sw(n, parts=128):
            a_ = _so[0]
            _so[0] += n
            assert _so[0] <= 26816
            return BIGW[0:parts, a_:a_ + n]

        W1s = sw(8192).rearrange("p (a j h) -> p a j h", a=2, j=32)
        SELM = sw(128).rearrange("p (a j) -> p a j", a=2)
        S0bf = sw(4096).rearrange("p (a d) -> p a d", a=64)
        KN = sw(512, 64).rearrange("p (a t) -> p a t", a=4)
        ZV = sw(256).rearrange("p (a g d) -> p a g d", a=2, g=2)
        QTs = sw(1024, 64).rearrange("p (h t) -> p h t", h=8)
        ATs = sw(1024).rearrange("p (h t) -> p h t", h=8)
        oiT = ATs[0:64]
        XcTb = sw(4096).rearrange("p (a n) -> p a n", a=4)
        _pgb = sw(1024)
        PGB = [_pgb[:, 0:512], _pgb[:, 512:1024]]
        hs_s = sw(128).rearrange("p (a j) -> p a j", a=4)
        KCa = sw(64, 68).rearrange("p (g j) -> p g j", g=2)
        vca = sw(130, 32).rearrange("p (g d) -> p g d", g=2)
        QTb = sw(64, 128).rearrange("p (h t) -> p h t", h=8)
        PTc = sw(32, 32)
        PTs = sw(544)
        PTw = sw(160)
        RHSMs = sw(32, 128).rearrange("p (h t) -> p h t", h=4)
        WPB = _pgb.rearrange("p (t c) -> p t c", t=4)
        OcT = sw(1024, 65).rearrange("p (h t) -> p h t", h=8)
        OsT = sw(1024, 65).rearrange("p (h t) -> p h t", h=8)
        OwT = sw(1024, 65).rearrange("p (h t) -> p h t", h=8)
        IDX = sw(1024).bitcast(mybir.dt.int32)
        MSKs = sw(64).rearrange("p (m n) -> p m n", m=2)
        kcbs = sw(128, 32).rearrange("p (g d) -> p g d", g=2)
        csts = sb("csts", [128, 74])
        selT = csts[0:32, 0:8]
        keepS = csts[0:8, 8:41]
        addS = csts[0:8, 41:74]
        iotac = sb("iotac", [128, 1])

        A("dve", [], AKEYS + ["fdummy"] + ["S0_%d" % b for b in range(16)], lambda e: e.memset(fdummy[:], 0.0))
        for b in range(16):
            A("pool", [], ["S0_%d" % b], lambda e, b=b: e.dma_start(
                out=S0[:, b * 4:(b + 1) * 4, :], in_=srnn[b].rearrange("(a e) k v -> (e k) a v", e=2)), dma=True)
        for b in range(16):
            A("pool", [], [], lambda e, b=b: e.dma_start(out=wins[b, 0:504, :], in_=cwin[b, 8:512, :]), dma=True, final=True)
        X, XT, kx, kxt = prep_x(xsm[:, :], 0)
        Zt, kz = Z[0], "Z0"
        mm(XT, kxt, 1, 384, C_KV)
        mm(XT, kxt, 2, 384, C_KV + 384)
        A("act", ["K1"], [kz], lambda e: e.activation(out=Zt[:, 0:384], in_=K[1][:, 0:384], func=AF.Copy))
        A("act", ["K2"], [kz], lambda e: e.activation(out=Zt[:, 384:768], in_=K[2][:, 0:384], func=AF.Copy))
        mm(XT, kxt, 1, 512, C_RF)
        mm(XT, kxt, 2, 512, C_RI)
        Z5 = k_norms(Zt, kz)
        A("pool", [kz], [], lambda e: e.dma_start(out=kvs[:, :], in_=Zt[:, 0:512]), dma=True, final=True)
        for b in range(16):
            A("pool", [kz], [], lambda e, b=b: e.dma_start(out=wins[b, 504:512, :], in_=Zt[b * 8:(b + 1) * 8, 512:768]),
              dma=True, final=True)
        rnn_gates(1, 2, True)
        mm(XT, kxt, 1, 512, C_Q)
        mm(XT, kxt, 2, 24, C_G)
        A("act", ["K1"], ["ZQ"], lambda e: e.activation(out=ZQ[:].rearrange("p h d -> p (h d)"), in_=K[1][:], func=AF.Copy))
        A("act", ["K2"], ["gsig"], lambda e: e.activation(out=gsig[:].rearrange("p h b -> p (h b)"), in_=K[2][:, 0:24],
                                                           func=AF.Exp, scale=-1.0))
        A("dve", ["gsig"], ["gsig"], lambda e: e.tensor_scalar_add(out=gsig[:], in0=gsig[:], scalar1=1.0))
        A("dve", ["gsig"], ["gsig"], lambda e: e.reciprocal(out=gsig[:], in_=gsig[:]))
        mm(XT, kxt, 1, 512, C_RQ)
        silu_from(1, QS, "QS")
        mm(XT, kxt, 2, 512, C_RG)
        silu_from(2, RGS, "RGS")
        A("dve", [], WK_ + SKEYS + ["fdummy"], lambda e: e.memset(fdummy[:], 0.0))
        A("sp", [], ["csts"], lambda e: e.dma_start(out=csts[:], in_=csts_d), dma=True)
        A("sp", [], ["iotac"], lambda e: e.dma_start(out=iotac[:], in_=iota_d), dma=True)
        A("dve", [kz], ["zb"], lambda e: e.tensor_copy(out=zb[:].rearrange("p (a g) d -> p a g d", a=2), in_=Z5[:, 1:3, 0, :, :]))
        for a4 in range(4):
            A("pe", ["zb", "idb"], ["K0"], lambda e, a4=a4: e.transpose(
                out=K0[0:64, a4 * 128:(a4 + 1) * 128], in_=zb[:, a4, :], identity=idb[:]))
        A("dve", ["K0"], ["KN"], lambda e: e.tensor_copy(out=KN[:].rearrange("p a t -> p (a t)"), in_=K0[0:64, 0:512]))
        A("dve", [kz], ["ZV"], lambda e: e.tensor_copy(out=ZV[:], in_=Z5[:, 1:3, 1, :, :]))
        head_rms(ZQ[:], "ZQ", 8)
        A("dve", ["ZQ", "st8"], ["ZQ"], lambda e: e.tensor_tensor(
            out=ZQ[:], in0=ZQ[:], in1=st8[:].unsqueeze(2).to_broadcast([128, 8, 64]), op=ALU.mult))
        A("dve", ["ZQ", "qgs"], ["QN"], lambda e: e.tensor_tensor(
            out=QN[:], in0=ZQ[:], in1=qgs[:].unsqueeze(1).to_broadcast([128, 8, 64]), op=ALU.mult))
        for h in range(8):
            A("pe", ["QN", "idb"], ["K0"], lambda e, h=h: e.transpose(
                out=K0[0:64, h * 128:(h + 1) * 128], in_=QN[:, h, :], identity=idb[:]))
        A("dve", ["K0"], ["QTs"], lambda e: e.tensor_copy(out=QTs[:].rearrange("p h t -> p (h t)"), in_=K0[0:64, :]))
        A("pe", ["tm", "T3"], ["K3"], lambda e: e.matmul(K[3][:], lhsT=tm[:, 2, :], rhs=T[3][:], start=True, stop=True))
        A("act", ["K3"], ["T4"], lambda e: e.activation(out=T[4][:], in_=K[3][:], func=AF.Exp))
        A("act", ["K3"], ["T5"], lambda e: e.activation(out=T[5][:], in_=K[3][:], func=AF.Exp, scale=-1.0))
        A("dve", ["QS", "T4"], ["QD"], lambda e: e.tensor_tensor(out=QD[:], in0=QS[:], in1=T[4][:], op=ALU.mult))
        A("dve", ["T2", "T5"], ["KDI"], lambda e: e.tensor_tensor(out=KDI[:], in0=T[2][:], in1=T[5][:], op=ALU.mult))
        for hp in range(4):
            A("pe", ["QD", "idb"], ["K0"], lambda e, hp=hp: e.transpose(
                out=K0[:, hp * 128:(hp + 1) * 128], in_=QD[:, hp * 128:(hp + 1) * 128], identity=idb[:]))
            A("pe", ["KDI", "idb"], ["K0"], lambda e, hp=hp: e.transpose(
                out=K0[:, (4 + hp) * 128:(5 + hp) * 128], in_=KDI[:, hp * 128:(hp + 1) * 128], identity=idb[:]))
        A("dve", ["K0"], ["qkT"], lambda e: e.tensor_copy(out=qkT[:].rearrange("p a t -> p (a t)"), in_=K0[:]))
        for ee in range(2):
            for hp in range(4):
                A("pe", ["qkT"], ["K%d" % (5 + ee)], lambda e, ee=ee, hp=hp: e.matmul(
                    K[5 + ee][:, hp * 128:(hp + 1) * 128], lhsT=qkT[64 * ee:64 * ee + 64, 4 + hp, :],
                    rhs=qkT[64 * ee:64 * ee + 64, hp, :], start=True, stop=True), rg=64 * ee)
        for ee in range(2):
            A("dve", ["K%d" % (5 + ee), "tm"], ["ATs"], lambda e, ee=ee: e.tensor_tensor(
                out=ATs[:, ee * 4:(ee + 1) * 4, :], in0=K[5 + ee][:].rearrange("p (a t) -> p a t", a=4),
                in1=tm[:, 2, :].unsqueeze(1).to_broadcast([128, 4, 128]), op=ALU.mult))
        A("dve", ["S0_%d" % b for b in range(16)], ["S0bf"], lambda e: e.tensor_copy(out=S0bf[:, 0:32, :], in_=S0[:, 0:32, :]))
        A("act", ["S0_%d" % b for b in range(16)], ["S0bf"], lambda e: e.activation(out=S0bf[:, 32:64, :], in_=S0[:, 32:64, :],
                                                                                     func=AF.Copy))
        pos_ = K[7][:].rearrange("p (h d) -> p h d", h=8)
        for h in range(8):
            hp, ee = h // 2, h % 2
            A("pe", ["ATs", "VB"], ["K7"], lambda e, h=h, hp=hp, ee=ee: e.matmul(
                pos_[:, h, :], lhsT=ATs[:, ee * 4 + hp, :], rhs=VB[:, h * 64:(h + 1) * 64], start=True, stop=True))
        for ee in range(2):
            for b in range(16):
                for hp in range(4):
                    A("pe", ["qkT", "S0bf"], ["K%d" % (5 + ee)], lambda e, ee=ee, b=b, hp=hp: e.matmul(
                        K[5 + ee][0:64, hp * 128 + b * 8:hp * 128 + b * 8 + 8], lhsT=S0bf[64 * ee:64 * ee + 64, b * 4 + hp, :],
                        rhs=qkT[64 * ee:64 * ee + 64, hp, b * 8:(b + 1) * 8], start=True, stop=True), rg=64 * ee)
        for ee in range(2):
            A("act", ["K%d" % (5 + ee)], ["ATs"], lambda e, ee=ee: e.activation(
                out=oiT[:, ee * 4:(ee + 1) * 4, :].rearrange("p a t -> p (a t)"), in_=K[5 + ee][0:64, :], func=AF.Copy))
        for h in range(8):
            hp, ee = h // 2, h % 2
            A("pe", ["ATs", "idb"], ["K0"], lambda e, h=h, hp=hp, ee=ee: e.transpose(
                out=K0[:, h * 64:(h + 1) * 64], in_=oiT[:, ee * 4 + hp, :], identity=idb[0:64, 0:64]))
        A("act", ["K0"], ["T5"], lambda e: e.activation(out=T[5][:], in_=K0[:, 0:512], func=AF.Copy))
        A("dve", ["K7", "T5"], ["OR32"], lambda e: e.tensor_tensor(out=OR32[:].rearrange("p h d -> p (h d)"), in0=K[7][:],
                                                                  in1=T[5][:], op=ALU.add))
        G = T[3]
        ptots = K[5][:, 0:64].rearrange("p (a b) -> p a b", a=4)
        for h in range(8):
            hp, ee = h // 2, h % 2
            A("pe", ["T3", "rmask"], ["K5"], lambda e, h=h, hp=hp, ee=ee: e.matmul(
                ptots[64 * ee:64 * ee + 64, hp, :], lhsT=G[:, h * 64:(h + 1) * 64], rhs=rmask[:], start=True, stop=True))
        A("act", ["K5"], ["etots"], lambda e: e.activation(out=etots[:], in_=ptots, func=AF.Exp))
        for b in range(16):
            q = b % 2
            A("dve", ["KD", "rmask"], ["PTb%d" % q], lambda e, b=b, q=q: e.tensor_scalar(
                out=kdm[q][:], in0=KD[:], scalar1=rmask[:, b:b + 1], scalar2=None, op0=ALU.mult))
            for h in range(8):
                hp, ee = h // 2, h % 2
                A("pe", ["PTb%d" % q, "VB"], ["K4"], lambda e, h=h, q=q, hp=hp, ee=ee: e.matmul(
                    pst[64 * ee:64 * ee + 64, hp, :], lhsT=kdm[q][:, h * 64:(h + 1) * 64], rhs=VB[:, h * 64:(h + 1) * 64],
                    start=True, stop=True))
            Sbv = S0[:, b * 4:(b + 1) * 4, :]
            A("dve", ["S0_%d" % b, "etots", "S0bf"], ["S0_%d" % b], lambda e, b=b, Sbv=Sbv: e.tensor_tensor(
                out=Sbv, in0=Sbv, in1=etots[:, :, b:b + 1].to_broadcast([128, 4, 64]), op=ALU.mult))
            A("dve", ["S0_%d" % b, "K4"], ["S0_%d" % b], lambda e, Sbv=Sbv: e.tensor_tensor(out=Sbv, in0=Sbv, in1=pst, op=ALU.add))
            A("pool", ["S0_%d" % b], ["rnns_o"], lambda e, b=b, Sbv=Sbv: e.dma_start(
                out=rnns[b].rearrange("(a e) k v -> (e k) a v", e=2), in_=Sbv), dma=True, final=True)
        if STAGE >= 8:
            A("dve", ["rnns_o"] + ["S0_%d" % b for b in range(16)], ["KTs", "fdummy"], lambda e: e.memset(fdummy[:], 0.0))
            A("dve", [], ["KTs"], lambda e: e.memset(KTs[64:128, :, 0:2176], 0.0))
            for c0 in range(0, 17 * 128, 1024):
                n = min(1024, 17 * 128 - c0)
                A("sp", [], ["H"], lambda e, c0=c0, n=n: e.dma_start(out=stgq[64:68, 0:n], in_=kpos_d[:, c0:c0 + n]), dma=True)
                for g in range(2):
                    A("dve", ["H"], ["KTs"], lambda e, c0=c0, n=n, g=g: e.tensor_copy(
                        out=KTs[64:68, g, c0:c0 + n], in_=stgq[64:68, 0:n]))
            A("sp", [], ["H"], lambda e: e.dma_start(out=stgq[64:68, 0:640], in_=kpos_d[:, 1536:2176]), dma=True)
            A("dve", ["H"], ["KTw"], lambda e: e.tensor_copy(
                out=KTw[64:68, :, 0:640], in_=stgq[64:68, 0:640].unsqueeze(1).to_broadcast([4, 2, 640])))
            A("dve", [], ["QTb"], lambda e: e.memset(QTb[64:128, :, :], 0.0))
            A("dve", [], ["RHSMs"], lambda e: e.memset(RHSMs[:, :, :], 0.0))
            A("sp", [], ["H"], lambda e: e.dma_start(out=stgq[64:68, 0:64], in_=sq_d[0:4, :]), dma=True)
            A("dve", ["H"], ["QTb"], lambda e: e.tensor_copy(out=QTb[64:68, :, :].rearrange("p h t -> p (h t)"),
                                                             in_=stgq[64:68, 0:64]))
            A("sp", [], ["H"], lambda e: e.dma_start(out=stgq[64:68, 0:32], in_=sq_d[4:8, 0:32]), dma=True)
            A("dve", ["H"], ["KCa"], lambda e: e.tensor_copy(
                out=KCa[64:68, :, :], in_=stgq[64:68, 0:32].unsqueeze(1).to_broadcast([4, 2, 32])))
            A("sp", [], ["H"], lambda e: e.dma_start(out=stgq[:, 0:192], in_=sm_d), dma=True)
            A("dve", ["H"], ["SELM"], lambda e: e.tensor_copy(out=SELM[:].rearrange("p a j -> p (a j)"), in_=stgq[:, 0:128]))
            A("dve", ["H"], ["MSKs"], lambda e: e.tensor_copy(out=MSKs[:].rearrange("p m n -> p (m n)"), in_=stgq[:, 128:192]))
            for typ in range(2):
                for jq in range(4):
                    A("sp", [], ["H"], lambda e, typ=typ, jq=jq: e.dma_start(
                        out=stgq[:].rearrange("p (j h) -> p j h", j=8),
                        in_=w1_d[typ, jq * 1024:(jq + 1) * 1024, :].rearrange("(j p) h -> p j h", p=128)), dma=True)
                    A("dve", ["H"], ["W1s"], lambda e, typ=typ, jq=jq: e.tensor_copy(
                        out=W1s[:, typ, jq * 8:(jq + 1) * 8, :], in_=stgq[:].rearrange("p (j h) -> p j h", j=8)))
            A("dve", [], ["KTs"], lambda e: e.memset(KTs[0:64, :, 2048:2176], 0.0))
            A("dve", [], ["VVs"], lambda e: e.memset(VVs[:, 16, :, 0:64], 0.0))
            A("dve", [], ["KTw"], lambda e: e.memset(KTw[0:64, :, 512:640], 0.0))
            A("dve", [], ["VVw"], lambda e: e.memset(VVw[:, 4, :, 0:64], 0.0))
            A("dve", [], ["vca"], lambda e: e.memset(vca[:, :, 64:65], 1.0))
            A("sp", [], ["IDX"], lambda e: e.dma_start(out=IDX[:, 0:256], in_=ptrep_d), dma=True)
            A("dve", ["IDX"], ["T0"], lambda e: e.tensor_copy(out=T[0][:, 0:256], in_=IDX[:, 0:256]))
            A("dve", ["T0", "iotac"], ["T0"], lambda e: e.tensor_scalar(out=T[0][:, 0:256], in0=T[0][:, 0:256], scalar1=128.0,
                                                                       scalar2=iotac[:, 0:1], op0=ALU.mult, op1=ALU.add))
            A("dve", ["T0"], ["IDX"], lambda e: e.tensor_copy(out=IDX[:, 256:512], in_=T[0][:, 0:256]))
            PG = [T[0], T[1], T[2], T[3]]
            K1v = K[1][:, 0:256].rearrange("p (a t) -> p a t", a=4)
            K3v = K[3][:, 0:256].rearrange("p (a d) -> p a d", a=4)
            WP = xs[1]
            pgc = [0]

            def sample_seq(b):
                for j in range(16):
                    q = pgc[0] % 4
                    q2 = pgc[0] % 2
                    pgc[0] += 1
                    n = b * 16 + j
                    pg, kpg, pgb, kpgb = PG[q], "T%d" % q, PGB[q2], "PGB%d" % q2
                    A("pool", ["IDX"], [kpg], lambda e, pg=pg, n=n: e.indirect_dma_start(
                        out=pg[:], out_offset=None, in_=cache_d,
                        in_offset=bass.IndirectOffsetOnAxis(ap=IDX[:, 256 + n:257 + n], axis=0)), dma=True)
                    if q2 == 0:
                        A("act", [kpg], [kpgb], lambda e, pg=pg, pgb=pgb: e.activation(out=pgb, in_=pg[:], func=AF.Copy))
                    else:
                        A("dve", [kpg], [kpgb], lambda e, pg=pg, pgb=pgb: e.tensor_copy(out=pgb, in_=pg[:]))
                    for tg in range(4):
                        for par in range(2):
                            A("pe", [kpgb, "SELM"], ["K1"], lambda e, tg=tg, par=par, pgb=pgb: e.matmul(
                                K1v[64 * par:64 * par + 64, tg, :], lhsT=pgb[:, tg * 64:(tg + 1) * 64], rhs=SELM[:, par, :],
                                start=True, stop=True))
                    for a in range(2):
                        A("dve", ["K1", "peT"], ["XcTb"], lambda e, a=a, j=j: e.tensor_tensor(
                            out=XcTb[:, 2 * a:2 * a + 2, j * 64:(j + 1) * 64].rearrange("p g (b j) -> p g b j", b=2),
                            in0=K1v[:, 2 * a:2 * a + 2, :].rearrange("p g (b j) -> p g b j", b=2),
                            in1=peT[:, a, :].unsqueeze(1).unsqueeze(1).to_broadcast([128, 2, 2, 32]), op=ALU.add))
                    for g in range(2):
                        A("pe", [kpgb, "idb"], ["K0"], lambda e, g=g, pgb=pgb: e.transpose(
                            out=K0[0:64, g * 128:(g + 1) * 128], in_=pgb[:, 256 + g * 64:256 + (g + 1) * 64], identity=idb[:]))
                    A("dve", ["K0"], ["KTs"], lambda e, j=j: e.tensor_copy(
                        out=KTs[0:64, :, j * 128:(j + 1) * 128], in_=K0[0:64, 0:256].rearrange("p (g t) -> p g t", g=2)))
                    A("act", [kpg], ["VVs"], lambda e, pg=pg, j=j: e.activation(
                        out=VVs[:, j, :, 0:64], in_=pg[:, 384:512].rearrange("p (g d) -> p g d", g=2), func=AF.Copy))
                A("dve", ["KN"], ["KTs"], lambda e: e.tensor_copy(out=KTs[0:64, :, 2048:2056], in_=KN[:, 0:2, b * 8:(b + 1) * 8]))
                for g in range(2):
                    A("sp", ["ZV"], ["VVs"], lambda e, g=g: e.dma_start(out=VVs[0:8, 16, g, 0:64], in_=ZV[b * 8:(b + 1) * 8, 0, g, :]),
                      dma=True)
                    A("sp", ["ZV"], ["VVw"], lambda e, g=g: e.dma_start(out=VVw[0:8, 4, g, 0:64], in_=ZV[b * 8:(b + 1) * 8, 1, g, :]),
                      dma=True)
                for tg in range(4):
                    typ = tg // 2
                    for j in range(32):
                        A("pe", ["XcTb", "W1s"], ["K2"], lambda e, tg=tg, typ=typ, j=j: e.matmul(
                            K[2][:, 0:32], lhsT=W1s[:, typ, j, :], rhs=XcTb[:, tg, j:1024:32], start=(j == 0), stop=(j == 31)))
                    A("act", ["K2"], ["T4"], lambda e: e.activation(out=T[4][:, 0:32], in_=K[2][:, 0:32], func=AF.Exp, scale=-1.0))
                    A("dve", ["T4"], ["T4"], lambda e: e.tensor_scalar_add(out=T[4][:, 0:32], in0=T[4][:, 0:32], scalar1=1.0))
                    A("dve", ["T4"], ["T4"], lambda e: e.reciprocal(out=T[4][:, 0:32], in_=T[4][:, 0:32]))
                    A("dve", ["T4", "K2"], ["hs_s"], lambda e, tg=tg: e.tensor_tensor(out=hs_s[:, tg, :], in0=K[2][:, 0:32],
                                                                                   in1=T[4][:, 0:32], op=ALU.mult))
                    A("pe", ["hs_s", "W2b"], ["K3"], lambda e, tg=tg, typ=typ: e.matmul(
                        K3v[0:32, tg, :], lhsT=hs_s[:, tg, :], rhs=W2b[:, typ, :], start=True, stop=True))
                A("act", ["K3"], ["kcr"], lambda e: e.activation(out=kcr[0:32], in_=K3v[0:32, 0:2, :], func=AF.Copy))
                A("act", ["K3"], ["vca"], lambda e: e.activation(out=vca[:, :, 0:64], in_=K3v[0:32, 2:4, :], func=AF.Copy))
                A("dve", ["kcr"], ["LG"], lambda e: e.tensor_tensor(out=LG[0:32, 0:2, 0:64], in0=kcr[0:32], in1=kcr[0:32], op=ALU.mult))
                A("dve", ["LG"], ["st4"], lambda e: e.tensor_reduce(out=st4[0:32, 0:2], in_=LG[0:32, 0:2, 0:64], axis=AX.X, op=ALU.add))
                A("act", ["st4"], ["st4"], lambda e: e.activation(out=st4[0:32, 0:2], in_=st4[0:32, 0:2], func=AF.Ln,
                                                                  scale=1.0 / 64, bias=EPS))
                A("act", ["st4"], ["st4"], lambda e: e.activation(out=st4[0:32, 0:2], in_=st4[0:32, 0:2], func=AF.Exp, scale=-0.5))
                A("dve", ["kcr", "st4"], ["kcr"], lambda e: e.tensor_tensor(
                    out=kcr[0:32], in0=kcr[0:32], in1=st4[0:32, 0:2].unsqueeze(2).to_broadcast([32, 2, 64]), op=ALU.mult))
                A("dve", ["kcr", "gvec"], ["kcbs"], lambda e: e.tensor_tensor(
                    out=kcbs[:], in0=kcr[0:32], in1=gvec[0:32, 0:64].unsqueeze(1).to_broadcast([32, 2, 64]), op=ALU.mult))
                for g in range(2):
                    A("pe", ["kcbs", "idb"], ["K0"], lambda e, g=g: e.transpose(
                        out=K0[0:64, 512 + g * 32:512 + (g + 1) * 32], in_=kcbs[:, g, :], identity=idb[0:32, 0:32]))
                A("dve", ["K0"], ["KCa"], lambda e: e.tensor_copy(
                    out=KCa[0:64, :, :], in_=K0[0:64, 512:576].rearrange("p (g j) -> p g j", g=2)))
                A("sp", [], ["xs1"], lambda e: e.dma_start(out=WP[:].rearrange("p (t c) -> p t c", t=4),
                                                           in_=cwin[b].rearrange("(t p) c -> p t c", p=128)), dma=True)
                A("dve", ["xs1"], ["PGB0", "PGB1"], lambda e: e.tensor_copy(out=WPB[:].rearrange("p t c -> p (t c)"), in_=WP[:]))
                for tw in range(4):
                    for g in range(2):
                        A("pe", ["PGB0", "PGB1", "idb"], ["K0"], lambda e, tw=tw, g=g: e.transpose(
                            out=K0[0:64, (tw * 2 + g) * 128:(tw * 2 + g + 1) * 128], in_=WPB[:, tw, g * 64:(g + 1) * 64],
                            identity=idb[:]))
                A("dve", ["K0"], ["KTw"], lambda e: e.tensor_copy(
                    out=KTw[0:64, :, 0:512].rearrange("p g (t k) -> p g t k", t=4),
                    in_=K0[0:64, :].rearrange("p (t g k) -> p g t k", t=4, g=2)))
                A("act", ["xs1"], ["VVw"], lambda e: e.activation(
                    out=VVw[:, 0:4, :, 0:64], in_=WP[:].rearrange("p (t c) -> p t c", t=4)[:, :, 128:256].rearrange(
                        "p t (g d) -> p t g d", g=2), func=AF.Copy))
                A("dve", ["KN"], ["KTw"], lambda e: e.tensor_copy(out=KTw[0:64, :, 512:520], in_=KN[:, 2:4, b * 8:(b + 1) * 8]))
                A("dve", ["QTs"], ["QTb"], lambda e: e.tensor_copy(out=QTb[0:64, :, :], in_=QTs[:, :, b * 8:(b + 1) * 8]))
                for g in range(2):
                    qrhs = QTb[:, g * 4:(g + 1) * 4, :].rearrange("p h t -> p (h t)")
                    qrhs68 = QTb[0:68, g * 4:(g + 1) * 4, :].rearrange("p h t -> p (h t)")
                    A("pe", ["KCa", "QTb"], ["K3"], lambda e, g=g, qrhs68=qrhs68: e.matmul(
                        K[3][0:32, 0:32], lhsT=KCa[0:68, g, :], rhs=qrhs68, start=True, stop=True))
                    A("act", ["K3"], ["PTc"], lambda e: e.activation(out=PTc, in_=K[3][0:32, 0:32], func=AF.Exp))
                    A("pe", ["vca", "PTc"], ["K7"], lambda e, g=g: e.matmul(K[7][0:65, 0:32], lhsT=vca[:, g, :], rhs=PTc,
                                                                         start=True, stop=True))
                    A("act", ["K7"], ["OcT"], lambda e, g=g: e.activation(
                        out=OcT[:, g * 4:(g + 1) * 4, b * 8:(b + 1) * 8], in_=K[7][0:65, 0:32].rearrange("p (h t) -> p h t", h=4),
                        func=AF.Copy))
                    A("pe", ["PTc", "idb"], ["K0"], lambda e: e.transpose(out=K0[0:32, 640:672], in_=PTc, identity=idb[0:32, 0:32]))
                    A("dve", ["K0"], ["EX"], lambda e: e.tensor_copy(out=EX[0:32, 0, 0:32], in_=K0[0:32, 640:672]))
                    A("dve", ["EX"], ["st4b"], lambda e: e.tensor_reduce(out=st4b[0:32, 0:1], in_=EX[0:32, 0, 0:32], axis=AX.X, op=ALU.add))
                    A("dve", ["st4b"], ["st4b"], lambda e: e.reciprocal(out=st4b[0:32, 0:1], in_=st4b[0:32, 0:1]))
                    A("dve", ["EX", "st4b"], ["EX"], lambda e: e.tensor_scalar(
                        out=EX[0:32, 0, 0:32], in0=EX[0:32, 0, 0:32], scalar1=st4b[0:32, 0:1], scalar2=None, op0=ALU.mult))
                    A("pe", ["csts", "EX"], ["K3"], lambda e: e.matmul(K[3][0:8, 64:96], lhsT=selT, rhs=EX[0:32, 0, 0:32],
                                                                      start=True, stop=True))
                    A("dve", [], ["SC"], lambda e: e.memset(SC[0:8, 0:33], 0.0))
                    A("dve", ["K3"], ["SC"], lambda e: e.tensor_copy(out=SC[0:8, 0:32], in_=K[3][0:8, 64:96]))
                    A("dve", ["SC", "csts"], ["SC"], lambda e: e.tensor_tensor(out=SC[0:8, 0:33], in0=SC[0:8, 0:33], in1=keepS, op=ALU.mult))
                    A("dve", ["SC", "csts"], ["SC"], lambda e: e.tensor_tensor(out=SC[0:8, 0:33], in0=SC[0:8, 0:33], in1=addS, op=ALU.add))
                    A("dve", ["SC"], ["m8"], lambda e: e.max(out=m8[0:8], in_=SC[0:8, 0:33]))
                    A("dve", ["SC", "m8"], ["WK"], lambda e: e.match_replace(out=WK[0:8, 0:33], in_to_replace=m8[0:8],
                                                                              in_values=SC[0:8, 0:33], imm_value=-2.0))
                    A("dve", ["WK"], ["m8"], lambda e: e.max(out=m8[0:8], in_=WK[0:8, 0:33]))
                    A("dve", ["m8"], ["thr"], lambda e: e.tensor_reduce(out=thr[0:8], in_=m8[0:8], axis=AX.X, op=ALU.min))
                    A("dve", ["SC", "thr"], ["SEL"], lambda e: e.tensor_scalar(
                        out=SEL[0:8, 0:33], in0=SC[0:8, 0:33], scalar1=thr[0:8, 0:1], scalar2=None, op0=ALU.is_ge))
                    A("dve", ["SEL"], ["MBF"], lambda e: e.tensor_scalar(
                        out=MBF[0:8, 0:33], in0=SEL[0:8, 0:33], scalar1=-1.0, scalar2=BIG, op0=ALU.add, op1=ALU.mult))
                    A("pe", ["MBF", "idb"], ["K0"], lambda e: e.transpose(out=K0[0:33, 704:712], in_=MBF[0:8, 0:33],
                                                                          identity=idb[0:8, 0:8]))
                    A("dve", ["K0"], ["RHSMs"], lambda e: e.tensor_copy(
                        out=RHSMs[0:33], in_=K0[0:33, 704:712].unsqueeze(1).to_broadcast([33, 4, 8])))
                    for kt in range(17):
                        dst = K[5][:, kt * 32:(kt + 1) * 32] if kt < 16 else K[6][:, 0:32]
                        kb = "K5" if kt < 16 else "K6"
                        A("pe", ["KTs", "QTb"], [kb], lambda e, g=g, kt=kt, dst=dst, qrhs=qrhs: e.matmul(
                            dst, lhsT=KTs[:, g, kt * 128:(kt + 1) * 128], rhs=qrhs, start=True, stop=False))
                        A("pe", ["Eexp", "RHSMs"], [kb], lambda e, kt=kt, dst=dst: e.matmul(
                            dst, lhsT=Eexp[:, kt * 128:(kt + 1) * 128], rhs=RHSMs[:].rearrange("p h t -> p (h t)"),
                            start=False, stop=(kt < 16)))
                        if kt == 16:
                            A("pe", ["idb", "MSKs"], [kb], lambda e, dst=dst: e.matmul(dst, lhsT=idb[:], rhs=MSKs[:, 0, :],
                                                                                       start=False, stop=True))
                    A("act", ["K5"], ["PTs"], lambda e: e.activation(out=PTs[:, 0:512], in_=K[5][:], func=AF.Exp))
                    A("act", ["K6"], ["PTs"], lambda e: e.activation(out=PTs[:, 512:544], in_=K[6][:, 0:32], func=AF.Exp))
                    for kt in range(17):
                        A("pe", ["PTs", "VVs"], ["K7"], lambda e, g=g, kt=kt: e.matmul(
                            K[7][0:65, 32:64], lhsT=VVs[:, kt, g, :], rhs=PTs[:, kt * 32:(kt + 1) * 32],
                            start=(kt == 0), stop=(kt == 16)))
                    A("act", ["K7"], ["OsT"], lambda e, g=g: e.activation(
                        out=OsT[:, g * 4:(g + 1) * 4, b * 8:(b + 1) * 8], in_=K[7][0:65, 32:64].rearrange("p (h t) -> p h t", h=4),
                        func=AF.Copy))
                    for s_ in range(5):
                        dst = K[4][:, s_ * 32:(s_ + 1) * 32]
                        msk = {0: 1, 4: 0}.get(s_)
                        A("pe", ["KTw", "QTb"], ["K4"], lambda e, g=g, s_=s_, dst=dst, qrhs=qrhs, msk=msk: e.matmul(
                            dst, lhsT=KTw[:, g, s_ * 128:(s_ + 1) * 128], rhs=qrhs, start=True, stop=(msk is None)))
                        if msk is not None:
                            A("pe", ["idb", "MSKs"], ["K4"], lambda e, dst=dst, msk=msk: e.matmul(
                                dst, lhsT=idb[:], rhs=MSKs[:, msk, :], start=False, stop=True))
                    A("act", ["K4"], ["PTw"], lambda e: e.activation(out=PTw, in_=K[4][:, 0:160], func=AF.Exp))
                    for s_ in range(5):
                        A("pe", ["PTw", "VVw"], ["K7"], lambda e, g=g, s_=s_: e.matmul(
                            K[7][0:65, 64:96], lhsT=VVw[:, s_, g, :], rhs=PTw[:, s_ * 32:(s_ + 1) * 32],
                            start=(s_ == 0), stop=(s_ == 4)))
                    A("act", ["K7"], ["OwT"], lambda e, g=g: e.activation(
                        out=OwT[:, g * 4:(g + 1) * 4, b * 8:(b + 1) * 8], in_=K[7][0:65, 64:96].rearrange("p (h t) -> p h t", h=4),
                        func=AF.Copy))

            for b_ in range(16):
                sample_seq(b_)
            for src, ks, dst, kd in ((OcT, "OcT", OSEL, "OSEL"), (OsT, "OsT", OSEL, "OSEL"), (OwT, "OwT", OWIN, "OWIN")):
                for h in range(8):
                    A("pe", [ks, "idb"], ["K0"], lambda e, src=src, h=h: e.transpose(
                        out=K0[:, h * 66:h * 66 + 65], in_=src[:, h, :], identity=idb[0:65, 0:65]))
                A("act", ["K0"], [kd], lambda e, dst=dst: e.activation(
                    out=dst[:], in_=K0[:, 0:528].rearrange("p (h d) -> p h d", h=8)[:, :, 0:65], func=AF.Copy))
                if ks == "OcT":
                    A("dve", ["OSEL"], ["st8"], lambda e: e.reciprocal(out=st8[:], in_=OSEL[:, :, 64]))
                    A("dve", ["OSEL", "st8"], ["OCMP"], lambda e: e.tensor_tensor(
                        out=OCMP[:], in0=OSEL[:, :, 0:64], in1=st8[:].unsqueeze(2).to_broadcast([128, 8, 64]), op=ALU.mult))
            finish(xs[0], "xs0", ys[:, :], "ys", NOWN if debug else None)
        BKEYS = WK_ + AKEYS + ["Eexp", "MSK", "xT0", "xT1", "OC", "OCT", "KD", "VB", "QD", "KDI", "WUD", "QN", "qkT", "AT", "Sb0", "Sb1", "PB", "hs", "RHSM0", "RHSM1", "PT4"] + \
            ["S0_%d" % b_ for b_ in range(16)] + SKEYS
        if STAGE >= 7:
            A("dve", [], BKEYS + ["fdummy"], lambda e: e.memset(fdummy[:], 0.0))
            A("sp", [], ["lnmc"], lambda e: e.dma_start(out=lnmc[:], in_=lnmcol), dma=True)
            stg = [(T[i_], "T%d" % i_) for i_ in range(6)]
            cnt_ = [0]

            def wload(src, dst, scale_ap):
                st, ks = stg[cnt_[0] % 6]
                use_act = (cnt_[0] % 2 == 0)
                cnt_[0] += 1
                A("sp", [], [ks], lambda e: e.dma_start(out=st[:], in_=src), dma=True)
                rk = [ks] + (["lnmc"] if scale_ap is not None else [])
                if use_act:
                    if scale_ap is not None:
                        A("act", rk, ["WUD"], lambda e: e.activation(out=dst, in_=st[:], func=AF.Copy, scale=scale_ap))
                    else:
                        A("act", rk, ["WUD"], lambda e: e.activation(out=dst, in_=st[:], func=AF.Copy))
                else:
                    if scale_ap is not None:
                        A("dve", rk, ["WUD"], lambda e: e.tensor_scalar(out=dst, in0=st[:], scalar1=scale_ap, scalar2=None,
                                                                         op0=ALU.mult))
                    else:
                        A("dve", rk, ["WUD"], lambda e: e.tensor_copy(out=dst, in_=st[:]))

            for k in range(8):
                for q8 in range(8):
                    wload(w_up[k * 128:(k + 1) * 128, q8 * 512:(q8 + 1) * 512], WU[:, k, q8 * 512:(q8 + 1) * 512],
                          lnmc[:, k:k + 1])
            for f in range(32):
                for q2 in range(2):
                    wload(w_down[f * 128:(f + 1) * 128, q2 * 512:(q2 + 1) * 512], WD[:, f, q2 * 512:(q2 + 1) * 512], None)

            def mlp_tile(hsrc, ydst, ky, idx):
                X, kx = xs[idx % 2], "xs%d" % (idx % 2)
                A("sp", [ky], [kx], lambda e: e.dma_start(out=X[:], in_=hsrc), dma=True)
                A("act", [kx], ["OSEL", "ss"], lambda e: e.activation(
                    out=OSEL[:].rearrange("p h d -> p (h d)")[:, 0:512], in_=X[:, 0:512], func=AF.Square, accum_out=ss[:]))
                A("act", [kx], ["OSEL", "rstd"], lambda e: e.activation(
                    out=OSEL[:].rearrange("p h d -> p (h d)")[:, 0:512], in_=X[:, 512:1024], func=AF.Square,
                    accum_out=rstd[:]))
                A("dve", ["ss", "rstd"], ["ss"], lambda e: e.tensor_tensor(out=ss[:], in0=ss[:], in1=rstd[:], op=ALU.add))
                A("act", ["ss"], ["rstd"], lambda e: e.activation(out=rstd[:], in_=ss[:], func=AF.Ln, scale=1.0 / D, bias=EPS))
                A("act", ["rstd"], ["rstd"], lambda e: e.activation(out=rstd[:], in_=rstd[:], func=AF.Exp, scale=-0.5))
                A("dve", [kx, "rstd"], ["xb"], lambda e: e.tensor_scalar(out=xb[:], in0=X[:], scalar1=rstd[:, 0:1],
                                                                         scalar2=None, op0=ALU.mult))
                for k in range(8):
                    A("pe", ["xb", "idb"], ["K0"], lambda e, k=k: e.transpose(
                        out=K0[:, k * 128:(k + 1) * 128], in_=xb[:, k * 128:(k + 1) * 128], identity=idb[:]))
                A("dve", ["K0"], ["QTa"], lambda e: e.tensor_copy(out=QTa[:].rearrange("p k t -> p (k t)"), in_=K0[:]))
                def up_round(rnd):
                    ub = (rnd % 2) * 8
                    ku = "uT%d" % (rnd % 2)
                    for j in range(8):
                        f = rnd * 8 + j
                        bank = 1 + (f % 4)
                        kb = "K%d" % bank
                        R, kr = PTb[f % 2], "PTb%d" % (f % 2)
                        for k in range(8):
                            A("pe", ["QTa", "WUD"], [kb], lambda e, f=f, k=k, bank=bank: e.matmul(
                                K[bank][:, 0:128], lhsT=WU[:, k, f * 128:(f + 1) * 128], rhs=QTa[:, k, :],
                                start=(k == 0), stop=(k == 7)))
                        A("act", [kb], [kr], lambda e, bank=bank, R=R: e.activation(out=R[:, 0:128], in_=K[bank][:, 0:128],
                                                                                    func=AF.Relu))
                        A("dve", [kr], [ku], lambda e, j=j, ub=ub, R=R: e.tensor_tensor(
                            out=uT[:, ub + j, :], in0=R[:, 0:128], in1=R[:, 0:128], op=ALU.mult))

                def down_round(rnd):
                    ub = (rnd % 2) * 8
                    ku = "uT%d" % (rnd % 2)
                    for n in range(2):
                        bank = 5 + n
                        kb = "K%d" % bank
                        for j in range(8):
                            f = rnd * 8 + j
                            A("pe", [ku, "WUD"], [kb], lambda e, f=f, j=j, ub=ub, n=n, bank=bank: e.matmul(
                                K[bank][:], lhsT=uT[:, ub + j, :], rhs=WD[:, f, n * 512:(n + 1) * 512],
                                start=(f == 0), stop=(f == 31)))

                up_round(0)
                up_round(1)
                down_round(0)
                up_round(2)
                down_round(1)
                up_round(3)
                down_round(2)
                down_round(3)
                for n in range(2):
                    bank = 5 + n
                    kb = "K%d" % bank
                    A("dve", [kb, kx], ["H"], lambda e, n=n, bank=bank: e.tensor_tensor(
                        out=H[:, n * 512:(n + 1) * 512], in0=K[bank][:], in1=X[:, n * 512:(n + 1) * 512], op=ALU.add))
                A("pool", ["H"], [ky], lambda e: e.dma_start(out=ydst, in_=H[:]), dma=True, final=True)

            for i_ in range(NOWN):
                mlp_tile(yp[i_ * 128:(i_ + 1) * 128, :], yp[i_ * 128:(i_ + 1) * 128, :], "yp%d" % i_, i_)
            if STAGE >= 8:
                mlp_tile(ys[:, :], ys[:, :], "ys", NOWN)
        S.emit()
    return nc


def make_sample_consts():
    f32 = np.float32
    slope = 2.0 ** (-(np.arange(8) + 1.0))
    csts = np.zeros((128, 74), f32)
    for hh in range(4):
        for t in range(8):
            csts[hh * 8 + t, t] = 1.0
    keep = np.ones(33, f32)
    add = np.zeros(33, f32)
    keep[0] = keep[32] = 0.0
    add[0] = add[32] = 5.0
    csts[:, 8:41] = keep[None]
    csts[:, 41:74] = add[None]
    iota = np.arange(128, dtype=f32).reshape(128, 1)
    sq = np.zeros((8, 64), f32)
    for h in range(8):
        for t in range(8):
            sq[0, h * 8 + t] = 128.0 * slope[h]
            sq[1, h * 8 + t] = slope[h]
            sq[2, h * 8 + t] = -128.0 * 16.0 * slope[h]
            sq[3, h * 8 + t] = -slope[h] * t
    j = np.arange(32)
    sq[4, 0:32] = j // 2
    sq[5, 0:32] = 64 * (j % 2) + 63
    sq[6, 0:32] = 1.0
    sq[7, 0:32] = 1.0
    sm = np.zeros((128, 192), f32)
    r = np.arange(128)
    for par in range(2):
        for jj in range(64):
            sm[2 * jj + par, par * 64 + jj] = 1.0
    kk = r[:, None]
    tt = np.arange(8)[None, :]
    mn = np.where(kk <= tt, 0.0, -BIG).astype(f32)
    mw = np.where(kk > tt, 0.0, -BIG).astype(f32)
    sm[:, 128:160] = np.tile(mn, (1, 4))
    sm[:, 160:192] = np.tile(mw, (1, 4))
    return csts, iota, sq, sm


def make_consts(half):
    f32 = np.float32
    t = np.arange(128)
    slope = 2.0 ** (-(np.arange(8) + 1.0))
    cst = np.zeros((128, 747), f32)
    ac = np.zeros((128, 8, 66), f32)
    ac[:] = (slope[:, None] * 64.0 * np.arange(66)[None, :])[None]
    if half == 1:
        ac[:, :, 0:2] -= BIG
    cst[:, 0:528] = ac.reshape(128, 528)
    cst[:, 528] = np.where(t >= 63, 0.0, -BIG)
    cst[:, 529] = np.where(t == 127, 0.0, -BIG)
    rv = np.ones((128, NOWN), f32)
    if half == 0:
        rv[:63, 0] = 0.0
    cst[:, 530:547] = rv
    keep = np.ones(66, f32)
    add = np.zeros(66, f32)
    j0 = 2 * half
    keep[j0], add[j0] = 0.0, 5.0
    if half == 1:
        keep[0:2], add[0:2] = 0.0, -1.0
    cst[:, 547:613] = keep[None]
    cst[:, 613:679] = add[None]
    cst[:, 679] = np.where(t < 64, 5.0, 0.0)
    cst[:, 680] = np.where(t < 64, 0.0, 1.0)
    cst[:, 681] = np.where(t < 64, -1.0, 5.0)
    s_ = t % 64
    cst[:, 682:746] = (s_[:, None] <= np.arange(64)[None, :]).astype(f32)
    kk, tt = t[:, None], t[None, :]
    tri = np.where(kk <= tt, 0.0, -BIG).astype(f32)
    anti = np.where(kk > tt, 0.0, -BIG).astype(f32)
    full = np.zeros((128, 128), f32) if half == 0 else np.full((128, 128), -BIG, f32)
    m0anti = anti if half == 0 else np.full((128, 128), -BIG, f32)
    masks = np.stack([np.tile(m, (1, 4)) for m in (tri, anti, full, m0anti)], axis=1).astype(f32)
    eexp = np.zeros((66, NT * 128), f32)
    n = np.arange(NT * 128)
    eexp[n // 64, n] = 1.0
    kpos = np.stack([(n // 128).astype(f32), (n % 128).astype(f32), np.ones_like(n, f32), np.ones_like(n, f32)], 0)
    qpos = np.zeros((NOWN, 4, 2, 4, 128), f32)
    for i in range(NOWN):
        v = 2 * i
        for g in range(2):
            for hh in range(4):
                s = slope[g * 4 + hh]
                qpos[i, 0, g, hh, :] = 128.0 * s
                qpos[i, 1, g, hh, :] = s
                qpos[i, 2, g, hh, :] = -128.0 * v * s
                qpos[i, 3, g, hh, :] = -s * t
    return cst, masks, eexp, kpos.astype(f32), qpos.reshape(NOWN, 4, 1024)


_NC = None


def kernel(x_prompt, x_sample, cache_kv, cache_win, state_rnn, page_table, ln_mix, w_in, q_norm, k_norm,
           cmp_pe, cmp_w1, cmp_w2, attn_out_norm, rnn_lb_logits, rnn_out_norm, w_out, ln_mlp, w_up, w_down,
           _debug=False):
    global _NC
    f32 = np.float32
    x_prompt = np.asarray(x_prompt, f32)
    x_sample = np.asarray(x_sample, f32)
    cache_win = np.asarray(cache_win, f32)
    state_rnn = np.asarray(state_rnn, f32)
    k_norm = np.asarray(k_norm, f32)
    cmp_pe = np.asarray(cmp_pe, f32)
    if _NC is None or _NC[0] != _debug:
        _NC = (_debug, build_nc(_debug))
    nc = _NC[1]
    t = np.arange(128)

    def tmats(c):
        same = (t[:, None] // c == t[None, :] // c)
        return ((t[:, None] <= t[None, :]) & same).astype(f32), ((t[:, None] > t[None, :]) & same).astype(f32)

    tc_p, tr_p = tmats(64)
    tc_s, tr_s = tmats(8)
    rowmask = (t[:, None] // 8 == np.arange(16)[None, :]).astype(f32)
    gv = np.concatenate([k_norm[0, 0], k_norm[0, 1], k_norm[0, 2], np.asarray(q_norm, f32)[0],
                         np.asarray(attn_out_norm, f32)[0], np.asarray(rnn_out_norm, f32)[0]])
    peT = np.ascontiguousarray(cmp_pe[0].reshape(2, 32, 2, 64).transpose(2, 3, 0, 1).reshape(128, 2, 32))
    common = {
        "w_in": np.ascontiguousarray(np.asarray(w_in, f32)[0]),
        "w_out": np.ascontiguousarray(np.asarray(w_out, f32)[0]),
        "lncol": np.ascontiguousarray(np.asarray(ln_mix, f32)[0].reshape(8, 128).T),
        "gvecd": np.ascontiguousarray(np.broadcast_to(gv[None], (128, 1280))),
        "lbl": np.ascontiguousarray(np.broadcast_to(np.asarray(rnn_lb_logits, f32)[None], (128, 2, 512))),
        "ident": np.eye(128, dtype=f32),
        "w_up": np.ascontiguousarray(np.asarray(w_up, f32)[0]),
        "w_down": np.ascontiguousarray(np.asarray(w_down, f32)[0]),
        "lnmcol": np.ascontiguousarray(np.asarray(ln_mlp, f32)[0].reshape(8, 128).T),
        "tmat": np.ascontiguousarray(np.stack([tc_p, tr_p, tc_s, tr_s], axis=1)),
        "rowmask": rowmask,
        "peTd": peT,
        "cmp_w1": np.ascontiguousarray(np.asarray(cmp_w1, f32)[0]),
        "cmp_w2": np.ascontiguousarray(np.asarray(cmp_w2, f32)[0].transpose(1, 0, 2)),
    }
    zt = np.zeros((128, D), f32)
    cc = [make_consts(0), make_consts(1)]
    csts, iota, sqd, smd = make_sample_consts()
    cache2 = np.ascontiguousarray(np.asarray(cache_kv, f32)[0].reshape(2560 * 128, 512))
    ptab = np.asarray(page_table).astype(np.int32)
    common.update({"cache": cache2, "cstsd": csts, "iotad": iota, "sqd": sqd, "smd": smd})
    in_maps = []
    for c in range(8):
        s, half = c // 2, c % 2
        xvv = np.concatenate([x_prompt[s], zt], 0) if half == 0 else np.concatenate([zt, x_prompt[s]], 0)
        m = dict(common)
        m["xv"] = np.ascontiguousarray(xvv)
        m["xsm"] = np.ascontiguousarray(x_sample[16 * c:16 * c + 16].reshape(128, D))
        m["cwin"] = np.ascontiguousarray(cache_win[0, 16 * c:16 * c + 16].reshape(16, 512, 256))
        m["srnn"] = np.ascontiguousarray(state_rnn[0, 16 * c:16 * c + 16])
        m["ptrep"] = np.ascontiguousarray(np.broadcast_to(ptab[16 * c:16 * c + 16].reshape(1, 256), (128, 256)))
        m["cstd"], m["masks4"], m["eexp"], m["kposrows"], m["qposrows"] = cc[half]
        in_maps.append(m)
    res = run_bass_kernel_spmd(nc, in_maps, core_ids=list(range(8))).results

    y_prompt = np.zeros((4, 4096, D), f32)
    y_sample = np.zeros((128, 8, D), f32)
    kv_prompt = np.zeros((1, 4, 4096, 4, 2, 64), f32)
    kv_sample = np.zeros((1, 128, 8, 4, 2, 64), f32)
    win_prompt = np.zeros((1, 4, 512, 2, 2, 64), f32)
    win_sample = np.zeros((1, 128, 512, 2, 2, 64), f32)
    rnn_prompt = np.zeros((1, 4, 8, 64, 64), f32)
    rnn_sample = np.zeros((1, 128, 8, 64, 64), f32)
    for c in range(8):
        r = res[c]
        s, half = c // 2, c % 2
        ypc = r["yp"].reshape(NOWN, 128, D)
        yps = y_prompt[s].reshape(32, 128, D)
        if half == 0:
            yps[0::2] = ypc[0:16]
            kv_prompt[0, s] = r["kvp"][:4096].reshape(4096, 4, 2, 64)
            win_prompt[0, s] = r["winp"][:512].reshape(512, 2, 2, 64)
        else:
            yps[1::2] = ypc[1:17]
            rnn_prompt[0, s] = r["rnnp"]
        kv_sample[0, 16 * c:16 * c + 16] = r["kvs"].reshape(16, 8, 4, 2, 64)
        y_sample[16 * c:16 * c + 16] = r["ys"].reshape(16, 8, D)
        win_sample[0, 16 * c:16 * c + 16] = r["wins"].reshape(16, 512, 2, 2, 64)
        rnn_sample[0, 16 * c:16 * c + 16] = r["rnns"]
    if _debug:
        kernel._dbg = [res[c]["dbg"] for c in range(8)]
    return (y_prompt, y_sample, kv_prompt, kv_sample, win_prompt, win_sample, rnn_prompt, rnn_sample)
```

```python
import contextlib
import numpy as np
import concourse.bass as bass
import concourse.mybir as mybir
from concourse.bass_utils import run_bass_kernel_spmd

F32 = mybir.dt.float32
BF16 = mybir.dt.bfloat16
AF = mybir.ActivationFunctionType
ALU = mybir.AluOpType
AX = mybir.AxisListType

NT = 33
D = 1024
INW = 3352
C_Q, C_KV, C_G, C_RQ, C_RF, C_RI, C_RG = 0, 512, 1280, 1304, 1816, 2328, 2840
EPS = 1e-6

ENGS = ("pe", "act", "dve", "pool", "sp")
DMA_POOL = 8


class Op:
    __slots__ = ("eng", "fn", "deps", "dma", "idx", "needed", "token")

    def __init__(self, eng, fn, deps, dma, idx):
        self.eng, self.fn, self.deps, self.dma, self.idx = eng, fn, deps, dma, idx
        self.needed = False
        self.token = None


class Sched:
    def __init__(self, nc):
        self.nc = nc
        self.ops = []
        self.last_w = {}
        self.readers = {}
        self.final = []

    def add(self, eng, fn, r=(), w=(), dma=False, final=False, rg=0):
        idx = len(self.ops)
        deps = {}
        if eng == "pe":
            prev = getattr(self, "prev_pe", None)
            if prev is not None and prev[1] != rg:
                deps[prev[0]] = "force"
            self.prev_pe = (idx, rg)
        for k in r:
            lw = self.last_w.get(k)
            if lw is not None and deps.get(lw) != "force":
                deps[lw] = True
        for k in w:
            lw = self.last_w.get(k)
            if lw is not None and deps.get(lw) != "force":
                deps[lw] = True
            for rd in self.readers.get(k, ()):
                if rd not in deps:
                    deps[rd] = False
        for k in r:
            self.readers.setdefault(k, []).append(idx)
        for k in w:
            self.last_w[k] = idx
            self.readers[k] = []
        self.ops.append(Op(eng, fn, deps, dma, idx))
        if final:
            self.final.append(idx)
        return idx

    def emit(self):
        nc = self.nc
        ops = self.ops
        for op in ops:
            nd = {}
            for d, strong in op.deps.items():
                p = ops[d]
                if p.eng == op.eng and not p.dma:
                    if op.eng == "pe" and strong != "force":
                        continue
                nd[d] = strong
            best = {}
            keep = {}
            for d, strong in nd.items():
                p = ops[d]
                if p.dma:
                    keep[d] = strong
                elif p.eng not in best or d > best[p.eng]:
                    best[p.eng] = d
            for d in best.values():
                keep[d] = True
            op.deps = keep
            for d in keep:
                ops[d].needed = True
        for f in self.final:
            ops[f].needed = True
        cnt = {e: 0 for e in ENGS}
        dcnt = {e: 0 for e in ENGS}
        with contextlib.ExitStack() as es:
            NEP = 8
            EPOCH = 1500
            csem = {e: [es.enter_context(nc.semaphore("c_%s%d" % (e, i))) for i in range(NEP if e in ("pe", "dve", "act") else 1)]
                    for e in ENGS}
            dsem = {e: [es.enter_context(nc.semaphore("d_%s%d" % (e, i))) for i in range(DMA_POOL)]
                    for e in ("sp", "pool", "act")}
            for op in ops:
                if op.dma:
                    j = dcnt[op.eng]
                    dcnt[op.eng] += 1
                    op.token = (dsem[op.eng][j % DMA_POOL], 16 * (j // DMA_POOL + 1))
                elif op.needed:
                    ep = cnt[op.eng] // EPOCH
                    assert ep < len(csem[op.eng]), (op.eng, cnt[op.eng])
                    op.token = (csem[op.eng][ep], cnt[op.eng] % EPOCH + 1)
                    cnt[op.eng] += 1
            per = {e: [op for op in ops if op.eng == e] for e in ENGS}
            final_tokens = [ops[f].token for f in self.final]

            def run(engname, eng):
                known = {}
                for op in per[engname]:
                    for d in op.deps:
                        sem, val = ops[d].token
                        if known.get(id(sem), 0) >= val:
                            continue
                        eng.wait_ge(sem, val)
                        known[id(sem)] = val
                    if op.dma:
                        sem, val = op.token
                        if val > 16 and known.get(id(sem), 0) < val - 16:
                            eng.wait_ge(sem, val - 16)
                            known[id(sem)] = val - 16
                        op.fn(eng).then_inc(sem, 16)
                    else:
                        ins = op.fn(eng)
                        if op.needed:
                            ins.then_inc(op.token[0], 1)
                if engname == "sp":
                    for sem, val in final_tokens:
                        if known.get(id(sem), 0) >= val:
                            continue
                        eng.wait_ge(sem, val)
                        known[id(sem)] = val

            with nc.Block() as block:
                @block.tensor
                def _(e):
                    run("pe", e)

                @block.scalar
                def _(e):
                    run("act", e)

                @block.vector
                def _(e):
                    run("dve", e)

                @block.gpsimd
                def _(e):
                    run("pool", e)

                @block.sync
                def _(e):
                    run("sp", e)


import os
BIG = 30000.0
STAGE = int(os.environ.get("KSTAGE", "9"))
SUB = int(os.environ.get("KSUB", "9"))
SKIP = os.environ.get("KSKIP", "")
NOWN = 17
SCALE = 0.125
DEBUG = False


def build_nc(debug=False):
    nc = bass.Bass("TRN2", target_bir_lowering=False)

    def din(name, shape, dt=F32):
        return nc.dram_tensor(name, list(shape), dt, kind="ExternalInput").ap()

    def dout(name, shape, dt=F32):
        return nc.dram_tensor(name, list(shape), dt, kind="ExternalOutput").ap()

    xv = din("xv", [NT * 128, D])
    xsm = din("xsm", [128, D])
    w_in = din("w_in", [D, INW])
    w_out = din("w_out", [D, D])
    w_up = din("w_up", [D, 4 * D])
    w_down = din("w_down", [4 * D, D])
    lnmcol = din("lnmcol", [128, 8])
    lncol = din("lncol", [128, 8])
    gvec_d = din("gvecd", [128, 1280])
    lbl = din("lbl", [128, 2, 512])
    ident = din("ident", [128, 128])
    tmat = din("tmat", [128, 4, 128])
    rowmask = din("rowmask", [128, 16])
    cst_d = din("cstd", [128, 747])
    masks_d = din("masks4", [128, 4, 512])
    eexp_d = din("eexp", [66, NT * 128])
    kpos_d = din("kposrows", [4, NT * 128])
    qpos_d = din("qposrows", [NOWN, 4, 1024])
    pet_d = din("peTd", [128, 2, 32])
    w1_d = din("cmp_w1", [2, 4096, 128])
    w2_d = din("cmp_w2", [128, 2, 64])
    cwin = din("cwin", [16, 512, 256])
    cache_d = din("cache", [327680, 512])
    ptrep_d = din("ptrep", [128, 256], mybir.dt.int32)
    csts_d = din("cstsd", [128, 74])
    iota_d = din("iotad", [128, 1])
    sq_d = din("sqd", [8, 64])
    sm_d = din("smd", [128, 192])
    srnn = din("srnn", [16, 8, 64, 64])

    yp = dout("yp", [NOWN * 128, D])
    ys = dout("ys", [128, D])
    kvp = dout("kvp", [NT * 128, 512])
    kvs = dout("kvs", [128, 512])
    winp = dout("winp", [5 * 128, 256])
    wins = dout("wins", [16, 512, 256])
    rnnp = dout("rnnp", [8, 64, 64])
    rnns = dout("rnns", [16, 8, 64, 64])
    if debug:
        dbg = dout("dbg", [(NOWN + 1) * 128, 5, 520])

    S = Sched(nc)
    with contextlib.ExitStack() as es:
        def sb(name, shape, dt=F32):
            return es.enter_context(nc.sbuf_tensor(name, list(shape), dt))

        def ps(name, shape, dt=F32):
            return es.enter_context(nc.psum_tensor(name, list(shape), dt))

        def A(eng, r, w, fn, **kw):
            S.add(eng, fn, r=r, w=w, **kw)

        BIGW = sb("BIGW", [128, 65536], BF16)
        W = BIGW[:, 0:26816].rearrange("p (k n) -> p k n", k=8)
        ARENA = BIGW[:, 26816:47808]
        WU = BIGW[:, 0:32768].rearrange("p (k n) -> p k n", k=8)
        WD = BIGW[:, 32768:65536].rearrange("p (f n) -> p f n", f=32)
        _bo = [47808]

        def bw(n, parts=128):
            a = _bo[0]
            _bo[0] += n
            assert _bo[0] <= 65536
            return BIGW[0:parts, a:a + n]
        XcT = ARENA[:, 0:8448].rearrange("p (a n) -> p a n", a=4)
        W1b = ARENA[:, 8448:16640].rearrange("p (a j h) -> p a j h", a=2, j=32)
        KTs = ARENA[:, 0:8448].rearrange("p (g n) -> p g n", g=2)
        VVs = ARENA[:, 8448:12738].rearrange("p (t g d) -> p t g d", t=NT, g=2)
        WO = ARENA[:, 12800:20992].rearrange("p (k n) -> p k n", k=8)
        AKEYS = ["XcT", "W1b", "KTs", "VVs", "WO"]
        KTw = sb("KTw", [128, 2, 768], BF16)
        VVw = sb("VVw", [128, 6, 2, 65], BF16)
        Eexp = bw(NT * 128, 128)
        MSK = bw(2048).rearrange("p (m n) -> p m n", m=4)
        cst = sb("cst", [128, 747])
        gvec = sb("gvec", [128, 1280])
        qgs = sb("qgs", [128, 64])
        lnc = sb("lnc", [128, 8])
        lbb = sb("lbb", [128, 512])
        omlb = sb("omlb", [128, 512])
        idb = sb("idb", [128, 128], BF16)
        tm = sb("tm", [128, 4, 128])
        rmask = sb("rmask", [128, 16])
        ones = sb("ones", [128, 1])
        peT = sb("peT", [128, 2, 32])
        W2b = sb("W2b", [128, 2, 64], BF16)
        kcT = sb("kcT", [64, 2, 66], BF16)
        vcb = sb("vcb", [66, 2, 64], BF16)
        fdummy = sb("fdummy", [128, 1])
        acore = cst[:, 0:528].rearrange("p (h j) -> p h j", h=8)
        fix2 = cst[:, 528:530]
        rowvalid = cst[:, 530:547]
        keepc = cst[:, 547:613]
        addc = cst[:, 613:679]
        mlo5 = cst[:, 679:680]
        mhi = cst[:, 680:681]
        c1col = cst[:, 681:682]
        tri2 = cst[:, 682:746]
        xs = [sb("xs%d" % i, [128, D]) for i in range(2)]
        xb = sb("xb", [128, D], BF16)
        xT = [bw(1024).rearrange("p (k t) -> p k t", k=8) for i in range(2)]
        ss = sb("ss", [128, 1])
        rstd = sb("rstd", [128, 1])
        Z = [sb("Z0", [128, 768])] * 2
        zb = sb("zb", [128, 4, 64], BF16)
        sq4 = sb("sq4", [128, 2, 2, 64])
        ms4 = sb("ms4", [128, 2, 2])
        T = [sb("T%d" % i, [128, 512]) for i in range(6)]
        QS = sb("QS", [128, 512])
        RGS = sb("RGS", [128, 512])
        KD = bw(512)
        VB = bw(512)
        QD = bw(512)
        KDI = bw(512)
        qkT = bw(1024).rearrange("p (a t) -> p a t", a=8)
        AT = bw(512).rearrange("p (h t) -> p h t", h=8)
        Sb = [bw(256).rearrange("p (a d) -> p a d", a=4) for i in range(2)]
        ET = sb("ET", [128, 2, 4])
        St = sb("St", [128, 4, 64])
        ZQ = sb("ZQ", [128, 8, 64])
        QN = bw(512).rearrange("p (h d) -> p h d", h=8)
        st8 = sb("st8", [128, 8])
        QTa = sb("QTa", [128, 8, 128], BF16)
        gsig = sb("gsig", [128, 8, 3])
        LG = sb("LG", [128, 4, 66])
        EX = sb("EX", [128, 4, 66])
        PB = bw(264).rearrange("p (h j) -> p h j", h=4)
        st4 = sb("st4", [128, 4])
        st4b = sb("st4b", [128, 4])
        SC = sb("SC", [128, 66])
        WK = sb("WK", [128, 66])
        SEL = sb("SEL", [128, 66])
        m8 = sb("m8", [128, 8])
        thr = sb("thr", [128, 1])
        MBF = sb("MBF", [128, 66], BF16)
        RHSM_ = [bw(512, 128).rearrange("p (h t) -> p h t", h=4) for i in range(2)]
        PT4 = bw(512, 66).rearrange("p (h t) -> p h t", h=4)
        PTb = [sb("PTb%d" % i, [128, 512], BF16) for i in range(3)]
        OSEL = sb("OSEL", [128, 8, 65])
        OWIN = sb("OWIN", [128, 8, 65])
        OCMP = sb("OCMP", [128, 8, 64])
        OR32 = sb("OR32", [128, 8, 64])
        g8 = sb("g8", [128, 8, 3])
        OC = bw(1024)
        OCT = bw(1024).rearrange("p (k t) -> p k t", k=8)
        uT = sb("uT", [128, 16, 128], BF16)
        lnmc = sb("lnmc", [128, 8])
        H = sb("H", [128, D])
        stgq = H
        TMP = T[5][:].rearrange("p (h d) -> p h d", h=8)
        ACC = T[4][:].rearrange("p (h d) -> p h d", h=8)
        junk = OCT
        hs = bw(264).rearrange("p (h j) -> p h j", h=4)
        kcr = sb("kcr", [66, 2, 64])
        kcbf = sb("kcbf", [66, 2, 64], BF16)
        S0 = ARENA[:, 0:8192].bitcast(F32).rearrange("p (a d) -> p a d", a=64)
        etots = sb("etots", [128, 4, 16])
        kdm = PTb

        K0 = ps("K0", [128, 1024], BF16)
        K = [None] + [ps("K%d" % i, [128, 512]) for i in range(1, 8)]

        def ld(dst, src, key, eng="sp"):
            S.add(eng, lambda e: e.dma_start(out=dst, in_=src), w=[key], dma=True)

        ld(lnc[:], lncol, "lnc")
        ld(gvec[:], gvec_d, "gvec")
        ld(cst[:], cst_d, "cst")
        ld(tm[:], tmat, "tm")
        ld(rmask[:], rowmask, "rmask")
        ld(peT[:], pet_d, "peT")
        ld(T[0][:], lbl[:, 0, :], "T0")
        ld(T[1][:], lbl[:, 1, :], "T1")
        ld(T[2][:, 0:128], ident, "T2")
        ld(T[3][:, 0:128], w2_d.rearrange("p a d -> p (a d)"), "T3")
        A("dve", ["T2"], ["idb"], lambda e: e.tensor_copy(out=idb[:], in_=T[2][:, 0:128]))
        A("dve", ["T3"], ["W2b"], lambda e: e.tensor_copy(out=W2b[:].rearrange("p a d -> p (a d)"), in_=T[3][:, 0:128]))
        A("dve", [], ["ones"], lambda e: e.memset(ones[:], 1.0))
        A("dve", [], ["St"], lambda e: e.memset(St[:], 0.0))
        A("dve", ["gvec"], ["qgs"], lambda e: e.tensor_scalar_mul(out=qgs[:], in0=gvec[:, 192:256], scalar1=SCALE))
        A("dve", ["T0", "T1"], ["lbb"], lambda e: e.tensor_sub(out=lbb[:], in0=T[1][:], in1=T[0][:]))
        A("act", ["lbb"], ["lbb"], lambda e: e.activation(out=lbb[:], in_=lbb[:], func=AF.Exp))
        A("dve", ["lbb"], ["lbb"], lambda e: e.tensor_scalar_add(out=lbb[:], in0=lbb[:], scalar1=1.0))
        A("dve", ["lbb"], ["lbb"], lambda e: e.reciprocal(out=lbb[:], in_=lbb[:]))
        A("dve", ["lbb"], ["omlb"], lambda e: e.tensor_scalar(out=omlb[:], in0=lbb[:], scalar1=-1.0, scalar2=1.0,
                                                              op0=ALU.mult, op1=ALU.add))

        wstg = [(H, "H"), (xs[0], "xs0"), (xs[1], "xs1")]
        wi_ = 0
        for k in range(8):
            for hf in range(4):
                c0, c1 = hf * 838, (hf + 1) * 838
                st_, ks_ = wstg[wi_ % 3]
                A("sp", [], [ks_], lambda e, k=k, c0=c0, c1=c1, st_=st_: e.dma_start(
                    out=st_[:, 0:838], in_=w_in[k * 128:(k + 1) * 128, c0:c1]), dma=True)
                if wi_ % 2 == 0:
                    A("act", [ks_, "lnc"], ["W%d" % k], lambda e, k=k, c0=c0, c1=c1, st_=st_: e.activation(
                        out=W[:, k, c0:c1], in_=st_[:, 0:838], func=AF.Copy, scale=lnc[:, k:k + 1]))
                else:
                    A("dve", [ks_, "lnc"], ["W%d" % k], lambda e, k=k, c0=c0, c1=c1, st_=st_: e.tensor_scalar(
                        out=W[:, k, c0:c1], in0=st_[:, 0:838], scalar1=lnc[:, k:k + 1], scalar2=None, op0=ALU.mult))
                wi_ += 1
        WK_ = ["W%d" % k for k in range(8)]

        def prep_x(xsrc, p, xbi=0):
            X, XT = xs[p], xT[p]
            kx, kxt = "xs%d" % p, "xT%d" % p
            XB, kxb = (xb[:], "xb") if xbi == 0 else (uT[:, 0:8, :].rearrange("p a t -> p (a t)"), "uT0")
            A("sp", [], [kx], lambda e: e.dma_start(out=X[:], in_=xsrc), dma=True)
            A("act", [kx], ["OCT", "ss"], lambda e: e.activation(out=OCT[:].rearrange("p k t -> p (k t)"), in_=X[:], func=AF.Square, accum_out=ss[:]))
            A("act", ["ss"], ["rstd"], lambda e: e.activation(out=rstd[:], in_=ss[:], func=AF.Ln, scale=1.0 / D, bias=EPS))
            A("act", ["rstd"], ["rstd"], lambda e: e.activation(out=rstd[:], in_=rstd[:], func=AF.Exp, scale=-0.5))
            A("dve", [kx, "rstd"], [kxb], lambda e: e.tensor_scalar(out=XB, in0=X[:], scalar1=rstd[:, 0:1],
                                                                    scalar2=None, op0=ALU.mult))
            for k in range(8):
                A("pe", [kxb, "idb"], ["K0"], lambda e, k=k: e.transpose(
                    out=K0[:, k * 128:(k + 1) * 128], in_=XB[:, k * 128:(k + 1) * 128], identity=idb[:]))
            A("dve", ["K0"], [kxt], lambda e: e.tensor_copy(out=XT[:].rearrange("p k t -> p (k t)"), in_=K0[:]))
            return X, XT, kx, kxt

        def mm(XT, kxt, bank, n, c0):
            for k in range(8):
                A("pe", [kxt] + WK_, ["K%d" % bank], lambda e, k=k: e.matmul(
                    K[bank][:, 0:n], lhsT=XT[:, k, :], rhs=W[:, k, c0:c0 + n], start=(k == 0), stop=(k == 7)))

        A("dve", [], AKEYS + ["fdummy"], lambda e: e.memset(fdummy[:], 0.0))
        for typ in range(2):
            for jq in range(4):
                A("sp", [], ["H"], lambda e, typ=typ, jq=jq: e.dma_start(
                    out=stgq[:].rearrange("p (j h) -> p j h", j=8),
                    in_=w1_d[typ, jq * 1024:(jq + 1) * 1024, :].rearrange("(j p) h -> p j h", p=128)), dma=True)
                A("dve", ["H"], ["W1b"], lambda e, typ=typ, jq=jq: e.tensor_copy(
                    out=W1b[:, typ, jq * 8:(jq + 1) * 8, :], in_=stgq[:].rearrange("p (j h) -> p j h", j=8)))
        def tile_1a(v):
            p = v % 2
            X, XT, kx, kxt = prep_x(xv[v * 128:(v + 1) * 128, :], p)
            K1v = K[1][:, 0:256].rearrange("p (a t) -> p a t", a=4)
            for tg in range(4):
                c0 = C_KV + tg * 64
                for par in range(2):
                    for k in range(8):
                        A("pe", [kxt] + WK_, ["K1"], lambda e, tg=tg, par=par, k=k, c0=c0: e.matmul(
                            K1v[64 * par:64 * par + 64, tg, :], lhsT=W[:, k, c0:c0 + 64], rhs=XT[:, k, par:128:2],
                            start=(k == 0), stop=(k == 7)))
            for a in range(2):
                A("dve", ["K1", "peT"], ["XcT"], lambda e, v=v, K1v=K1v, a=a: e.tensor_tensor(
                    out=XcT[:, 2 * a:2 * a + 2, v * 64:(v + 1) * 64].rearrange("p g (b j) -> p g b j", b=2),
                    in0=K1v[:, 2 * a:2 * a + 2, :].rearrange("p g (b j) -> p g b j", b=2),
                    in1=peT[:, a, :].unsqueeze(1).unsqueeze(1).to_broadcast([128, 2, 2, 32]), op=ALU.add))
        for v_ in range(NT if STAGE >= 1 else 0):
            tile_1a(v_)
        K3v = K[3][:, 0:256].rearrange("p (a d) -> p a d", a=4)
        for tg in range(4):
            typ = tg // 2
            for j in range(32):
                A("pe", ["XcT", "W1b"], ["K2"], lambda e, tg=tg, typ=typ, j=j: e.matmul(
                    K[2][:, 0:66], lhsT=W1b[:, typ, j, :], rhs=XcT[:, tg, j:2112:32], start=(j == 0), stop=(j == 31)))
            A("act", ["K2"], ["T4"], lambda e: e.activation(out=T[4][:, 0:66], in_=K[2][:, 0:66], func=AF.Exp, scale=-1.0))
            A("dve", ["T4"], ["T4"], lambda e: e.tensor_scalar_add(out=T[4][:, 0:66], in0=T[4][:, 0:66], scalar1=1.0))
            A("dve", ["T4"], ["T4"], lambda e: e.reciprocal(out=T[4][:, 0:66], in_=T[4][:, 0:66]))
            A("dve", ["T4", "K2"], ["hs"], lambda e, tg=tg: e.tensor_tensor(out=hs[:, tg, :], in0=K[2][:, 0:66],
                                                                           in1=T[4][:, 0:66], op=ALU.mult))
            A("pe", ["hs", "W2b"], ["K3"], lambda e, tg=tg, typ=typ: e.matmul(
                K3v[0:66, tg, :], lhsT=hs[:, tg, :], rhs=W2b[:, typ, :], start=True, stop=True))
        A("act", ["K3"], ["kcr"], lambda e: e.activation(out=kcr[:], in_=K3v[0:66, 0:2, :], func=AF.Copy))
        A("act", ["K3"], ["vcb"], lambda e: e.activation(out=vcb[:], in_=K3v[0:66, 2:4, :], func=AF.Copy))
        A("dve", ["kcr"], ["T5"], lambda e: e.tensor_tensor(out=T[5][0:66, 0:128].rearrange("p (g d) -> p g d", g=2),
                                                            in0=kcr[:], in1=kcr[:], op=ALU.mult))
        A("dve", ["T5"], ["st4"], lambda e: e.tensor_reduce(out=st4[0:66, 0:2],
                                                            in_=T[5][0:66, 0:128].rearrange("p (g d) -> p g d", g=2),
                                                            axis=AX.X, op=ALU.add))
        A("act", ["st4"], ["st4"], lambda e: e.activation(out=st4[0:66, 0:2], in_=st4[0:66, 0:2], func=AF.Ln,
                                                          scale=1.0 / 64, bias=EPS))
        A("act", ["st4"], ["st4"], lambda e: e.activation(out=st4[0:66, 0:2], in_=st4[0:66, 0:2], func=AF.Exp, scale=-0.5))
        A("dve", ["kcr", "st4"], ["kcr"], lambda e: e.tensor_tensor(
            out=kcr[:], in0=kcr[:], in1=st4[0:66, 0:2].unsqueeze(2).to_broadcast([66, 2, 64]), op=ALU.mult))
        A("dve", ["kcr", "gvec"], ["kcbf"], lambda e: e.tensor_tensor(
            out=kcbf[:], in0=kcr[:], in1=gvec[0:66, 0:64].unsqueeze(1).to_broadcast([66, 2, 64]), op=ALU.mult))
        for g in range(2):
            A("pe", ["kcbf", "idb"], ["K0"], lambda e, g=g: e.transpose(
                out=K0[0:64, g * 128:g * 128 + 66], in_=kcbf[:, g, :], identity=idb[0:66, 0:66]))
        A("dve", ["K0"], ["kcT"], lambda e: e.tensor_copy(
            out=kcT[:], in_=K0[0:64, 0:256].rearrange("p (g n) -> p g n", g=2)[:, :, 0:66]))
        A("dve", [], AKEYS + ["fdummy"], lambda e: e.memset(fdummy[:], 0.0))

        A("dve", [], ["KTs"], lambda e: e.memset(KTs[64:128, :, :], 0.0))
        sst = [(H, "H"), (xs[0], "xs0"), (xs[1], "xs1")]
        si_ = [0]

        def nst():
            t_ = sst[si_[0] % 3]
            si_[0] += 1
            return t_

        for c0 in range(0, NT * 128, 1024):
            n = min(1024, NT * 128 - c0)
            st_, ks_ = nst()
            A("sp", [], [ks_], lambda e, c0=c0, n=n, st_=st_: e.dma_start(out=st_[0:66, 0:n], in_=eexp_d[:, c0:c0 + n]), dma=True)
            A("dve", [], ["Eexp"], lambda e, c0=c0, n=n: e.memset(Eexp[64:128, c0:c0 + n], 0.0))
            A("dve", [ks_], ["Eexp"], lambda e, c0=c0, n=n, st_=st_: e.tensor_copy(out=Eexp[0:66, c0:c0 + n], in_=st_[0:66, 0:n]))
            st_, ks_ = nst()
            A("sp", [], [ks_], lambda e, c0=c0, n=n, st_=st_: e.dma_start(out=st_[64:68, 0:n], in_=kpos_d[:, c0:c0 + n]), dma=True)
            for g in range(2):
                A("dve", [ks_], ["KTs"], lambda e, c0=c0, n=n, g=g, st_=st_: e.tensor_copy(
                    out=KTs[64:68, g, c0:c0 + n], in_=st_[64:68, 0:n]))
        for m in range(4):
            st_, ks_ = nst()
            A("sp", [], [ks_], lambda e, m=m, st_=st_: e.dma_start(out=st_[:, 0:512], in_=masks_d[:, m, :]), dma=True)
            A("dve", [ks_], ["MSK"], lambda e, m=m, st_=st_: e.tensor_copy(out=MSK[:, m, :], in_=st_[:, 0:512]))
        for k in range(8):
            st_, ks_ = nst()
            A("sp", [], [ks_], lambda e, k=k, st_=st_: e.dma_start(out=st_[:], in_=w_out[k * 128:(k + 1) * 128, :]), dma=True)
            if k % 2 == 0:
                A("act", [ks_], ["WO"], lambda e, k=k, st_=st_: e.activation(out=WO[:, k, :], in_=st_[:], func=AF.Copy))
            else:
                A("dve", [ks_], ["WO"], lambda e, k=k, st_=st_: e.tensor_copy(out=WO[:, k, :], in_=st_[:]))
        A("dve", [], ["KTw"], lambda e: e.memset(KTw[64:128, :, :], 0.0))
        A("dve", [], ["QTa"], lambda e: e.memset(QTa[64:128, :, :], 0.0))
        for g_ in range(2):
            A("dve", [], ["RHSM%d" % g_], lambda e, g_=g_: e.memset(RHSM_[g_][64:128, :, :], 0.0))
        A("dve", [], ["VVs"], lambda e: e.memset(VVs[:, :, :, 64:65], 1.0))
        A("dve", [], ["VVw"], lambda e: e.memset(VVw[:, :, :, 64:65], 1.0))

        def rnn_gates(bf, bi, sample):
            kf, ki = "K%d" % bf, "K%d" % bi
            e1, uu, kk, G = T[0], T[1], T[2], T[3]
            A("act", [kf], ["T0"], lambda e: e.activation(out=e1[:], in_=K[bf][:], func=AF.Exp, scale=-1.0))
            A("act", [ki], ["VB"], lambda e: e.activation(out=VB[:], in_=K[bi][:], func=AF.Copy))
            A("dve", ["T0"], ["T0"], lambda e: e.tensor_scalar_add(out=e1[:], in0=e1[:], scalar1=1.0))
            A("dve", ["T0"], ["T0"], lambda e: e.reciprocal(out=e1[:], in_=e1[:]))
            A("dve", ["T0", "omlb"], ["T1"], lambda e: e.tensor_tensor(out=uu[:], in0=e1[:], in1=omlb[:], op=ALU.mult))
            A("dve", ["T1", "lbb"], ["T0"], lambda e: e.tensor_tensor(out=e1[:], in0=uu[:], in1=lbb[:], op=ALU.add))
            A("dve", ["T1", "omlb"], ["T2"], lambda e: e.tensor_tensor(out=kk[:], in0=omlb[:], in1=uu[:], op=ALU.subtract))
            A("act", ["T0"], ["T3"], lambda e: e.activation(out=G[:], in_=e1[:], func=AF.Ln))
            mi = 3 if sample else 1
            A("pe", ["tm", "T3"], ["K3"], lambda e: e.matmul(K[3][:], lhsT=tm[:, mi, :], rhs=G[:], start=True, stop=True))
            A("act", ["K3"], ["T4"], lambda e: e.activation(out=T[4][:], in_=K[3][:], func=AF.Exp))
            A("dve", ["T2", "T4"], ["KD"], lambda e: e.tensor_tensor(out=KD[:], in0=kk[:], in1=T[4][:], op=ALU.mult))

        pst = K[4][:, 0:256].rearrange("p (a d) -> p a d", a=4)
        ptot = K[4][:, 256:264].rearrange("p (c a) -> p c a", c=2)

        def state_tot():
            G = T[3]
            for c in range(2):
                for h in range(8):
                    hp, ee = h // 2, h % 2
                    A("pe", ["T3", "ones"], ["K4"], lambda e, c=c, h=h, hp=hp, ee=ee: e.matmul(
                        ptot[64 * ee:64 * ee + 64, c, hp:hp + 1], lhsT=G[c * 64:(c + 1) * 64, h * 64:(h + 1) * 64],
                        rhs=ones[c * 64:(c + 1) * 64, 0:1], start=True, stop=True), rg=64 * c)
            A("act", ["K4"], ["ET"], lambda e: e.activation(out=ET[:], in_=ptot, func=AF.Exp))

        def state_update(c):
            for h in range(8):
                hp, ee = h // 2, h % 2
                A("pe", ["KD", "VB"], ["K4"], lambda e, c=c, h=h, hp=hp, ee=ee: e.matmul(
                    pst[64 * ee:64 * ee + 64, hp, :], lhsT=KD[c * 64:(c + 1) * 64, h * 64:(h + 1) * 64],
                    rhs=VB[c * 64:(c + 1) * 64, h * 64:(h + 1) * 64], start=True, stop=True), rg=64 * c)
            A("dve", ["St", "ET"], ["St"], lambda e, c=c: e.tensor_tensor(
                out=St[:], in0=St[:], in1=ET[:, c, :].unsqueeze(2).to_broadcast([128, 4, 64]), op=ALU.mult))
            A("dve", ["St", "K4"], ["St"], lambda e: e.tensor_tensor(out=St[:], in0=St[:], in1=pst, op=ALU.add))

        def k_norms(Zt, kz):
            Z5 = Zt[:].rearrange("p (a b g d) -> p a b g d", a=3, b=2, g=2)
            KS = Z5[:, 1:3, 0, :, :]
            A("dve", [kz], ["sq4"], lambda e: e.tensor_tensor(out=sq4[:], in0=KS, in1=KS, op=ALU.mult))
            A("dve", ["sq4"], ["ms4"], lambda e: e.tensor_reduce(out=ms4[:], in_=sq4[:], axis=AX.X, op=ALU.add))
            A("act", ["ms4"], ["ms4"], lambda e: e.activation(out=ms4[:], in_=ms4[:], func=AF.Ln, scale=1.0 / 64, bias=EPS))
            A("act", ["ms4"], ["ms4"], lambda e: e.activation(out=ms4[:], in_=ms4[:], func=AF.Exp, scale=-0.5))
            A("dve", [kz, "ms4"], [kz], lambda e: e.tensor_tensor(
                out=KS, in0=KS, in1=ms4[:].unsqueeze(3).to_broadcast([128, 2, 2, 64]), op=ALU.mult))
            A("dve", [kz, "gvec"], [kz], lambda e: e.tensor_tensor(
                out=KS, in0=KS, in1=gvec[:, 64:192].rearrange("p (a d) -> p a d", a=2).unsqueeze(2).to_broadcast(
                    [128, 2, 2, 64]), op=ALU.mult))
            return Z5

        def silu_from(bank, dst, kdst):
            kb = "K%d" % bank
            A("act", [kb], ["T5"], lambda e: e.activation(out=T[5][:], in_=K[bank][:], func=AF.Exp, scale=-1.0))
            A("dve", ["T5"], ["T5"], lambda e: e.tensor_scalar_add(out=T[5][:], in0=T[5][:], scalar1=1.0))
            A("dve", ["T5"], ["T5"], lambda e: e.reciprocal(out=T[5][:], in_=T[5][:]))
            A("dve", ["T5", kb], [kdst], lambda e: e.tensor_tensor(out=dst[:], in0=K[bank][:], in1=T[5][:], op=ALU.mult))

        def head_rms(src3, ksrc, n):
            A("dve", [ksrc], ["T5"], lambda e: e.tensor_tensor(out=TMP[:, 0:n, :], in0=src3, in1=src3, op=ALU.mult))
            A("dve", ["T5"], ["st8"], lambda e: e.tensor_reduce(out=st8[:, 0:n], in_=TMP[:, 0:n, :], axis=AX.X, op=ALU.add))
            A("act", ["st8"], ["st8"], lambda e: e.activation(out=st8[:, 0:n], in_=st8[:, 0:n], func=AF.Ln,
                                                              scale=1.0 / 64, bias=EPS))
            A("act", ["st8"], ["st8"], lambda e: e.activation(out=st8[:, 0:n], in_=st8[:, 0:n], func=AF.Exp, scale=-0.5))

        def finish(X, kx, ydst, ykey, dbi):
            A("dve", ["OSEL"], ["st8"], lambda e: e.tensor_scalar_max(out=st8[:], in0=OSEL[:, :, 64], scalar1=1e-30))
            A("dve", ["st8"], ["st8"], lambda e: e.reciprocal(out=st8[:], in_=st8[:]))
            A("dve", ["st8", "gsig"], ["g8"], lambda e: e.tensor_tensor(out=g8[:, :, 1], in0=gsig[:, :, 1], in1=st8[:],
                                                                         op=ALU.mult))
            A("dve", ["OWIN"], ["st8"], lambda e: e.tensor_scalar_max(out=st8[:], in0=OWIN[:, :, 64], scalar1=1e-30))
            A("dve", ["st8"], ["st8"], lambda e: e.reciprocal(out=st8[:], in_=st8[:]))
            A("dve", ["st8", "gsig"], ["g8"], lambda e: e.tensor_tensor(out=g8[:, :, 2], in0=gsig[:, :, 2], in1=st8[:],
                                                                         op=ALU.mult))
            A("dve", ["OCMP", "gsig"], ["T4"], lambda e: e.tensor_tensor(
                out=ACC[:], in0=OCMP[:], in1=gsig[:, :, 0:1].to_broadcast([128, 8, 64]), op=ALU.mult))
            A("dve", ["OSEL", "g8"], ["T5"], lambda e: e.tensor_tensor(
                out=TMP[:], in0=OSEL[:, :, 0:64], in1=g8[:, :, 1:2].to_broadcast([128, 8, 64]), op=ALU.mult))
            A("dve", ["T4", "T5"], ["T4"], lambda e: e.tensor_tensor(out=ACC[:], in0=ACC[:], in1=TMP[:], op=ALU.add))
            A("dve", ["OWIN", "g8"], ["T5"], lambda e: e.tensor_tensor(
                out=TMP[:], in0=OWIN[:, :, 0:64], in1=g8[:, :, 2:3].to_broadcast([128, 8, 64]), op=ALU.mult))
            A("dve", ["T4", "T5"], ["T4"], lambda e: e.tensor_tensor(out=ACC[:], in0=ACC[:], in1=TMP[:], op=ALU.add))
            if debug and dbi is not None:
                for di, (src, kk_, n) in enumerate([(OCMP, "OCMP", 512), (OSEL, "OSEL", 520), (OWIN, "OWIN", 520),
                                                    (OR32, "OR32", 512), (ACC, "T4", 512)]):
                    A("pool", [kk_], [], lambda e, di=di, src=src, n=n: e.dma_start(
                        out=dbg[dbi * 128:(dbi + 1) * 128, di, 0:n], in_=src[:].rearrange("p h d -> p (h d)")),
                      dma=True, final=True)
            head_rms(ACC[:], "T4", 8)
            A("dve", ["T4", "st8"], ["T4"], lambda e: e.tensor_tensor(
                out=ACC[:], in0=ACC[:], in1=st8[:].unsqueeze(2).to_broadcast([128, 8, 64]), op=ALU.mult))
            A("dve", ["T4", "gvec"], ["OC"], lambda e: e.tensor_tensor(
                out=OC[:, 0:512], in0=ACC[:].rearrange("p h d -> p (h d)"), in1=gvec[:, 256:768], op=ALU.mult))
            head_rms(OR32[:], "OR32", 8)
            A("dve", ["OR32", "st8"], ["OR32"], lambda e: e.tensor_tensor(
                out=OR32[:], in0=OR32[:], in1=st8[:].unsqueeze(2).to_broadcast([128, 8, 64]), op=ALU.mult))
            A("dve", ["OR32", "gvec"], ["OR32"], lambda e: e.tensor_tensor(
                out=OR32[:].rearrange("p h d -> p (h d)"), in0=OR32[:].rearrange("p h d -> p (h d)"),
                in1=gvec[:, 768:1280], op=ALU.mult))
            A("dve", ["OR32", "RGS"], ["OC"], lambda e: e.tensor_tensor(
                out=OC[:, 512:1024], in0=OR32[:].rearrange("p h d -> p (h d)"), in1=RGS[:], op=ALU.mult))
            for k in range(8):
                A("pe", ["OC", "idb"], ["K0"], lambda e, k=k: e.transpose(
                    out=K0[:, k * 128:(k + 1) * 128], in_=OC[:, k * 128:(k + 1) * 128], identity=idb[:]))
            A("dve", ["K0"], ["OCT"], lambda e: e.tensor_copy(out=OCT[:].rearrange("p k t -> p (k t)"), in_=K0[:]))
            for n in range(2):
                for k in range(8):
                    A("pe", ["OCT", "WO"], ["K%d" % (1 + n)], lambda e, n=n, k=k: e.matmul(
                        K[1 + n][:], lhsT=OCT[:, k, :], rhs=WO[:, k, n * 512:(n + 1) * 512], start=(k == 0), stop=(k == 7)))
                A("dve", ["K%d" % (1 + n), kx], ["H"], lambda e, n=n, X=X: e.tensor_tensor(
                    out=H[:, n * 512:(n + 1) * 512], in0=K[1 + n][:], in1=X[:, n * 512:(n + 1) * 512], op=ALU.add))
            A("pool", ["H"], [ykey], lambda e: e.dma_start(out=ydst, in_=H[:]), dma=True, final=True)

        pre1b = {}

        def tile_1b(v):
            p = v % 2
            own = (v % 2 == 0) and STAGE >= 3
            i_own = v // 2
            if v not in pre1b:
                pre1b[v] = prep_x(xv[v * 128:(v + 1) * 128, :], p, v % 2)
            X, XT, kx, kxt = pre1b[v]

            def prep_next():
                if v + 1 < NT and (v + 1) not in pre1b:
                    pre1b[v + 1] = prep_x(xv[(v + 1) * 128:(v + 2) * 128, :], (v + 1) % 2, (v + 1) % 2)
            Zt, kz = Z[p], "Z0"
            mm(XT, kxt, 1, 512, C_KV)
            mm(XT, kxt, 2, 256, C_KV + 512)
            A("act", ["K1"], [kz], lambda e, Zt=Zt: e.activation(out=Zt[:, 0:512], in_=K[1][:, 0:512], func=AF.Copy))
            A("act", ["K2"], [kz], lambda e, Zt=Zt: e.activation(out=Zt[:, 512:768], in_=K[2][:, 0:256], func=AF.Copy))
            mm(XT, kxt, 3, 512, C_RF)
            mm(XT, kxt, 5, 512, C_RI)
            Z5 = k_norms(Zt, kz)
            A("pool", [kz], [], lambda e, v=v, Zt=Zt: e.dma_start(out=kvp[v * 128:(v + 1) * 128, :], in_=Zt[:, 0:512]),
              dma=True, final=True)
            if v >= 28:
                A("pool", [kz], [], lambda e, v=v, Zt=Zt: e.dma_start(
                    out=winp[(v - 28) * 128:(v - 27) * 128, :], in_=Zt[:, 512:768]), dma=True, final=True)
            A("dve", [kz], ["zb"], lambda e, Z5=Z5: e.tensor_copy(out=zb[:].rearrange("p (a g) d -> p a g d", a=2),
                                                                  in_=Z5[:, 1:3, 0, :, :]))
            for a4 in range(4):
                A("pe", ["zb", "idb"], ["K0"], lambda e, a4=a4: e.transpose(
                    out=K0[0:64, a4 * 128:(a4 + 1) * 128], in_=zb[:, a4, :], identity=idb[:]))
            A("dve", ["K0"], ["KTs"], lambda e, v=v: e.tensor_copy(
                out=KTs[0:64, :, v * 128:(v + 1) * 128], in_=K0[0:64, 0:256].rearrange("p (g t) -> p g t", g=2)))
            sl = v % 6
            A("dve", ["K0"], ["KTw"], lambda e, sl=sl: e.tensor_copy(
                out=KTw[0:64, :, sl * 128:(sl + 1) * 128], in_=K0[0:64, 256:512].rearrange("p (g t) -> p g t", g=2)))
            A("sp", [], ["H"], lambda e, v=v: e.dma_start(out=stgq[64:68, 0:128], in_=kpos_d[:, v * 128:(v + 1) * 128]),
              dma=True)
            A("dve", ["H"], ["KTw"], lambda e, sl=sl: e.tensor_copy(
                out=KTw[64:68, :, sl * 128:(sl + 1) * 128], in_=stgq[64:68, 0:128].unsqueeze(1).to_broadcast([4, 2, 128])))
            A("dve", [kz], ["VVs"], lambda e, v=v, Z5=Z5: e.tensor_copy(out=VVs[:, v, :, 0:64], in_=Z5[:, 1, 1, :, :]))
            A("dve", [kz], ["VVw"], lambda e, sl=sl, Z5=Z5: e.tensor_copy(out=VVw[:, sl, :, 0:64], in_=Z5[:, 2, 1, :, :]))
            rnn_gates(3, 5, False)
            state_tot()
            if not own:
                prep_next()
            if own:
                mm(XT, kxt, 1, 512, C_Q)
                mm(XT, kxt, 2, 24, C_G)
                A("act", ["K1"], ["ZQ"], lambda e: e.activation(out=ZQ[:].rearrange("p h d -> p (h d)"), in_=K[1][:],
                                                                 func=AF.Copy))
                A("act", ["K2"], ["gsig"], lambda e: e.activation(out=gsig[:].rearrange("p h b -> p (h b)"),
                                                                   in_=K[2][:, 0:24], func=AF.Exp, scale=-1.0))
                A("dve", ["gsig"], ["gsig"], lambda e: e.tensor_scalar_add(out=gsig[:], in0=gsig[:], scalar1=1.0))
                A("dve", ["gsig"], ["gsig"], lambda e: e.reciprocal(out=gsig[:], in_=gsig[:]))
                mm(XT, kxt, 6, 512, C_RQ)
                silu_from(6, QS, "QS")
                mm(XT, kxt, 7, 512, C_RG)
                silu_from(7, RGS, "RGS")
                head_rms(ZQ[:], "ZQ", 8)
                A("dve", ["ZQ", "st8"], ["ZQ"], lambda e: e.tensor_tensor(
                    out=ZQ[:], in0=ZQ[:], in1=st8[:].unsqueeze(2).to_broadcast([128, 8, 64]), op=ALU.mult))
                A("dve", ["ZQ", "qgs"], ["QN"], lambda e: e.tensor_tensor(
                    out=QN[:], in0=ZQ[:], in1=qgs[:].unsqueeze(1).to_broadcast([128, 8, 64]), op=ALU.mult))
                for h in range(8):
                    A("pe", ["QN", "idb"], ["K0"], lambda e, h=h: e.transpose(
                        out=K0[0:64, h * 128:(h + 1) * 128], in_=QN[:, h, :], identity=idb[:]))
                A("dve", ["K0"], ["QTa"], lambda e: e.tensor_copy(out=QTa[0:64, :, :].rearrange("p h t -> p (h t)"),
                                                                  in_=K0[0:64, :]))
                A("sp", [], ["H"], lambda e, i_own=i_own: e.dma_start(out=stgq[64:68, :], in_=qpos_d[i_own]), dma=True)
                A("dve", ["H"], ["QTa"], lambda e: e.tensor_copy(out=QTa[64:68, :, :].rearrange("p h t -> p (h t)"),
                                                                    in_=stgq[64:68, :]))
                if SUB < 1:
                    return
                A("pe", ["tm", "T3"], ["K3"], lambda e: e.matmul(K[3][:], lhsT=tm[:, 0, :], rhs=T[3][:], start=True, stop=True))
                A("act", ["K3"], ["T4"], lambda e: e.activation(out=T[4][:], in_=K[3][:], func=AF.Exp))
                A("act", ["K3"], ["T5"], lambda e: e.activation(out=T[5][:], in_=K[3][:], func=AF.Exp, scale=-1.0))
                A("dve", ["QS", "T4"], ["QD"], lambda e: e.tensor_tensor(out=QD[:], in0=QS[:], in1=T[4][:], op=ALU.mult))
                A("dve", ["T2", "T5"], ["KDI"], lambda e: e.tensor_tensor(out=KDI[:], in0=T[2][:], in1=T[5][:], op=ALU.mult))
                for hp in range(4):
                    A("pe", ["QD", "idb"], ["K0"], lambda e, hp=hp: e.transpose(
                        out=K0[:, hp * 128:(hp + 1) * 128], in_=QD[:, hp * 128:(hp + 1) * 128], identity=idb[:]))
                    A("pe", ["KDI", "idb"], ["K0"], lambda e, hp=hp: e.transpose(
                        out=K0[:, (4 + hp) * 128:(5 + hp) * 128], in_=KDI[:, hp * 128:(hp + 1) * 128], identity=idb[:]))
                A("dve", ["K0"], ["qkT"], lambda e: e.tensor_copy(out=qkT[:].rearrange("p a t -> p (a t)"), in_=K0[:]))
                if SUB < 2:
                    return
                pA = K[5][:].rearrange("p (h t) -> p h t", h=8)
                for c in range(2):
                    for h in [0, 2, 4, 6, 1, 3, 5, 7]:
                        hp, ee = h // 2, h % 2
                        A("pe", ["qkT"], ["K5"], lambda e, c=c, h=h, hp=hp, ee=ee: e.matmul(
                            pA[64 * c:64 * c + 64, h, :], lhsT=qkT[64 * ee:64 * ee + 64, 4 + hp, c * 64:(c + 1) * 64],
                            rhs=qkT[64 * ee:64 * ee + 64, hp, c * 64:(c + 1) * 64], start=True, stop=True), rg=64 * ee)
                if "a" not in SKIP:
                    A("dve", ["K5", "cst"], ["AT"], lambda e: e.tensor_tensor(
                        out=AT[:], in0=pA, in1=tri2.unsqueeze(1).to_broadcast([128, 8, 64]), op=ALU.mult))
                if "s" not in SKIP:
                    A("dve", ["St"], ["Sb0"], lambda e: e.tensor_copy(out=Sb[0][:], in_=St[:]))
            state_update(0)
            if own:
                A("dve", ["St"], ["Sb1"], lambda e: e.tensor_copy(out=Sb[1][:], in_=St[:]))
            state_update(1)
            if not own:
                return
            if SUB < 3:
                return
            po = K[6][:].rearrange("p (h d) -> p h d", h=8)
            for c in range(2):
                for h in range(8):
                    A("pe", ["AT", "VB"], ["K6"], lambda e, c=c, h=h: e.matmul(
                        po[64 * c:64 * c + 64, h, :], lhsT=AT[64 * c:64 * c + 64, h, :],
                        rhs=VB[64 * c:64 * c + 64, h * 64:(h + 1) * 64], start=(h == 0), stop=False), rg=64 * c)
                order = [h for h in range(8) if h % 2 == c] + [h for h in range(8) if h % 2 != c]
                for j, h in enumerate(order):
                    hp, ee = h // 2, h % 2
                    A("pe", ["qkT", "Sb%d" % c], ["K6"], lambda e, c=c, h=h, hp=hp, ee=ee, j=j: e.matmul(
                        po[64 * c:64 * c + 64, h, :], lhsT=qkT[64 * ee:64 * ee + 64, hp, c * 64:(c + 1) * 64],
                        rhs=Sb[c][64 * ee:64 * ee + 64, hp, :], start=False, stop=(j == 7)), rg=64 * ee)
            A("act", ["K6"], ["OR32"], lambda e: e.activation(out=OR32[:].rearrange("p h d -> p (h d)"), in_=K[6][:],
                                                               func=AF.Copy))
            if STAGE < 4:
                return
            prep_next()
            nb = 2 * v + 2
            pc4 = K[3][:, 0:264].rearrange("p (h j) -> p h j", h=4)
            pocmp = K[4][:, 0:256].rearrange("p (h d) -> p h d", h=4)
            for g in range(2):
                for hh in range(4):
                    A("pe", ["QTa", "kcT"], ["K3"], lambda e, g=g, hh=hh: e.matmul(
                        pc4[:, hh, 0:nb], lhsT=QTa[0:64, g * 4 + hh, :], rhs=kcT[0:64, g, 0:nb], start=True, stop=True))
                A("dve", ["K3", "cst"], ["LG"], lambda e, g=g: e.tensor_tensor(
                    out=LG[:, :, 0:nb], in0=pc4[:, :, 0:nb], in1=acore[:, g * 4:(g + 1) * 4, 0:nb], op=ALU.add))
                A("dve", ["LG", "cst"], ["LG"], lambda e: e.tensor_tensor(
                    out=LG[:, :, 2 * v:2 * v + 2], in0=LG[:, :, 2 * v:2 * v + 2],
                    in1=fix2.unsqueeze(1).to_broadcast([128, 4, 2]), op=ALU.add))
                A("dve", ["LG"], ["st4"], lambda e: e.tensor_reduce(out=st4[:], in_=LG[:, :, 0:nb], axis=AX.X, op=ALU.max))
                A("dve", ["LG", "st4"], ["LG"], lambda e: e.tensor_tensor(
                    out=LG[:, :, 0:nb], in0=LG[:, :, 0:nb], in1=st4[:].unsqueeze(2).to_broadcast([128, 4, nb]),
                    op=ALU.subtract))
                A("act", ["LG"], ["EX"], lambda e: e.activation(out=EX[:, :, 0:nb], in_=LG[:, :, 0:nb], func=AF.Exp))
                A("dve", ["EX"], ["st4b"], lambda e: e.tensor_reduce(out=st4b[:], in_=EX[:, :, 0:nb], axis=AX.X, op=ALU.add))
                A("dve", ["st4b"], ["st4b"], lambda e: e.reciprocal(out=st4b[:], in_=st4b[:]))
                A("dve", ["st4b", "cst"], ["st4b"], lambda e: e.tensor_scalar(
                    out=st4b[:], in0=st4b[:], scalar1=rowvalid[:, i_own:i_own + 1], scalar2=None, op0=ALU.mult))
                A("dve", ["EX", "st4b"], ["EX"], lambda e: e.tensor_tensor(
                    out=EX[:, :, 0:nb], in0=EX[:, :, 0:nb], in1=st4b[:].unsqueeze(2).to_broadcast([128, 4, nb]),
                    op=ALU.mult))
                A("dve", ["EX"], ["PB"], lambda e: e.tensor_copy(out=PB[:, :, 0:nb], in_=EX[:, :, 0:nb]))
                A("dve", ["EX"], ["SC"], lambda e: e.tensor_reduce(
                    out=SC[:, 0:nb], in_=EX[:, :, 0:nb].rearrange("p h j -> p j h"), axis=AX.X, op=ALU.add))
                if nb < 66:
                    A("dve", [], ["SC"], lambda e: e.memset(SC[:, nb:66], -1.0))
                A("dve", ["SC", "cst"], ["SC"], lambda e: e.scalar_tensor_tensor(
                    out=SC[:, 2 * v:2 * v + 1], in0=SC[:, 2 * v:2 * v + 1], scalar=mhi, in1=mlo5,
                    op0=ALU.mult, op1=ALU.add))
                A("dve", ["cst"], ["SC"], lambda e: e.tensor_copy(out=SC[:, 2 * v + 1:2 * v + 2], in_=c1col))
                A("dve", ["SC", "cst"], ["SC"], lambda e: e.tensor_tensor(out=SC[:], in0=SC[:], in1=keepc, op=ALU.mult))
                A("dve", ["SC", "cst"], ["SC"], lambda e: e.tensor_tensor(out=SC[:], in0=SC[:], in1=addc, op=ALU.add))
                A("dve", ["SC"], ["m8"], lambda e: e.max(out=m8[:], in_=SC[:]))
                A("dve", ["SC", "m8"], ["WK"], lambda e: e.match_replace(out=WK[:], in_to_replace=m8[:], in_values=SC[:],
                                                                          imm_value=-2.0))
                A("dve", ["WK"], ["m8"], lambda e: e.max(out=m8[:], in_=WK[:]))
                A("dve", ["m8"], ["thr"], lambda e: e.tensor_reduce(out=thr[:], in_=m8[:], axis=AX.X, op=ALU.min))
                A("dve", ["SC", "thr"], ["SEL"], lambda e: e.tensor_scalar(
                    out=SEL[:], in0=SC[:], scalar1=thr[:, 0:1], scalar2=None, op0=ALU.is_ge))
                A("dve", ["SC"], ["WK"], lambda e: e.tensor_single_scalar(out=WK[:], in_=SC[:], scalar=0.0, op=ALU.is_ge))
                A("dve", ["SEL", "WK"], ["SEL"], lambda e: e.tensor_tensor(out=SEL[:], in0=SEL[:], in1=WK[:], op=ALU.mult))
                A("dve", ["SEL"], ["MBF"], lambda e: e.tensor_scalar(
                    out=MBF[:], in0=SEL[:], scalar1=-1.0, scalar2=BIG, op0=ALU.add, op1=ALU.mult))
                A("pe", ["MBF", "idb"], ["K0"], lambda e: e.transpose(out=K0[0:66, 0:128], in_=MBF[:], identity=idb[:]))
                for hh in range(4):
                    A("pe", ["PB", "idb"], ["K0"], lambda e, hh=hh: e.transpose(
                        out=K0[0:nb, (1 + hh) * 128:(2 + hh) * 128], in_=PB[:, hh, 0:nb], identity=idb[:]))
                A("dve", ["K0"], ["RHSM%d" % g], lambda e, g=g: e.tensor_copy(
                    out=RHSM_[g][0:66], in_=K0[0:66, 0:128].unsqueeze(1).to_broadcast([66, 4, 128])))
                A("dve", ["K0"], ["PT4"], lambda e: e.tensor_copy(
                    out=PT4[0:nb, :, :], in_=K0[0:nb, 128:640].rearrange("p (h t) -> p h t", h=4)))
                for hh in range(4):
                    A("pe", ["PT4", "vcb"], ["K4"], lambda e, g=g, hh=hh: e.matmul(
                        pocmp[:, hh, :], lhsT=PT4[0:nb, hh, :], rhs=vcb[0:nb, g, :], start=True, stop=True))
                A("act", ["K4"], ["OCMP"], lambda e, g=g: e.activation(out=OCMP[:, g * 4:(g + 1) * 4, :], in_=pocmp,
                                                                        func=AF.Copy))
            if STAGE < 5:
                return
            posb = [(K[7][:, 0:260].rearrange("p (h d) -> p h d", h=4), "K7"),
                    (K[4][:, 0:260].rearrange("p (h d) -> p h d", h=4), "K4")]
            its = []
            gi = 0
            for g in range(2):
                for br in range(2):
                    kts = list(range(0, v + 1)) if br == 0 else list(range(max(0, v - 4), v + 1))
                    for kt in kts:
                        extra = []
                        if br == 0:
                            extra.append((Eexp[:, kt * 128:(kt + 1) * 128],
                                          RHSM_[g][:].rearrange("p h t -> p (h t)"), ["Eexp", "RHSM%d" % g]))
                            if kt == v:
                                extra.append((idb[:], MSK[:, 0, :], ["idb", "MSK"]))
                            lhs_k, rk = KTs[:, g, kt * 128:(kt + 1) * 128], ["KTs"]
                            rv, kv_ = VVs[:, kt, g, :], "VVs"
                        else:
                            if kt == v:
                                extra.append((idb[:], MSK[:, 0, :], ["idb", "MSK"]))
                            elif kt == v - 4:
                                extra.append((idb[:], MSK[:, 3 if kt == 0 else 1, :], ["idb", "MSK"]))
                            elif kt == 0:
                                extra.append((idb[:], MSK[:, 2, :], ["idb", "MSK"]))
                            slk = kt % 6
                            lhs_k, rk = KTw[:, g, slk * 128:(slk + 1) * 128], ["KTw"]
                            rv, kv_ = VVw[:, slk, g, :], "VVw"
                        its.append(dict(g=g, br=br, first=(kt == kts[0]), last=(kt == kts[-1]), extra=extra, lhs_k=lhs_k,
                                        rk=rk, rv=rv, kv=kv_, gi=gi))
                    gi += 1

            SBK = [5, 6, 1]

            def s_stage(i):
                d = its[i]
                bank = SBK[i % 3]
                kb = "K%d" % bank
                g, extra = d["g"], d["extra"]
                A("pe", d["rk"] + ["QTa"], [kb], lambda e, bank=bank, lhs_k=d["lhs_k"], g=g, ne=len(extra): e.matmul(
                    K[bank][:], lhsT=lhs_k, rhs=QTa[:, g * 4:(g + 1) * 4, :].rearrange("p h t -> p (h t)"),
                    start=True, stop=(ne == 0)))
                for xi, (l_, r_, ks_) in enumerate(extra):
                    A("pe", ks_, [kb], lambda e, bank=bank, l_=l_, r_=r_, last=(xi == len(extra) - 1): e.matmul(
                        K[bank][:], lhsT=l_, rhs=r_, start=False, stop=last))

            def ep_stage(i):
                d = its[i]
                bank = SBK[i % 3]
                kb = "K%d" % bank
                PTt, kpt = PTb[i % 3], "PTb%d" % (i % 3)
                pos, kpos_ = posb[d["gi"] % 2]
                A("act", [kb], [kpt], lambda e, bank=bank, PTt=PTt: e.activation(out=PTt[:], in_=K[bank][:], func=AF.Exp))
                for hh in range(4):
                    A("pe", [kpt, d["kv"]], [kpos_], lambda e, hh=hh, PTt=PTt, rv=d["rv"], pos=pos,
                      first=(d["first"] and hh == 0), last=(d["last"] and hh == 3): e.matmul(
                          pos[:, hh, :], lhsT=PTt[:, hh * 128:(hh + 1) * 128], rhs=rv, start=first, stop=last))
                if d["last"]:
                    dst, kd = (OSEL, "OSEL") if d["br"] == 0 else (OWIN, "OWIN")
                    A("act", [kpos_], [kd], lambda e, dst=dst, g=d["g"], pos=pos: e.activation(
                        out=dst[:, g * 4:(g + 1) * 4, :], in_=pos, func=AF.Copy))

            s_stage(0)
            if len(its) > 1:
                s_stage(1)
            for i in range(len(its)):
                if i + 2 < len(its):
                    s_stage(i + 2)
                ep_stage(i)
            if STAGE < 6:
                return
            finish(X, kx, yp[i_own * 128:(i_own + 1) * 128, :], "yp%d" % i_own, i_own)

        for v_ in range(NT if STAGE >= 2 else 0):
            tile_1b(v_)
        A("pool", ["St"], [], lambda e: e.dma_start(out=rnnp.rearrange("(a e) k v -> (e k) a v", e=2), in_=St[:]),
          dma=True, final=True)

        SKEYS = ["W1s", "SELM", "S0bf", "KN", "ZV", "QTs", "ATs", "ATs", "XcTb", "PGB0", "PGB1", "hs_s", "KCa", "vca",
                 "QTb", "PTc", "PTs", "PTw", "RHSMs", "WPB", "OcT", "OsT", "OwT", "IDX", "MSKs", "kcbs"]
        _so = [0]

        def sw(n, parts=128):
            a_ = _so[0]
            _so[0] += n
            assert _so[0] <= 26816
            return BIGW[0:parts, a_:a_ + n]

        W1s = sw(8192).rearrange("p (a j h) -> p a j h", a=2, j=32)
        SELM = sw(128).rearrange("p (a j) -> p a j", a=2)
        S0bf = sw(4096).rearrange("p (a d) -> p a d", a=64)
        KN = sw(512, 64).rearrange("p (a t) -> p a t", a=4)
        ZV = sw(256).rearrange("p (a g d) -> p a g d", a=2, g=2)
        QTs = sw(1024, 64).rearrange("p (h t) -> p h t", h=8)
        ATs = sw(1024).rearrange("p (h t) -> p h t", h=8)
        oiT = ATs[0:64]
        XcTb = sw(4096).rearrange("p (a n) -> p a n", a=4)
        _pgb = sw(1024)
        PGB = [_pgb[:, 0:512], _pgb[:, 512:1024]]
        hs_s = sw(128).rearrange("p (a j) -> p a j", a=4)
        KCa = sw(64, 68).rearrange("p (g j) -> p g j", g=2)
        vca = sw(130, 32).rearrange("p (g d) -> p g d", g=2)
        QTb = sw(64, 128).rearrange("p (h t) -> p h t", h=8)
        PTc = sw(32, 32)
        PTs = sw(544)
        PTw = sw(160)
        RHSMs = sw(32, 128).rearrange("p (h t) -> p h t", h=4)
        WPB = _pgb.rearrange("p (t c) -> p t c", t=4)
        OcT = sw(1024, 65).rearrange("p (h t) -> p h t", h=8)
        OsT = sw(1024, 65).rearrange("p (h t) -> p h t", h=8)
        OwT = sw(1024, 65).rearrange("p (h t) -> p h t", h=8)
        IDX = sw(1024).bitcast(mybir.dt.int32)
        MSKs = sw(64).rearrange("p (m n) -> p m n", m=2)
        kcbs = sw(128, 32).rearrange("p (g d) -> p g d", g=2)
        csts = sb("csts", [128, 74])
        selT = csts[0:32, 0:8]
        keepS = csts[0:8, 8:41]
        addS = csts[0:8, 41:74]
        iotac = sb("iotac", [128, 1])

        A("dve", [], AKEYS + ["fdummy"] + ["S0_%d" % b for b in range(16)], lambda e: e.memset(fdummy[:], 0.0))
        for b in range(16):
            A("pool", [], ["S0_%d" % b], lambda e, b=b: e.dma_start(
                out=S0[:, b * 4:(b + 1) * 4, :], in_=srnn[b].rearrange("(a e) k v -> (e k) a v", e=2)), dma=True)
        for b in range(16):
            A("pool", [], [], lambda e, b=b: e.dma_start(out=wins[b, 0:504, :], in_=cwin[b, 8:512, :]), dma=True, final=True)
        X, XT, kx, kxt = prep_x(xsm[:, :], 0)
        Zt, kz = Z[0], "Z0"
        mm(XT, kxt, 1, 384, C_KV)
        mm(XT, kxt, 2, 384, C_KV + 384)
        A("act", ["K1"], [kz], lambda e: e.activation(out=Zt[:, 0:384], in_=K[1][:, 0:384], func=AF.Copy))
        A("act", ["K2"], [kz], lambda e: e.activation(out=Zt[:, 384:768], in_=K[2][:, 0:384], func=AF.Copy))
        mm(XT, kxt, 1, 512, C_RF)
        mm(XT, kxt, 2, 512, C_RI)
        Z5 = k_norms(Zt, kz)
        A("pool", [kz], [], lambda e: e.dma_start(out=kvs[:, :], in_=Zt[:, 0:512]), dma=True, final=True)
        for b in range(16):
            A("pool", [kz], [], lambda e, b=b: e.dma_start(out=wins[b, 504:512, :], in_=Zt[b * 8:(b + 1) * 8, 512:768]),
              dma=True, final=True)
        rnn_gates(1, 2, True)
        mm(XT, kxt, 1, 512, C_Q)
        mm(XT, kxt, 2, 24, C_G)
        A("act", ["K1"], ["ZQ"], lambda e: e.activation(out=ZQ[:].rearrange("p h d -> p (h d)"), in_=K[1][:], func=AF.Copy))
        A("act", ["K2"], ["gsig"], lambda e: e.activation(out=gsig[:].rearrange("p h b -> p (h b)"), in_=K[2][:, 0:24],
                                                           func=AF.Exp, scale=-1.0))
        A("dve", ["gsig"], ["gsig"], lambda e: e.tensor_scalar_add(out=gsig[:], in0=gsig[:], scalar1=1.0))
        A("dve", ["gsig"], ["gsig"], lambda e: e.reciprocal(out=gsig[:], in_=gsig[:]))
        mm(XT, kxt, 1, 512, C_RQ)
        silu_from(1, QS, "QS")
        mm(XT, kxt, 2, 512, C_RG)
        silu_from(2, RGS, "RGS")
        A("dve", [], WK_ + SKEYS + ["fdummy"], lambda e: e.memset(fdummy[:], 0.0))
        A("sp", [], ["csts"], lambda e: e.dma_start(out=csts[:], in_=csts_d), dma=True)
        A("sp", [], ["iotac"], lambda e: e.dma_start(out=iotac[:], in_=iota_d), dma=True)
        A("dve", [kz], ["zb"], lambda e: e.tensor_copy(out=zb[:].rearrange("p (a g) d -> p a g d", a=2), in_=Z5[:, 1:3, 0, :, :]))
        for a4 in range(4):
            A("pe", ["zb", "idb"], ["K0"], lambda e, a4=a4: e.transpose(
                out=K0[0:64, a4 * 128:(a4 + 1) * 128], in_=zb[:, a4, :], identity=idb[:]))
        A("dve", ["K0"], ["KN"], lambda e: e.tensor_copy(out=KN[:].rearrange("p a t -> p (a t)"), in_=K0[0:64, 0:512]))
        A("dve", [kz], ["ZV"], lambda e: e.tensor_copy(out=ZV[:], in_=Z5[:, 1:3, 1, :, :]))
        head_rms(ZQ[:], "ZQ", 8)
        A("dve", ["ZQ", "st8"], ["ZQ"], lambda e: e.tensor_tensor(
            out=ZQ[:], in0=ZQ[:], in1=st8[:].unsqueeze(2).to_broadcast([128, 8, 64]), op=ALU.mult))
        A("dve", ["ZQ", "qgs"], ["QN"], lambda e: e.tensor_tensor(
            out=QN[:], in0=ZQ[:], in1=qgs[:].unsqueeze(1).to_broadcast([128, 8, 64]), op=ALU.mult))
        for h in range(8):
            A("pe", ["QN", "idb"], ["K0"], lambda e, h=h: e.transpose(
                out=K0[0:64, h * 128:(h + 1) * 128], in_=QN[:, h, :], identity=idb[:]))
        A("dve", ["K0"], ["QTs"], lambda e: e.tensor_copy(out=QTs[:].rearrange("p h t -> p (h t)"), in_=K0[0:64, :]))
        A("pe", ["tm", "T3"], ["K3"], lambda e: e.matmul(K[3][:], lhsT=tm[:, 2, :], rhs=T[3][:], start=True, stop=True))
        A("act", ["K3"], ["T4"], lambda e: e.activation(out=T[4][:], in_=K[3][:], func=AF.Exp))
        A("act", ["K3"], ["T5"], lambda e: e.activation(out=T[5][:], in_=K[3][:], func=AF.Exp, scale=-1.0))
        A("dve", ["QS", "T4"], ["QD"], lambda e: e.tensor_tensor(out=QD[:], in0=QS[:], in1=T[4][:], op=ALU.mult))
        A("dve", ["T2", "T5"], ["KDI"], lambda e: e.tensor_tensor(out=KDI[:], in0=T[2][:], in1=T[5][:], op=ALU.mult))
        for hp in range(4):
            A("pe", ["QD", "idb"], ["K0"], lambda e, hp=hp: e.transpose(
                out=K0[:, hp * 128:(hp + 1) * 128], in_=QD[:, hp * 128:(hp + 1) * 128], identity=idb[:]))
            A("pe", ["KDI", "idb"], ["K0"], lambda e, hp=hp: e.transpose(
                out=K0[:, (4 + hp) * 128:(5 + hp) * 128], in_=KDI[:, hp * 128:(hp + 1) * 128], identity=idb[:]))
        A("dve", ["K0"], ["qkT"], lambda e: e.tensor_copy(out=qkT[:].rearrange("p a t -> p (a t)"), in_=K0[:]))
        for ee in range(2):
            for hp in range(4):
                A("pe", ["qkT"], ["K%d" % (5 + ee)], lambda e, ee=ee, hp=hp: e.matmul(
                    K[5 + ee][:, hp * 128:(hp + 1) * 128], lhsT=qkT[64 * ee:64 * ee + 64, 4 + hp, :],
                    rhs=qkT[64 * ee:64 * ee + 64, hp, :], start=True, stop=True), rg=64 * ee)
        for ee in range(2):
            A("dve", ["K%d" % (5 + ee), "tm"], ["ATs"], lambda e, ee=ee: e.tensor_tensor(
                out=ATs[:, ee * 4:(ee + 1) * 4, :], in0=K[5 + ee][:].rearrange("p (a t) -> p a t", a=4),
                in1=tm[:, 2, :].unsqueeze(1).to_broadcast([128, 4, 128]), op=ALU.mult))
        A("dve", ["S0_%d" % b for b in range(16)], ["S0bf"], lambda e: e.tensor_copy(out=S0bf[:, 0:32, :], in_=S0[:, 0:32, :]))
        A("act", ["S0_%d" % b for b in range(16)], ["S0bf"], lambda e: e.activation(out=S0bf[:, 32:64, :], in_=S0[:, 32:64, :],
                                                                                     func=AF.Copy))
        pos_ = K[7][:].rearrange("p (h d) -> p h d", h=8)
        for h in range(8):
            hp, ee = h // 2, h % 2
            A("pe", ["ATs", "VB"], ["K7"], lambda e, h=h, hp=hp, ee=ee: e.matmul(
                pos_[:, h, :], lhsT=ATs[:, ee * 4 + hp, :], rhs=VB[:, h * 64:(h + 1) * 64], start=True, stop=True))
        for ee in range(2):
            for b in range(16):
                for hp in range(4):
                    A("pe", ["qkT", "S0bf"], ["K%d" % (5 + ee)], lambda e, ee=ee, b=b, hp=hp: e.matmul(
                        K[5 + ee][0:64, hp * 128 + b * 8:hp * 128 + b * 8 + 8], lhsT=S0bf[64 * ee:64 * ee + 64, b * 4 + hp, :],
                        rhs=qkT[64 * ee:64 * ee + 64, hp, b * 8:(b + 1) * 8], start=True, stop=True), rg=64 * ee)
        for ee in range(2):
            A("act", ["K%d" % (5 + ee)], ["ATs"], lambda e, ee=ee: e.activation(
                out=oiT[:, ee * 4:(ee + 1) * 4, :].rearrange("p a t -> p (a t)"), in_=K[5 + ee][0:64, :], func=AF.Copy))
        for h in range(8):
            hp, ee = h // 2, h % 2
            A("pe", ["ATs", "idb"], ["K0"], lambda e, h=h, hp=hp, ee=ee: e.transpose(
                out=K0[:, h * 64:(h + 1) * 64], in_=oiT[:, ee * 4 + hp, :], identity=idb[0:64, 0:64]))
        A("act", ["K0"], ["T5"], lambda e: e.activation(out=T[5][:], in_=K0[:, 0:512], func=AF.Copy))
        A("dve", ["K7", "T5"], ["OR32"], lambda e: e.tensor_tensor(out=OR32[:].rearrange("p h d -> p (h d)"), in0=K[7][:],
                                                                  in1=T[5][:], op=ALU.add))
        G = T[3]
        ptots = K[5][:, 0:64].rearrange("p (a b) -> p a b", a=4)
        for h in range(8):
            hp, ee = h // 2, h % 2
            A("pe", ["T3", "rmask"], ["K5"], lambda e, h=h, hp=hp, ee=ee: e.matmul(
                ptots[64 * ee:64 * ee + 64, hp, :], lhsT=G[:, h * 64:(h + 1) * 64], rhs=rmask[:], start=True, stop=True))
        A("act", ["K5"], ["etots"], lambda e: e.activation(out=etots[:], in_=ptots, func=AF.Exp))
        for b in range(16):
            q = b % 2
            A("dve", ["KD", "rmask"], ["PTb%d" % q], lambda e, b=b, q=q: e.tensor_scalar(
                out=kdm[q][:], in0=KD[:], scalar1=rmask[:, b:b + 1], scalar2=None, op0=ALU.mult))
            for h in range(8):
                hp, ee = h // 2, h % 2
                A("pe", ["PTb%d" % q, "VB"], ["K4"], lambda e, h=h, q=q, hp=hp, ee=ee: e.matmul(
                    pst[64 * ee:64 * ee + 64, hp, :], lhsT=kdm[q][:, h * 64:(h + 1) * 64], rhs=VB[:, h * 64:(h + 1) * 64],
                    start=True, stop=True))
            Sbv = S0[:, b * 4:(b + 1) * 4, :]
            A("dve", ["S0_%d" % b, "etots", "S0bf"], ["S0_%d" % b], lambda e, b=b, Sbv=Sbv: e.tensor_tensor(
                out=Sbv, in0=Sbv, in1=etots[:, :, b:b + 1].to_broadcast([128, 4, 64]), op=ALU.mult))
            A("dve", ["S0_%d" % b, "K4"], ["S0_%d" % b], lambda e, Sbv=Sbv: e.tensor_tensor(out=Sbv, in0=Sbv, in1=pst, op=ALU.add))
            A("pool", ["S0_%d" % b], ["rnns_o"], lambda e, b=b, Sbv=Sbv: e.dma_start(
                out=rnns[b].rearrange("(a e) k v -> (e k) a v", e=2), in_=Sbv), dma=True, final=True)
        if STAGE >= 8:
            A("dve", ["rnns_o"] + ["S0_%d" % b for b in range(16)], ["KTs", "fdummy"], lambda e: e.memset(fdummy[:], 0.0))
            A("dve", [], ["KTs"], lambda e: e.memset(KTs[64:128, :, 0:2176], 0.0))
            for c0 in range(0, 17 * 128, 1024):
                n = min(1024, 17 * 128 - c0)
                A("sp", [], ["H"], lambda e, c0=c0, n=n: e.dma_start(out=stgq[64:68, 0:n], in_=kpos_d[:, c0:c0 + n]), dma=True)
                for g in range(2):
                    A("dve", ["H"], ["KTs"], lambda e, c0=c0, n=n, g=g: e.tensor_copy(
                        out=KTs[64:68, g, c0:c0 + n], in_=stgq[64:68, 0:n]))
            A("sp", [], ["H"], lambda e: e.dma_start(out=stgq[64:68, 0:640], in_=kpos_d[:, 1536:2176]), dma=True)
            A("dve", ["H"], ["KTw"], lambda e: e.tensor_copy(
                out=KTw[64:68, :, 0:640], in_=stgq[64:68, 0:640].unsqueeze(1).to_broadcast([4, 2, 640])))
            A("dve", [], ["QTb"], lambda e: e.memset(QTb[64:128, :, :], 0.0))
            A("dve", [], ["RHSMs"], lambda e: e.memset(RHSMs[:, :, :], 0.0))
            A("sp", [], ["H"], lambda e: e.dma_start(out=stgq[64:68, 0:64], in_=sq_d[0:4, :]), dma=True)
            A("dve", ["H"], ["QTb"], lambda e: e.tensor_copy(out=QTb[64:68, :, :].rearrange("p h t -> p (h t)"),
                                                             in_=stgq[64:68, 0:64]))
            A("sp", [], ["H"], lambda e: e.dma_start(out=stgq[64:68, 0:32], in_=sq_d[4:8, 0:32]), dma=True)
            A("dve", ["H"], ["KCa"], lambda e: e.tensor_copy(
                out=KCa[64:68, :, :], in_=stgq[64:68, 0:32].unsqueeze(1).to_broadcast([4, 2, 32])))
            A("sp", [], ["H"], lambda e: e.dma_start(out=stgq[:, 0:192], in_=sm_d), dma=True)
            A("dve", ["H"], ["SELM"], lambda e: e.tensor_copy(out=SELM[:].rearrange("p a j -> p (a j)"), in_=stgq[:, 0:128]))
            A("dve", ["H"], ["MSKs"], lambda e: e.tensor_copy(out=MSKs[:].rearrange("p m n -> p (m n)"), in_=stgq[:, 128:192]))
            for typ in range(2):
                for jq in range(4):
                    A("sp", [], ["H"], lambda e, typ=typ, jq=jq: e.dma_start(
                        out=stgq[:].rearrange("p (j h) -> p j h", j=8),
                        in_=w1_d[typ, jq * 1024:(jq + 1) * 1024, :].rearrange("(j p) h -> p j h", p=128)), dma=True)
                    A("dve", ["H"], ["W1s"], lambda e, typ=typ, jq=jq: e.tensor_copy(
                        out=W1s[:, typ, jq * 8:(jq + 1) * 8, :], in_=stgq[:].rearrange("p (j h) -> p j h", j=8)))
            A("dve", [], ["KTs"], lambda e: e.memset(KTs[0:64, :, 2048:2176], 0.0))
            A("dve", [], ["VVs"], lambda e: e.memset(VVs[:, 16, :, 0:64], 0.0))
            A("dve", [], ["KTw"], lambda e: e.memset(KTw[0:64, :, 512:640], 0.0))
            A("dve", [], ["VVw"], lambda e: e.memset(VVw[:, 4, :, 0:64], 0.0))
            A("dve", [], ["vca"], lambda e: e.memset(vca[:, :, 64:65], 1.0))
            A("sp", [], ["IDX"], lambda e: e.dma_start(out=IDX[:, 0:256], in_=ptrep_d), dma=True)
            A("dve", ["IDX"], ["T0"], lambda e: e.tensor_copy(out=T[0][:, 0:256], in_=IDX[:, 0:256]))
            A("dve", ["T0", "iotac"], ["T0"], lambda e: e.tensor_scalar(out=T[0][:, 0:256], in0=T[0][:, 0:256], scalar1=128.0,
                                                                       scalar2=iotac[:, 0:1], op0=ALU.mult, op1=ALU.add))
            A("dve", ["T0"], ["IDX"], lambda e: e.tensor_copy(out=IDX[:, 256:512], in_=T[0][:, 0:256]))
            PG = [T[0], T[1], T[2], T[3]]
            K1v = K[1][:, 0:256].rearrange("p (a t) -> p a t", a=4)
            K3v = K[3][:, 0:256].rearrange("p (a d) -> p a d", a=4)
            WP = xs[1]
            pgc = [0]

            def sample_seq(b):
                for j in range(16):
                    q = pgc[0] % 4
                    q2 = pgc[0] % 2
                    pgc[0] += 1
                    n = b * 16 + j
                    pg, kpg, pgb, kpgb = PG[q], "T%d" % q, PGB[q2], "PGB%d" % q2
                    A("pool", ["IDX"], [kpg], lambda e, pg=pg, n=n: e.indirect_dma_start(
                        out=pg[:], out_offset=None, in_=cache_d,
                        in_offset=bass.IndirectOffsetOnAxis(ap=IDX[:, 256 + n:257 + n], axis=0)), dma=True)
                    if q2 == 0:
                        A("act", [kpg], [kpgb], lambda e, pg=pg, pgb=pgb: e.activation(out=pgb, in_=pg[:], func=AF.Copy))
                    else:
                        A("dve", [kpg], [kpgb], lambda e, pg=pg, pgb=pgb: e.tensor_copy(out=pgb, in_=pg[:]))
                    for tg in range(4):
                        for par in range(2):
                            A("pe", [kpgb, "SELM"], ["K1"], lambda e, tg=tg, par=par, pgb=pgb: e.matmul(
                                K1v[64 * par:64 * par + 64, tg, :], lhsT=pgb[:, tg * 64:(tg + 1) * 64], rhs=SELM[:, par, :],
                                start=True, stop=True))
                    for a in range(2):
                        A("dve", ["K1", "peT"], ["XcTb"], lambda e, a=a, j=j: e.tensor_tensor(
                            out=XcTb[:, 2 * a:2 * a + 2, j * 64:(j + 1) * 64].rearrange("p g (b j) -> p g b j", b=2),
                            in0=K1v[:, 2 * a:2 * a + 2, :].rearrange("p g (b j) -> p g b j", b=2),
                            in1=peT[:, a, :].unsqueeze(1).unsqueeze(1).to_broadcast([128, 2, 2, 32]), op=ALU.add))
                    for g in range(2):
                        A("pe", [kpgb, "idb"], ["K0"], lambda e, g=g, pgb=pgb: e.transpose(
                            out=K0[0:64, g * 128:(g + 1) * 128], in_=pgb[:, 256 + g * 64:256 + (g + 1) * 64], identity=idb[:]))
                    A("dve", ["K0"], ["KTs"], lambda e, j=j: e.tensor_copy(
                        out=KTs[0:64, :, j * 128:(j + 1) * 128], in_=K0[0:64, 0:256].rearrange("p (g t) -> p g t", g=2)))
                    A("act", [kpg], ["VVs"], lambda e, pg=pg, j=j: e.activation(
                        out=VVs[:, j, :, 0:64], in_=pg[:, 384:512].rearrange("p (g d) -> p g d", g=2), func=AF.Copy))
                A("dve", ["KN"], ["KTs"], lambda e: e.tensor_copy(out=KTs[0:64, :, 2048:2056], in_=KN[:, 0:2, b * 8:(b + 1) * 8]))
                for g in range(2):
                    A("sp", ["ZV"], ["VVs"], lambda e, g=g: e.dma_start(out=VVs[0:8, 16, g, 0:64], in_=ZV[b * 8:(b + 1) * 8, 0, g, :]),
                      dma=True)
                    A("sp", ["ZV"], ["VVw"], lambda e, g=g: e.dma_start(out=VVw[0:8, 4, g, 0:64], in_=ZV[b * 8:(b + 1) * 8, 1, g, :]),
                      dma=True)
                for tg in range(4):
                    typ = tg // 2
                    for j in range(32):
                        A("pe", ["XcTb", "W1s"], ["K2"], lambda e, tg=tg, typ=typ, j=j: e.matmul(
                            K[2][:, 0:32], lhsT=W1s[:, typ, j, :], rhs=XcTb[:, tg, j:1024:32], start=(j == 0), stop=(j == 31)))
                    A("act", ["K2"], ["T4"], lambda e: e.activation(out=T[4][:, 0:32], in_=K[2][:, 0:32], func=AF.Exp, scale=-1.0))
                    A("dve", ["T4"], ["T4"], lambda e: e.tensor_scalar_add(out=T[4][:, 0:32], in0=T[4][:, 0:32], scalar1=1.0))
                    A("dve", ["T4"], ["T4"], lambda e: e.reciprocal(out=T[4][:, 0:32], in_=T[4][:, 0:32]))
                    A("dve", ["T4", "K2"], ["hs_s"], lambda e, tg=tg: e.tensor_tensor(out=hs_s[:, tg, :], in0=K[2][:, 0:32],
                                                                                   in1=T[4][:, 0:32], op=ALU.mult))
                    A("pe", ["hs_s", "W2b"], ["K3"], lambda e, tg=tg, typ=typ: e.matmul(
                        K3v[0:32, tg, :], lhsT=hs_s[:, tg, :], rhs=W2b[:, typ, :], start=True, stop=True))
                A("act", ["K3"], ["kcr"], lambda e: e.activation(out=kcr[0:32], in_=K3v[0:32, 0:2, :], func=AF.Copy))
                A("act", ["K3"], ["vca"], lambda e: e.activation(out=vca[:, :, 0:64], in_=K3v[0:32, 2:4, :], func=AF.Copy))
                A("dve", ["kcr"], ["LG"], lambda e: e.tensor_tensor(out=LG[0:32, 0:2, 0:64], in0=kcr[0:32], in1=kcr[0:32], op=ALU.mult))
                A("dve", ["LG"], ["st4"], lambda e: e.tensor_reduce(out=st4[0:32, 0:2], in_=LG[0:32, 0:2, 0:64], axis=AX.X, op=ALU.add))
                A("act", ["st4"], ["st4"], lambda e: e.activation(out=st4[0:32, 0:2], in_=st4[0:32, 0:2], func=AF.Ln,
                                                                  scale=1.0 / 64, bias=EPS))
                A("act", ["st4"], ["st4"], lambda e: e.activation(out=st4[0:32, 0:2], in_=st4[0:32, 0:2], func=AF.Exp, scale=-0.5))
                A("dve", ["kcr", "st4"], ["kcr"], lambda e: e.tensor_tensor(
                    out=kcr[0:32], in0=kcr[0:32], in1=st4[0:32, 0:2].unsqueeze(2).to_broadcast([32, 2, 64]), op=ALU.mult))
                A("dve", ["kcr", "gvec"], ["kcbs"], lambda e: e.tensor_tensor(
                    out=kcbs[:], in0=kcr[0:32], in1=gvec[0:32, 0:64].unsqueeze(1).to_broadcast([32, 2, 64]), op=ALU.mult))
                for g in range(2):
                    A("pe", ["kcbs", "idb"], ["K0"], lambda e, g=g: e.transpose(
                        out=K0[0:64, 512 + g * 32:512 + (g + 1) * 32], in_=kcbs[:, g, :], identity=idb[0:32, 0:32]))
                A("dve", ["K0"], ["KCa"], lambda e: e.tensor_copy(
                    out=KCa[0:64, :, :], in_=K0[0:64, 512:576].rearrange("p (g j) -> p g j", g=2)))
                A("sp", [], ["xs1"], lambda e: e.dma_start(out=WP[:].rearrange("p (t c) -> p t c", t=4),
                                                           in_=cwin[b].rearrange("(t p) c -> p t c", p=128)), dma=True)
                A("dve", ["xs1"], ["PGB0", "PGB1"], lambda e: e.tensor_copy(out=WPB[:].rearrange("p t c -> p (t c)"), in_=WP[:]))
                for tw in range(4):
                    for g in range(2):
                        A("pe", ["PGB0", "PGB1", "idb"], ["K0"], lambda e, tw=tw, g=g: e.transpose(
                            out=K0[0:64, (tw * 2 + g) * 128:(tw * 2 + g + 1) * 128], in_=WPB[:, tw, g * 64:(g + 1) * 64],
                            identity=idb[:]))
                A("dve", ["K0"], ["KTw"], lambda e: e.tensor_copy(
                    out=KTw[0:64, :, 0:512].rearrange("p g (t k) -> p g t k", t=4),
                    in_=K0[0:64, :].rearrange("p (t g k) -> p g t k", t=4, g=2)))
                A("act", ["xs1"], ["VVw"], lambda e: e.activation(
                    out=VVw[:, 0:4, :, 0:64], in_=WP[:].rearrange("p (t c) -> p t c", t=4)[:, :, 128:256].rearrange(
                        "p t (g d) -> p t g d", g=2), func=AF.Copy))
                A("dve", ["KN"], ["KTw"], lambda e: e.tensor_copy(out=KTw[0:64, :, 512:520], in_=KN[:, 2:4, b * 8:(b + 1) * 8]))
                A("dve", ["QTs"], ["QTb"], lambda e: e.tensor_copy(out=QTb[0:64, :, :], in_=QTs[:, :, b * 8:(b + 1) * 8]))
                for g in range(2):
                    qrhs = QTb[:, g * 4:(g + 1) * 4, :].rearrange("p h t -> p (h t)")
                    qrhs68 = QTb[0:68, g * 4:(g + 1) * 4, :].rearrange("p h t -> p (h t)")
                    A("pe", ["KCa", "QTb"], ["K3"], lambda e, g=g, qrhs68=qrhs68: e.matmul(
                        K[3][0:32, 0:32], lhsT=KCa[0:68, g, :], rhs=qrhs68, start=True, stop=True))
                    A("act", ["K3"], ["PTc"], lambda e: e.activation(out=PTc, in_=K[3][0:32, 0:32], func=AF.Exp))
                    A("pe", ["vca", "PTc"], ["K7"], lambda e, g=g: e.matmul(K[7][0:65, 0:32], lhsT=vca[:, g, :], rhs=PTc,
                                                                         start=True, stop=True))
                    A("act", ["K7"], ["OcT"], lambda e, g=g: e.activation(
                        out=OcT[:, g * 4:(g + 1) * 4, b * 8:(b + 1) * 8], in_=K[7][0:65, 0:32].rearrange("p (h t) -> p h t", h=4),
                        func=AF.Copy))
                    A("pe", ["PTc", "idb"], ["K0"], lambda e: e.transpose(out=K0[0:32, 640:672], in_=PTc, identity=idb[0:32, 0:32]))
                    A("dve", ["K0"], ["EX"], lambda e: e.tensor_copy(out=EX[0:32, 0, 0:32], in_=K0[0:32, 640:672]))
                    A("dve", ["EX"], ["st4b"], lambda e: e.tensor_reduce(out=st4b[0:32, 0:1], in_=EX[0:32, 0, 0:32], axis=AX.X, op=ALU.add))
                    A("dve", ["st4b"], ["st4b"], lambda e: e.reciprocal(out=st4b[0:32, 0:1], in_=st4b[0:32, 0:1]))
                    A("dve", ["EX", "st4b"], ["EX"], lambda e: e.tensor_scalar(
                        out=EX[0:32, 0, 0:32], in0=EX[0:32, 0, 0:32], scalar1=st4b[0:32, 0:1], scalar2=None, op0=ALU.mult))
                    A("pe", ["csts", "EX"], ["K3"], lambda e: e.matmul(K[3][0:8, 64:96], lhsT=selT, rhs=EX[0:32, 0, 0:32],
                                                                      start=True, stop=True))
                    A("dve", [], ["SC"], lambda e: e.memset(SC[0:8, 0:33], 0.0))
                    A("dve", ["K3"], ["SC"], lambda e: e.tensor_copy(out=SC[0:8, 0:32], in_=K[3][0:8, 64:96]))
                    A("dve", ["SC", "csts"], ["SC"], lambda e: e.tensor_tensor(out=SC[0:8, 0:33], in0=SC[0:8, 0:33], in1=keepS, op=ALU.mult))
                    A("dve", ["SC", "csts"], ["SC"], lambda e: e.tensor_tensor(out=SC[0:8, 0:33], in0=SC[0:8, 0:33], in1=addS, op=ALU.add))
                    A("dve", ["SC"], ["m8"], lambda e: e.max(out=m8[0:8], in_=SC[0:8, 0:33]))
                    A("dve", ["SC", "m8"], ["WK"], lambda e: e.match_replace(out=WK[0:8, 0:33], in_to_replace=m8[0:8],
                                                                              in_values=SC[0:8, 0:33], imm_value=-2.0))
                    A("dve", ["WK"], ["m8"], lambda e: e.max(out=m8[0:8], in_=WK[0:8, 0:33]))
                    A("dve", ["m8"], ["thr"], lambda e: e.tensor_reduce(out=thr[0:8], in_=m8[0:8], axis=AX.X, op=ALU.min))
                    A("dve", ["SC", "thr"], ["SEL"], lambda e: e.tensor_scalar(
                        out=SEL[0:8, 0:33], in0=SC[0:8, 0:33], scalar1=thr[0:8, 0:1], scalar2=None, op0=ALU.is_ge))
                    A("dve", ["SEL"], ["MBF"], lambda e: e.tensor_scalar(
                        out=MBF[0:8, 0:33], in0=SEL[0:8, 0:33], scalar1=-1.0, scalar2=BIG, op0=ALU.add, op1=ALU.mult))
                    A("pe", ["MBF", "idb"], ["K0"], lambda e: e.transpose(out=K0[0:33, 704:712], in_=MBF[0:8, 0:33],
                                                                          identity=idb[0:8, 0:8]))
                    A("dve", ["K0"], ["RHSMs"], lambda e: e.tensor_copy(
                        out=RHSMs[0:33], in_=K0[0:33, 704:712].unsqueeze(1).to_broadcast([33, 4, 8])))
                    for kt in range(17):
                        dst = K[5][:, kt * 32:(kt + 1) * 32] if kt < 16 else K[6][:, 0:32]
                        kb = "K5" if kt < 16 else "K6"
                        A("pe", ["KTs", "QTb"], [kb], lambda e, g=g, kt=kt, dst=dst, qrhs=qrhs: e.matmul(
                            dst, lhsT=KTs[:, g, kt * 128:(kt + 1) * 128], rhs=qrhs, start=True, stop=False))
                        A("pe", ["Eexp", "RHSMs"], [kb], lambda e, kt=kt, dst=dst: e.matmul(
                            dst, lhsT=Eexp[:, kt * 128:(kt + 1) * 128], rhs=RHSMs[:].rearrange("p h t -> p (h t)"),
                            start=False, stop=(kt < 16)))
                        if kt == 16:
                            A("pe", ["idb", "MSKs"], [kb], lambda e, dst=dst: e.matmul(dst, lhsT=idb[:], rhs=MSKs[:, 0, :],
                                                                                       start=False, stop=True))
                    A("act", ["K5"], ["PTs"], lambda e: e.activation(out=PTs[:, 0:512], in_=K[5][:], func=AF.Exp))
                    A("act", ["K6"], ["PTs"], lambda e: e.activation(out=PTs[:, 512:544], in_=K[6][:, 0:32], func=AF.Exp))
                    for kt in range(17):
                        A("pe", ["PTs", "VVs"], ["K7"], lambda e, g=g, kt=kt: e.matmul(
                            K[7][0:65, 32:64], lhsT=VVs[:, kt, g, :], rhs=PTs[:, kt * 32:(kt + 1) * 32],
                            start=(kt == 0), stop=(kt == 16)))
                    A("act", ["K7"], ["OsT"], lambda e, g=g: e.activation(
                        out=OsT[:, g * 4:(g + 1) * 4, b * 8:(b + 1) * 8], in_=K[7][0:65, 32:64].rearrange("p (h t) -> p h t", h=4),
                        func=AF.Copy))
                    for s_ in range(5):
                        dst = K[4][:, s_ * 32:(s_ + 1) * 32]
                        msk = {0: 1, 4: 0}.get(s_)
                        A("pe", ["KTw", "QTb"], ["K4"], lambda e, g=g, s_=s_, dst=dst, qrhs=qrhs, msk=msk: e.matmul(
                            dst, lhsT=KTw[:, g, s_ * 128:(s_ + 1) * 128], rhs=qrhs, start=True, stop=(msk is None)))
                        if msk is not None:
                            A("pe", ["idb", "MSKs"], ["K4"], lambda e, dst=dst, msk=msk: e.matmul(
                                dst, lhsT=idb[:], rhs=MSKs[:, msk, :], start=False, stop=True))
                    A("act", ["K4"], ["PTw"], lambda e: e.activation(out=PTw, in_=K[4][:, 0:160], func=AF.Exp))
                    for s_ in range(5):
                        A("pe", ["PTw", "VVw"], ["K7"], lambda e, g=g, s_=s_: e.matmul(
                            K[7][0:65, 64:96], lhsT=VVw[:, s_, g, :], rhs=PTw[:, s_ * 32:(s_ + 1) * 32],
                            start=(s_ == 0), stop=(s_ == 4)))
                    A("act", ["K7"], ["OwT"], lambda e, g=g: e.activation(
                        out=OwT[:, g * 4:(g + 1) * 4, b * 8:(b + 1) * 8], in_=K[7][0:65, 64:96].rearrange("p (h t) -> p h t", h=4),
                        func=AF.Copy))

            for b_ in range(16):
                sample_seq(b_)
            for src, ks, dst, kd in ((OcT, "OcT", OSEL, "OSEL"), (OsT, "OsT", OSEL, "OSEL"), (OwT, "OwT", OWIN, "OWIN")):
                for h in range(8):
                    A("pe", [ks, "idb"], ["K0"], lambda e, src=src, h=h: e.transpose(
                        out=K0[:, h * 66:h * 66 + 65], in_=src[:, h, :], identity=idb[0:65, 0:65]))
                A("act", ["K0"], [kd], lambda e, dst=dst: e.activation(
                    out=dst[:], in_=K0[:, 0:528].rearrange("p (h d) -> p h d", h=8)[:, :, 0:65], func=AF.Copy))
                if ks == "OcT":
                    A("dve", ["OSEL"], ["st8"], lambda e: e.reciprocal(out=st8[:], in_=OSEL[:, :, 64]))
                    A("dve", ["OSEL", "st8"], ["OCMP"], lambda e: e.tensor_tensor(
                        out=OCMP[:], in0=OSEL[:, :, 0:64], in1=st8[:].unsqueeze(2).to_broadcast([128, 8, 64]), op=ALU.mult))
            finish(xs[0], "xs0", ys[:, :], "ys", NOWN if debug else None)
        BKEYS = WK_ + AKEYS + ["Eexp", "MSK", "xT0", "xT1", "OC", "OCT", "KD", "VB", "QD", "KDI", "WUD", "QN", "qkT", "AT", "Sb0", "Sb1", "PB", "hs", "RHSM0", "RHSM1", "PT4"] + \
            ["S0_%d" % b_ for b_ in range(16)] + SKEYS
        if STAGE >= 7:
            A("dve", [], BKEYS + ["fdummy"], lambda e: e.memset(fdummy[:], 0.0))
            A("sp", [], ["lnmc"], lambda e: e.dma_start(out=lnmc[:], in_=lnmcol), dma=True)
            stg = [(T[i_], "T%d" % i_) for i_ in range(6)]
            cnt_ = [0]

            def wload(src, dst, scale_ap):
                st, ks = stg[cnt_[0] % 6]
                use_act = (cnt_[0] % 2 == 0)
                cnt_[0] += 1
                A("sp", [], [ks], lambda e: e.dma_start(out=st[:], in_=src), dma=True)
                rk = [ks] + (["lnmc"] if scale_ap is not None else [])
                if use_act:
                    if scale_ap is not None:
                        A("act", rk, ["WUD"], lambda e: e.activation(out=dst, in_=st[:], func=AF.Copy, scale=scale_ap))
                    else:
                        A("act", rk, ["WUD"], lambda e: e.activation(out=dst, in_=st[:], func=AF.Copy))
                else:
                    if scale_ap is not None:
                        A("dve", rk, ["WUD"], lambda e: e.tensor_scalar(out=dst, in0=st[:], scalar1=scale_ap, scalar2=None,
                                                                         op0=ALU.mult))
                    else:
                        A("dve", rk, ["WUD"], lambda e: e.tensor_copy(out=dst, in_=st[:]))

            for k in range(8):
                for q8 in range(8):
                    wload(w_up[k * 128:(k + 1) * 128, q8 * 512:(q8 + 1) * 512], WU[:, k, q8 * 512:(q8 + 1) * 512],
                          lnmc[:, k:k + 1])
            for f in range(32):
                for q2 in range(2):
                    wload(w_down[f * 128:(f + 1) * 128, q2 * 512:(q2 + 1) * 512], WD[:, f, q2 * 512:(q2 + 1) * 512], None)

            def mlp_tile(hsrc, ydst, ky, idx):
                X, kx = xs[idx % 2], "xs%d" % (idx % 2)
                A("sp", [ky], [kx], lambda e: e.dma_start(out=X[:], in_=hsrc), dma=True)
                A("act", [kx], ["OSEL", "ss"], lambda e: e.activation(
                    out=OSEL[:].rearrange("p h d -> p (h d)")[:, 0:512], in_=X[:, 0:512], func=AF.Square, accum_out=ss[:]))
                A("act", [kx], ["OSEL", "rstd"], lambda e: e.activation(
                    out=OSEL[:].rearrange("p h d -> p (h d)")[:, 0:512], in_=X[:, 512:1024], func=AF.Square,
                    accum_out=rstd[:]))
                A("dve", ["ss", "rstd"], ["ss"], lambda e: e.tensor_tensor(out=ss[:], in0=ss[:], in1=rstd[:], op=ALU.add))
                A("act", ["ss"], ["rstd"], lambda e: e.activation(out=rstd[:], in_=ss[:], func=AF.Ln, scale=1.0 / D, bias=EPS))
                A("act", ["rstd"], ["rstd"], lambda e: e.activation(out=rstd[:], in_=rstd[:], func=AF.Exp, scale=-0.5))
                A("dve", [kx, "rstd"], ["xb"], lambda e: e.tensor_scalar(out=xb[:], in0=X[:], scalar1=rstd[:, 0:1],
                                                                         scalar2=None, op0=ALU.mult))
                for k in range(8):
                    A("pe", ["xb", "idb"], ["K0"], lambda e, k=k: e.transpose(
                        out=K0[:, k * 128:(k + 1) * 128], in_=xb[:, k * 128:(k + 1) * 128], identity=idb[:]))
                A("dve", ["K0"], ["QTa"], lambda e: e.tensor_copy(out=QTa[:].rearrange("p k t -> p (k t)"), in_=K0[:]))
                def up_round(rnd):
                    ub = (rnd % 2) * 8
                    ku = "uT%d" % (rnd % 2)
                    for j in range(8):
                        f = rnd * 8 + j
                        bank = 1 + (f % 4)
                        kb = "K%d" % bank
                        R, kr = PTb[f % 2], "PTb%d" % (f % 2)
                        for k in range(8):
                            A("pe", ["QTa", "WUD"], [kb], lambda e, f=f, k=k, bank=bank: e.matmul(
                                K[bank][:, 0:128], lhsT=WU[:, k, f * 128:(f + 1) * 128], rhs=QTa[:, k, :],
                                start=(k == 0), stop=(k == 7)))
                        A("act", [kb], [kr], lambda e, bank=bank, R=R: e.activation(out=R[:, 0:128], in_=K[bank][:, 0:128],
                                                                                    func=AF.Relu))
                        A("dve", [kr], [ku], lambda e, j=j, ub=ub, R=R: e.tensor_tensor(
                            out=uT[:, ub + j, :], in0=R[:, 0:128], in1=R[:, 0:128], op=ALU.mult))

                def down_round(rnd):
                    ub = (rnd % 2) * 8
                    ku = "uT%d" % (rnd % 2)
                    for n in range(2):
                        bank = 5 + n
                        kb = "K%d" % bank
                        for j in range(8):
                            f = rnd * 8 + j
                            A("pe", [ku, "WUD"], [kb], lambda e, f=f, j=j, ub=ub, n=n, bank=bank: e.matmul(
                                K[bank][:], lhsT=uT[:, ub + j, :], rhs=WD[:, f, n * 512:(n + 1) * 512],
                                start=(f == 0), stop=(f == 31)))

                up_round(0)
                up_round(1)
                down_round(0)
                up_round(2)
                down_round(1)
                up_round(3)
                down_round(2)
                down_round(3)
                for n in range(2):
                    bank = 5 + n
                    kb = "K%d" % bank
                    A("dve", [kb, kx], ["H"], lambda e, n=n, bank=bank: e.tensor_tensor(
                        out=H[:, n * 512:(n + 1) * 512], in0=K[bank][:], in1=X[:, n * 512:(n + 1) * 512], op=ALU.add))
                A("pool", ["H"], [ky], lambda e: e.dma_start(out=ydst, in_=H[:]), dma=True, final=True)

            for i_ in range(NOWN):
                mlp_tile(yp[i_ * 128:(i_ + 1) * 128, :], yp[i_ * 128:(i_ + 1) * 128, :], "yp%d" % i_, i_)
            if STAGE >= 8:
                mlp_tile(ys[:, :], ys[:, :], "ys", NOWN)
        S.emit()
    return nc


def make_sample_consts():
    f32 = np.float32
    slope = 2.0 ** (-(np.arange(8) + 1.0))
    csts = np.zeros((128, 74), f32)
    for hh in range(4):
        for t in range(8):
            csts[hh * 8 + t, t] = 1.0
    keep = np.ones(33, f32)
    add = np.zeros(33, f32)
    keep[0] = keep[32] = 0.0
    add[0] = add[32] = 5.0
    csts[:, 8:41] = keep[None]
    csts[:, 41:74] = add[None]
    iota = np.arange(128, dtype=f32).reshape(128, 1)
    sq = np.zeros((8, 64), f32)
    for h in range(8):
        for t in range(8):
            sq[0, h * 8 + t] = 128.0 * slope[h]
            sq[1, h * 8 + t] = slope[h]
            sq[2, h * 8 + t] = -128.0 * 16.0 * slope[h]
            sq[3, h * 8 + t] = -slope[h] * t
    j = np.arange(32)
    sq[4, 0:32] = j // 2
    sq[5, 0:32] = 64 * (j % 2) + 63
    sq[6, 0:32] = 1.0
    sq[7, 0:32] = 1.0
    sm = np.zeros((128, 192), f32)
    r = np.arange(128)
    for par in range(2):
        for jj in range(64):
            sm[2 * jj + par, par * 64 + jj] = 1.0
    kk = r[:, None]
    tt = np.arange(8)[None, :]
    mn = np.where(kk <= tt, 0.0, -BIG).astype(f32)
    mw = np.where(kk > tt, 0.0, -BIG).astype(f32)
    sm[:, 128:160] = np.tile(mn, (1, 4))
    sm[:, 160:192] = np.tile(mw, (1, 4))
    return csts, iota, sq, sm


def make_consts(half):
    f32 = np.float32
    t = np.arange(128)
    slope = 2.0 ** (-(np.arange(8) + 1.0))
    cst = np.zeros((128, 747), f32)
    ac = np.zeros((128, 8, 66), f32)
    ac[:] = (slope[:, None] * 64.0 * np.arange(66)[None, :])[None]
    if half == 1:
        ac[:, :, 0:2] -= BIG
    cst[:, 0:528] = ac.reshape(128, 528)
    cst[:, 528] = np.where(t >= 63, 0.0, -BIG)
    cst[:, 529] = np.where(t == 127, 0.0, -BIG)
    rv = np.ones((128, NOWN), f32)
    if half == 0:
        rv[:63, 0] = 0.0
    cst[:, 530:547] = rv
    keep = np.ones(66, f32)
    add = np.zeros(66, f32)
    j0 = 2 * half
    keep[j0], add[j0] = 0.0, 5.0
    if half == 1:
        keep[0:2], add[0:2] = 0.0, -1.0
    cst[:, 547:613] = keep[None]
    cst[:, 613:679] = add[None]
    cst[:, 679] = np.where(t < 64, 5.0, 0.0)
    cst[:, 680] = np.where(t < 64, 0.0, 1.0)
    cst[:, 681] = np.where(t < 64, -1.0, 5.0)
    s_ = t % 64
    cst[:, 682:746] = (s_[:, None] <= np.arange(64)[None, :]).astype(f32)
    kk, tt = t[:, None], t[None, :]
    tri = np.where(kk <= tt, 0.0, -BIG).astype(f32)
    anti = np.where(kk > tt, 0.0, -BIG).astype(f32)
    full = np.zeros((128, 128), f32) if half == 0 else np.full((128, 128), -BIG, f32)
    m0anti = anti if half == 0 else np.full((128, 128), -BIG, f32)
    masks = np.stack([np.tile(m, (1, 4)) for m in (tri, anti, full, m0anti)], axis=1).astype(f32)
    eexp = np.zeros((66, NT * 128), f32)
    n = np.arange(NT * 128)
    eexp[n // 64, n] = 1.0
    kpos = np.stack([(n // 128).astype(f32), (n % 128).astype(f32), np.ones_like(n, f32), np.ones_like(n, f32)], 0)
    qpos = np.zeros((NOWN, 4, 2, 4, 128), f32)
    for i in range(NOWN):
        v = 2 * i
        for g in range(2):
            for hh in range(4):
                s = slope[g * 4 + hh]
                qpos[i, 0, g, hh, :] = 128.0 * s
                qpos[i, 1, g, hh, :] = s
                qpos[i, 2, g, hh, :] = -128.0 * v * s
                qpos[i, 3, g, hh, :] = -s * t
    return cst, masks, eexp, kpos.astype(f32), qpos.reshape(NOWN, 4, 1024)


_NC = None


def kernel(x_prompt, x_sample, cache_kv, cache_win, state_rnn, page_table, ln_mix, w_in, q_norm, k_norm,
           cmp_pe, cmp_w1, cmp_w2, attn_out_norm, rnn_lb_logits, rnn_out_norm, w_out, ln_mlp, w_up, w_down,
           _debug=False):
    global _NC
    f32 = np.float32
    x_prompt = np.asarray(x_prompt, f32)
    x_sample = np.asarray(x_sample, f32)
    cache_win = np.asarray(cache_win, f32)
    state_rnn = np.asarray(state_rnn, f32)
    k_norm = np.asarray(k_norm, f32)
    cmp_pe = np.asarray(cmp_pe, f32)
    if _NC is None or _NC[0] != _debug:
        _NC = (_debug, build_nc(_debug))
    nc = _NC[1]
    t = np.arange(128)

    def tmats(c):
        same = (t[:, None] // c == t[None, :] // c)
        return ((t[:, None] <= t[None, :]) & same).astype(f32), ((t[:, None] > t[None, :]) & same).astype(f32)

    tc_p, tr_p = tmats(64)
    tc_s, tr_s = tmats(8)
    rowmask = (t[:, None] // 8 == np.arange(16)[None, :]).astype(f32)
    gv = np.concatenate([k_norm[0, 0], k_norm[0, 1], k_norm[0, 2], np.asarray(q_norm, f32)[0],
                         np.asarray(attn_out_norm, f32)[0], np.asarray(rnn_out_norm, f32)[0]])
    peT = np.ascontiguousarray(cmp_pe[0].reshape(2, 32, 2, 64).transpose(2, 3, 0, 1).reshape(128, 2, 32))
    common = {
        "w_in": np.ascontiguousarray(np.asarray(w_in, f32)[0]),
        "w_out": np.ascontiguousarray(np.asarray(w_out, f32)[0]),
        "lncol": np.ascontiguousarray(np.asarray(ln_mix, f32)[0].reshape(8, 128).T),
        "gvecd": np.ascontiguousarray(np.broadcast_to(gv[None], (128, 1280))),
        "lbl": np.ascontiguousarray(np.broadcast_to(np.asarray(rnn_lb_logits, f32)[None], (128, 2, 512))),
        "ident": np.eye(128, dtype=f32),
        "w_up": np.ascontiguousarray(np.asarray(w_up, f32)[0]),
        "w_down": np.ascontiguousarray(np.asarray(w_down, f32)[0]),
        "lnmcol": np.ascontiguousarray(np.asarray(ln_mlp, f32)[0].reshape(8, 128).T),
        "tmat": np.ascontiguousarray(np.stack([tc_p, tr_p, tc_s, tr_s], axis=1)),
        "rowmask": rowmask,
        "peTd": peT,
        "cmp_w1": np.ascontiguousarray(np.asarray(cmp_w1, f32)[0]),
        "cmp_w2": np.ascontiguousarray(np.asarray(cmp_w2, f32)[0].transpose(1, 0, 2)),
    }
    zt = np.zeros((128, D), f32)
    cc = [make_consts(0), make_consts(1)]
    csts, iota, sqd, smd = make_sample_consts()
    cache2 = np.ascontiguousarray(np.asarray(cache_kv, f32)[0].reshape(2560 * 128, 512))
    ptab = np.asarray(page_table).astype(np.int32)
    common.update({"cache": cache2, "cstsd": csts, "iotad": iota, "sqd": sqd, "smd": smd})
    in_maps = []
    for c in range(8):
        s, half = c // 2, c % 2
        xvv = np.concatenate([x_prompt[s], zt], 0) if half == 0 else np.concatenate([zt, x_prompt[s]], 0)
        m = dict(common)
        m["xv"] = np.ascontiguousarray(xvv)
        m["xsm"] = np.ascontiguousarray(x_sample[16 * c:16 * c + 16].reshape(128, D))
        m["cwin"] = np.ascontiguousarray(cache_win[0, 16 * c:16 * c + 16].reshape(16, 512, 256))
        m["srnn"] = np.ascontiguousarray(state_rnn[0, 16 * c:16 * c + 16])
        m["ptrep"] = np.ascontiguousarray(np.broadcast_to(ptab[16 * c:16 * c + 16].reshape(1, 256), (128, 256)))
        m["cstd"], m["masks4"], m["eexp"], m["kposrows"], m["qposrows"] = cc[half]
        in_maps.append(m)
    res = run_bass_kernel_spmd(nc, in_maps, core_ids=list(range(8))).results

    y_prompt = np.zeros((4, 4096, D), f32)
    y_sample = np.zeros((128, 8, D), f32)
    kv_prompt = np.zeros((1, 4, 4096, 4, 2, 64), f32)
    kv_sample = np.zeros((1, 128, 8, 4, 2, 64), f32)
    win_prompt = np.zeros((1, 4, 512, 2, 2, 64), f32)
    win_sample = np.zeros((1, 128, 512, 2, 2, 64), f32)
    rnn_prompt = np.zeros((1, 4, 8, 64, 64), f32)
    rnn_sample = np.zeros((1, 128, 8, 64, 64), f32)
    for c in range(8):
        r = res[c]
        s, half = c // 2, c % 2
        ypc = r["yp"].reshape(NOWN, 128, D)
        yps = y_prompt[s].reshape(32, 128, D)
        if half == 0:
            yps[0::2] = ypc[0:16]
            kv_prompt[0, s] = r["kvp"][:4096].reshape(4096, 4, 2, 64)
            win_prompt[0, s] = r["winp"][:512].reshape(512, 2, 2, 64)
        else:
            yps[1::2] = ypc[1:17]
            rnn_prompt[0, s] = r["rnnp"]
        kv_sample[0, 16 * c:16 * c + 16] = r["kvs"].reshape(16, 8, 4, 2, 64)
        y_sample[16 * c:16 * c + 16] = r["ys"].reshape(16, 8, D)
        win_sample[0, 16 * c:16 * c + 16] = r["wins"].reshape(16, 512, 2, 2, 64)
        rnn_sample[0, 16 * c:16 * c + 16] = r["rnns"]
    if _debug:
        kernel._dbg = [res[c]["dbg"] for c in range(8)]
    return (y_prompt, y_sample, kv_prompt, kv_sample, win_prompt, win_sample, rnn_prompt, rnn_sample)
```

```python
import contextlib
import numpy as np
import concourse.bass as bass
import concourse.mybir as mybir
from concourse.bass_utils import run_bass_kernel_spmd

F32 = mybir.dt.float32
BF16 = mybir.dt.bfloat16
AF = mybir.ActivationFunctionType
ALU = mybir.AluOpType
AX = mybir.AxisListType

NT = 33
D = 1024
INW = 3352
C_Q, C_KV, C_G, C_RQ, C_RF, C_RI, C_RG = 0, 512, 1280, 1304, 1816, 2328, 2840
EPS = 1e-6

ENGS = ("pe", "act", "dve", "pool", "sp")
DMA_POOL = 8


class Op:
    __slots__ = ("eng", "fn", "deps", "dma", "idx", "needed", "token")

    def __init__(self, eng, fn, deps, dma, idx):
        self.eng, self.fn, self.deps, self.dma, self.idx = eng, fn, deps, dma, idx
        self.needed = False
        self.token = None


class Sched:
    def __init__(self, nc):
        self.nc = nc
        self.ops = []
        self.last_w = {}
        self.readers = {}
        self.final = []

    def add(self, eng, fn, r=(), w=(), dma=False, final=False, rg=0):
        idx = len(self.ops)
        deps = {}
        if eng == "pe":
            prev = getattr(self, "prev_pe", None)
            if prev is not None and prev[1] != rg:
                deps[prev[0]] = "force"
            self.prev_pe = (idx, rg)
        for k in r:
            lw = self.last_w.get(k)
            if lw is not None and deps.get(lw) != "force":
                deps[lw] = True
        for k in w:
            lw = self.last_w.get(k)
            if lw is not None and deps.get(lw) != "force":
                deps[lw] = True
            for rd in self.readers.get(k, ()):
                if rd not in deps:
                    deps[rd] = False
        for k in r:
            self.readers.setdefault(k, []).append(idx)
        for k in w:
            self.last_w[k] = idx
            self.readers[k] = []
        self.ops.append(Op(eng, fn, deps, dma, idx))
        if final:
            self.final.append(idx)
        return idx

    def emit(self):
        nc = self.nc
        ops = self.ops
        for op in ops:
            nd = {}
            for d, strong in op.deps.items():
                p = ops[d]
                if p.eng == op.eng and not p.dma:
                    if op.eng == "pe" and strong != "force":
                        continue
                nd[d] = strong
            best = {}
            keep = {}
            for d, strong in nd.items():
                p = ops[d]
                if p.dma:
                    keep[d] = strong
                elif p.eng not in best or d > best[p.eng]:
                    best[p.eng] = d
            for d in best.values():
                keep[d] = True
            op.deps = keep
            for d in keep:
                ops[d].needed = True
        for f in self.final:
            ops[f].needed = True
        cnt = {e: 0 for e in ENGS}
        dcnt = {e: 0 for e in ENGS}
        with contextlib.ExitStack() as es:
            NEP = 8
            EPOCH = 1500
            csem = {e: [es.enter_context(nc.semaphore("c_%s%d" % (e, i))) for i in range(NEP if e in ("pe", "dve", "act") else 1)]
                    for e in ENGS}
            dsem = {e: [es.enter_context(nc.semaphore("d_%s%d" % (e, i))) for i in range(DMA_POOL)]
                    for e in ("sp", "pool", "act")}
            for op in ops:
                if op.dma:
                    j = dcnt[op.eng]
                    dcnt[op.eng] += 1
                    op.token = (dsem[op.eng][j % DMA_POOL], 16 * (j // DMA_POOL + 1))
                elif op.needed:
                    ep = cnt[op.eng] // EPOCH
                    assert ep < len(csem[op.eng]), (op.eng, cnt[op.eng])
                    op.token = (csem[op.eng][ep], cnt[op.eng] % EPOCH + 1)
                    cnt[op.eng] += 1
            per = {e: [op for op in ops if op.eng == e] for e in ENGS}
            final_tokens = [ops[f].token for f in self.final]

            def run(engname, eng):
                known = {}
                for op in per[engname]:
                    for d in op.deps:
                        sem, val = ops[d].token
                        if known.get(id(sem), 0) >= val:
                            continue
                        eng.wait_ge(sem, val)
                        known[id(sem)] = val
                    if op.dma:
                        sem, val = op.token
                        if val > 16 and known.get(id(sem), 0) < val - 16:
                            eng.wait_ge(sem, val - 16)
                            known[id(sem)] = val - 16
                        op.fn(eng).then_inc(sem, 16)
                    else:
                        ins = op.fn(eng)
                        if op.needed:
                            ins.then_inc(op.token[0], 1)
                if engname == "sp":
                    for sem, val in final_tokens:
                        if known.get(id(sem), 0) >= val:
                            continue
                        eng.wait_ge(sem, val)
                        known[id(sem)] = val

            with nc.Block() as block:
                @block.tensor
                def _(e):
                    run("pe", e)

                @block.scalar
                def _(e):
                    run("act", e)

                @block.vector
                def _(e):
                    run("dve", e)

                @block.gpsimd
                def _(e):
                    run("pool", e)

                @block.sync
                def _(e):
                    run("sp", e)


import os
BIG = 30000.0
STAGE = int(os.environ.get("KSTAGE", "9"))
SUB = int(os.environ.get("KSUB", "9"))
SKIP = os.environ.get("KSKIP", "")
NOWN = 17
SCALE = 0.125
DEBUG = False


def build_nc(debug=False):
    nc = bass.Bass("TRN2", target_bir_lowering=False)

    def din(name, shape, dt=F32):
        return nc.dram_tensor(name, list(shape), dt, kind="ExternalInput").ap()

    def dout(name, shape, dt=F32):
        return nc.dram_tensor(name, list(shape), dt, kind="ExternalOutput").ap()

    xv = din("xv", [NT * 128, D])
    xsm = din("xsm", [128, D])
    w_in = din("w_in", [D, INW])
    w_out = din("w_out", [D, D])
    w_up = din("w_up", [D, 4 * D])
    w_down = din("w_down", [4 * D, D])
    lnmcol = din("lnmcol", [128, 8])
    lncol = din("lncol", [128, 8])
    gvec_d = din("gvecd", [128, 1280])
    lbl = din("lbl", [128, 2, 512])
    ident = din("ident", [128, 128])
    tmat = din("tmat", [128, 4, 128])
    rowmask = din("rowmask", [128, 16])
    cst_d = din("cstd", [128, 747])
    masks_d = din("masks4", [128, 4, 512])
    eexp_d = din("eexp", [66, NT * 128])
    kpos_d = din("kposrows", [4, NT * 128])
    qpos_d = din("qposrows", [NOWN, 4, 1024])
    pet_d = din("peTd", [128, 2, 32])
    w1_d = din("cmp_w1", [2, 4096, 128])
    w2_d = din("cmp_w2", [128, 2, 64])
    cwin = din("cwin", [16, 512, 256])
    cache_d = din("cache", [327680, 512])
    ptrep_d = din("ptrep", [128, 256], mybir.dt.int32)
    csts_d = din("cstsd", [128, 74])
    iota_d = din("iotad", [128, 1])
    sq_d = din("sqd", [8, 64])
    sm_d = din("smd", [128, 192])
    srnn = din("srnn", [16, 8, 64, 64])

    yp = dout("yp", [NOWN * 128, D])
    ys = dout("ys", [128, D])
    kvp = dout("kvp", [NT * 128, 512])
    kvs = dout("kvs", [128, 512])
    winp = dout("winp", [5 * 128, 256])
    wins = dout("wins", [16, 512, 256])
    rnnp = dout("rnnp", [8, 64, 64])
    rnns = dout("rnns", [16, 8, 64, 64])
    if debug:
        dbg = dout("dbg", [(NOWN + 1) * 128, 5, 520])

    S = Sched(nc)
    with contextlib.ExitStack() as es:
        def sb(name, shape, dt=F32):
            return es.enter_context(nc.sbuf_tensor(name, list(shape), dt))

        def ps(name, shape, dt=F32):
            return es.enter_context(nc.psum_tensor(name, list(shape), dt))

        def A(eng, r, w, fn, **kw):
            S.add(eng, fn, r=r, w=w, **kw)

        BIGW = sb("BIGW", [128, 65536], BF16)
        W = BIGW[:, 0:26816].rearrange("p (k n) -> p k n", k=8)
        ARENA = BIGW[:, 26816:47808]
        WU = BIGW[:, 0:32768].rearrange("p (k n) -> p k n", k=8)
        WD = BIGW[:, 32768:65536].rearrange("p (f n) -> p f n", f=32)
        _bo = [47808]

        def bw(n, parts=128):
            a = _bo[0]
            _bo[0] += n
            assert _bo[0] <= 65536
            return BIGW[0:parts, a:a + n]
        XcT = ARENA[:, 0:8448].rearrange("p (a n) -> p a n", a=4)
        W1b = ARENA[:, 8448:16640].rearrange("p (a j h) -> p a j h", a=2, j=32)
        KTs = ARENA[:, 0:8448].rearrange("p (g n) -> p g n", g=2)
        VVs = ARENA[:, 8448:12738].rearrange("p (t g d) -> p t g d", t=NT, g=2)
        WO = ARENA[:, 12800:20992].rearrange("p (k n) -> p k n", k=8)
        AKEYS = ["XcT", "W1b", "KTs", "VVs", "WO"]
        KTw = sb("KTw", [128, 2, 768], BF16)
        VVw = sb("VVw", [128, 6, 2, 65], BF16)
        Eexp = bw(NT * 128, 128)
        MSK = bw(2048).rearrange("p (m n) -> p m n", m=4)
        cst = sb("cst", [128, 747])
        gvec = sb("gvec", [128, 1280])
        qgs = sb("qgs", [128, 64])
        lnc = sb("lnc", [128, 8])
        lbb = sb("lbb", [128, 512])
        omlb = sb("omlb", [128, 512])
        idb = sb("idb", [128, 128], BF16)
        tm = sb("tm", [128, 4, 128])
        rmask = sb("rmask", [128, 16])
        ones = sb("ones", [128, 1])
        peT = sb("peT", [128, 2, 32])
        W2b = sb("W2b", [128, 2, 64], BF16)
        kcT = sb("kcT", [64, 2, 66], BF16)
        vcb = sb("vcb", [66, 2, 64], BF16)
        fdummy = sb("fdummy", [128, 1])
        acore = cst[:, 0:528].rearrange("p (h j) -> p h j", h=8)
        fix2 = cst[:, 528:530]
        rowvalid = cst[:, 530:547]
        keepc = cst[:, 547:613]
        addc = cst[:, 613:679]
        mlo5 = cst[:, 679:680]
        mhi = cst[:, 680:681]
        c1col = cst[:, 681:682]
        tri2 = cst[:, 682:746]
        xs = [sb("xs%d" % i, [128, D]) for i in range(2)]
        xb = sb("xb", [128, D], BF16)
        xT = [bw(1024).rearrange("p (k t) -> p k t", k=8) for i in range(2)]
        ss = sb("ss", [128, 1])
        rstd = sb("rstd", [128, 1])
        Z = [sb("Z0", [128, 768])] * 2
        zb = sb("zb", [128, 4, 64], BF16)
        sq4 = sb("sq4", [128, 2, 2, 64])
        ms4 = sb("ms4", [128, 2, 2])
        T = [sb("T%d" % i, [128, 512]) for i in range(6)]
        QS = sb("QS", [128, 512])
        RGS = sb("RGS", [128, 512])
        KD = bw(512)
        VB = bw(512)
        QD = bw(512)
        KDI = bw(512)
        qkT = bw(1024).rearrange("p (a t) -> p a t", a=8)
        AT = bw(512).rearrange("p (h t) -> p h t", h=8)
        Sb = [bw(256).rearrange("p (a d) -> p a d", a=4) for i in range(2)]
        ET = sb("ET", [128, 2, 4])
        St = sb("St", [128, 4, 64])
        ZQ = sb("ZQ", [128, 8, 64])
        QN = bw(512).rearrange("p (h d) -> p h d", h=8)
        st8 = sb("st8", [128, 8])
        QTa = sb("QTa", [128, 8, 128], BF16)
        gsig = sb("gsig", [128, 8, 3])
        LG = sb("LG", [128, 4, 66])
        EX = sb("EX", [128, 4, 66])
        PB = bw(264).rearrange("p (h j) -> p h j", h=4)
        st4 = sb("st4", [128, 4])
        st4b = sb("st4b", [128, 4])
        SC = sb("SC", [128, 66])
        WK = sb("WK", [128, 66])
        SEL = sb("SEL", [128, 66])
        m8 = sb("m8", [128, 8])
        thr = sb("thr", [128, 1])
        MBF = sb("MBF", [128, 66], BF16)
        RHSM_ = [bw(512, 128).rearrange("p (h t) -> p h t", h=4) for i in range(2)]
        PT4 = bw(512, 66).rearrange("p (h t) -> p h t", h=4)
        PTb = [sb("PTb%d" % i, [128, 512], BF16) for i in range(3)]
        OSEL = sb("OSEL", [128, 8, 65])
        OWIN = sb("OWIN", [128, 8, 65])
        OCMP = sb("OCMP", [128, 8, 64])
        OR32 = sb("OR32", [128, 8, 64])
        g8 = sb("g8", [128, 8, 3])
        OC = bw(1024)
        OCT = bw(1024).rearrange("p (k t) -> p k t", k=8)
        uT = sb("uT", [128, 16, 128], BF16)
        lnmc = sb("lnmc", [128, 8])
        H = sb("H", [128, D])
        stgq = H
        TMP = T[5][:].rearrange("p (h d) -> p h d", h=8)
        ACC = T[4][:].rearrange("p (h d) -> p h d", h=8)
        junk = OCT
        hs = bw(264).rearrange("p (h j) -> p h j", h=4)
        kcr = sb("kcr", [66, 2, 64])
        kcbf = sb("kcbf", [66, 2, 64], BF16)
        S0 = ARENA[:, 0:8192].bitcast(F32).rearrange("p (a d) -> p a d", a=64)
        etots = sb("etots", [128, 4, 16])
        kdm = PTb

        K0 = ps("K0", [128, 1024], BF16)
        K = [None] + [ps("K%d" % i, [128, 512]) for i in range(1, 8)]

        def ld(dst, src, key, eng="sp"):
            S.add(eng, lambda e: e.dma_start(out=dst, in_=src), w=[key], dma=True)

        ld(lnc[:], lncol, "lnc")
        ld(gvec[:], gvec_d, "gvec")
        ld(cst[:], cst_d, "cst")
        ld(tm[:], tmat, "tm")
        ld(rmask[:], rowmask, "rmask")
        ld(peT[:], pet_d, "peT")
        ld(T[0][:], lbl[:, 0, :], "T0")
        ld(T[1][:], lbl[:, 1, :], "T1")
        ld(T[2][:, 0:128], ident, "T2")
        ld(T[3][:, 0:128], w2_d.rearrange("p a d -> p (a d)"), "T3")
        A("dve", ["T2"], ["idb"], lambda e: e.tensor_copy(out=idb[:], in_=T[2][:, 0:128]))
        A("dve", ["T3"], ["W2b"], lambda e: e.tensor_copy(out=W2b[:].rearrange("p a d -> p (a d)"), in_=T[3][:, 0:128]))
        A("dve", [], ["ones"], lambda e: e.memset(ones[:], 1.0))
        A("dve", [], ["St"], lambda e: e.memset(St[:], 0.0))
        A("dve", ["gvec"], ["qgs"], lambda e: e.tensor_scalar_mul(out=qgs[:], in0=gvec[:, 192:256], scalar1=SCALE))
        A("dve", ["T0", "T1"], ["lbb"], lambda e: e.tensor_sub(out=lbb[:], in0=T[1][:], in1=T[0][:]))
        A("act", ["lbb"], ["lbb"], lambda e: e.activation(out=lbb[:], in_=lbb[:], func=AF.Exp))
        A("dve", ["lbb"], ["lbb"], lambda e: e.tensor_scalar_add(out=lbb[:], in0=lbb[:], scalar1=1.0))
        A("dve", ["lbb"], ["lbb"], lambda e: e.reciprocal(out=lbb[:], in_=lbb[:]))
        A("dve", ["lbb"], ["omlb"], lambda e: e.tensor_scalar(out=omlb[:], in0=lbb[:], scalar1=-1.0, scalar2=1.0,
                                                              op0=ALU.mult, op1=ALU.add))

        wstg = [(H, "H"), (xs[0], "xs0"), (xs[1], "xs1")]
        wi_ = 0
        for k in range(8):
            for hf in range(4):
                c0, c1 = hf * 838, (hf + 1) * 838
                st_, ks_ = wstg[wi_ % 3]
                A("sp", [], [ks_], lambda e, k=k, c0=c0, c1=c1, st_=st_: e.dma_start(
                    out=st_[:, 0:838], in_=w_in[k * 128:(k + 1) * 128, c0:c1]), dma=True)
                if wi_ % 2 == 0:
                    A("act", [ks_, "lnc"], ["W%d" % k], lambda e, k=k, c0=c0, c1=c1, st_=st_: e.activation(
                        out=W[:, k, c0:c1], in_=st_[:, 0:838], func=AF.Copy, scale=lnc[:, k:k + 1]))
                else:
                    A("dve", [ks_, "lnc"], ["W%d" % k], lambda e, k=k, c0=c0, c1=c1, st_=st_: e.tensor_scalar(
                        out=W[:, k, c0:c1], in0=st_[:, 0:838], scalar1=lnc[:, k:k + 1], scalar2=None, op0=ALU.mult))
                wi_ += 1
        WK_ = ["W%d" % k for k in range(8)]

        def prep_x(xsrc, p, xbi=0):
            X, XT = xs[p], xT[p]
            kx, kxt = "xs%d" % p, "xT%d" % p
            XB, kxb = (xb[:], "xb") if xbi == 0 else (uT[:, 0:8, :].rearrange("p a t -> p (a t)"), "uT0")
            A("sp", [], [kx], lambda e: e.dma_start(out=X[:], in_=xsrc), dma=True)
            A("act", [kx], ["OCT", "ss"], lambda e: e.activation(out=OCT[:].rearrange("p k t -> p (k t)"), in_=X[:], func=AF.Square, accum_out=ss[:]))
            A("act", ["ss"], ["rstd"], lambda e: e.activation(out=rstd[:], in_=ss[:], func=AF.Ln, scale=1.0 / D, bias=EPS))
            A("act", ["rstd"], ["rstd"], lambda e: e.activation(out=rstd[:], in_=rstd[:], func=AF.Exp, scale=-0.5))
            A("dve", [kx, "rstd"], [kxb], lambda e: e.tensor_scalar(out=XB, in0=X[:], scalar1=rstd[:, 0:1],
                                                                    scalar2=None, op0=ALU.mult))
            for k in range(8):
                A("pe", [kxb, "idb"], ["K0"], lambda e, k=k: e.transpose(
                    out=K0[:, k * 128:(k + 1) * 128], in_=XB[:, k * 128:(k + 1) * 128], identity=idb[:]))
            A("dve", ["K0"], [kxt], lambda e: e.tensor_copy(out=XT[:].rearrange("p k t -> p (k t)"), in_=K0[:]))
            return X, XT, kx, kxt

        def mm(XT, kxt, bank, n, c0):
            for k in range(8):
                A("pe", [kxt] + WK_, ["K%d" % bank], lambda e, k=k: e.matmul(
                    K[bank][:, 0:n], lhsT=XT[:, k, :], rhs=W[:, k, c0:c0 + n], start=(k == 0), stop=(k == 7)))

        A("dve", [], AKEYS + ["fdummy"], lambda e: e.memset(fdummy[:], 0.0))
        for typ in range(2):
            for jq in range(4):
                A("sp", [], ["H"], lambda e, typ=typ, jq=jq: e.dma_start(
                    out=stgq[:].rearrange("p (j h) -> p j h", j=8),
                    in_=w1_d[typ, jq * 1024:(jq + 1) * 1024, :].rearrange("(j p) h -> p j h", p=128)), dma=True)
                A("dve", ["H"], ["W1b"], lambda e, typ=typ, jq=jq: e.tensor_copy(
                    out=W1b[:, typ, jq * 8:(jq + 1) * 8, :], in_=stgq[:].rearrange("p (j h) -> p j h", j=8)))
        pre1a = {}

        def tile_1a(v):
            p = v % 2
            if v not in pre1a:
                pre1a[v] = prep_x(xv[v * 128:(v + 1) * 128, :], p, v % 2)
            X, XT, kx, kxt = pre1a[v]
            if v + 1 < NT:
                pre1a[v + 1] = prep_x(xv[(v + 1) * 128:(v + 2) * 128, :], (v + 1) % 2, (v + 1) % 2)
            K1v = K[1][:, 0:256].rearrange("p (a t) -> p a t", a=4)
            for tg in range(4):
                c0 = C_KV + tg * 64
                for par in range(2):
                    for k in range(8):
                        A("pe", [kxt] + WK_, ["K1"], lambda e, tg=tg, par=par, k=k, c0=c0: e.matmul(
                            K1v[64 * par:64 * par + 64, tg, :], lhsT=W[:, k, c0:c0 + 64], rhs=XT[:, k, par:128:2],
                            start=(k == 0), stop=(k == 7)))
            for a in range(2):
                A("dve", ["K1", "peT"], ["XcT"], lambda e, v=v, K1v=K1v, a=a: e.tensor_tensor(
                    out=XcT[:, 2 * a:2 * a + 2, v * 64:(v + 1) * 64].rearrange("p g (b j) -> p g b j", b=2),
                    in0=K1v[:, 2 * a:2 * a + 2, :].rearrange("p g (b j) -> p g b j", b=2),
                    in1=peT[:, a, :].unsqueeze(1).unsqueeze(1).to_broadcast([128, 2, 2, 32]), op=ALU.add))
        for v_ in range(NT if STAGE >= 1 else 0):
            tile_1a(v_)
        K3v = K[3][:, 0:256].rearrange("p (a d) -> p a d", a=4)
        for tg in range(4):
            typ = tg // 2
            for j in range(32):
                A("pe", ["XcT", "W1b"], ["K2"], lambda e, tg=tg, typ=typ, j=j: e.matmul(
                    K[2][:, 0:66], lhsT=W1b[:, typ, j, :], rhs=XcT[:, tg, j:2112:32], start=(j == 0), stop=(j == 31)))
            A("act", ["K2"], ["T4"], lambda e: e.activation(out=T[4][:, 0:66], in_=K[2][:, 0:66], func=AF.Exp, scale=-1.0))
            A("dve", ["T4"], ["T4"], lambda e: e.tensor_scalar_add(out=T[4][:, 0:66], in0=T[4][:, 0:66], scalar1=1.0))
            A("dve", ["T4"], ["T4"], lambda e: e.reciprocal(out=T[4][:, 0:66], in_=T[4][:, 0:66]))
            A("dve", ["T4", "K2"], ["hs"], lambda e, tg=tg: e.tensor_tensor(out=hs[:, tg, :], in0=K[2][:, 0:66],
                                                                           in1=T[4][:, 0:66], op=ALU.mult))
            A("pe", ["hs", "W2b"], ["K3"], lambda e, tg=tg, typ=typ: e.matmul(
                K3v[0:66, tg, :], lhsT=hs[:, tg, :], rhs=W2b[:, typ, :], start=True, stop=True))
        A("act", ["K3"], ["kcr"], lambda e: e.activation(out=kcr[:], in_=K3v[0:66, 0:2, :], func=AF.Copy))
        A("act", ["K3"], ["vcb"], lambda e: e.activation(out=vcb[:], in_=K3v[0:66, 2:4, :], func=AF.Copy))
        A("dve", ["kcr"], ["T5"], lambda e: e.tensor_tensor(out=T[5][0:66, 0:128].rearrange("p (g d) -> p g d", g=2),
                                                            in0=kcr[:], in1=kcr[:], op=ALU.mult))
        A("dve", ["T5"], ["st4"], lambda e: e.tensor_reduce(out=st4[0:66, 0:2],
                                                            in_=T[5][0:66, 0:128].rearrange("p (g d) -> p g d", g=2),
                                                            axis=AX.X, op=ALU.add))
        A("act", ["st4"], ["st4"], lambda e: e.activation(out=st4[0:66, 0:2], in_=st4[0:66, 0:2], func=AF.Ln,
                                                          scale=1.0 / 64, bias=EPS))
        A("act", ["st4"], ["st4"], lambda e: e.activation(out=st4[0:66, 0:2], in_=st4[0:66, 0:2], func=AF.Exp, scale=-0.5))
        A("dve", ["kcr", "st4"], ["kcr"], lambda e: e.tensor_tensor(
            out=kcr[:], in0=kcr[:], in1=st4[0:66, 0:2].unsqueeze(2).to_broadcast([66, 2, 64]), op=ALU.mult))
        A("dve", ["kcr", "gvec"], ["kcbf"], lambda e: e.tensor_tensor(
            out=kcbf[:], in0=kcr[:], in1=gvec[0:66, 0:64].unsqueeze(1).to_broadcast([66, 2, 64]), op=ALU.mult))
        for g in range(2):
            A("pe", ["kcbf", "idb"], ["K0"], lambda e, g=g: e.transpose(
                out=K0[0:64, g * 128:g * 128 + 66], in_=kcbf[:, g, :], identity=idb[0:66, 0:66]))
        A("dve", ["K0"], ["kcT"], lambda e: e.tensor_copy(
            out=kcT[:], in_=K0[0:64, 0:256].rearrange("p (g n) -> p g n", g=2)[:, :, 0:66]))
        A("dve", [], AKEYS + ["fdummy"], lambda e: e.memset(fdummy[:], 0.0))

        A("dve", [], ["KTs"], lambda e: e.memset(KTs[64:128, :, :], 0.0))
        sst = [(H, "H"), (xs[0], "xs0"), (xs[1], "xs1")]
        si_ = [0]

        def nst():
            t_ = sst[si_[0] % 3]
            si_[0] += 1
            return t_

        for c0 in range(0, NT * 128, 1024):
            n = min(1024, NT * 128 - c0)
            st_, ks_ = nst()
            A("sp", [], [ks_], lambda e, c0=c0, n=n, st_=st_: e.dma_start(out=st_[0:66, 0:n], in_=eexp_d[:, c0:c0 + n]), dma=True)
            A("dve", [], ["Eexp"], lambda e, c0=c0, n=n: e.memset(Eexp[64:128, c0:c0 + n], 0.0))
            A("dve", [ks_], ["Eexp"], lambda e, c0=c0, n=n, st_=st_: e.tensor_copy(out=Eexp[0:66, c0:c0 + n], in_=st_[0:66, 0:n]))
            st_, ks_ = nst()
            A("sp", [], [ks_], lambda e, c0=c0, n=n, st_=st_: e.dma_start(out=st_[64:68, 0:n], in_=kpos_d[:, c0:c0 + n]), dma=True)
            for g in range(2):
                A("dve", [ks_], ["KTs"], lambda e, c0=c0, n=n, g=g, st_=st_: e.tensor_copy(
                    out=KTs[64:68, g, c0:c0 + n], in_=st_[64:68, 0:n]))
        for m in range(4):
            st_, ks_ = nst()
            A("sp", [], [ks_], lambda e, m=m, st_=st_: e.dma_start(out=st_[:, 0:512], in_=masks_d[:, m, :]), dma=True)
            A("dve", [ks_], ["MSK"], lambda e, m=m, st_=st_: e.tensor_copy(out=MSK[:, m, :], in_=st_[:, 0:512]))
        for k in range(8):
            st_, ks_ = nst()
            A("sp", [], [ks_], lambda e, k=k, st_=st_: e.dma_start(out=st_[:], in_=w_out[k * 128:(k + 1) * 128, :]), dma=True)
            if k % 2 == 0:
                A("act", [ks_], ["WO"], lambda e, k=k, st_=st_: e.activation(out=WO[:, k, :], in_=st_[:], func=AF.Copy))
            else:
                A("dve", [ks_], ["WO"], lambda e, k=k, st_=st_: e.tensor_copy(out=WO[:, k, :], in_=st_[:]))
        A("dve", [], ["KTw"], lambda e: e.memset(KTw[64:128, :, :], 0.0))
        A("dve", [], ["QTa"], lambda e: e.memset(QTa[64:128, :, :], 0.0))
        for g_ in range(2):
            A("dve", [], ["RHSM%d" % g_], lambda e, g_=g_: e.memset(RHSM_[g_][64:128, :, :], 0.0))
        A("dve", [], ["VVs"], lambda e: e.memset(VVs[:, :, :, 64:65], 1.0))
        A("dve", [], ["VVw"], lambda e: e.memset(VVw[:, :, :, 64:65], 1.0))

        def rnn_gates(bf, bi, sample):
            kf, ki = "K%d" % bf, "K%d" % bi
            e1, uu, kk, G = T[0], T[1], T[2], T[3]
            A("act", [kf], ["T0"], lambda e: e.activation(out=e1[:], in_=K[bf][:], func=AF.Exp, scale=-1.0))
            A("act", [ki], ["VB"], lambda e: e.activation(out=VB[:], in_=K[bi][:], func=AF.Copy))
            A("dve", ["T0"], ["T0"], lambda e: e.tensor_scalar_add(out=e1[:], in0=e1[:], scalar1=1.0))
            A("dve", ["T0"], ["T0"], lambda e: e.reciprocal(out=e1[:], in_=e1[:]))
            A("dve", ["T0", "omlb"], ["T1"], lambda e: e.tensor_tensor(out=uu[:], in0=e1[:], in1=omlb[:], op=ALU.mult))
            A("dve", ["T1", "lbb"], ["T0"], lambda e: e.tensor_tensor(out=e1[:], in0=uu[:], in1=lbb[:], op=ALU.add))
            A("dve", ["T1", "omlb"], ["T2"], lambda e: e.tensor_tensor(out=kk[:], in0=omlb[:], in1=uu[:], op=ALU.subtract))
            A("act", ["T0"], ["T3"], lambda e: e.activation(out=G[:], in_=e1[:], func=AF.Ln))
            mi = 3 if sample else 1
            A("pe", ["tm", "T3"], ["K3"], lambda e: e.matmul(K[3][:], lhsT=tm[:, mi, :], rhs=G[:], start=True, stop=True))
            A("act", ["K3"], ["T4"], lambda e: e.activation(out=T[4][:], in_=K[3][:], func=AF.Exp))
            A("dve", ["T2", "T4"], ["KD"], lambda e: e.tensor_tensor(out=KD[:], in0=kk[:], in1=T[4][:], op=ALU.mult))

        pst = K[4][:, 0:256].rearrange("p (a d) -> p a d", a=4)
        ptot = K[4][:, 256:264].rearrange("p (c a) -> p c a", c=2)

        def state_tot():
            G = T[3]
            for c in range(2):
                for h in range(8):
                    hp, ee = h // 2, h % 2
                    A("pe", ["T3", "ones"], ["K4"], lambda e, c=c, h=h, hp=hp, ee=ee: e.matmul(
                        ptot[64 * ee:64 * ee + 64, c, hp:hp + 1], lhsT=G[c * 64:(c + 1) * 64, h * 64:(h + 1) * 64],
                        rhs=ones[c * 64:(c + 1) * 64, 0:1], start=True, stop=True), rg=64 * c)
            A("act", ["K4"], ["ET"], lambda e: e.activation(out=ET[:], in_=ptot, func=AF.Exp))

        def state_update(c):
            for h in range(8):
                hp, ee = h // 2, h % 2
                A("pe", ["KD", "VB"], ["K4"], lambda e, c=c, h=h, hp=hp, ee=ee: e.matmul(
                    pst[64 * ee:64 * ee + 64, hp, :], lhsT=KD[c * 64:(c + 1) * 64, h * 64:(h + 1) * 64],
                    rhs=VB[c * 64:(c + 1) * 64, h * 64:(h + 1) * 64], start=True, stop=True), rg=64 * c)
            A("dve", ["St", "ET"], ["St"], lambda e, c=c: e.tensor_tensor(
                out=St[:], in0=St[:], in1=ET[:, c, :].unsqueeze(2).to_broadcast([128, 4, 64]), op=ALU.mult))
            A("dve", ["St", "K4"], ["St"], lambda e: e.tensor_tensor(out=St[:], in0=St[:], in1=pst, op=ALU.add))

        def k_norms(Zt, kz):
            Z5 = Zt[:].rearrange("p (a b g d) -> p a b g d", a=3, b=2, g=2)
            KS = Z5[:, 1:3, 0, :, :]
            A("dve", [kz], ["sq4"], lambda e: e.tensor_tensor(out=sq4[:], in0=KS, in1=KS, op=ALU.mult))
            A("dve", ["sq4"], ["ms4"], lambda e: e.tensor_reduce(out=ms4[:], in_=sq4[:], axis=AX.X, op=ALU.add))
            A("act", ["ms4"], ["ms4"], lambda e: e.activation(out=ms4[:], in_=ms4[:], func=AF.Ln, scale=1.0 / 64, bias=EPS))
            A("act", ["ms4"], ["ms4"], lambda e: e.activation(out=ms4[:], in_=ms4[:], func=AF.Exp, scale=-0.5))
            A("dve", [kz, "ms4"], [kz], lambda e: e.tensor_tensor(
                out=KS, in0=KS, in1=ms4[:].unsqueeze(3).to_broadcast([128, 2, 2, 64]), op=ALU.mult))
            A("dve", [kz, "gvec"], [kz], lambda e: e.tensor_tensor(
                out=KS, in0=KS, in1=gvec[:, 64:192].rearrange("p (a d) -> p a d", a=2).unsqueeze(2).to_broadcast(
                    [128, 2, 2, 64]), op=ALU.mult))
            return Z5

        def silu_from(bank, dst, kdst):
            kb = "K%d" % bank
            A("act", [kb], ["T5"], lambda e: e.activation(out=T[5][:], in_=K[bank][:], func=AF.Exp, scale=-1.0))
            A("dve", ["T5"], ["T5"], lambda e: e.tensor_scalar_add(out=T[5][:], in0=T[5][:], scalar1=1.0))
            A("dve", ["T5"], ["T5"], lambda e: e.reciprocal(out=T[5][:], in_=T[5][:]))
            A("dve", ["T5", kb], [kdst], lambda e: e.tensor_tensor(out=dst[:], in0=K[bank][:], in1=T[5][:], op=ALU.mult))

        def head_rms(src3, ksrc, n):
            A("dve", [ksrc], ["T5"], lambda e: e.tensor_tensor(out=TMP[:, 0:n, :], in0=src3, in1=src3, op=ALU.mult))
            A("dve", ["T5"], ["st8"], lambda e: e.tensor_reduce(out=st8[:, 0:n], in_=TMP[:, 0:n, :], axis=AX.X, op=ALU.add))
            A("act", ["st8"], ["st8"], lambda e: e.activation(out=st8[:, 0:n], in_=st8[:, 0:n], func=AF.Ln,
                                                              scale=1.0 / 64, bias=EPS))
            A("act", ["st8"], ["st8"], lambda e: e.activation(out=st8[:, 0:n], in_=st8[:, 0:n], func=AF.Exp, scale=-0.5))

        def finish(X, kx, ydst, ykey, dbi):
            A("dve", ["OSEL"], ["st8"], lambda e: e.tensor_scalar_max(out=st8[:], in0=OSEL[:, :, 64], scalar1=1e-30))
            A("dve", ["st8"], ["st8"], lambda e: e.reciprocal(out=st8[:], in_=st8[:]))
            A("dve", ["st8", "gsig"], ["g8"], lambda e: e.tensor_tensor(out=g8[:, :, 1], in0=gsig[:, :, 1], in1=st8[:],
                                                                         op=ALU.mult))
            A("dve", ["OWIN"], ["st8"], lambda e: e.tensor_scalar_max(out=st8[:], in0=OWIN[:, :, 64], scalar1=1e-30))
            A("dve", ["st8"], ["st8"], lambda e: e.reciprocal(out=st8[:], in_=st8[:]))
            A("dve", ["st8", "gsig"], ["g8"], lambda e: e.tensor_tensor(out=g8[:, :, 2], in0=gsig[:, :, 2], in1=st8[:],
                                                                         op=ALU.mult))
            A("dve", ["OCMP", "gsig"], ["T4"], lambda e: e.tensor_tensor(
                out=ACC[:], in0=OCMP[:], in1=gsig[:, :, 0:1].to_broadcast([128, 8, 64]), op=ALU.mult))
            A("dve", ["OSEL", "g8"], ["T5"], lambda e: e.tensor_tensor(
                out=TMP[:], in0=OSEL[:, :, 0:64], in1=g8[:, :, 1:2].to_broadcast([128, 8, 64]), op=ALU.mult))
            A("dve", ["T4", "T5"], ["T4"], lambda e: e.tensor_tensor(out=ACC[:], in0=ACC[:], in1=TMP[:], op=ALU.add))
            A("dve", ["OWIN", "g8"], ["T5"], lambda e: e.tensor_tensor(
                out=TMP[:], in0=OWIN[:, :, 0:64], in1=g8[:, :, 2:3].to_broadcast([128, 8, 64]), op=ALU.mult))
            A("dve", ["T4", "T5"], ["T4"], lambda e: e.tensor_tensor(out=ACC[:], in0=ACC[:], in1=TMP[:], op=ALU.add))
            if debug and dbi is not None:
                for di, (src, kk_, n) in enumerate([(OCMP, "OCMP", 512), (OSEL, "OSEL", 520), (OWIN, "OWIN", 520),
                                                    (OR32, "OR32", 512), (ACC, "T4", 512)]):
                    A("pool", [kk_], [], lambda e, di=di, src=src, n=n: e.dma_start(
                        out=dbg[dbi * 128:(dbi + 1) * 128, di, 0:n], in_=src[:].rearrange("p h d -> p (h d)")),
                      dma=True, final=True)
            head_rms(ACC[:], "T4", 8)
            A("dve", ["T4", "st8"], ["T4"], lambda e: e.tensor_tensor(
                out=ACC[:], in0=ACC[:], in1=st8[:].unsqueeze(2).to_broadcast([128, 8, 64]), op=ALU.mult))
            A("dve", ["T4", "gvec"], ["OC"], lambda e: e.tensor_tensor(
                out=OC[:, 0:512], in0=ACC[:].rearrange("p h d -> p (h d)"), in1=gvec[:, 256:768], op=ALU.mult))
            head_rms(OR32[:], "OR32", 8)
            A("dve", ["OR32", "st8"], ["OR32"], lambda e: e.tensor_tensor(
                out=OR32[:], in0=OR32[:], in1=st8[:].unsqueeze(2).to_broadcast([128, 8, 64]), op=ALU.mult))
            A("dve", ["OR32", "gvec"], ["OR32"], lambda e: e.tensor_tensor(
                out=OR32[:].rearrange("p h d -> p (h d)"), in0=OR32[:].rearrange("p h d -> p (h d)"),
                in1=gvec[:, 768:1280], op=ALU.mult))
            A("dve", ["OR32", "RGS"], ["OC"], lambda e: e.tensor_tensor(
                out=OC[:, 512:1024], in0=OR32[:].rearrange("p h d -> p (h d)"), in1=RGS[:], op=ALU.mult))
            for k in range(8):
                A("pe", ["OC", "idb"], ["K0"], lambda e, k=k: e.transpose(
                    out=K0[:, k * 128:(k + 1) * 128], in_=OC[:, k * 128:(k + 1) * 128], identity=idb[:]))
            A("dve", ["K0"], ["OCT"], lambda e: e.tensor_copy(out=OCT[:].rearrange("p k t -> p (k t)"), in_=K0[:]))
            for n in range(2):
                for k in range(8):
                    A("pe", ["OCT", "WO"], ["K%d" % (1 + n)], lambda e, n=n, k=k: e.matmul(
                        K[1 + n][:], lhsT=OCT[:, k, :], rhs=WO[:, k, n * 512:(n + 1) * 512], start=(k == 0), stop=(k == 7)))
                A("dve", ["K%d" % (1 + n), kx], ["H"], lambda e, n=n, X=X: e.tensor_tensor(
                    out=H[:, n * 512:(n + 1) * 512], in0=K[1 + n][:], in1=X[:, n * 512:(n + 1) * 512], op=ALU.add))
            A("pool", ["H"], [ykey], lambda e: e.dma_start(out=ydst, in_=H[:]), dma=True, final=True)

        pre1b = {}

        def tile_1b(v):
            p = v % 2
            own = (v % 2 == 0) and STAGE >= 3
            i_own = v // 2
            if v not in pre1b:
                pre1b[v] = prep_x(xv[v * 128:(v + 1) * 128, :], p, v % 2)
            X, XT, kx, kxt = pre1b[v]

            def prep_next():
                if v + 1 < NT and (v + 1) not in pre1b:
                    pre1b[v + 1] = prep_x(xv[(v + 1) * 128:(v + 2) * 128, :], (v + 1) % 2, (v + 1) % 2)
            Zt, kz = Z[p], "Z0"
            mm(XT, kxt, 1, 512, C_KV)
            mm(XT, kxt, 2, 256, C_KV + 512)
            A("act", ["K1"], [kz], lambda e, Zt=Zt: e.activation(out=Zt[:, 0:512], in_=K[1][:, 0:512], func=AF.Copy))
            A("act", ["K2"], [kz], lambda e, Zt=Zt: e.activation(out=Zt[:, 512:768], in_=K[2][:, 0:256], func=AF.Copy))
            mm(XT, kxt, 3, 512, C_RF)
            mm(XT, kxt, 5, 512, C_RI)
            Z5 = k_norms(Zt, kz)
            A("pool", [kz], [], lambda e, v=v, Zt=Zt: e.dma_start(out=kvp[v * 128:(v + 1) * 128, :], in_=Zt[:, 0:512]),
              dma=True, final=True)
            if v >= 28:
                A("pool", [kz], [], lambda e, v=v, Zt=Zt: e.dma_start(
                    out=winp[(v - 28) * 128:(v - 27) * 128, :], in_=Zt[:, 512:768]), dma=True, final=True)
            A("dve", [kz], ["zb"], lambda e, Z5=Z5: e.tensor_copy(out=zb[:].rearrange("p (a g) d -> p a g d", a=2),
                                                                  in_=Z5[:, 1:3, 0, :, :]))
            for a4 in range(4):
                A("pe", ["zb", "idb"], ["K0"], lambda e, a4=a4: e.transpose(
                    out=K0[0:64, a4 * 128:(a4 + 1) * 128], in_=zb[:, a4, :], identity=idb[:]))
            A("dve", ["K0"], ["KTs"], lambda e, v=v: e.tensor_copy(
                out=KTs[0:64, :, v * 128:(v + 1) * 128], in_=K0[0:64, 0:256].rearrange("p (g t) -> p g t", g=2)))
            sl = v % 6
            A("dve", ["K0"], ["KTw"], lambda e, sl=sl: e.tensor_copy(
                out=KTw[0:64, :, sl * 128:(sl + 1) * 128], in_=K0[0:64, 256:512].rearrange("p (g t) -> p g t", g=2)))
            A("sp", [], ["H"], lambda e, v=v: e.dma_start(out=stgq[64:68, 0:128], in_=kpos_d[:, v * 128:(v + 1) * 128]),
              dma=True)
            A("dve", ["H"], ["KTw"], lambda e, sl=sl: e.tensor_copy(
                out=KTw[64:68, :, sl * 128:(sl + 1) * 128], in_=stgq[64:68, 0:128].unsqueeze(1).to_broadcast([4, 2, 128])))
            A("dve", [kz], ["VVs"], lambda e, v=v, Z5=Z5: e.tensor_copy(out=VVs[:, v, :, 0:64], in_=Z5[:, 1, 1, :, :]))
            A("dve", [kz], ["VVw"], lambda e, sl=sl, Z5=Z5: e.tensor_copy(out=VVw[:, sl, :, 0:64], in_=Z5[:, 2, 1, :, :]))
            rnn_gates(3, 5, False)
            state_tot()
            if not own:
                prep_next()
            if own:
                mm(XT, kxt, 1, 512, C_Q)
                mm(XT, kxt, 2, 24, C_G)
                A("act", ["K1"], ["ZQ"], lambda e: e.activation(out=ZQ[:].rearrange("p h d -> p (h d)"), in_=K[1][:],
                                                                 func=AF.Copy))
                A("act", ["K2"], ["gsig"], lambda e: e.activation(out=gsig[:].rearrange("p h b -> p (h b)"),
                                                                   in_=K[2][:, 0:24], func=AF.Exp, scale=-1.0))
                A("dve", ["gsig"], ["gsig"], lambda e: e.tensor_scalar_add(out=gsig[:], in0=gsig[:], scalar1=1.0))
                A("dve", ["gsig"], ["gsig"], lambda e: e.reciprocal(out=gsig[:], in_=gsig[:]))
                mm(XT, kxt, 6, 512, C_RQ)
                silu_from(6, QS, "QS")
                mm(XT, kxt, 7, 512, C_RG)
                silu_from(7, RGS, "RGS")
                head_rms(ZQ[:], "ZQ", 8)
                A("dve", ["ZQ", "st8"], ["ZQ"], lambda e: e.tensor_tensor(
                    out=ZQ[:], in0=ZQ[:], in1=st8[:].unsqueeze(2).to_broadcast([128, 8, 64]), op=ALU.mult))
                A("dve", ["ZQ", "qgs"], ["QN"], lambda e: e.tensor_tensor(
                    out=QN[:], in0=ZQ[:], in1=qgs[:].unsqueeze(1).to_broadcast([128, 8, 64]), op=ALU.mult))
                for h in range(8):
                    A("pe", ["QN", "idb"], ["K0"], lambda e, h=h: e.transpose(
                        out=K0[0:64, h * 128:(h + 1) * 128], in_=QN[:, h, :], identity=idb[:]))
                A("dve", ["K0"], ["QTa"], lambda e: e.tensor_copy(out=QTa[0:64, :, :].rearrange("p h t -> p (h t)"),
                                                                  in_=K0[0:64, :]))
                A("sp", [], ["H"], lambda e, i_own=i_own: e.dma_start(out=stgq[64:68, :], in_=qpos_d[i_own]), dma=True)
                A("dve", ["H"], ["QTa"], lambda e: e.tensor_copy(out=QTa[64:68, :, :].rearrange("p h t -> p (h t)"),
                                                                    in_=stgq[64:68, :]))
                if SUB < 1:
                    return
                A("pe", ["tm", "T3"], ["K3"], lambda e: e.matmul(K[3][:], lhsT=tm[:, 0, :], rhs=T[3][:], start=True, stop=True))
                A("act", ["K3"], ["T4"], lambda e: e.activation(out=T[4][:], in_=K[3][:], func=AF.Exp))
                A("act", ["K3"], ["T5"], lambda e: e.activation(out=T[5][:], in_=K[3][:], func=AF.Exp, scale=-1.0))
                A("dve", ["QS", "T4"], ["QD"], lambda e: e.tensor_tensor(out=QD[:], in0=QS[:], in1=T[4][:], op=ALU.mult))
                A("dve", ["T2", "T5"], ["KDI"], lambda e: e.tensor_tensor(out=KDI[:], in0=T[2][:], in1=T[5][:], op=ALU.mult))
                for hp in range(4):
                    A("pe", ["QD", "idb"], ["K0"], lambda e, hp=hp: e.transpose(
                        out=K0[:, hp * 128:(hp + 1) * 128], in_=QD[:, hp * 128:(hp + 1) * 128], identity=idb[:]))
                    A("pe", ["KDI", "idb"], ["K0"], lambda e, hp=hp: e.transpose(
                        out=K0[:, (4 + hp) * 128:(5 + hp) * 128], in_=KDI[:, hp * 128:(hp + 1) * 128], identity=idb[:]))
                A("dve", ["K0"], ["qkT"], lambda e: e.tensor_copy(out=qkT[:].rearrange("p a t -> p (a t)"), in_=K0[:]))
                if SUB < 2:
                    return
                pA = K[5][:].rearrange("p (h t) -> p h t", h=8)
                for c in range(2):
                    for h in [0, 2, 4, 6, 1, 3, 5, 7]:
                        hp, ee = h // 2, h % 2
                        A("pe", ["qkT"], ["K5"], lambda e, c=c, h=h, hp=hp, ee=ee: e.matmul(
                            pA[64 * c:64 * c + 64, h, :], lhsT=qkT[64 * ee:64 * ee + 64, 4 + hp, c * 64:(c + 1) * 64],
                            rhs=qkT[64 * ee:64 * ee + 64, hp, c * 64:(c + 1) * 64], start=True, stop=True), rg=64 * ee)
                if "a" not in SKIP:
                    A("dve", ["K5", "cst"], ["AT"], lambda e: e.tensor_tensor(
                        out=AT[:], in0=pA, in1=tri2.unsqueeze(1).to_broadcast([128, 8, 64]), op=ALU.mult))
                if "s" not in SKIP:
                    A("dve", ["St"], ["Sb0"], lambda e: e.tensor_copy(out=Sb[0][:], in_=St[:]))
            state_update(0)
            if own:
                A("dve", ["St"], ["Sb1"], lambda e: e.tensor_copy(out=Sb[1][:], in_=St[:]))
            state_update(1)
            if not own:
                return
            if SUB < 3:
                return
            po = K[6][:].rearrange("p (h d) -> p h d", h=8)
            for c in range(2):
                for h in range(8):
                    A("pe", ["AT", "VB"], ["K6"], lambda e, c=c, h=h: e.matmul(
                        po[64 * c:64 * c + 64, h, :], lhsT=AT[64 * c:64 * c + 64, h, :],
                        rhs=VB[64 * c:64 * c + 64, h * 64:(h + 1) * 64], start=(h == 0), stop=False), rg=64 * c)
                order = [h for h in range(8) if h % 2 == c] + [h for h in range(8) if h % 2 != c]
                for j, h in enumerate(order):
                    hp, ee = h // 2, h % 2
                    A("pe", ["qkT", "Sb%d" % c], ["K6"], lambda e, c=c, h=h, hp=hp, ee=ee, j=j: e.matmul(
                        po[64 * c:64 * c + 64, h, :], lhsT=qkT[64 * ee:64 * ee + 64, hp, c * 64:(c + 1) * 64],
                        rhs=Sb[c][64 * ee:64 * ee + 64, hp, :], start=False, stop=(j == 7)), rg=64 * ee)
            A("act", ["K6"], ["OR32"], lambda e: e.activation(out=OR32[:].rearrange("p h d -> p (h d)"), in_=K[6][:],
                                                               func=AF.Copy))
            if STAGE < 4:
                return
            prep_next()
            nb = 2 * v + 2
            pc4 = K[3][:, 0:264].rearrange("p (h j) -> p h j", h=4)
            pocmp = K[4][:, 0:256].rearrange("p (h d) -> p h d", h=4)
            for g in range(2):
                for hh in range(4):
                    A("pe", ["QTa", "kcT"], ["K3"], lambda e, g=g, hh=hh: e.matmul(
                        pc4[:, hh, 0:nb], lhsT=QTa[0:64, g * 4 + hh, :], rhs=kcT[0:64, g, 0:nb], start=True, stop=True))
                A("dve", ["K3", "cst"], ["LG"], lambda e, g=g: e.tensor_tensor(
                    out=LG[:, :, 0:nb], in0=pc4[:, :, 0:nb], in1=acore[:, g * 4:(g + 1) * 4, 0:nb], op=ALU.add))
                A("dve", ["LG", "cst"], ["LG"], lambda e: e.tensor_tensor(
                    out=LG[:, :, 2 * v:2 * v + 2], in0=LG[:, :, 2 * v:2 * v + 2],
                    in1=fix2.unsqueeze(1).to_broadcast([128, 4, 2]), op=ALU.add))
                A("dve", ["LG"], ["st4"], lambda e: e.tensor_reduce(out=st4[:], in_=LG[:, :, 0:nb], axis=AX.X, op=ALU.max))
                A("dve", ["LG", "st4"], ["LG"], lambda e: e.tensor_tensor(
                    out=LG[:, :, 0:nb], in0=LG[:, :, 0:nb], in1=st4[:].unsqueeze(2).to_broadcast([128, 4, nb]),
                    op=ALU.subtract))
                A("act", ["LG"], ["EX"], lambda e: e.activation(out=EX[:, :, 0:nb], in_=LG[:, :, 0:nb], func=AF.Exp))
                A("dve", ["EX"], ["st4b"], lambda e: e.tensor_reduce(out=st4b[:], in_=EX[:, :, 0:nb], axis=AX.X, op=ALU.add))
                A("dve", ["st4b"], ["st4b"], lambda e: e.reciprocal(out=st4b[:], in_=st4b[:]))
                A("dve", ["st4b", "cst"], ["st4b"], lambda e: e.tensor_scalar(
                    out=st4b[:], in0=st4b[:], scalar1=rowvalid[:, i_own:i_own + 1], scalar2=None, op0=ALU.mult))
                A("dve", ["EX", "st4b"], ["EX"], lambda e: e.tensor_tensor(
                    out=EX[:, :, 0:nb], in0=EX[:, :, 0:nb], in1=st4b[:].unsqueeze(2).to_broadcast([128, 4, nb]),
                    op=ALU.mult))
                A("dve", ["EX"], ["PB"], lambda e: e.tensor_copy(out=PB[:, :, 0:nb], in_=EX[:, :, 0:nb]))
                A("dve", ["EX"], ["SC"], lambda e: e.tensor_reduce(
                    out=SC[:, 0:nb], in_=EX[:, :, 0:nb].rearrange("p h j -> p j h"), axis=AX.X, op=ALU.add))
                if nb < 66:
                    A("dve", [], ["SC"], lambda e: e.memset(SC[:, nb:66], -1.0))
                A("dve", ["SC", "cst"], ["SC"], lambda e: e.scalar_tensor_tensor(
                    out=SC[:, 2 * v:2 * v + 1], in0=SC[:, 2 * v:2 * v + 1], scalar=mhi, in1=mlo5,
                    op0=ALU.mult, op1=ALU.add))
                A("dve", ["cst"], ["SC"], lambda e: e.tensor_copy(out=SC[:, 2 * v + 1:2 * v + 2], in_=c1col))
                A("dve", ["SC", "cst"], ["SC"], lambda e: e.tensor_tensor(out=SC[:], in0=SC[:], in1=keepc, op=ALU.mult))
                A("dve", ["SC", "cst"], ["SC"], lambda e: e.tensor_tensor(out=SC[:], in0=SC[:], in1=addc, op=ALU.add))
                A("dve", ["SC"], ["m8"], lambda e: e.max(out=m8[:], in_=SC[:]))
                A("dve", ["SC", "m8"], ["WK"], lambda e: e.match_replace(out=WK[:], in_to_replace=m8[:], in_values=SC[:],
                                                                          imm_value=-2.0))
                A("dve", ["WK"], ["m8"], lambda e: e.max(out=m8[:], in_=WK[:]))
                A("dve", ["m8"], ["thr"], lambda e: e.tensor_reduce(out=thr[:], in_=m8[:], axis=AX.X, op=ALU.min))
                A("dve", ["SC", "thr"], ["SEL"], lambda e: e.tensor_scalar(
                    out=SEL[:], in0=SC[:], scalar1=thr[:, 0:1], scalar2=None, op0=ALU.is_ge))
                A("dve", ["SC"], ["WK"], lambda e: e.tensor_single_scalar(out=WK[:], in_=SC[:], scalar=0.0, op=ALU.is_ge))
                A("dve", ["SEL", "WK"], ["SEL"], lambda e: e.tensor_tensor(out=SEL[:], in0=SEL[:], in1=WK[:], op=ALU.mult))
                A("dve", ["SEL"], ["MBF"], lambda e: e.tensor_scalar(
                    out=MBF[:], in0=SEL[:], scalar1=-1.0, scalar2=BIG, op0=ALU.add, op1=ALU.mult))
                A("pe", ["MBF", "idb"], ["K0"], lambda e: e.transpose(out=K0[0:66, 0:128], in_=MBF[:], identity=idb[:]))
                for hh in range(4):
                    A("pe", ["PB", "idb"], ["K0"], lambda e, hh=hh: e.transpose(
                        out=K0[0:nb, (1 + hh) * 128:(2 + hh) * 128], in_=PB[:, hh, 0:nb], identity=idb[:]))
                A("dve", ["K0"], ["RHSM%d" % g], lambda e, g=g: e.tensor_copy(
                    out=RHSM_[g][0:66], in_=K0[0:66, 0:128].unsqueeze(1).to_broadcast([66, 4, 128])))
                A("dve", ["K0"], ["PT4"], lambda e: e.tensor_copy(
                    out=PT4[0:nb, :, :], in_=K0[0:nb, 128:640].rearrange("p (h t) -> p h t", h=4)))
                for hh in range(4):
                    A("pe", ["PT4", "vcb"], ["K4"], lambda e, g=g, hh=hh: e.matmul(
                        pocmp[:, hh, :], lhsT=PT4[0:nb, hh, :], rhs=vcb[0:nb, g, :], start=True, stop=True))
                A("act", ["K4"], ["OCMP"], lambda e, g=g: e.activation(out=OCMP[:, g * 4:(g + 1) * 4, :], in_=pocmp,
                                                                        func=AF.Copy))
            if STAGE < 5:
                return
            posb = [(K[7][:, 0:260].rearrange("p (h d) -> p h d", h=4), "K7"),
                    (K[4][:, 0:260].rearrange("p (h d) -> p h d", h=4), "K4")]
            its = []
            gi = 0
            for g in range(2):
                for br in range(2):
                    kts = list(range(0, v + 1)) if br == 0 else list(range(max(0, v - 4), v + 1))
                    for kt in kts:
                        extra = []
                        if br == 0:
                            extra.append((Eexp[:, kt * 128:(kt + 1) * 128],
                                          RHSM_[g][:].rearrange("p h t -> p (h t)"), ["Eexp", "RHSM%d" % g]))
                            if kt == v:
                                extra.append((idb[:], MSK[:, 0, :], ["idb", "MSK"]))
                            lhs_k, rk = KTs[:, g, kt * 128:(kt + 1) * 128], ["KTs"]
                            rv, kv_ = VVs[:, kt, g, :], "VVs"
                        else:
                            if kt == v:
                                extra.append((idb[:], MSK[:, 0, :], ["idb", "MSK"]))
                            elif kt == v - 4:
                                extra.append((idb[:], MSK[:, 3 if kt == 0 else 1, :], ["idb", "MSK"]))
                            elif kt == 0:
                                extra.append((idb[:], MSK[:, 2, :], ["idb", "MSK"]))
                            slk = kt % 6
                            lhs_k, rk = KTw[:, g, slk * 128:(slk + 1) * 128], ["KTw"]
                            rv, kv_ = VVw[:, slk, g, :], "VVw"
                        its.append(dict(g=g, br=br, first=(kt == kts[0]), last=(kt == kts[-1]), extra=extra, lhs_k=lhs_k,
                                        rk=rk, rv=rv, kv=kv_, gi=gi))
                    gi += 1

            SBK = [5, 6, 1]

            def s_stage(i):
                d = its[i]
                bank = SBK[i % 3]
                kb = "K%d" % bank
                g, extra = d["g"], d["extra"]
                A("pe", d["rk"] + ["QTa"], [kb], lambda e, bank=bank, lhs_k=d["lhs_k"], g=g, ne=len(extra): e.matmul(
                    K[bank][:], lhsT=lhs_k, rhs=QTa[:, g * 4:(g + 1) * 4, :].rearrange("p h t -> p (h t)"),
                    start=True, stop=(ne == 0)))
                for xi, (l_, r_, ks_) in enumerate(extra):
                    A("pe", ks_, [kb], lambda e, bank=bank, l_=l_, r_=r_, last=(xi == len(extra) - 1): e.matmul(
                        K[bank][:], lhsT=l_, rhs=r_, start=False, stop=last))

            def ep_stage(i):
                d = its[i]
                bank = SBK[i % 3]
                kb = "K%d" % bank
                PTt, kpt = PTb[i % 3], "PTb%d" % (i % 3)
                pos, kpos_ = posb[d["gi"] % 2]
                A("act", [kb], [kpt], lambda e, bank=bank, PTt=PTt: e.activation(out=PTt[:], in_=K[bank][:], func=AF.Exp))
                for hh in range(4):
                    A("pe", [kpt, d["kv"]], [kpos_], lambda e, hh=hh, PTt=PTt, rv=d["rv"], pos=pos,
                      first=(d["first"] and hh == 0), last=(d["last"] and hh == 3): e.matmul(
                          pos[:, hh, :], lhsT=PTt[:, hh * 128:(hh + 1) * 128], rhs=rv, start=first, stop=last))
                if d["last"]:
                    dst, kd = (OSEL, "OSEL") if d["br"] == 0 else (OWIN, "OWIN")
                    A("act", [kpos_], [kd], lambda e, dst=dst, g=d["g"], pos=pos: e.activation(
                        out=dst[:, g * 4:(g + 1) * 4, :], in_=pos, func=AF.Copy))

            s_stage(0)
            if len(its) > 1:
                s_stage(1)
            for i in range(len(its)):
                if i + 2 < len(its):
                    s_stage(i + 2)
                ep_stage(i)
            if STAGE < 6:
                return
            finish(X, kx, yp[i_own * 128:(i_own + 1) * 128, :], "yp%d" % i_own, i_own)

        for v_ in range(NT if STAGE >= 2 else 0):
            tile_1b(v_)
        A("pool", ["St"], [], lambda e: e.dma_start(out=rnnp.rearrange("(a e) k v -> (e k) a v", e=2), in_=St[:]),
          dma=True, final=True)

        SKEYS = ["W1s", "SELM", "S0bf", "KN", "ZV", "QTs", "ATs", "ATs", "XcTb", "PGB0", "PGB1", "hs_s", "KCa", "vca",
                 "QTb", "PTc", "PTs", "PTw", "RHSMs", "WPB", "OcT", "OsT", "OwT", "IDX", "MSKs", "kcbs"]
        _so = [0]

        def sw(n, parts=128):
            a_ = _so[0]
            _so[0] += n
            assert _so[0] <= 26816
            return BIGW[0:parts, a_:a_ + n]

        W1s = sw(8192).rearrange("p (a j h) -> p a j h", a=2, j=32)
        SELM = sw(128).rearrange("p (a j) -> p a j", a=2)
        S0bf = sw(4096).rearrange("p (a d) -> p a d", a=64)
        KN = sw(512, 64).rearrange("p (a t) -> p a t", a=4)
        ZV = sw(256).rearrange("p (a g d) -> p a g d", a=2, g=2)
        QTs = sw(1024, 64).rearrange("p (h t) -> p h t", h=8)
        ATs = sw(1024).rearrange("p (h t) -> p h t", h=8)
        oiT = ATs[0:64]
        XcTb = sw(4096).rearrange("p (a n) -> p a n", a=4)
        _pgb = sw(1024)
        PGB = [_pgb[:, 0:512], _pgb[:, 512:1024]]
        hs_s = sw(128).rearrange("p (a j) -> p a j", a=4)
        KCa = sw(64, 68).rearrange("p (g j) -> p g j", g=2)
        vca = sw(130, 32).rearrange("p (g d) -> p g d", g=2)
        QTb = sw(64, 128).rearrange("p (h t) -> p h t", h=8)
        PTc = sw(32, 32)
        PTs = sw(544)
        PTw = sw(160)
        RHSMs = sw(32, 128).rearrange("p (h t) -> p h t", h=4)
        WPB = _pgb.rearrange("p (t c) -> p t c", t=4)
        OcT = sw(1024, 65).rearrange("p (h t) -> p h t", h=8)
        OsT = sw(1024, 65).rearrange("p (h t) -> p h t", h=8)
        OwT = sw(1024, 65).rearrange("p (h t) -> p h t", h=8)
        IDX = sw(1024).bitcast(mybir.dt.int32)
        MSKs = sw(64).rearrange("p (m n) -> p m n", m=2)
        kcbs = sw(128, 32).rearrange("p (g d) -> p g d", g=2)
        csts = sb("csts", [128, 74])
        selT = csts[0:32, 0:8]
        keepS = csts[0:8, 8:41]
        addS = csts[0:8, 41:74]
        iotac = sb("iotac", [128, 1])

        A("dve", [], AKEYS + ["fdummy"] + ["S0_%d" % b for b in range(16)], lambda e: e.memset(fdummy[:], 0.0))
        for b in range(16):
            A("pool", [], ["S0_%d" % b], lambda e, b=b: e.dma_start(
                out=S0[:, b * 4:(b + 1) * 4, :], in_=srnn[b].rearrange("(a e) k v -> (e k) a v", e=2)), dma=True)
        for b in range(16):
            A("pool", [], [], lambda e, b=b: e.dma_start(out=wins[b, 0:504, :], in_=cwin[b, 8:512, :]), dma=True, final=True)
        X, XT, kx, kxt = prep_x(xsm[:, :], 0)
        Zt, kz = Z[0], "Z0"
        mm(XT, kxt, 1, 384, C_KV)
        mm(XT, kxt, 2, 384, C_KV + 384)
        A("act", ["K1"], [kz], lambda e: e.activation(out=Zt[:, 0:384], in_=K[1][:, 0:384], func=AF.Copy))
        A("act", ["K2"], [kz], lambda e: e.activation(out=Zt[:, 384:768], in_=K[2][:, 0:384], func=AF.Copy))
        mm(XT, kxt, 1, 512, C_RF)
        mm(XT, kxt, 2, 512, C_RI)
        Z5 = k_norms(Zt, kz)
        A("pool", [kz], [], lambda e: e.dma_start(out=kvs[:, :], in_=Zt[:, 0:512]), dma=True, final=True)
        for b in range(16):
            A("pool", [kz], [], lambda e, b=b: e.dma_start(out=wins[b, 504:512, :], in_=Zt[b * 8:(b + 1) * 8, 512:768]),
              dma=True, final=True)
        rnn_gates(1, 2, True)
        mm(XT, kxt, 1, 512, C_Q)
        mm(XT, kxt, 2, 24, C_G)
        A("act", ["K1"], ["ZQ"], lambda e: e.activation(out=ZQ[:].rearrange("p h d -> p (h d)"), in_=K[1][:], func=AF.Copy))
        A("act", ["K2"], ["gsig"], lambda e: e.activation(out=gsig[:].rearrange("p h b -> p (h b)"), in_=K[2][:, 0:24],
                                                           func=AF.Exp, scale=-1.0))
        A("dve", ["gsig"], ["gsig"], lambda e: e.tensor_scalar_add(out=gsig[:], in0=gsig[:], scalar1=1.0))
        A("dve", ["gsig"], ["gsig"], lambda e: e.reciprocal(out=gsig[:], in_=gsig[:]))
        mm(XT, kxt, 1, 512, C_RQ)
        silu_from(1, QS, "QS")
        mm(XT, kxt, 2, 512, C_RG)
        silu_from(2, RGS, "RGS")
        A("dve", [], WK_ + SKEYS + ["fdummy"], lambda e: e.memset(fdummy[:], 0.0))
        A("sp", [], ["csts"], lambda e: e.dma_start(out=csts[:], in_=csts_d), dma=True)
        A("sp", [], ["iotac"], lambda e: e.dma_start(out=iotac[:], in_=iota_d), dma=True)
        A("dve", [kz], ["zb"], lambda e: e.tensor_copy(out=zb[:].rearrange("p (a g) d -> p a g d", a=2), in_=Z5[:, 1:3, 0, :, :]))
        for a4 in range(4):
            A("pe", ["zb", "idb"], ["K0"], lambda e, a4=a4: e.transpose(
                out=K0[0:64, a4 * 128:(a4 + 1) * 128], in_=zb[:, a4, :], identity=idb[:]))
        A("dve", ["K0"], ["KN"], lambda e: e.tensor_copy(out=KN[:].rearrange("p a t -> p (a t)"), in_=K0[0:64, 0:512]))
        A("dve", [kz], ["ZV"], lambda e: e.tensor_copy(out=ZV[:], in_=Z5[:, 1:3, 1, :, :]))
        head_rms(ZQ[:], "ZQ", 8)
        A("dve", ["ZQ", "st8"], ["ZQ"], lambda e: e.tensor_tensor(
            out=ZQ[:], in0=ZQ[:], in1=st8[:].unsqueeze(2).to_broadcast([128, 8, 64]), op=ALU.mult))
        A("dve", ["ZQ", "qgs"], ["QN"], lambda e: e.tensor_tensor(
            out=QN[:], in0=ZQ[:], in1=qgs[:].unsqueeze(1).to_broadcast([128, 8, 64]), op=ALU.mult))
        for h in range(8):
            A("pe", ["QN", "idb"], ["K0"], lambda e, h=h: e.transpose(
                out=K0[0:64, h * 128:(h + 1) * 128], in_=QN[:, h, :], identity=idb[:]))
        A("dve", ["K0"], ["QTs"], lambda e: e.tensor_copy(out=QTs[:].rearrange("p h t -> p (h t)"), in_=K0[0:64, :]))
        A("pe", ["tm", "T3"], ["K3"], lambda e: e.matmul(K[3][:], lhsT=tm[:, 2, :], rhs=T[3][:], start=True, stop=True))
        A("act", ["K3"], ["T4"], lambda e: e.activation(out=T[4][:], in_=K[3][:], func=AF.Exp))
        A("act", ["K3"], ["T5"], lambda e: e.activation(out=T[5][:], in_=K[3][:], func=AF.Exp, scale=-1.0))
        A("dve", ["QS", "T4"], ["QD"], lambda e: e.tensor_tensor(out=QD[:], in0=QS[:], in1=T[4][:], op=ALU.mult))
        A("dve", ["T2", "T5"], ["KDI"], lambda e: e.tensor_tensor(out=KDI[:], in0=T[2][:], in1=T[5][:], op=ALU.mult))
        for hp in range(4):
            A("pe", ["QD", "idb"], ["K0"], lambda e, hp=hp: e.transpose(
                out=K0[:, hp * 128:(hp + 1) * 128], in_=QD[:, hp * 128:(hp + 1) * 128], identity=idb[:]))
            A("pe", ["KDI", "idb"], ["K0"], lambda e, hp=hp: e.transpose(
                out=K0[:, (4 + hp) * 128:(5 + hp) * 128], in_=KDI[:, hp * 128:(hp + 1) * 128], identity=idb[:]))
        A("dve", ["K0"], ["qkT"], lambda e: e.tensor_copy(out=qkT[:].rearrange("p a t -> p (a t)"), in_=K0[:]))
        for ee in range(2):
            for hp in range(4):
                A("pe", ["qkT"], ["K%d" % (5 + ee)], lambda e, ee=ee, hp=hp: e.matmul(
                    K[5 + ee][:, hp * 128:(hp + 1) * 128], lhsT=qkT[64 * ee:64 * ee + 64, 4 + hp, :],
                    rhs=qkT[64 * ee:64 * ee + 64, hp, :], start=True, stop=True), rg=64 * ee)
        for ee in range(2):
            A("dve", ["K%d" % (5 + ee), "tm"], ["ATs"], lambda e, ee=ee: e.tensor_tensor(
                out=ATs[:, ee * 4:(ee + 1) * 4, :], in0=K[5 + ee][:].rearrange("p (a t) -> p a t", a=4),
                in1=tm[:, 2, :].unsqueeze(1).to_broadcast([128, 4, 128]), op=ALU.mult))
        A("dve", ["S0_%d" % b for b in range(16)], ["S0bf"], lambda e: e.tensor_copy(out=S0bf[:, 0:32, :], in_=S0[:, 0:32, :]))
        A("act", ["S0_%d" % b for b in range(16)], ["S0bf"], lambda e: e.activation(out=S0bf[:, 32:64, :], in_=S0[:, 32:64, :],
                                                                                     func=AF.Copy))
        pos_ = K[7][:].rearrange("p (h d) -> p h d", h=8)
        for h in range(8):
            hp, ee = h // 2, h % 2
            A("pe", ["ATs", "VB"], ["K7"], lambda e, h=h, hp=hp, ee=ee: e.matmul(
                pos_[:, h, :], lhsT=ATs[:, ee * 4 + hp, :], rhs=VB[:, h * 64:(h + 1) * 64], start=True, stop=True))
        for ee in range(2):
            for b in range(16):
                for hp in range(4):
                    A("pe", ["qkT", "S0bf"], ["K%d" % (5 + ee)], lambda e, ee=ee, b=b, hp=hp: e.matmul(
                        K[5 + ee][0:64, hp * 128 + b * 8:hp * 128 + b * 8 + 8], lhsT=S0bf[64 * ee:64 * ee + 64, b * 4 + hp, :],
                        rhs=qkT[64 * ee:64 * ee + 64, hp, b * 8:(b + 1) * 8], start=True, stop=True), rg=64 * ee)
        for ee in range(2):
            A("act", ["K%d" % (5 + ee)], ["ATs"], lambda e, ee=ee: e.activation(
                out=oiT[:, ee * 4:(ee + 1) * 4, :].rearrange("p a t -> p (a t)"), in_=K[5 + ee][0:64, :], func=AF.Copy))
        for h in range(8):
            hp, ee = h // 2, h % 2
            A("pe", ["ATs", "idb"], ["K0"], lambda e, h=h, hp=hp, ee=ee: e.transpose(
                out=K0[:, h * 64:(h + 1) * 64], in_=oiT[:, ee * 4 + hp, :], identity=idb[0:64, 0:64]))
        A("act", ["K0"], ["T5"], lambda e: e.activation(out=T[5][:], in_=K0[:, 0:512], func=AF.Copy))
        A("dve", ["K7", "T5"], ["OR32"], lambda e: e.tensor_tensor(out=OR32[:].rearrange("p h d -> p (h d)"), in0=K[7][:],
                                                                  in1=T[5][:], op=ALU.add))
        G = T[3]
        ptots = K[5][:, 0:64].rearrange("p (a b) -> p a b", a=4)
        for h in range(8):
            hp, ee = h // 2, h % 2
            A("pe", ["T3", "rmask"], ["K5"], lambda e, h=h, hp=hp, ee=ee: e.matmul(
                ptots[64 * ee:64 * ee + 64, hp, :], lhsT=G[:, h * 64:(h + 1) * 64], rhs=rmask[:], start=True, stop=True))
        A("act", ["K5"], ["etots"], lambda e: e.activation(out=etots[:], in_=ptots, func=AF.Exp))
        for b in range(16):
            q = b % 2
            A("dve", ["KD", "rmask"], ["PTb%d" % q], lambda e, b=b, q=q: e.tensor_scalar(
                out=kdm[q][:], in0=KD[:], scalar1=rmask[:, b:b + 1], scalar2=None, op0=ALU.mult))
            for h in range(8):
                hp, ee = h // 2, h % 2
                A("pe", ["PTb%d" % q, "VB"], ["K4"], lambda e, h=h, q=q, hp=hp, ee=ee: e.matmul(
                    pst[64 * ee:64 * ee + 64, hp, :], lhsT=kdm[q][:, h * 64:(h + 1) * 64], rhs=VB[:, h * 64:(h + 1) * 64],
                    start=True, stop=True))
            Sbv = S0[:, b * 4:(b + 1) * 4, :]
            A("dve", ["S0_%d" % b, "etots", "S0bf"], ["S0_%d" % b], lambda e, b=b, Sbv=Sbv: e.tensor_tensor(
                out=Sbv, in0=Sbv, in1=etots[:, :, b:b + 1].to_broadcast([128, 4, 64]), op=ALU.mult))
            A("dve", ["S0_%d" % b, "K4"], ["S0_%d" % b], lambda e, Sbv=Sbv: e.tensor_tensor(out=Sbv, in0=Sbv, in1=pst, op=ALU.add))
            A("pool", ["S0_%d" % b], ["rnns_o"], lambda e, b=b, Sbv=Sbv: e.dma_start(
                out=rnns[b].rearrange("(a e) k v -> (e k) a v", e=2), in_=Sbv), dma=True, final=True)
        if STAGE >= 8:
            A("dve", ["rnns_o"] + ["S0_%d" % b for b in range(16)], ["KTs", "fdummy"], lambda e: e.memset(fdummy[:], 0.0))
            A("dve", [], ["KTs"], lambda e: e.memset(KTs[64:128, :, 0:2176], 0.0))
            for c0 in range(0, 17 * 128, 1024):
                n = min(1024, 17 * 128 - c0)
                A("sp", [], ["H"], lambda e, c0=c0, n=n: e.dma_start(out=stgq[64:68, 0:n], in_=kpos_d[:, c0:c0 + n]), dma=True)
                for g in range(2):
                    A("dve", ["H"], ["KTs"], lambda e, c0=c0, n=n, g=g: e.tensor_copy(
                        out=KTs[64:68, g, c0:c0 + n], in_=stgq[64:68, 0:n]))
            A("sp", [], ["H"], lambda e: e.dma_start(out=stgq[64:68, 0:640], in_=kpos_d[:, 1536:2176]), dma=True)
            A("dve", ["H"], ["KTw"], lambda e: e.tensor_copy(
                out=KTw[64:68, :, 0:640], in_=stgq[64:68, 0:640].unsqueeze(1).to_broadcast([4, 2, 640])))
            A("dve", [], ["QTb"], lambda e: e.memset(QTb[64:128, :, :], 0.0))
            A("dve", [], ["RHSMs"], lambda e: e.memset(RHSMs[:, :, :], 0.0))
            A("sp", [], ["H"], lambda e: e.dma_start(out=stgq[64:68, 0:64], in_=sq_d[0:4, :]), dma=True)
            A("dve", ["H"], ["QTb"], lambda e: e.tensor_copy(out=QTb[64:68, :, :].rearrange("p h t -> p (h t)"),
                                                             in_=stgq[64:68, 0:64]))
            A("sp", [], ["H"], lambda e: e.dma_start(out=stgq[64:68, 0:32], in_=sq_d[4:8, 0:32]), dma=True)
            A("dve", ["H"], ["KCa"], lambda e: e.tensor_copy(
                out=KCa[64:68, :, :], in_=stgq[64:68, 0:32].unsqueeze(1).to_broadcast([4, 2, 32])))
            A("sp", [], ["H"], lambda e: e.dma_start(out=stgq[:, 0:192], in_=sm_d), dma=True)
            A("dve", ["H"], ["SELM"], lambda e: e.tensor_copy(out=SELM[:].rearrange("p a j -> p (a j)"), in_=stgq[:, 0:128]))
            A("dve", ["H"], ["MSKs"], lambda e: e.tensor_copy(out=MSKs[:].rearrange("p m n -> p (m n)"), in_=stgq[:, 128:192]))
            for typ in range(2):
                for jq in range(4):
                    A("sp", [], ["H"], lambda e, typ=typ, jq=jq: e.dma_start(
                        out=stgq[:].rearrange("p (j h) -> p j h", j=8),
                        in_=w1_d[typ, jq * 1024:(jq + 1) * 1024, :].rearrange("(j p) h -> p j h", p=128)), dma=True)
                    A("dve", ["H"], ["W1s"], lambda e, typ=typ, jq=jq: e.tensor_copy(
                        out=W1s[:, typ, jq * 8:(jq + 1) * 8, :], in_=stgq[:].rearrange("p (j h) -> p j h", j=8)))
            A("dve", [], ["KTs"], lambda e: e.memset(KTs[0:64, :, 2048:2176], 0.0))
            A("dve", [], ["VVs"], lambda e: e.memset(VVs[:, 16, :, 0:64], 0.0))
            A("dve", [], ["KTw"], lambda e: e.memset(KTw[0:64, :, 512:640], 0.0))
            A("dve", [], ["VVw"], lambda e: e.memset(VVw[:, 4, :, 0:64], 0.0))
            A("dve", [], ["vca"], lambda e: e.memset(vca[:, :, 64:65], 1.0))
            A("sp", [], ["IDX"], lambda e: e.dma_start(out=IDX[:, 0:256], in_=ptrep_d), dma=True)
            A("dve", ["IDX"], ["T0"], lambda e: e.tensor_copy(out=T[0][:, 0:256], in_=IDX[:, 0:256]))
            A("dve", ["T0", "iotac"], ["T0"], lambda e: e.tensor_scalar(out=T[0][:, 0:256], in0=T[0][:, 0:256], scalar1=128.0,
                                                                       scalar2=iotac[:, 0:1], op0=ALU.mult, op1=ALU.add))
            A("dve", ["T0"], ["IDX"], lambda e: e.tensor_copy(out=IDX[:, 256:512], in_=T[0][:, 0:256]))
            PG = [T[0], T[1], T[2], T[3]]
            K1v = K[1][:, 0:256].rearrange("p (a t) -> p a t", a=4)
            K3v = K[3][:, 0:256].rearrange("p (a d) -> p a d", a=4)
            WP = xs[1]
            pgc = [0]

            def sample_seq(b):
                for j in range(16):
                    q = pgc[0] % 4
                    q2 = pgc[0] % 2
                    pgc[0] += 1
                    n = b * 16 + j
                    pg, kpg, pgb, kpgb = PG[q], "T%d" % q, PGB[q2], "PGB%d" % q2
                    A("pool", ["IDX"], [kpg], lambda e, pg=pg, n=n: e.indirect_dma_start(
                        out=pg[:], out_offset=None, in_=cache_d,
                        in_offset=bass.IndirectOffsetOnAxis(ap=IDX[:, 256 + n:257 + n], axis=0)), dma=True)
                    if q2 == 0:
                        A("act", [kpg], [kpgb], lambda e, pg=pg, pgb=pgb: e.activation(out=pgb, in_=pg[:], func=AF.Copy))
                    else:
                        A("dve", [kpg], [kpgb], lambda e, pg=pg, pgb=pgb: e.tensor_copy(out=pgb, in_=pg[:]))
                    for tg in range(4):
                        for par in range(2):
                            A("pe", [kpgb, "SELM"], ["K1"], lambda e, tg=tg, par=par, pgb=pgb: e.matmul(
                                K1v[64 * par:64 * par + 64, tg, :], lhsT=pgb[:, tg * 64:(tg + 1) * 64], rhs=SELM[:, par, :],
                                start=True, stop=True))
                    for a in range(2):
                        A("dve", ["K1", "peT"], ["XcTb"], lambda e, a=a, j=j: e.tensor_tensor(
                            out=XcTb[:, 2 * a:2 * a + 2, j * 64:(j + 1) * 64].rearrange("p g (b j) -> p g b j", b=2),
                            in0=K1v[:, 2 * a:2 * a + 2, :].rearrange("p g (b j) -> p g b j", b=2),
                            in1=peT[:, a, :].unsqueeze(1).unsqueeze(1).to_broadcast([128, 2, 2, 32]), op=ALU.add))
                    for g in range(2):
                        A("pe", [kpgb, "idb"], ["K0"], lambda e, g=g, pgb=pgb: e.transpose(
                            out=K0[0:64, g * 128:(g + 1) * 128], in_=pgb[:, 256 + g * 64:256 + (g + 1) * 64], identity=idb[:]))
                    A("dve", ["K0"], ["KTs"], lambda e, j=j: e.tensor_copy(
                        out=KTs[0:64, :, j * 128:(j + 1) * 128], in_=K0[0:64, 0:256].rearrange("p (g t) -> p g t", g=2)))
                    A("act", [kpg], ["VVs"], lambda e, pg=pg, j=j: e.activation(
                        out=VVs[:, j, :, 0:64], in_=pg[:, 384:512].rearrange("p (g d) -> p g d", g=2), func=AF.Copy))
                A("dve", ["KN"], ["KTs"], lambda e: e.tensor_copy(out=KTs[0:64, :, 2048:2056], in_=KN[:, 0:2, b * 8:(b + 1) * 8]))
                for g in range(2):
                    A("sp", ["ZV"], ["VVs"], lambda e, g=g: e.dma_start(out=VVs[0:8, 16, g, 0:64], in_=ZV[b * 8:(b + 1) * 8, 0, g, :]),
                      dma=True)
                    A("sp", ["ZV"], ["VVw"], lambda e, g=g: e.dma_start(out=VVw[0:8, 4, g, 0:64], in_=ZV[b * 8:(b + 1) * 8, 1, g, :]),
                      dma=True)
                for tg in range(4):
                    typ = tg // 2
                    for j in range(32):
                        A("pe", ["XcTb", "W1s"], ["K2"], lambda e, tg=tg, typ=typ, j=j: e.matmul(
                            K[2][:, 0:32], lhsT=W1s[:, typ, j, :], rhs=XcTb[:, tg, j:1024:32], start=(j == 0), stop=(j == 31)))
                    A("act", ["K2"], ["T4"], lambda e: e.activation(out=T[4][:, 0:32], in_=K[2][:, 0:32], func=AF.Exp, scale=-1.0))
                    A("dve", ["T4"], ["T4"], lambda e: e.tensor_scalar_add(out=T[4][:, 0:32], in0=T[4][:, 0:32], scalar1=1.0))
                    A("dve", ["T4"], ["T4"], lambda e: e.reciprocal(out=T[4][:, 0:32], in_=T[4][:, 0:32]))
                    A("dve", ["T4", "K2"], ["hs_s"], lambda e, tg=tg: e.tensor_tensor(out=hs_s[:, tg, :], in0=K[2][:, 0:32],
                                                                                   in1=T[4][:, 0:32], op=ALU.mult))
                    A("pe", ["hs_s", "W2b"], ["K3"], lambda e, tg=tg, typ=typ: e.matmul(
                        K3v[0:32, tg, :], lhsT=hs_s[:, tg, :], rhs=W2b[:, typ, :], start=True, stop=True))
                A("act", ["K3"], ["kcr"], lambda e: e.activation(out=kcr[0:32], in_=K3v[0:32, 0:2, :], func=AF.Copy))
                A("act", ["K3"], ["vca"], lambda e: e.activation(out=vca[:, :, 0:64], in_=K3v[0:32, 2:4, :], func=AF.Copy))
                A("dve", ["kcr"], ["LG"], lambda e: e.tensor_tensor(out=LG[0:32, 0:2, 0:64], in0=kcr[0:32], in1=kcr[0:32], op=ALU.mult))
                A("dve", ["LG"], ["st4"], lambda e: e.tensor_reduce(out=st4[0:32, 0:2], in_=LG[0:32, 0:2, 0:64], axis=AX.X, op=ALU.add))
                A("act", ["st4"], ["st4"], lambda e: e.activation(out=st4[0:32, 0:2], in_=st4[0:32, 0:2], func=AF.Ln,
                                                                  scale=1.0 / 64, bias=EPS))
                A("act", ["st4"], ["st4"], lambda e: e.activation(out=st4[0:32, 0:2], in_=st4[0:32, 0:2], func=AF.Exp, scale=-0.5))
                A("dve", ["kcr", "st4"], ["kcr"], lambda e: e.tensor_tensor(
                    out=kcr[0:32], in0=kcr[0:32], in1=st4[0:32, 0:2].unsqueeze(2).to_broadcast([32, 2, 64]), op=ALU.mult))
                A("dve", ["kcr", "gvec"], ["kcbs"], lambda e: e.tensor_tensor(
                    out=kcbs[:], in0=kcr[0:32], in1=gvec[0:32, 0:64].unsqueeze(1).to_broadcast([32, 2, 64]), op=ALU.mult))
                for g in range(2):
                    A("pe", ["kcbs", "idb"], ["K0"], lambda e, g=g: e.transpose(
                        out=K0[0:64, 512 + g * 32:512 + (g + 1) * 32], in_=kcbs[:, g, :], identity=idb[0:32, 0:32]))
                A("dve", ["K0"], ["KCa"], lambda e: e.tensor_copy(
                    out=KCa[0:64, :, :], in_=K0[0:64, 512:576].rearrange("p (g j) -> p g j", g=2)))
                A("sp", [], ["xs1"], lambda e: e.dma_start(out=WP[:].rearrange("p (t c) -> p t c", t=4),
                                                           in_=cwin[b].rearrange("(t p) c -> p t c", p=128)), dma=True)
                A("dve", ["xs1"], ["PGB0", "PGB1"], lambda e: e.tensor_copy(out=WPB[:].rearrange("p t c -> p (t c)"), in_=WP[:]))
                for tw in range(4):
                    for g in range(2):
                        A("pe", ["PGB0", "PGB1", "idb"], ["K0"], lambda e, tw=tw, g=g: e.transpose(
                            out=K0[0:64, (tw * 2 + g) * 128:(tw * 2 + g + 1) * 128], in_=WPB[:, tw, g * 64:(g + 1) * 64],
                            identity=idb[:]))
                A("dve", ["K0"], ["KTw"], lambda e: e.tensor_copy(
                    out=KTw[0:64, :, 0:512].rearrange("p g (t k) -> p g t k", t=4),
                    in_=K0[0:64, :].rearrange("p (t g k) -> p g t k", t=4, g=2)))
                A("act", ["xs1"], ["VVw"], lambda e: e.activation(
                    out=VVw[:, 0:4, :, 0:64], in_=WP[:].rearrange("p (t c) -> p t c", t=4)[:, :, 128:256].rearrange(
                        "p t (g d) -> p t g d", g=2), func=AF.Copy))
                A("dve", ["KN"], ["KTw"], lambda e: e.tensor_copy(out=KTw[0:64, :, 512:520], in_=KN[:, 2:4, b * 8:(b + 1) * 8]))
                A("dve", ["QTs"], ["QTb"], lambda e: e.tensor_copy(out=QTb[0:64, :, :], in_=QTs[:, :, b * 8:(b + 1) * 8]))
                for g in range(2):
                    qrhs = QTb[:, g * 4:(g + 1) * 4, :].rearrange("p h t -> p (h t)")
                    qrhs68 = QTb[0:68, g * 4:(g + 1) * 4, :].rearrange("p h t -> p (h t)")
                    A("pe", ["KCa", "QTb"], ["K3"], lambda e, g=g, qrhs68=qrhs68: e.matmul(
                        K[3][0:32, 0:32], lhsT=KCa[0:68, g, :], rhs=qrhs68, start=True, stop=True))
                    A("act", ["K3"], ["PTc"], lambda e: e.activation(out=PTc, in_=K[3][0:32, 0:32], func=AF.Exp))
                    A("pe", ["vca", "PTc"], ["K7"], lambda e, g=g: e.matmul(K[7][0:65, 0:32], lhsT=vca[:, g, :], rhs=PTc,
                                                                         start=True, stop=True))
                    A("act", ["K7"], ["OcT"], lambda e, g=g: e.activation(
                        out=OcT[:, g * 4:(g + 1) * 4, b * 8:(b + 1) * 8], in_=K[7][0:65, 0:32].rearrange("p (h t) -> p h t", h=4),
                        func=AF.Copy))
                    A("pe", ["PTc", "idb"], ["K0"], lambda e: e.transpose(out=K0[0:32, 640:672], in_=PTc, identity=idb[0:32, 0:32]))
                    A("dve", ["K0"], ["EX"], lambda e: e.tensor_copy(out=EX[0:32, 0, 0:32], in_=K0[0:32, 640:672]))
                    A("dve", ["EX"], ["st4b"], lambda e: e.tensor_reduce(out=st4b[0:32, 0:1], in_=EX[0:32, 0, 0:32], axis=AX.X, op=ALU.add))
                    A("dve", ["st4b"], ["st4b"], lambda e: e.reciprocal(out=st4b[0:32, 0:1], in_=st4b[0:32, 0:1]))
                    A("dve", ["EX", "st4b"], ["EX"], lambda e: e.tensor_scalar(
                        out=EX[0:32, 0, 0:32], in0=EX[0:32, 0, 0:32], scalar1=st4b[0:32, 0:1], scalar2=None, op0=ALU.mult))
                    A("pe", ["csts", "EX"], ["K3"], lambda e: e.matmul(K[3][0:8, 64:96], lhsT=selT, rhs=EX[0:32, 0, 0:32],
                                                                      start=True, stop=True))
                    A("dve", [], ["SC"], lambda e: e.memset(SC[0:8, 0:33], 0.0))
                    A("dve", ["K3"], ["SC"], lambda e: e.tensor_copy(out=SC[0:8, 0:32], in_=K[3][0:8, 64:96]))
                    A("dve", ["SC", "csts"], ["SC"], lambda e: e.tensor_tensor(out=SC[0:8, 0:33], in0=SC[0:8, 0:33], in1=keepS, op=ALU.mult))
                    A("dve", ["SC", "csts"], ["SC"], lambda e: e.tensor_tensor(out=SC[0:8, 0:33], in0=SC[0:8, 0:33], in1=addS, op=ALU.add))
                    A("dve", ["SC"], ["m8"], lambda e: e.max(out=m8[0:8], in_=SC[0:8, 0:33]))
                    A("dve", ["SC", "m8"], ["WK"], lambda e: e.match_replace(out=WK[0:8, 0:33], in_to_replace=m8[0:8],
                                                                              in_values=SC[0:8, 0:33], imm_value=-2.0))
                    A("dve", ["WK"], ["m8"], lambda e: e.max(out=m8[0:8], in_=WK[0:8, 0:33]))
                    A("dve", ["m8"], ["thr"], lambda e: e.tensor_reduce(out=thr[0:8], in_=m8[0:8], axis=AX.X, op=ALU.min))
                    A("dve", ["SC", "thr"], ["SEL"], lambda e: e.tensor_scalar(
                        out=SEL[0:8, 0:33], in0=SC[0:8, 0:33], scalar1=thr[0:8, 0:1], scalar2=None, op0=ALU.is_ge))
                    A("dve", ["SEL"], ["MBF"], lambda e: e.tensor_scalar(
                        out=MBF[0:8, 0:33], in0=SEL[0:8, 0:33], scalar1=-1.0, scalar2=BIG, op0=ALU.add, op1=ALU.mult))
                    A("pe", ["MBF", "idb"], ["K0"], lambda e: e.transpose(out=K0[0:33, 704:712], in_=MBF[0:8, 0:33],
                                                                          identity=idb[0:8, 0:8]))
                    A("dve", ["K0"], ["RHSMs"], lambda e: e.tensor_copy(
                        out=RHSMs[0:33], in_=K0[0:33, 704:712].unsqueeze(1).to_broadcast([33, 4, 8])))
                    for kt in range(17):
                        dst = K[5][:, kt * 32:(kt + 1) * 32] if kt < 16 else K[6][:, 0:32]
                        kb = "K5" if kt < 16 else "K6"
                        A("pe", ["KTs", "QTb"], [kb], lambda e, g=g, kt=kt, dst=dst, qrhs=qrhs: e.matmul(
                            dst, lhsT=KTs[:, g, kt * 128:(kt + 1) * 128], rhs=qrhs, start=True, stop=False))
                        A("pe", ["Eexp", "RHSMs"], [kb], lambda e, kt=kt, dst=dst: e.matmul(
                            dst, lhsT=Eexp[:, kt * 128:(kt + 1) * 128], rhs=RHSMs[:].rearrange("p h t -> p (h t)"),
                            start=False, stop=(kt < 16)))
                        if kt == 16:
                            A("pe", ["idb", "MSKs"], [kb], lambda e, dst=dst: e.matmul(dst, lhsT=idb[:], rhs=MSKs[:, 0, :],
                                                                                       start=False, stop=True))
                    A("act", ["K5"], ["PTs"], lambda e: e.activation(out=PTs[:, 0:512], in_=K[5][:], func=AF.Exp))
                    A("act", ["K6"], ["PTs"], lambda e: e.activation(out=PTs[:, 512:544], in_=K[6][:, 0:32], func=AF.Exp))
                    for kt in range(17):
                        A("pe", ["PTs", "VVs"], ["K7"], lambda e, g=g, kt=kt: e.matmul(
                            K[7][0:65, 32:64], lhsT=VVs[:, kt, g, :], rhs=PTs[:, kt * 32:(kt + 1) * 32],
                            start=(kt == 0), stop=(kt == 16)))
                    A("act", ["K7"], ["OsT"], lambda e, g=g: e.activation(
                        out=OsT[:, g * 4:(g + 1) * 4, b * 8:(b + 1) * 8], in_=K[7][0:65, 32:64].rearrange("p (h t) -> p h t", h=4),
                        func=AF.Copy))
                    for s_ in range(5):
                        dst = K[4][:, s_ * 32:(s_ + 1) * 32]
                        msk = {0: 1, 4: 0}.get(s_)
                        A("pe", ["KTw", "QTb"], ["K4"], lambda e, g=g, s_=s_, dst=dst, qrhs=qrhs, msk=msk: e.matmul(
                            dst, lhsT=KTw[:, g, s_ * 128:(s_ + 1) * 128], rhs=qrhs, start=True, stop=(msk is None)))
                        if msk is not None:
                            A("pe", ["idb", "MSKs"], ["K4"], lambda e, dst=dst, msk=msk: e.matmul(
                                dst, lhsT=idb[:], rhs=MSKs[:, msk, :], start=False, stop=True))
                    A("act", ["K4"], ["PTw"], lambda e: e.activation(out=PTw, in_=K[4][:, 0:160], func=AF.Exp))
                    for s_ in range(5):
                        A("pe", ["PTw", "VVw"], ["K7"], lambda e, g=g, s_=s_: e.matmul(
                            K[7][0:65, 64:96], lhsT=VVw[:, s_, g, :], rhs=PTw[:, s_ * 32:(s_ + 1) * 32],
                            start=(s_ == 0), stop=(s_ == 4)))
                    A("act", ["K7"], ["OwT"], lambda e, g=g: e.activation(
                        out=OwT[:, g * 4:(g + 1) * 4, b * 8:(b + 1) * 8], in_=K[7][0:65, 64:96].rearrange("p (h t) -> p h t", h=4),
                        func=AF.Copy))

            for b_ in range(16):
                sample_seq(b_)
            for src, ks, dst, kd in ((OcT, "OcT", OSEL, "OSEL"), (OsT, "OsT", OSEL, "OSEL"), (OwT, "OwT", OWIN, "OWIN")):
                for h in range(8):
                    A("pe", [ks, "idb"], ["K0"], lambda e, src=src, h=h: e.transpose(
                        out=K0[:, h * 66:h * 66 + 65], in_=src[:, h, :], identity=idb[0:65, 0:65]))
                A("act", ["K0"], [kd], lambda e, dst=dst: e.activation(
                    out=dst[:], in_=K0[:, 0:528].rearrange("p (h d) -> p h d", h=8)[:, :, 0:65], func=AF.Copy))
                if ks == "OcT":
                    A("dve", ["OSEL"], ["st8"], lambda e: e.reciprocal(out=st8[:], in_=OSEL[:, :, 64]))
                    A("dve", ["OSEL", "st8"], ["OCMP"], lambda e: e.tensor_tensor(
                        out=OCMP[:], in0=OSEL[:, :, 0:64], in1=st8[:].unsqueeze(2).to_broadcast([128, 8, 64]), op=ALU.mult))
            finish(xs[0], "xs0", ys[:, :], "ys", NOWN if debug else None)
        BKEYS = WK_ + AKEYS + ["Eexp", "MSK", "xT0", "xT1", "OC", "OCT", "KD", "VB", "QD", "KDI", "WUD", "QN", "qkT", "AT", "Sb0", "Sb1", "PB", "hs", "RHSM0", "RHSM1", "PT4"] + \
            ["S0_%d" % b_ for b_ in range(16)] + SKEYS
        if STAGE >= 7:
            A("dve", [], BKEYS + ["fdummy"], lambda e: e.memset(fdummy[:], 0.0))
            A("sp", [], ["lnmc"], lambda e: e.dma_start(out=lnmc[:], in_=lnmcol), dma=True)
            stg = [(T[i_], "T%d" % i_) for i_ in range(6)]
            cnt_ = [0]

            def wload(src, dst, scale_ap):
                st, ks = stg[cnt_[0] % 6]
                use_act = (cnt_[0] % 2 == 0)
                cnt_[0] += 1
                A("sp", [], [ks], lambda e: e.dma_start(out=st[:], in_=src), dma=True)
                rk = [ks] + (["lnmc"] if scale_ap is not None else [])
                if use_act:
                    if scale_ap is not None:
                        A("act", rk, ["WUD"], lambda e: e.activation(out=dst, in_=st[:], func=AF.Copy, scale=scale_ap))
                    else:
                        A("act", rk, ["WUD"], lambda e: e.activation(out=dst, in_=st[:], func=AF.Copy))
                else:
                    if scale_ap is not None:
                        A("dve", rk, ["WUD"], lambda e: e.tensor_scalar(out=dst, in0=st[:], scalar1=scale_ap, scalar2=None,
                                                                         op0=ALU.mult))
                    else:
                        A("dve", rk, ["WUD"], lambda e: e.tensor_copy(out=dst, in_=st[:]))

            for k in range(8):
                for q8 in range(8):
                    wload(w_up[k * 128:(k + 1) * 128, q8 * 512:(q8 + 1) * 512], WU[:, k, q8 * 512:(q8 + 1) * 512],
                          lnmc[:, k:k + 1])
            for f in range(32):
                for q2 in range(2):
                    wload(w_down[f * 128:(f + 1) * 128, q2 * 512:(q2 + 1) * 512], WD[:, f, q2 * 512:(q2 + 1) * 512], None)

            mlp_list = []
            mlp_pre = {}

            def mlp_prep(idx):
                hsrc, ydst, ky = mlp_list[idx]
                X, kx = xs[idx % 2], "xs%d" % (idx % 2)
                A("sp", [ky], [kx], lambda e: e.dma_start(out=X[:], in_=hsrc), dma=True)
                A("act", [kx], ["OSEL", "ss"], lambda e: e.activation(
                    out=OSEL[:].rearrange("p h d -> p (h d)")[:, 0:512], in_=X[:, 0:512], func=AF.Square, accum_out=ss[:]))
                A("act", [kx], ["OSEL", "rstd"], lambda e: e.activation(
                    out=OSEL[:].rearrange("p h d -> p (h d)")[:, 0:512], in_=X[:, 512:1024], func=AF.Square,
                    accum_out=rstd[:]))
                A("dve", ["ss", "rstd"], ["ss"], lambda e: e.tensor_tensor(out=ss[:], in0=ss[:], in1=rstd[:], op=ALU.add))
                A("act", ["ss"], ["rstd"], lambda e: e.activation(out=rstd[:], in_=ss[:], func=AF.Ln, scale=1.0 / D, bias=EPS))
                A("act", ["rstd"], ["rstd"], lambda e: e.activation(out=rstd[:], in_=rstd[:], func=AF.Exp, scale=-0.5))
                A("dve", [kx, "rstd"], ["xb"], lambda e: e.tensor_scalar(out=xb[:], in0=X[:], scalar1=rstd[:, 0:1],
                                                                         scalar2=None, op0=ALU.mult))
                for k in range(8):
                    A("pe", ["xb", "idb"], ["K0"], lambda e, k=k: e.transpose(
                        out=K0[:, k * 128:(k + 1) * 128], in_=xb[:, k * 128:(k + 1) * 128], identity=idb[:]))
                A("dve", ["K0"], ["QTa"], lambda e: e.tensor_copy(out=QTa[:].rearrange("p k t -> p (k t)"), in_=K0[:]))
                mlp_pre[idx] = (X, kx)

            def mlp_tile(idx):
                hsrc, ydst, ky = mlp_list[idx]
                if idx not in mlp_pre:
                    mlp_prep(idx)
                X, kx = mlp_pre[idx]
                def up_round(rnd):
                    ub = (rnd % 2) * 8
                    ku = "uT%d" % (rnd % 2)
                    for j in range(8):
                        f = rnd * 8 + j
                        bank = 1 + (f % 4)
                        kb = "K%d" % bank
                        R, kr = PTb[f % 2], "PTb%d" % (f % 2)
                        for k in range(8):
                            A("pe", ["QTa", "WUD"], [kb], lambda e, f=f, k=k, bank=bank: e.matmul(
                                K[bank][:, 0:128], lhsT=WU[:, k, f * 128:(f + 1) * 128], rhs=QTa[:, k, :],
                                start=(k == 0), stop=(k == 7)))
                        A("act", [kb], [kr], lambda e, bank=bank, R=R: e.activation(out=R[:, 0:128], in_=K[bank][:, 0:128],
                                                                                    func=AF.Relu))
                        A("dve", [kr], [ku], lambda e, j=j, ub=ub, R=R: e.tensor_tensor(
                            out=uT[:, ub + j, :], in0=R[:, 0:128], in1=R[:, 0:128], op=ALU.mult))

                def down_round(rnd):
                    ub = (rnd % 2) * 8
                    ku = "uT%d" % (rnd % 2)
                    for n in range(2):
                        bank = 5 + n
                        kb = "K%d" % bank
                        for j in range(8):
                            f = rnd * 8 + j
                            A("pe", [ku, "WUD"], [kb], lambda e, f=f, j=j, ub=ub, n=n, bank=bank: e.matmul(
                                K[bank][:], lhsT=uT[:, ub + j, :], rhs=WD[:, f, n * 512:(n + 1) * 512],
                                start=(f == 0), stop=(f == 31)))

                up_round(0)
                up_round(1)
                down_round(0)
                up_round(2)
                down_round(1)
                up_round(3)
                if idx + 1 < len(mlp_list):
                    mlp_prep(idx + 1)
                down_round(2)
                down_round(3)
                for n in range(2):
                    bank = 5 + n
                    kb = "K%d" % bank
                    A("dve", [kb, kx], ["H"], lambda e, n=n, bank=bank: e.tensor_tensor(
                        out=H[:, n * 512:(n + 1) * 512], in0=K[bank][:], in1=X[:, n * 512:(n + 1) * 512], op=ALU.add))
                A("pool", ["H"], [ky], lambda e: e.dma_start(out=ydst, in_=H[:]), dma=True, final=True)

            for i_ in range(NOWN):
                mlp_list.append((yp[i_ * 128:(i_ + 1) * 128, :], yp[i_ * 128:(i_ + 1) * 128, :], "yp%d" % i_))
            if STAGE >= 8:
                mlp_list.append((ys[:, :], ys[:, :], "ys"))
            for i_ in range(len(mlp_list)):
                mlp_tile(i_)
        S.emit()
    return nc


def make_sample_consts():
    f32 = np.float32
    slope = 2.0 ** (-(np.arange(8) + 1.0))
    csts = np.zeros((128, 74), f32)
    for hh in range(4):
        for t in range(8):
            csts[hh * 8 + t, t] = 1.0
    keep = np.ones(33, f32)
    add = np.zeros(33, f32)
    keep[0] = keep[32] = 0.0
    add[0] = add[32] = 5.0
    csts[:, 8:41] = keep[None]
    csts[:, 41:74] = add[None]
    iota = np.arange(128, dtype=f32).reshape(128, 1)
    sq = np.zeros((8, 64), f32)
    for h in range(8):
        for t in range(8):
            sq[0, h * 8 + t] = 128.0 * slope[h]
            sq[1, h * 8 + t] = slope[h]
            sq[2, h * 8 + t] = -128.0 * 16.0 * slope[h]
            sq[3, h * 8 + t] = -slope[h] * t
    j = np.arange(32)
    sq[4, 0:32] = j // 2
    sq[5, 0:32] = 64 * (j % 2) + 63
    sq[6, 0:32] = 1.0
    sq[7, 0:32] = 1.0
    sm = np.zeros((128, 192), f32)
    r = np.arange(128)
    for par in range(2):
        for jj in range(64):
            sm[2 * jj + par, par * 64 + jj] = 1.0
    kk = r[:, None]
    tt = np.arange(8)[None, :]
    mn = np.where(kk <= tt, 0.0, -BIG).astype(f32)
    mw = np.where(kk > tt, 0.0, -BIG).astype(f32)
    sm[:, 128:160] = np.tile(mn, (1, 4))
    sm[:, 160:192] = np.tile(mw, (1, 4))
    return csts, iota, sq, sm


def make_consts(half):
    f32 = np.float32
    t = np.arange(128)
    slope = 2.0 ** (-(np.arange(8) + 1.0))
    cst = np.zeros((128, 747), f32)
    ac = np.zeros((128, 8, 66), f32)
    ac[:] = (slope[:, None] * 64.0 * np.arange(66)[None, :])[None]
    if half == 1:
        ac[:, :, 0:2] -= BIG
    cst[:, 0:528] = ac.reshape(128, 528)
    cst[:, 528] = np.where(t >= 63, 0.0, -BIG)
    cst[:, 529] = np.where(t == 127, 0.0, -BIG)
    rv = np.ones((128, NOWN), f32)
    if half == 0:
        rv[:63, 0] = 0.0
    cst[:, 530:547] = rv
    keep = np.ones(66, f32)
    add = np.zeros(66, f32)
    j0 = 2 * half
    keep[j0], add[j0] = 0.0, 5.0
    if half == 1:
        keep[0:2], add[0:2] = 0.0, -1.0
    cst[:, 547:613] = keep[None]
    cst[:, 613:679] = add[None]
    cst[:, 679] = np.where(t < 64, 5.0, 0.0)
    cst[:, 680] = np.where(t < 64, 0.0, 1.0)
    cst[:, 681] = np.where(t < 64, -1.0, 5.0)
    s_ = t % 64
    cst[:, 682:746] = (s_[:, None] <= np.arange(64)[None, :]).astype(f32)
    kk, tt = t[:, None], t[None, :]
    tri = np.where(kk <= tt, 0.0, -BIG).astype(f32)
    anti = np.where(kk > tt, 0.0, -BIG).astype(f32)
    full = np.zeros((128, 128), f32) if half == 0 else np.full((128, 128), -BIG, f32)
    m0anti = anti if half == 0 else np.full((128, 128), -BIG, f32)
    masks = np.stack([np.tile(m, (1, 4)) for m in (tri, anti, full, m0anti)], axis=1).astype(f32)
    eexp = np.zeros((66, NT * 128), f32)
    n = np.arange(NT * 128)
    eexp[n // 64, n] = 1.0
    kpos = np.stack([(n // 128).astype(f32), (n % 128).astype(f32), np.ones_like(n, f32), np.ones_like(n, f32)], 0)
    qpos = np.zeros((NOWN, 4, 2, 4, 128), f32)
    for i in range(NOWN):
        v = 2 * i
        for g in range(2):
            for hh in range(4):
                s = slope[g * 4 + hh]
                qpos[i, 0, g, hh, :] = 128.0 * s
                qpos[i, 1, g, hh, :] = s
                qpos[i, 2, g, hh, :] = -128.0 * v * s
                qpos[i, 3, g, hh, :] = -s * t
    return cst, masks, eexp, kpos.astype(f32), qpos.reshape(NOWN, 4, 1024)


_NC = None


def kernel(x_prompt, x_sample, cache_kv, cache_win, state_rnn, page_table, ln_mix, w_in, q_norm, k_norm,
           cmp_pe, cmp_w1, cmp_w2, attn_out_norm, rnn_lb_logits, rnn_out_norm, w_out, ln_mlp, w_up, w_down,
           _debug=False):
    global _NC
    f32 = np.float32
    x_prompt = np.asarray(x_prompt, f32)
    x_sample = np.asarray(x_sample, f32)
    cache_win = np.asarray(cache_win, f32)
    state_rnn = np.asarray(state_rnn, f32)
    k_norm = np.asarray(k_norm, f32)
    cmp_pe = np.asarray(cmp_pe, f32)
    if _NC is None or _NC[0] != _debug:
        _NC = (_debug, build_nc(_debug))
    nc = _NC[1]
    t = np.arange(128)

    def tmats(c):
        same = (t[:, None] // c == t[None, :] // c)
        return ((t[:, None] <= t[None, :]) & same).astype(f32), ((t[:, None] > t[None, :]) & same).astype(f32)

    tc_p, tr_p = tmats(64)
    tc_s, tr_s = tmats(8)
    rowmask = (t[:, None] // 8 == np.arange(16)[None, :]).astype(f32)
    gv = np.concatenate([k_norm[0, 0], k_norm[0, 1], k_norm[0, 2], np.asarray(q_norm, f32)[0],
                         np.asarray(attn_out_norm, f32)[0], np.asarray(rnn_out_norm, f32)[0]])
    peT = np.ascontiguousarray(cmp_pe[0].reshape(2, 32, 2, 64).transpose(2, 3, 0, 1).reshape(128, 2, 32))
    common = {
        "w_in": np.ascontiguousarray(np.asarray(w_in, f32)[0]),
        "w_out": np.ascontiguousarray(np.asarray(w_out, f32)[0]),
        "lncol": np.ascontiguousarray(np.asarray(ln_mix, f32)[0].reshape(8, 128).T),
        "gvecd": np.ascontiguousarray(np.broadcast_to(gv[None], (128, 1280))),
        "lbl": np.ascontiguousarray(np.broadcast_to(np.asarray(rnn_lb_logits, f32)[None], (128, 2, 512))),
        "ident": np.eye(128, dtype=f32),
        "w_up": np.ascontiguousarray(np.asarray(w_up, f32)[0]),
        "w_down": np.ascontiguousarray(np.asarray(w_down, f32)[0]),
        "lnmcol": np.ascontiguousarray(np.asarray(ln_mlp, f32)[0].reshape(8, 128).T),
        "tmat": np.ascontiguousarray(np.stack([tc_p, tr_p, tc_s, tr_s], axis=1)),
        "rowmask": rowmask,
        "peTd": peT,
        "cmp_w1": np.ascontiguousarray(np.asarray(cmp_w1, f32)[0]),
        "cmp_w2": np.ascontiguousarray(np.asarray(cmp_w2, f32)[0].transpose(1, 0, 2)),
    }
    zt = np.zeros((128, D), f32)
    cc = [make_consts(0), make_consts(1)]
    csts, iota, sqd, smd = make_sample_consts()
    cache2 = np.ascontiguousarray(np.asarray(cache_kv, f32)[0].reshape(2560 * 128, 512))
    ptab = np.asarray(page_table).astype(np.int32)
    common.update({"cache": cache2, "cstsd": csts, "iotad": iota, "sqd": sqd, "smd": smd})
    in_maps = []
    for c in range(8):
        s, half = c // 2, c % 2
        xvv = np.concatenate([x_prompt[s], zt], 0) if half == 0 else np.concatenate([zt, x_prompt[s]], 0)
        m = dict(common)
        m["xv"] = np.ascontiguousarray(xvv)
        m["xsm"] = np.ascontiguousarray(x_sample[16 * c:16 * c + 16].reshape(128, D))
        m["cwin"] = np.ascontiguousarray(cache_win[0, 16 * c:16 * c + 16].reshape(16, 512, 256))
        m["srnn"] = np.ascontiguousarray(state_rnn[0, 16 * c:16 * c + 16])
        m["ptrep"] = np.ascontiguousarray(np.broadcast_to(ptab[16 * c:16 * c + 16].reshape(1, 256), (128, 256)))
        m["cstd"], m["masks4"], m["eexp"], m["kposrows"], m["qposrows"] = cc[half]
        in_maps.append(m)
    res = run_bass_kernel_spmd(nc, in_maps, core_ids=list(range(8))).results

    y_prompt = np.zeros((4, 4096, D), f32)
    y_sample = np.zeros((128, 8, D), f32)
    kv_prompt = np.zeros((1, 4, 4096, 4, 2, 64), f32)
    kv_sample = np.zeros((1, 128, 8, 4, 2, 64), f32)
    win_prompt = np.zeros((1, 4, 512, 2, 2, 64), f32)
    win_sample = np.zeros((1, 128, 512, 2, 2, 64), f32)
    rnn_prompt = np.zeros((1, 4, 8, 64, 64), f32)
    rnn_sample = np.zeros((1, 128, 8, 64, 64), f32)
    for c in range(8):
        r = res[c]
        s, half = c // 2, c % 2
        ypc = r["yp"].reshape(NOWN, 128, D)
        yps = y_prompt[s].reshape(32, 128, D)
        if half == 0:
            yps[0::2] = ypc[0:16]
            kv_prompt[0, s] = r["kvp"][:4096].reshape(4096, 4, 2, 64)
            win_prompt[0, s] = r["winp"][:512].reshape(512, 2, 2, 64)
        else:
            yps[1::2] = ypc[1:17]
            rnn_prompt[0, s] = r["rnnp"]
        kv_sample[0, 16 * c:16 * c + 16] = r["kvs"].reshape(16, 8, 4, 2, 64)
        y_sample[16 * c:16 * c + 16] = r["ys"].reshape(16, 8, D)
        win_sample[0, 16 * c:16 * c + 16] = r["wins"].reshape(16, 512, 2, 2, 64)
        rnn_sample[0, 16 * c:16 * c + 16] = r["rnns"]
    if _debug:
        kernel._dbg = [res[c]["dbg"] for c in range(8)]
    return (y_prompt, y_sample, kv_prompt, kv_sample, win_prompt, win_sample, rnn_prompt, rnn_sample)
```
